# Optimizing a Trainium2 kernel written in Bass

```python
import math
import jax, jax.numpy as jnp
from jax import lax
import numpy as np

D_MODEL = 2048
BATCH = 1
SEQ = 8192
DEPTH = 4

N_BRANCH = 4
BRANCH_WIDTH = D_MODEL // N_BRANCH
RW_HEAD = 64
RW_HEADS = BRANCH_WIDTH // RW_HEAD
RW_LORA = D_MODEL // 32
RW_COLS = 3 * BRANCH_WIDTH + 4 * RW_LORA
SHIFT_WIDTH = 3
GN_EPS = 64e-5
DIFF_HEAD = 64
DIFF_HEADS = BRANCH_WIDTH // (2 * DIFF_HEAD)
MLA_HEADS = 4
MLA_NOPE = 128
MLA_ROPE = 64
MLA_V = BRANCH_WIDTH // MLA_HEADS
MLA_Q_LORA = 3 * D_MODEL // 16
MLA_KV_LORA = D_MODEL // 8
MEM_LEN = 256
MEM_HEADS = 4
MEM_HEAD = BRANCH_WIDTH // MEM_HEADS
T5_BUCKETS = 32
T5_MAX_DIST = 128
ROPE_THETA = 10000.0
Q_BLOCK = 128
NORM_EPS = 1e-6
IN_WIDTHS = (RW_COLS, BRANCH_WIDTH, BRANCH_WIDTH, BRANCH_WIDTH,
             MLA_Q_LORA, MLA_KV_LORA, MLA_ROPE, BRANCH_WIDTH,
             N_BRANCH * BRANCH_WIDTH, N_BRANCH * D_MODEL)
N_IN = sum(IN_WIDTHS)

kernel_name = "hybrid_rwkv7_diffattn_mla_mem_encoder"


def _rms_norm(t, g):
    tf = t.astype(jnp.float32)
    y = tf * lax.rsqrt(jnp.mean(tf * tf, axis=-1, keepdims=True) + NORM_EPS)
    return (y * g.astype(jnp.float32)).astype(t.dtype)


def _split_cols(p, widths):
    offs = np.cumsum(np.array(widths))[:-1].tolist()
    return jnp.split(p, offs, axis=-1)


def _rope(t, cos, sin):
    t1, t2 = jnp.split(t, 2, axis=-1)
    return jnp.concatenate([t1 * cos - t2 * sin, t2 * cos + t1 * sin], axis=-1)


def _t5_bucket(rel):
    nb = T5_BUCKETS // 2
    max_exact = nb // 2
    n = jnp.abs(rel)
    n_f = jnp.maximum(n, max_exact).astype(jnp.float32)
    large = max_exact + (jnp.log(n_f / max_exact) / math.log(T5_MAX_DIST / max_exact)
                         * (nb - max_exact)).astype(jnp.int32)
    large = jnp.minimum(large, nb - 1)
    return jnp.where(rel > 0, nb, 0) + jnp.where(n < max_exact, n, large)


def _map_query_blocks(fn, *qs):
    B, S = qs[0].shape[:2]
    nblk = S // Q_BLOCK
    blocks = tuple(jnp.moveaxis(q.reshape((B, nblk, Q_BLOCK) + q.shape[2:]), 1, 0) for q in qs)
    out = lax.map(lambda a: fn(*a), blocks)
    return jnp.moveaxis(out, 0, 1).reshape((B, S) + out.shape[3:])


def _rwkv7_bidir(u, shift, w0, w_up, a0, a_up, k_k, k_a, r_k, gn_g, gn_b):
    B, S, _ = u.shape
    f32 = jnp.float32
    half = SHIFT_WIDTH // 2
    up = jnp.pad(u.astype(f32), ((0, 0), (half, half), (0, 0)))
    sh = shift.astype(f32)
    u = sum(sh[j] * up[:, j:j + S] for j in range(SHIFT_WIDTH))
    r, k, v, wd_f, wd_b, ad_f, ad_b = _split_cols(u, [BRANCH_WIDTH] * 3 + [RW_LORA] * 4)
    wd = jnp.stack([wd_f, wd_b])
    ad = jnp.stack([ad_f, ad_b])
    w_log = -jax.nn.softplus(-(w0.astype(f32)[:, None, None, :]
                               + jnp.einsum('nbsl,nlc->nbsc', jnp.tanh(wd), w_up.astype(f32)))) - 0.5
    decay = jnp.exp(-jnp.exp(w_log))
    a = jax.nn.sigmoid(a0.astype(f32)[:, None, None, :]
                       + jnp.einsum('nbsl,nlc->nbsc', ad, a_up.astype(f32)))

    def hd(t):
        return t.reshape(t.shape[:-1] + (RW_HEADS, RW_HEAD))

    kk = hd(k * k_k.astype(f32))
    kk = kk / jnp.maximum(jnp.sqrt(jnp.sum(kk * kk, axis=-1, keepdims=True)), 1e-12)
    kd = hd(k[None] * (1.0 + (a - 1.0) * k_a.astype(f32)))
    r_h, v_h = hd(r), hd(v)

    def both(t):
        return jnp.stack([t, t])

    def to_scan(t):
        t = jnp.stack([t[0], jnp.flip(t[1], axis=1)])
        return jnp.moveaxis(t, 2, 0)

    xs = tuple(to_scan(t) for t in (both(r_h), hd(decay), kd, both(v_h), both(kk), kk[None] * hd(a)))

    def step(state, inp):
        r_t, w_t, k_t, v_t, kk_t, b_t = inp
        sa = jnp.einsum('nbhij,nbhj->nbhi', state, -kk_t)
        state = (state * w_t[..., None, :] + sa[..., :, None] * b_t[..., None, :]
                 + v_t[..., :, None] * k_t[..., None, :])
        return state, jnp.einsum('nbhij,nbhj->nbhi', state, r_t)

    s0 = jnp.zeros((2, B, RW_HEADS, RW_HEAD, RW_HEAD), f32)
    _, ys = lax.scan(step, s0, xs)
    ys = jnp.moveaxis(ys, 0, 2)
    y = ys[0] + jnp.flip(ys[1], axis=1)
    mu = jnp.mean(y, axis=-1, keepdims=True)
    var = jnp.mean(jnp.square(y - mu), axis=-1, keepdims=True)
    y = (y - mu) * lax.rsqrt(var + GN_EPS) * hd(gn_g.astype(f32)) + hd(gn_b.astype(f32))
    bonus = jnp.sum(r_h[None] * kd * r_k.astype(f32), axis=-1, keepdims=True) * v_h[None]
    y = y + jnp.sum(bonus, axis=0)
    return y.reshape(B, S, BRANCH_WIDTH)


def _diff_attention(d_q, d_k, d_v, qk_g, lam, positions, rel_bias):
    B, S, _ = d_q.shape
    f32 = jnp.float32
    q = _rms_norm(d_q.reshape(B, S, DIFF_HEADS, 2, DIFF_HEAD).astype(f32), qk_g[0])
    k = _rms_norm(d_k.reshape(B, S, DIFF_HEADS, 2, DIFF_HEAD).astype(f32), qk_g[1])
    v = d_v.reshape(B, S, DIFF_HEADS, 2 * DIFF_HEAD).astype(f32)
    k1, k2 = k[..., 0, :], k[..., 1, :]
    table = rel_bias.astype(f32)
    scale = DIFF_HEAD ** -0.5

    def block(q1b, q2b, pb):
        rel = positions[:, None, :] - pb[:, :, None]
        bias = jnp.transpose(table[_t5_bucket(rel)], (0, 3, 1, 2))
        s1 = jnp.einsum('bqhd,bkhd->bhqk', q1b, k1) * scale + bias
        s2 = jnp.einsum('bqhd,bkhd->bhqk', q2b, k2) * scale + bias
        p = jax.nn.softmax(s1, axis=-1) - lam * jax.nn.softmax(s2, axis=-1)
        return jnp.einsum('bhqk,bkhe->bqhe', p, v)

    return _map_query_blocks(block, q[..., 0, :], q[..., 1, :], positions)


def _mla(q_lat, kv_lat, k_rope, q_lat_g, kv_lat_g, w_uq, w_ukv, nope_g, rope_g, cos, sin):
    B, S, _ = q_lat.shape
    f32 = jnp.float32
    q = (_rms_norm(q_lat, q_lat_g) @ w_uq).astype(f32).reshape(B, S, MLA_HEADS, MLA_NOPE + MLA_ROPE)
    kv = (_rms_norm(kv_lat, kv_lat_g) @ w_ukv).astype(f32).reshape(B, S, MLA_HEADS, MLA_NOPE + MLA_V)
    q_nope = _rms_norm(q[..., :MLA_NOPE], nope_g[0])
    q_rot = _rope(_rms_norm(q[..., MLA_NOPE:], rope_g[0]), cos[:, :, None, :], sin[:, :, None, :])
    k_nope = _rms_norm(kv[..., :MLA_NOPE], nope_g[1])
    v = kv[..., MLA_NOPE:]
    k_rot = _rope(_rms_norm(k_rope.astype(f32), rope_g[1]), cos, sin)
    scale = (MLA_NOPE + MLA_ROPE) ** -0.5

    def block(qn_b, qr_b):
        s = (jnp.einsum('bqhd,bkhd->bhqk', qn_b, k_nope)
             + jnp.einsum('bqhd,bkd->bhqk', qr_b, k_rot)) * scale
        p = jax.nn.softmax(s, axis=-1)
        return jnp.einsum('bhqk,bkhe->bqhe', p, v)

    out = _map_query_blocks(block, q_nope, q_rot)
    return out.reshape(B, S, BRANCH_WIDTH)


def _mem_attention(m_q, mem_n, w_kv, qk_g):
    B, S, _ = m_q.shape
    M = mem_n.shape[1]
    f32 = jnp.float32
    q = _rms_norm(m_q.reshape(B, S, MEM_HEADS, MEM_HEAD).astype(f32), qk_g[0])
    kv = (mem_n @ w_kv).astype(f32).reshape(B, M, 2, MEM_HEADS, MEM_HEAD)
    k = _rms_norm(kv[:, :, 0], qk_g[1])
    v = kv[:, :, 1]
    s = jnp.einsum('bqhd,bkhd->bhqk', q, k) * (MEM_HEAD ** -0.5)
    p = jax.nn.softmax(s, axis=-1)
    return jnp.einsum('bhqk,bkhe->bqhe', p, v).reshape(B, S, BRANCH_WIDTH)


def setup_inputs(seed: int = 0) -> dict:
    key = jax.random.key(seed)
    ks = iter(jax.random.split(key, 40))
    f32 = jnp.float32

    def nrm(shape, scale):
        return jax.random.normal(next(ks), shape, f32) * scale

    def gain(shape):
        return 1.0 + nrm(shape, 0.02)

    x = nrm((BATCH, SEQ, D_MODEL), 1.0)
    mem = nrm((BATCH, MEM_LEN, D_MODEL), 1.0)
    offs = jax.random.randint(next(ks), (BATCH, 1), 0, 1024, dtype=jnp.int32)
    positions = (jnp.arange(SEQ, dtype=jnp.int32)[None, :] + offs).astype(jnp.int32)
    norm_g = gain((DEPTH, D_MODEL))
    w_in = nrm((DEPTH, D_MODEL, N_IN), D_MODEL ** -0.5)
    rw_shift = jnp.array([0.25, 0.5, 0.25], f32)[None, :, None] + nrm((DEPTH, SHIFT_WIDTH, RW_COLS), 0.05)
    rw_w0 = jax.random.uniform(next(ks), (DEPTH, 2, BRANCH_WIDTH), f32, -3.0, 1.0)
    rw_w_up = nrm((DEPTH, 2, RW_LORA, BRANCH_WIDTH), 0.5 * RW_LORA ** -0.5)
    rw_a0 = nrm((DEPTH, 2, BRANCH_WIDTH), 0.1)
    rw_a_up = nrm((DEPTH, 2, RW_LORA, BRANCH_WIDTH), 0.5 * RW_LORA ** -0.5)
    rw_k_k = 0.85 + nrm((DEPTH, BRANCH_WIDTH), 0.05)
    rw_k_a = 1.0 + nrm((DEPTH, BRANCH_WIDTH), 0.05)
    rw_r_k = nrm((DEPTH, RW_HEADS, RW_HEAD), 0.1)
    rw_gn_g = gain((DEPTH, BRANCH_WIDTH))
    rw_gn_b = nrm((DEPTH, BRANCH_WIDTH), 0.02)
    diff_qk_g = gain((DEPTH, 2, DIFF_HEAD))
    diff_lambda = nrm((DEPTH, 4, DIFF_HEAD), 0.1)
    diff_sub_g = gain((DEPTH, 2 * DIFF_HEAD))
    rel_bias = nrm((T5_BUCKETS, DIFF_HEADS), 0.5)
    mla_q_lat_g = gain((DEPTH, MLA_Q_LORA))
    mla_kv_lat_g = gain((DEPTH, MLA_KV_LORA))
    mla_w_uq = nrm((DEPTH, MLA_Q_LORA, MLA_HEADS * (MLA_NOPE + MLA_ROPE)), MLA_Q_LORA ** -0.5)
    mla_w_ukv = nrm((DEPTH, MLA_KV_LORA, MLA_HEADS * (MLA_NOPE + MLA_V)), MLA_KV_LORA ** -0.5)
    mla_nope_g = gain((DEPTH, 2, MLA_NOPE))
    mla_rope_g = gain((DEPTH, 2, MLA_ROPE))
    mem_norm_g = gain((DEPTH, D_MODEL))
    mem_w_kv = nrm((DEPTH, D_MODEL, 2 * MEM_HEADS * MEM_HEAD), D_MODEL ** -0.5)
    mem_qk_g = gain((DEPTH, 2, MEM_HEAD))
    w_branch = nrm((DEPTH, N_BRANCH, BRANCH_WIDTH, D_MODEL), BRANCH_WIDTH ** -0.5)
    w_out = nrm((DEPTH, D_MODEL, D_MODEL), D_MODEL ** -0.5)
    return {"x": x, "mem": mem, "positions": positions, "norm_g": norm_g, "w_in": w_in,
            "rw_shift": rw_shift, "rw_w0": rw_w0, "rw_w_up": rw_w_up, "rw_a0": rw_a0,
            "rw_a_up": rw_a_up, "rw_k_k": rw_k_k, "rw_k_a": rw_k_a, "rw_r_k": rw_r_k,
            "rw_gn_g": rw_gn_g, "rw_gn_b": rw_gn_b, "diff_qk_g": diff_qk_g,
            "diff_lambda": diff_lambda, "diff_sub_g": diff_sub_g, "rel_bias": rel_bias,
            "mla_q_lat_g": mla_q_lat_g, "mla_kv_lat_g": mla_kv_lat_g, "mla_w_uq": mla_w_uq,
            "mla_w_ukv": mla_w_ukv, "mla_nope_g": mla_nope_g, "mla_rope_g": mla_rope_g,
            "mem_norm_g": mem_norm_g, "mem_w_kv": mem_w_kv, "mem_qk_g": mem_qk_g,
            "w_branch": w_branch, "w_out": w_out}


def reference(x, mem, positions, norm_g, w_in, rw_shift, rw_w0, rw_w_up, rw_a0, rw_a_up,
              rw_k_k, rw_k_a, rw_r_k, rw_gn_g, rw_gn_b, diff_qk_g, diff_lambda, diff_sub_g,
              rel_bias, mla_q_lat_g, mla_kv_lat_g, mla_w_uq, mla_w_ukv, mla_nope_g, mla_rope_g,
              mem_norm_g, mem_w_kv, mem_qk_g, w_branch, w_out):
    B, S, _ = x.shape
    f32 = jnp.float32
    inv_freq = ROPE_THETA ** (-jnp.arange(0, MLA_ROPE, 2, dtype=f32) / MLA_ROPE)
    ang = positions.astype(f32)[..., None] * inv_freq
    cos, sin = jnp.cos(ang), jnp.sin(ang)
    for l in range(DEPTH):
        h = _rms_norm(x, norm_g[l])
        p = h @ w_in[l]
        (rw_u, d_q, d_k, d_v, q_lat, kv_lat, k_rope, m_q, gate_in, merge_in) = _split_cols(p, IN_WIDTHS)
        y_a = _rwkv7_bidir(rw_u, rw_shift[l], rw_w0[l], rw_w_up[l], rw_a0[l], rw_a_up[l],
                           rw_k_k[l], rw_k_a[l], rw_r_k[l], rw_gn_g[l], rw_gn_b[l])
        lam_init = 0.8 - 0.6 * math.exp(-0.3 * l)
        lq = diff_lambda[l].astype(f32)
        lam = jnp.exp(jnp.sum(lq[0] * lq[1])) - jnp.exp(jnp.sum(lq[2] * lq[3])) + lam_init
        y_b = _diff_attention(d_q, d_k, d_v, diff_qk_g[l], lam, positions, rel_bias)
        y_b = (_rms_norm(y_b, diff_sub_g[l]) * (1.0 - lam_init)).reshape(B, S, BRANCH_WIDTH)
        y_c = _mla(q_lat, kv_lat, k_rope, mla_q_lat_g[l], mla_kv_lat_g[l], mla_w_uq[l],
                   mla_w_ukv[l], mla_nope_g[l], mla_rope_g[l], cos, sin)
        mem_n = _rms_norm(mem, mem_norm_g[l])
        y_d = _mem_attention(m_q, mem_n, mem_w_kv[l], mem_qk_g[l])
        gates = gate_in.reshape(B, S, N_BRANCH, BRANCH_WIDTH)
        merges = merge_in.reshape(B, S, N_BRANCH, D_MODEL)
        z = None
        for bi, y in enumerate((y_a, y_b, y_c, y_d)):
            zb = jax.nn.sigmoid(merges[:, :, bi]) * (
                (y.astype(x.dtype) * jax.nn.silu(gates[:, :, bi])) @ w_branch[l, bi])
            z = zb if z is None else z + zb
        x = x + z @ w_out[l]
    return x
```

```python
import math
from contextlib import ExitStack
import numpy as np
import concourse.bass as bass
import concourse.mybir as mybir
from concourse.bass_utils import run_bass_kernel_spmd

F32 = mybir.dt.float32
BF16 = mybir.dt.bfloat16
I32 = mybir.dt.int32
ALU = mybir.AluOpType
AF = mybir.ActivationFunctionType
AX = mybir.AxisListType
ENGS = ("pe", "act", "dve", "pool", "sp")
NCORES = 8


class _Op:
    __slots__ = ("eng", "fn", "waits", "signal", "dma_key", "idx", "sigval")

    def __init__(self, eng, fn, dma_key):
        self.eng = eng
        self.fn = fn
        self.waits = []
        self.signal = False
        self.dma_key = dma_key
        self.idx = None
        self.sigval = None


class _Res:
    __slots__ = ("w", "r")

    def __init__(self):
        self.w = None
        self.r = []


class Prog:
    def __init__(self, nc):
        self.nc = nc
        self.ops = {e: [] for e in ENGS}
        self.res = {}
        self.dma_cnt = {}
        self.dma_last = {}
        self.waited = {e: {} for e in ENGS}

    def _need(self, op, tok, isd):
        if tok is None:
            return
        kind, src, val = tok
        if kind == "e" and src == op.eng and not isd and src == "pe":
            return
        w = self.waited[op.eng]
        k = (kind, src)
        if w.get(k, -1) >= val:
            return
        w[k] = val
        op.waits.append(tok)
        if kind == "e":
            self.ops[src][val].signal = True

    def op(self, eng, fn, reads=(), writes=(), dma=None):
        o = _Op(eng, fn, dma)
        o.idx = len(self.ops[eng])
        isd = dma is not None
        if isd:
            n = self.dma_cnt.get(dma, 0) + 1
            self.dma_cnt[dma] = n
            self._need(o, self.dma_last.get(dma), True)
            tok = ("d", dma, n)
            self.dma_last[dma] = tok
        else:
            tok = ("e", eng, o.idx)
        for r in reads:
            st = self.res.setdefault(r, _Res())
            self._need(o, st.w, isd)
        for r in writes:
            st = self.res.setdefault(r, _Res())
            self._need(o, st.w, isd)
            for t in st.r:
                self._need(o, t, isd)
        for r in reads:
            st = self.res[r]
            st.r.append(tok)
            if len(st.r) > 48:
                st.r = st.r[-48:]
        for r in writes:
            st = self.res[r]
            st.w = tok
            st.r = []
        self.ops[eng].append(o)
        return tok

    def wait_tokens(self, eng, toks):
        o = _Op(eng, None, None)
        o.idx = len(self.ops[eng])
        for t in toks:
            self._need(o, t, True)
        self.ops[eng].append(o)

    def emit(self):
        nc = self.nc
        esem = {e: nc.alloc_semaphore(name=f"s_{e}") for e in ENGS}
        dsem = {k: nc.alloc_semaphore(name=f"d_{i}") for i, k in enumerate(self.dma_cnt)}
        for e in ENGS:
            c = 0
            for o in self.ops[e]:
                if o.signal:
                    c += 1
                    o.sigval = c
        ops = self.ops

        def body(e):
            def f(eng):
                for o in ops[e]:
                    for kind, src, val in o.waits:
                        if kind == "e":
                            eng.wait_ge(esem[src], ops[src][val].sigval)
                        else:
                            eng.wait_ge(dsem[src], 16 * val)
                    if o.fn is None:
                        continue
                    inst = o.fn(eng)
                    if o.dma_key is not None:
                        inst.then_inc(dsem[o.dma_key], 16)
                    elif o.signal:
                        inst.then_inc(esem[e], 1)
            return f

        with nc.Block() as block:
            block.tensor(body("pe"))
            block.scalar(body("act"))
            block.vector(body("dve"))
            block.gpsimd(body("pool"))
            block.sync(body("sp"))


class B:
    def __init__(self):
        self.nc = bass.Bass("TRN2", target_bir_lowering=False)
        self.P = Prog(self.nc)
        self.es = ExitStack()
        self.ps = [self.es.enter_context(self.nc.psum_tensor(f"ps{i}", [128, 512], F32)) for i in range(8)]
        self.psi = 0
        self.psrot = list(range(8))
        self.rings = {}
        self.outtoks = []
        self.ndq = 0
        self.attn_banks = [(4, 5), (6, 7)]
        self.attn_par = 0

    def din(self, name, shape, dt=F32):
        return self.nc.dram_tensor(name, list(shape), dt, kind="ExternalInput").ap()

    def dout(self, name, shape, dt=F32):
        return self.nc.dram_tensor(name, list(shape), dt, kind="ExternalOutput").ap()

    def sb(self, name, shape, dt=F32):
        return self.es.enter_context(self.nc.sbuf_tensor("s_" + name, list(shape), dt))

    def nps(self):
        rot = self.psrot
        i = rot[self.psi % len(rot)]
        self.psi += 1
        return i

    def ring(self, name, n, shape, dt=F32):
        self.rings[name] = [[self.sb(f"{name}{i}", shape, dt) for i in range(n)], 0]

    def nx(self, name):
        r = self.rings[name]
        i = r[1]
        r[1] = (i + 1) % len(r[0])
        return r[0][i], (name, i)

    def mm(self, pi, M, N, lhsT, rhs, st, sp, rd, po=0):
        out = self.ps[pi][po:po + M, 0:N]
        self.P.op("pe", lambda e: e.matmul(out, lhsT, rhs, start=st, stop=sp), reads=rd, writes=[("ps", pi)])

    def act(self, out, in_, func, rd, wr, bias=None, scale=None):
        kw = {}
        if bias is not None:
            kw["bias"] = bias
        if scale is not None:
            kw["scale"] = scale
        self.P.op("act", lambda e: e.activation(out=out, in_=in_, func=func, **kw), reads=rd, writes=wr)

    def stt(self, out, in0, scalar, in1, op0, op1, rd, wr):
        self.P.op("dve", lambda e: e.scalar_tensor_tensor(out=out, in0=in0, scalar=scalar, in1=in1,
                                                          op0=op0, op1=op1), reads=rd, writes=wr)

    def tt(self, eng, out, in0, in1, op, rd, wr):
        self.P.op(eng, lambda e: e.tensor_tensor(out=out, in0=in0, in1=in1, op=op), reads=rd, writes=wr)

    def ts(self, eng, out, in0, s1, s2, op0, op1, rd, wr):
        if op1 is None:
            self.P.op(eng, lambda e: e.tensor_scalar(out=out, in0=in0, scalar1=s1, scalar2=None, op0=op0),
                      reads=rd, writes=wr)
        else:
            self.P.op(eng, lambda e: e.tensor_scalar(out=out, in0=in0, scalar1=s1, scalar2=s2, op0=op0, op1=op1),
                      reads=rd, writes=wr)

    def cp(self, eng, out, in_, rd, wr):
        if eng == "act":
            self.P.op("act", lambda e: e.copy(out=out, in_=in_), reads=rd, writes=wr)
        else:
            self.P.op(eng, lambda e: e.tensor_copy(out=out, in_=in_), reads=rd, writes=wr)

    def recip(self, out, in_, rd, wr):
        self.P.op("dve", lambda e: e.reciprocal(out=out, in_=in_), reads=rd, writes=wr)

    def memset(self, eng, ap, val, wr):
        self.P.op(eng, lambda e: e.memset(ap, val), writes=wr)

    def load(self, out, in_, wr, key, q="sp"):
        return self.P.op(q, lambda e: e.dma_start(out=out, in_=in_), writes=wr, dma=key)

    def store(self, out, in_, rd, q="pool"):
        self.ndq += 1
        key = ("st", self.ndq % 6)
        t = self.P.op(q, lambda e: e.dma_start(out=out, in_=in_), reads=rd, dma=key)
        self.outtoks.append(t)

    def finish(self):
        last = {}
        for t in self.outtoks:
            last[t[1]] = t
        self.P.wait_tokens("pool", list(last.values()))
        self.P.emit()
        self.es.close()
        return self.nc


def fm_rstd(b, sq_list, ones_ap, Pn, N, inv_n, eps, consts_res):
    pi = b.nps()
    for i, (ap, res) in enumerate(sq_list):
        b.mm(pi, Pn, N, ones_ap, ap, i == 0, i == len(sq_list) - 1, [res, consts_res])
    ln, lnr = b.nx("t")
    b.act(ln[0:Pn, 0:N], b.ps[pi][0:Pn, 0:N], AF.Ln, [("ps", pi)], [lnr], bias=b.epsc[eps][0:Pn, :], scale=inv_n)
    rs, rsr = b.nx("t")
    b.act(rs[0:Pn, 0:N], ln[0:Pn, 0:N], AF.Exp, [lnr], [rsr], scale=-0.5)
    return rs, rsr


def setup_consts(b):
    b.ones = b.sb("ones", [128, 128])
    b.blk = b.sb("blk", [128, 128])
    b.memset("pool", b.ones[:], 1.0, ["ones"])
    b.memset("pool", b.blk[:], 0.0, ["blk"])
    b.memset("pool", b.blk[0:64, 0:64], 1.0, ["blk"])
    b.memset("pool", b.blk[64:128, 64:128], 1.0, ["blk"])
    b.onesb = b.sb("onesb", [128, 128], BF16)
    b.memset("pool", b.onesb[:], 1.0, ["onesb"])
    b.epsc = {}
    for i, v in enumerate((1e-6, 64e-5)):
        t = b.sb(f"epsc{i}", [128, 1])
        b.memset("pool", t[:], v, [f"epsc{i}"])
        b.epsc[v] = t
        b.P.res


def attn_core(b, qlist, kts, vts, nkt, scale, out_M, bias_fn=None, tag="a"):
    po, pz = b.attn_banks[b.attn_par]
    b.attn_par ^= 1
    for kt in range(nkt):
        pi = b.nps()
        ks = kts(kt)
        for i, ((qa, qr), (ka, kr)) in enumerate(zip(qlist, ks)):
            b.mm(pi, 128, 512, ka, qa, i == 0, i == len(qlist) - 1, [qr, kr])
        pt, ptr = b.nx("pt")
        if bias_fn is not None and bias_fn(kt) is not None:
            kind, val = bias_fn(kt)
            if kind == "const":
                b.act(pt[:], b.ps[pi][:, :], AF.Exp, [("ps", pi), "tab"], [ptr], bias=val, scale=scale)
            else:
                tmp, tr = b.nx("t")
                bap, bres = val
                b.stt(tmp[:], b.ps[pi][:, :], scale, bap, ALU.mult, ALU.add, [("ps", pi), bres], [tr])
                b.act(pt[:], tmp[:], AF.Exp, [tr], [ptr])
        else:
            b.act(pt[:], b.ps[pi][:, :], AF.Exp, [("ps", pi)], [ptr], scale=scale)
        va, vr = vts(kt)
        b.mm(po, out_M, 512, va, pt[:], kt == 0, kt == nkt - 1, [vr, ptr])
        b.mm(pz, out_M, 512, b.onesb[:, 0:out_M], pt[:], kt == 0, kt == nkt - 1, ["onesb", ptr])
    return po, pz


TT = 512
TH = TT + 2
NT1 = 2
PC = {}
_c = 0
for _n, _w in (("ng", 16), ("sh", 42), ("w0", 8), ("a0", 8), ("kk", 4), ("ka", 4), ("rk", 4), ("dqg", 1), ("dkg", 1),
               ("qlg", 3), ("kvg", 2), ("npg", 2), ("rpg", 2), ("rpgs", 2), ("mqg", 2), ("mng", 16), ("invf", 1),
               ("sgn", 1), ("omka", 4)):
    PC[_n] = _c
    _c += _w
NPAR = _c
RW0 = 0
def _p1_cols():
    cols = []
    cols += [(RW0 + 1536, 128), (RW0 + 1664, 128)]
    for c in range(4):
        cols += [(c * 128, 128), (512 + c * 128, 128), (1024 + c * 128, 128)]
    o = 1792
    cols += [(o + i * 128, 128) for i in range(4)]
    cols += [(o + 512 + i * 128, 128) for i in range(4)]
    o2 = 1792 + 1536
    cols += [(o2 + i * 128, 128) for i in range(3)]
    cols += [(o2 + 384 + i * 128, 128) for i in range(2)]
    cols += [("krope", 128)]
    o3 = o2 + 384 + 256 + 64
    cols += [(o3 + i * 128, 128) for i in range(4)]
    cols += [(1792 + 1024 + i * 128, 128) for i in range(4)]
    return cols
P1COLS = _p1_cols()
NCH1 = len(P1COLS)
KROPE0 = 1792 + 1536 + 384 + 256


def build_p1():
    b = B()
    P = b.P
    xT = b.din("xT", [NT1, 128, 16, TH])
    pos = b.din("pos", [NT1, TH], I32)
    par_d = b.din("par", [128, NPAR])
    w = b.din("w", [NCH1, 128, 16, 128])
    wup_d = b.din("wup", [128, 512])
    aup_d = b.din("aup", [128, 512])
    wuq_d = b.din("wuq", [128, 3, 768])
    wuqs_d = b.din("wuqs", [128, 3, 256])
    wukvk_d = b.din("wukvk", [128, 2, 512])
    wukvv_d = b.din("wukvv", [128, 2, 512])
    memT_d = b.din("memT", [128, 16, 256])
    wkv_d = b.din("wkv", [8, 128, 16, 128])
    o_rw = b.dout("o_rw", [9, 512, NT1 * TT])
    o_bonus = b.dout("o_bonus", [512, NT1 * TT])
    o_dq = b.dout("o_dq", [512, NT1 * TT])
    o_dk = b.dout("o_dk", [512, NT1 * TT])
    o_dv = b.dout("o_dv", [NT1 * TT, 512])
    o_mqn = b.dout("o_mqn", [512, NT1 * TT])
    o_mqr = b.dout("o_mqr", [256, NT1 * TT])
    o_mkn = b.dout("o_mkn", [512, NT1 * TT])
    o_mkr = b.dout("o_mkr", [64, NT1 * TT])
    o_mv = b.dout("o_mv", [NT1 * TT, 512])
    o_yd = b.dout("o_yd", [512, NT1 * TT])

    setup_consts(b)
    par = b.sb("par", [128, NPAR])
    b.load(par[:], par_d, ["par"], "par")
    pc = lambda n, i=0: par[:, PC[n] + i:PC[n] + i + 1]
    b.ts("dve", par[:, PC["omka"]:PC["omka"] + 4], par[:, PC["ka"]:PC["ka"] + 4], -1.0, 1.0, ALU.mult, ALU.add,
         ["par"], ["par"])
    xs = b.sb("xs", [128, 16, TH])
    hT = b.sb("hT", [128, 16, TH], BF16)
    wst = b.sb("wst", [128, 2, 16, 128])
    wbf = b.sb("wbf", [128, 2, 16, 128], BF16)
    stg = b.sb("stg", [128, 3072])
    wup = b.sb("wup_s", [128, 512])
    aup = b.sb("aup_s", [128, 512])
    wuq = b.sb("wuq_s", [128, 3, 768], BF16)
    wuqs = b.sb("wuqs_s", [128, 3, 256], BF16)
    wukvk = b.sb("wukvk_s", [128, 2, 512], BF16)
    wukvv = b.sb("wukvv_s", [128, 2, 512], BF16)
    memn = b.sb("memn", [128, 16, 256], BF16)
    kmem = b.sb("kmem", [128, 4, 256], BF16)
    vmem = b.sb("vmem", [128, 2, 512], BF16)
    b.ring("t", 14, [128, TT])
    b.ring("th", 3, [128, TH])
    b.ring("rkv", 4, [128, TT])
    b.ring("bf", 4, [128, TT], BF16)
    b.ring("pt", 3, [128, TT], BF16)
    twd = b.sb("twd", [128, TT])
    adl = b.sb("adl", [128, TT])
    ropC = b.sb("ropC", [64, TT])
    ropS = b.sb("ropS", [64, TT])
    rsx = b.sb("rsx", [128, TH])
    ql = b.sb("ql", [128, 3, TT])
    qln = b.sb("qln", [128, 3, TT], BF16)
    kvl = b.sb("kvl", [128, 2, TT])
    kvn = b.sb("kvn", [128, 2, TT], BF16)
    posi = b.sb("posi", [64, TH], I32)

    b.load(wup[:], wup_d, ["wup"], "wl0")
    b.load(aup[:], aup_d, ["aup"], "wl1")
    for dst, src, n, nm in ((wuq, wuq_d, 3 * 768, "wuq"), (wuqs, wuqs_d, 3 * 256, "wuqs"),
                            (wukvk, wukvk_d, 1024, "wukvk"), (wukvv, wukvv_d, 1024, "wukvv")):
        b.load(stg[:, 0:n], src.rearrange("p a b -> p (a b)"), ["stg"], "stg")
        b.cp("dve", dst[:].rearrange("p a b -> p (a b)"), stg[:, 0:n], ["stg"], [nm])

    def rmsnorm_cols(src_tile, nkc, N, gname, dst_tile, res_src, res_dst, inv_n):
        pi = b.nps()
        pih = b.nps() if N > 512 else None
        for kc in range(nkc):
            sq, sqr = b.nx("th")
            b.act(sq[:, 0:N], src_tile[:, kc, 0:N], AF.Square, [res_src], [sqr])
            b.mm(pi, 128, min(N, 512), b.ones[:], sq[:, 0:min(N, 512)], kc == 0, kc == nkc - 1, [sqr, "ones"])
            if pih is not None:
                b.mm(pih, 128, N - 512, b.ones[:], sq[:, 512:N], kc == 0, kc == nkc - 1, [sqr, "ones"])
        ln, lnr = b.nx("th")
        b.act(ln[:, 0:min(N, 512)], b.ps[pi][:, 0:min(N, 512)], AF.Ln, [("ps", pi)], [lnr],
              bias=b.epsc[1e-6][:], scale=inv_n)
        if pih is not None:
            b.act(ln[:, 512:N], b.ps[pih][:, 0:N - 512], AF.Ln, [("ps", pih)], [lnr], bias=b.epsc[1e-6][:], scale=inv_n)
        b.act(rsx[:, 0:N], ln[:, 0:N], AF.Exp, [lnr], ["rsx"], scale=-0.5)
        for kc in range(nkc):
            b.stt(dst_tile[:, kc, 0:N], src_tile[:, kc, 0:N], pc(gname, kc), rsx[:, 0:N], ALU.mult, ALU.mult,
                  [res_src, "rsx", "par"], [res_dst])

    b.load(xs[:, :, 0:256], memT_d, ["xs"], "xs")
    rmsnorm_cols(xs, 16, 256, "mng", memn, "xs", "memn", 1.0 / 2048)
    for ci in range(8):
        sl = ci % 2
        b.load(wst[:, sl], wkv_d[ci], [("wst", sl)], ("wst", sl))
        b.cp("pool", wbf[:, sl], wst[:, sl], [("wst", sl)], [("wbf", sl)])
        if ci < 4:
            pi = b.nps()
            for kc in range(16):
                b.mm(pi, 128, 256, wbf[:, sl, kc, :], memn[:, kc, :], kc == 0, kc == 15, [("wbf", sl), "memn"])
            tq, tqr = b.nx("t")
            b.cp("act", tq[:, 0:256], b.ps[pi][:, 0:256], [("ps", pi)], [tqr])
            sq, sqr = b.nx("t")
            b.act(sq[:, 0:256], b.ps[pi][:, 0:256], AF.Square, [("ps", pi)], [sqr])
            rs, rsr = fm_rstd(b, [(sq[:, 0:256], sqr)], b.ones[:], 128, 256, 1.0 / 128, 1e-6, "ones")
            b.stt(kmem[:, ci, :], tq[:, 0:256], pc("mqg", 1), rs[:, 0:256], ALU.mult, ALU.mult, [tqr, rsr, "par"], ["kmem"])
        else:
            for tb in range(2):
                pi = b.nps()
                for kc in range(16):
                    b.mm(pi, 128, 128, memn[:, kc, tb * 128:(tb + 1) * 128], wbf[:, sl, kc, :], kc == 0, kc == 15,
                         [("wbf", sl), "memn"])
                b.cp("act", vmem[:, tb, (ci - 4) * 128:(ci - 3) * 128], b.ps[pi][:, 0:128], [("ps", pi)], ["vmem"])

    def wload(tile, ci):
        sl = (tile * NCH1 + ci) % 2
        b.load(wst[:, sl], w[ci], [("wst", sl)], ("wst", sl))

    def shift(pi, pih, rc, dst, dres):
        u, ur = b.nx("th")
        b.cp("act", u[:, 0:TT], b.ps[pi][:, :], [("ps", pi)], [ur])
        b.cp("act", u[:, TT:TH], b.ps[pih][:, 0:2], [("ps", pih)], [ur])
        s0 = pc("sh", 0 * 14 + rc); s1 = pc("sh", 1 * 14 + rc); s2 = pc("sh", 2 * 14 + rc)
        b.ts("dve", dst[:, :], u[:, 0:TT], s1, None, ALU.mult, None, [ur, "par"], [dres])
        b.stt(dst[:, 1:TT], u[:, 0:TT - 1], s0, dst[:, 1:TT], ALU.mult, ALU.add, [ur, "par", dres], [dres])
        b.stt(dst[:, 0:1], u[:, TT:TT + 1], s0, dst[:, 0:1], ALU.mult, ALU.add, [ur, "par", dres], [dres])
        b.stt(dst[:, 0:TT - 1], u[:, 1:TT], s2, dst[:, 0:TT - 1], ALU.mult, ALU.add, [ur, "par", dres], [dres])
        b.stt(dst[:, TT - 1:TT], u[:, TT + 1:TT + 2], s2, dst[:, TT - 1:TT], ALU.mult, ALU.add, [ur, "par", dres], [dres])

    def head_norm(pi, Pn, ones_ap, inv_n, gcol, dst_ap, dres, N=TT):
        tq, tqr = b.nx("t")
        b.cp("act", tq[0:Pn, 0:N], b.ps[pi][0:Pn, 0:N], [("ps", pi)], [tqr])
        sq, sqr = b.nx("t")
        b.act(sq[0:Pn, 0:N], b.ps[pi][0:Pn, 0:N], AF.Square, [("ps", pi)], [sqr])
        rs, rsr = fm_rstd(b, [(sq[0:Pn, 0:N], sqr)], ones_ap, Pn, N, inv_n, 1e-6, "ones")
        b.stt(dst_ap, tq[0:Pn, 0:N], gcol, rs[0:Pn, 0:N], ALU.mult, ALU.mult, [tqr, rsr, "par"], [dres])
        return tq, tqr, rs, rsr

    for tile in range(NT1):
        t0 = tile * TT
        b.load(xs[:], xT[tile], ["xs"], "xs")
        b.load(posi[:], pos[tile, :].partition_broadcast(64), ["posi"], "posi")
        wload(tile, 0)
        rmsnorm_cols(xs, 16, TH, "ng", hT, "xs", "hT", 1.0 / 2048)
        posf, posr = b.nx("th")
        b.cp("dve", posf[0:64, :], posi[:], ["posi"], [posr])
        for which, dstt, dres in ((0, ropS, "ropS"), (1, ropC, "ropC")):
            a, ar = b.nx("t")
            b.ts("dve", a[0:64, :], posf[0:64, 0:TT], pc("invf")[0:64, :], (math.pi / 2 if which else 0.0),
                 ALU.mult, ALU.add, [posr, "par"], [ar])
            y, yr = b.nx("t")
            b.ts("dve", y[0:64, :], a[0:64, :], 1.0 / (2 * math.pi), None, ALU.mult, None, [ar], [yr])
            ni = b.sb(f"ni{tile}{which}", [64, TT], I32)
            b.cp("dve", ni[:], y[0:64, :], [yr], [f"ni{tile}{which}"])
            nf, nfr = b.nx("t")
            b.cp("dve", nf[0:64, :], ni[:], [f"ni{tile}{which}"], [nfr])
            r, rr = b.nx("t")
            b.stt(r[0:64, :], nf[0:64, :], -2 * math.pi, a[0:64, :], ALU.mult, ALU.add, [nfr, ar], [rr])
            m, mr = b.nx("t")
            b.ts("dve", m[0:64, :], r[0:64, :], math.pi, -2 * math.pi, ALU.is_gt, ALU.mult, [rr], [mr])
            b.tt("dve", r[0:64, :], r[0:64, :], m[0:64, :], ALU.add, [rr, mr], [rr])
            b.ts("dve", m[0:64, :], r[0:64, :], -math.pi, 2 * math.pi, ALU.is_lt, ALU.mult, [rr], [mr])
            b.tt("dve", r[0:64, :], r[0:64, :], m[0:64, :], ALU.add, [rr, mr], [rr])
            if which == 0:
                sn, snr = b.nx("t")
                b.act(sn[0:64, :], r[0:64, :], AF.Sin, [rr], [snr])
                b.ts("dve", ropS[:], sn[0:64, :], pc("sgn")[0:64, :], None, ALU.mult, None, [snr, "par"], ["ropS"])
            else:
                b.act(ropC[:], r[0:64, :], AF.Sin, [rr], ["ropC"])

        def rope_out(t_tq, t_r, sw_tq, sw_r, rs, rsr, gi, dst_dram):
            a, ar = b.nx("t")
            b.stt(a[0:64, :], t_tq[0:64, :], pc("rpg", gi)[0:64, :], ropC[:], ALU.mult, ALU.mult, [t_r, "par", "ropC"], [ar])
            c, cr = b.nx("t")
            b.stt(c[0:64, :], sw_tq[0:64, :], pc("rpgs", gi)[0:64, :], ropS[:], ALU.mult, ALU.mult, [sw_r, "par", "ropS"], [cr])
            b.tt("dve", a[0:64, :], a[0:64, :], c[0:64, :], ALU.add, [ar, cr], [ar])
            b.tt("dve", a[0:64, :], a[0:64, :], rs[0:64, :], ALU.mult, [ar, rsr], [ar])
            b.store(dst_dram, a[0:64, :], [ar])

        rcur = {}
        for ci in range(NCH1):
            sl = (tile * NCH1 + ci) % 2
            if ci + 1 < NCH1:
                wload(tile, ci + 1)
            elif tile + 1 < NT1:
                wload(tile + 1, 0)
            b.cp("pool", wbf[:, sl], wst[:, sl], [("wst", sl)], [("wbf", sl)])
            wr = ("wbf", sl)
            if ci >= 32:
                j = ci - 32
                for tb in range(4):
                    pi = b.nps()
                    for kc in range(16):
                        b.mm(pi, 128, 128, hT[:, kc, tb * 128:(tb + 1) * 128], wbf[:, sl, kc, :], kc == 0, kc == 15, [wr, "hT"])
                    o, orr = b.nx("t")
                    b.cp("act", o[:, 0:128], b.ps[pi][:, 0:128], [("ps", pi)], [orr])
                    b.store(o_dv[t0 + tb * 128:t0 + (tb + 1) * 128, j * 128:(j + 1) * 128], o[:, 0:128], [orr])
                continue
            if ci == 27:
                pis = []
                for hh in range(2):
                    pi = b.nps()
                    for kc in range(16):
                        b.mm(pi, 64, TT, wbf[:, sl, kc, hh * 64:(hh + 1) * 64], hT[:, kc, 0:TT], kc == 0, kc == 15, [wr, "hT"])
                    pis.append(pi)
                tq, tqr = b.nx("t")
                b.cp("act", tq[0:64, :], b.ps[pis[0]][0:64, :], [("ps", pis[0])], [tqr])
                sq, sqr = b.nx("t")
                b.act(sq[0:64, :], b.ps[pis[0]][0:64, :], AF.Square, [("ps", pis[0])], [sqr])
                sw, swr = b.nx("t")
                b.cp("act", sw[0:64, :], b.ps[pis[1]][0:64, :], [("ps", pis[1])], [swr])
                rs, rsr = fm_rstd(b, [(sq[0:64, :], sqr)], b.ones[0:64, 0:64], 64, TT, 1.0 / 64, 1e-6, "ones")
                rope_out(tq, tqr, sw, swr, rs, rsr, 1, o_mkr[:, t0:t0 + TT])
                continue
            pi = b.nps()
            for kc in range(16):
                b.mm(pi, 128, TT, wbf[:, sl, kc, :], hT[:, kc, 0:TT], kc == 0, kc == 15, [wr, "hT"])
            pih = None
            if ci < 14:
                pih = b.nps()
                for kc in range(16):
                    b.mm(pih, 128, 2, wbf[:, sl, kc, :], hT[:, kc, TT:TH], kc == 0, kc == 15, [wr, "hT"])
            if ci == 0:
                tmp, tr = b.nx("t")
                shift(pi, pih, 12, tmp, tr)
                b.act(twd[:], tmp[:], AF.Tanh, [tr], ["twd"])
            elif ci == 1:
                shift(pi, pih, 13, adl, "adl")
            elif ci < 14:
                c = (ci - 2) // 3
                kind = (ci - 2) % 3
                dst, dres = b.nx("rkv")
                shift(pi, pih, kind * 4 + c, dst, dres)
                rcur[kind] = (dst, dres)
                if kind == 0:
                    b.store(o_rw[0, c * 128:(c + 1) * 128, t0:t0 + TT], dst[:], [dres])
                if kind == 2:
                    r_t, r_r = rcur[0]
                    k_t, k_r = rcur[1]
                    v_t, v_r = rcur[2]
                    b.store(o_rw[1, c * 128:(c + 1) * 128, t0:t0 + TT], v_t[:], [v_r])
                    kr_, krr = b.nx("t")
                    b.ts("dve", kr_[:], k_t[:], pc("kk", c), None, ALU.mult, None, [k_r, "par"], [krr])
                    sq, sqr = b.nx("t")
                    b.act(sq[:], kr_[:], AF.Square, [krr], [sqr])
                    pj = b.nps()
                    b.mm(pj, 128, TT, b.blk[:], sq[:], True, True, [sqr, "blk"])
                    nr, nrr = b.nx("t")
                    b.act(nr[:], b.ps[pj][:, :], AF.Sqrt, [("ps", pj)], [nrr])
                    b.ts("dve", nr[:], nr[:], 1e-12, None, ALU.max, None, [nrr], [nrr])
                    b.recip(nr[:], nr[:], [nrr], [nrr])
                    kk_, kkr = b.nx("t")
                    b.tt("dve", kk_[:], kr_[:], nr[:], ALU.mult, [krr, nrr], [kkr])
                    b.store(o_rw[2, c * 128:(c + 1) * 128, t0:t0 + TT], kk_[:], [kkr])
                    kds = []
                    for d in range(2):
                        pw = b.nps()
                        b.mm(pw, 128, TT, wup[d * 64:(d + 1) * 64, c * 128:(c + 1) * 128], twd[d * 64:(d + 1) * 64, :],
                             True, True, ["wup", "twd"])
                        sg, sgr = b.nx("t")
                        b.act(sg[:], b.ps[pw][:, :], AF.Sigmoid, [("ps", pw), "par"], [sgr], bias=pc("w0", d * 4 + c))
                        dec, decr = b.nx("t")
                        b.act(dec[:], sg[:], AF.Exp, [sgr], [decr], scale=-math.exp(-0.5))
                        b.store(o_rw[5 + 3 * d, c * 128:(c + 1) * 128, t0:t0 + TT], dec[:], [decr])
                        pa = b.nps()
                        b.mm(pa, 128, TT, aup[d * 64:(d + 1) * 64, c * 128:(c + 1) * 128], adl[d * 64:(d + 1) * 64, :],
                             True, True, ["aup", "adl"])
                        a_, a_r = b.nx("t")
                        b.act(a_[:], b.ps[pa][:, :], AF.Sigmoid, [("ps", pa), "par"], [a_r], bias=pc("a0", d * 4 + c))
                        nb, nbr = b.nx("t")
                        b.stt(nb[:], kk_[:], -1.0, a_[:], ALU.mult, ALU.mult, [kkr, a_r], [nbr])
                        b.store(o_rw[3 + 3 * d, c * 128:(c + 1) * 128, t0:t0 + TT], nb[:], [nbr])
                        kd, kdr = b.nx("t")
                        b.ts("dve", kd[:], a_[:], pc("ka", c), pc("omka", c), ALU.mult, ALU.add, [a_r, "par"], [kdr])
                        b.tt("dve", kd[:], kd[:], k_t[:], ALU.mult, [kdr, k_r], [kdr])
                        b.store(o_rw[4 + 3 * d, c * 128:(c + 1) * 128, t0:t0 + TT], kd[:], [kdr])
                        kds.append((kd, kdr))
                    s_, s_r = b.nx("t")
                    b.tt("dve", s_[:], kds[0][0][:], kds[1][0][:], ALU.add, [kds[0][1], kds[1][1]], [s_r])
                    b.stt(s_[:], r_t[:], pc("rk", c), s_[:], ALU.mult, ALU.mult, [r_r, "par", s_r], [s_r])
                    pb_ = b.nps()
                    b.mm(pb_, 128, TT, b.blk[:], s_[:], True, True, [s_r, "blk"])
                    bo, bor = b.nx("t")
                    b.tt("dve", bo[:], v_t[:], b.ps[pb_][:, :], ALU.mult, [v_r, ("ps", pb_)], [bor])
                    b.store(o_bonus[c * 128:(c + 1) * 128, t0:t0 + TT], bo[:], [bor])
            elif ci < 22:
                isk = ci >= 18
                j = ci - (18 if isk else 14)
                o, orr = b.nx("t")
                head_norm(pi, 128, b.blk[:], 1.0 / 64, pc("dkg" if isk else "dqg"), o[:], orr)
                b.store((o_dk if isk else o_dq)[j * 128:(j + 1) * 128, t0:t0 + TT], o[:], [orr])
            elif ci < 25:
                j = ci - 22
                b.cp("act", ql[:, j, :], b.ps[pi][:, :], [("ps", pi)], ["ql"])
                if j == 2:
                    rmsnorm_cols(ql, 3, TT, "qlg", qln, "ql", "qln", 1.0 / 384)
                    for h in range(4):
                        pn = b.nps()
                        for kc in range(3):
                            b.mm(pn, 128, TT, wuq[:, kc, h * 192:h * 192 + 128], qln[:, kc, :], kc == 0, kc == 2, ["wuq", "qln"])
                        o, orr = b.nx("t")
                        head_norm(pn, 128, b.ones[:], 1.0 / 128, pc("npg", 0), o[:], orr)
                        b.store(o_mqn[h * 128:(h + 1) * 128, t0:t0 + TT], o[:], [orr])
                        pr = b.nps()
                        for kc in range(3):
                            b.mm(pr, 64, TT, wuq[:, kc, h * 192 + 128:h * 192 + 192], qln[:, kc, :], kc == 0, kc == 2, ["wuq", "qln"])
                        psw = b.nps()
                        for kc in range(3):
                            b.mm(psw, 64, TT, wuqs[:, kc, h * 64:(h + 1) * 64], qln[:, kc, :], kc == 0, kc == 2, ["wuqs", "qln"])
                        tq, tqr = b.nx("t")
                        b.cp("act", tq[0:64, :], b.ps[pr][0:64, :], [("ps", pr)], [tqr])
                        sq, sqr = b.nx("t")
                        b.act(sq[0:64, :], b.ps[pr][0:64, :], AF.Square, [("ps", pr)], [sqr])
                        sw, swr = b.nx("t")
                        b.cp("act", sw[0:64, :], b.ps[psw][0:64, :], [("ps", psw)], [swr])
                        rs, rsr = fm_rstd(b, [(sq[0:64, :], sqr)], b.ones[0:64, 0:64], 64, TT, 1.0 / 64, 1e-6, "ones")
                        rope_out(tq, tqr, sw, swr, rs, rsr, 0, o_mqr[h * 64:(h + 1) * 64, t0:t0 + TT])
            elif ci < 27:
                j = ci - 25
                b.cp("act", kvl[:, j, :], b.ps[pi][:, :], [("ps", pi)], ["kvl"])
                if j == 1:
                    rmsnorm_cols(kvl, 2, TT, "kvg", kvn, "kvl", "kvn", 1.0 / 256)
                    for h in range(4):
                        pn = b.nps()
                        for kc in range(2):
                            b.mm(pn, 128, TT, wukvk[:, kc, h * 128:(h + 1) * 128], kvn[:, kc, :], kc == 0, kc == 1, ["wukvk", "kvn"])
                        o, orr = b.nx("t")
                        head_norm(pn, 128, b.ones[:], 1.0 / 128, pc("npg", 1), o[:], orr)
                        b.store(o_mkn[h * 128:(h + 1) * 128, t0:t0 + TT], o[:], [orr])
                    for tb in range(4):
                        pv = b.nps()
                        for kc in range(2):
                            b.mm(pv, 128, 512, kvn[:, kc, tb * 128:(tb + 1) * 128], wukvv[:, kc, :], kc == 0, kc == 1, ["wukvv", "kvn"])
                        o, orr = b.nx("t")
                        b.cp("act", o[:], b.ps[pv][:, :], [("ps", pv)], [orr])
                        b.store(o_mv[t0 + tb * 128:t0 + (tb + 1) * 128, :], o[:], [orr])
            else:
                j = ci - 28
                qb, qbr = b.nx("bf")
                head_norm(pi, 128, b.ones[:], 1.0 / 128, pc("mqg", 0), qb[:], qbr)
                b.psrot = [0, 1, 2, 3]
                po, pz = attn_core(b, [(qb[:], qbr)],
                                   lambda kt, j=j: [(kmem[:, j, kt * 128:(kt + 1) * 128], "kmem")],
                                   lambda kt, j=j: (vmem[:, kt, j * 128:(j + 1) * 128], "vmem"),
                                   2, 128 ** -0.5, 128)
                b.psrot = list(range(8))
                rz, rzr = b.nx("t")
                b.recip(rz[:], b.ps[pz][:, :], [("ps", pz)], [rzr])
                o, orr = b.nx("t")
                b.tt("dve", o[:], b.ps[po][:, :], rz[:], ALU.mult, [("ps", po), rzr], [orr])
                b.store(o_yd[j * 128:(j + 1) * 128, t0:t0 + TT], o[:], [orr])
    return b.finish()


def _fm(a, nk):
    return np.ascontiguousarray(a.reshape(nk, 128, a.shape[1]).transpose(1, 0, 2))


def _col(par, name, arr, i=0):
    par[:arr.shape[0], PC[name] + i] = arr


def prep_p1(inp, l, x_cur):
    f = np.float32
    w_in = inp["w_in"][l]
    chunks = []
    for col0, n in P1COLS:
        if col0 == "krope":
            idx = [KROPE0 + j for j in range(64)] + [KROPE0 + (j + 32) % 64 for j in range(64)]
            W = w_in[:, idx]
        else:
            W = w_in[:, col0:col0 + 128]
        chunks.append(_fm(W, 16))
    w = np.stack(chunks)
    par = np.zeros((128, NPAR), f)
    par[:, PC["ng"]:PC["ng"] + 16] = inp["norm_g"][l].reshape(16, 128).T
    sh = inp["rw_shift"][l]
    for j in range(3):
        for rc in range(14):
            _col(par, "sh", sh[j, rc * 128:(rc + 1) * 128], j * 14 + rc)
    for d in range(2):
        for c in range(4):
            _col(par, "w0", inp["rw_w0"][l][d, c * 128:(c + 1) * 128], d * 4 + c)
            _col(par, "a0", inp["rw_a0"][l][d, c * 128:(c + 1) * 128], d * 4 + c)
    rk = inp["rw_r_k"][l].reshape(512)
    for c in range(4):
        _col(par, "kk", inp["rw_k_k"][l][c * 128:(c + 1) * 128], c)
        _col(par, "ka", inp["rw_k_a"][l][c * 128:(c + 1) * 128], c)
        _col(par, "rk", rk[c * 128:(c + 1) * 128], c)
    _col(par, "dqg", np.tile(inp["diff_qk_g"][l][0], 2))
    _col(par, "dkg", np.tile(inp["diff_qk_g"][l][1], 2))
    par[:, PC["qlg"]:PC["qlg"] + 3] = inp["mla_q_lat_g"][l].reshape(3, 128).T
    par[:, PC["kvg"]:PC["kvg"] + 2] = inp["mla_kv_lat_g"][l].reshape(2, 128).T
    par[:, PC["npg"]:PC["npg"] + 2] = inp["mla_nope_g"][l].T
    for gi in range(2):
        g = inp["mla_rope_g"][l][gi]
        _col(par, "rpg", g, gi)
        _col(par, "rpgs", np.concatenate([g[32:], g[:32]]), gi)
    par[:, PC["mqg"]:PC["mqg"] + 2] = inp["mem_qk_g"][l].T
    par[:, PC["mng"]:PC["mng"] + 16] = inp["mem_norm_g"][l].reshape(16, 128).T
    invf = (10000.0 ** (-np.arange(0, 64, 2, dtype=np.float32) / 64)).astype(f)
    _col(par, "invf", np.tile(invf, 2))
    _col(par, "sgn", np.concatenate([-np.ones(32, f), np.ones(32, f)]))
    wuq = inp["mla_w_uq"][l]
    sw_idx = [h * 192 + 128 + (j + 32) % 64 for h in range(4) for j in range(64)]
    wukv = inp["mla_w_ukv"][l].reshape(256, 4, 256)
    common = {
        "par": par, "w": w,
        "wup": np.ascontiguousarray(inp["rw_w_up"][l].reshape(128, 512)),
        "aup": np.ascontiguousarray(inp["rw_a_up"][l].reshape(128, 512)),
        "wuq": _fm(wuq, 3), "wuqs": _fm(wuq[:, sw_idx], 3),
        "wukvk": _fm(np.ascontiguousarray(wukv[:, :, :128]).reshape(256, 512), 2),
        "wukvv": _fm(np.ascontiguousarray(wukv[:, :, 128:]).reshape(256, 512), 2),
        "memT": np.ascontiguousarray(inp["mem"][0].reshape(256, 16, 128).transpose(2, 1, 0)),
        "wkv": np.stack([_fm(inp["mem_w_kv"][l][:, ci * 128:(ci + 1) * 128], 16) for ci in range(8)]),
    }
    S = x_cur.shape[0]
    xpad = np.concatenate([np.zeros((1, 2048), f), x_cur, np.zeros((1, 2048), f)], 0)
    posv = inp["positions"][0]
    maps = []
    for c in range(NCORES):
        xt = np.empty((NT1, 128, 16, TH), f)
        pp = np.zeros((NT1, TH), np.int32)
        for t in range(NT1):
            n0 = c * (NT1 * TT) + t * TT
            xe = np.concatenate([xpad[n0 + 1:n0 + 1 + TT], xpad[n0:n0 + 1], xpad[n0 + 1 + TT:n0 + 2 + TT]], 0)
            xt[t] = xe.reshape(TH, 16, 128).transpose(2, 1, 0)
            pp[t, :TT] = posv[n0:n0 + TT]
        m = dict(common)
        m["xT"] = xt
        m["pos"] = pp
        maps.append(m)
    return maps


_NC_CACHE = {}


def _run(name, builder, maps):
    if name not in _NC_CACHE:
        _NC_CACHE[name] = builder()
    res = run_bass_kernel_spmd(_NC_CACHE[name], maps, core_ids=list(range(NCORES)))
    return res.results


def _cat(results, key, axis):
    return np.concatenate([r[key] for r in results], axis=axis)


SEQ = 8192
SC_CH = 32
SC_NV = 5


def build_scan(T=SEQ):
    b = B()
    P = b.P
    bc = b.din("bc", [2, T, SC_NV * 64])
    vcol = b.din("vcol", [128, T])
    y = b.dout("y", [128, T])
    nch = T // SC_CH
    W = SC_NV * 64
    bcb = b.sb("bcb", [128, 2, SC_CH * W])
    vc = b.sb("vc", [128, T])
    yt = b.sb("yt", [128, T])
    S = b.sb("S", [128, 64])
    Sd = b.sb("Sd", [128, 64])
    sa = b.sb("sa", [128, 1])
    b.ring("ja", 2, [128, 64])
    b.ring("jb", 2, [128, 64])
    b.load(vc[:], vcol, ["vc"], "vc")
    b.memset("dve", S[:], 0.0, ["S"])

    def load(c):
        sl = c % 2
        for pr in range(2):
            src = bc[pr, c * SC_CH:(c + 1) * SC_CH, :].rearrange("t f -> (t f)").partition_broadcast(64)
            b.load(bcb[pr * 64:(pr + 1) * 64, sl, :], src, [("bcb", sl, pr)], ("bc", sl, pr))

    load(0)
    for c in range(nch):
        if c + 1 < nch:
            load(c + 1)
        sl = c % 2
        rd = [("bcb", sl, 0), ("bcb", sl, 1)]
        for s in range(SC_CH):
            t = c * SC_CH + s
            o = s * W
            kap = bcb[:, sl, o:o + 64]
            nb = bcb[:, sl, o + 64:o + 128]
            kd = bcb[:, sl, o + 128:o + 192]
            rt = bcb[:, sl, o + 192:o + 256]
            dec = bcb[:, sl, o + 256:o + 320]
            ja, jar = b.nx("ja")
            P.op("dve", lambda e, kap=kap, ja=ja: e.scalar_tensor_tensor(
                out=ja[:], in0=S[:], scalar=1.0, in1=kap, op0=ALU.mult, op1=ALU.mult, accum_out=sa[:]),
                reads=rd + ["S"], writes=[jar, "sa"])
            b.tt("dve", Sd[:], S[:], dec, ALU.mult, rd + ["S"], ["Sd"])
            b.stt(Sd[:], nb, sa[:, 0:1], Sd[:], ALU.mult, ALU.add, rd + ["Sd", "sa"], ["Sd"])
            b.stt(S[:], kd, vc[:, t:t + 1], Sd[:], ALU.mult, ALU.add, rd + ["Sd", "vc"], ["S"])
            jb, jbr = b.nx("jb")
            P.op("dve", lambda e, rt=rt, jb=jb, t=t: e.scalar_tensor_tensor(
                out=jb[:], in0=S[:], scalar=1.0, in1=rt, op0=ALU.mult, op1=ALU.mult, accum_out=yt[:, t:t + 1]),
                reads=rd + ["S"], writes=[jbr, "yt"])
    b.store(y, yt[:], ["yt"])
    return b.finish()


def prep_scan(o_rw, T=SEQ):
    maps = []
    for h in range(NCORES):
        sl = slice(h * 64, (h + 1) * 64)
        bc = np.empty((2, T, SC_NV * 64), np.float32)
        for d in range(2):
            parts = [o_rw[2][sl], o_rw[3 + 3 * d][sl], o_rw[4 + 3 * d][sl], o_rw[0][sl], o_rw[5 + 3 * d][sl]]
            a = np.concatenate([p.T for p in parts], axis=1)
            bc[d] = a if d == 0 else a[::-1]
        v = o_rw[1][sl]
        vcol = np.concatenate([v, v[:, ::-1]], 0)
        maps.append({"bc": bc, "vcol": np.ascontiguousarray(vcol)})
    return maps


def post_scan(results):
    yf = np.concatenate([r["y"][0:64] for r in results], 0)
    yb = np.concatenate([r["y"][64:128][:, ::-1] for r in results], 0)
    return np.ascontiguousarray(yf), np.ascontiguousarray(yb)


NQ = SEQ // 2
NKT = SEQ // 128


def _t5_breaks():
    nb, max_exact = 16, 8
    n = np.arange(0, 1024, dtype=np.int32)
    n_f = np.maximum(n, max_exact).astype(np.float32)
    large = max_exact + (np.log(n_f / np.float32(max_exact)) / np.float32(math.log(128 / max_exact))
                         * np.float32(nb - max_exact)).astype(np.int32)
    large = np.minimum(large, nb - 1)
    f = np.where(n < max_exact, n, large)
    rels = np.arange(-1023, 1024)
    bk = np.where(rels > 0, 16, 0) + f[np.abs(rels)]
    order = [int(bk[0])]
    breaks = []
    for i in range(1, len(rels)):
        if bk[i] != bk[i - 1]:
            order.append(int(bk[i]))
            breaks.append(int(rels[i]))
    return order, breaks


T5_ORDER, T5_BREAKS = _t5_breaks()
NBK = len(T5_ORDER)


def build_attn(kind):
    b = B()
    P = b.P
    diff = kind == "diff"
    qa_d = b.din("qa", [128, NQ])
    ka_d = b.din("ka", [128, SEQ])
    v_d = b.din("v", [128, NKT, 128])
    if diff:
        tab_d = b.din("tab", [1, NBK])
        lq_d = b.din("lq", [1, 256])
        cst_d = b.din("cst", [128, 2])
    else:
        qb_d = b.din("qb", [64, NQ])
        kb_d = b.din("kb", [64, SEQ])
    o_d = b.dout("o", [128, NQ])
    setup_consts(b)
    ka = b.sb("ka", [128, SEQ], BF16)
    qa = b.sb("qa", [128, NQ], BF16)
    vv = b.sb("vv", [128, NKT, 128], BF16)
    b.ring("stg", 2, [128, 2048])
    b.ring("t", 6, [128, 512])
    b.ring("pt", 4, [128, 512], BF16)
    b.ring("o", 2, [128, 512])

    def load_cast(dst, src, Pn, n, res):
        for i in range(0, n, 2048):
            st, sr = b.nx("stg")
            b.load(st[0:Pn, :], src[:, i:i + 2048], [sr], sr)
            b.cp("pool", dst[0:Pn, i:i + 2048], st[0:Pn, :], [sr], [res])

    load_cast(ka, ka_d, 128, SEQ, "ka")
    load_cast(qa, qa_d, 128, NQ, "qa")
    load_cast(vv[:].rearrange("p a b -> p (a b)"), v_d.rearrange("p a b -> p (a b)"), 128, NKT * 128, "vv")
    if not diff:
        kb = b.sb("kb", [64, SEQ], BF16)
        qb = b.sb("qb", [64, NQ], BF16)
        load_cast(kb, kb_d, 64, SEQ, "kb")
        load_cast(qb, qb_d, 64, NQ, "qb")
    else:
        lq = b.sb("lq", [128, 256])
        cst = b.sb("cst", [128, 2])
        tab = b.sb("tab", [128, NBK])
        b.load(lq[:], lq_d[0, :].partition_broadcast(128), ["lq"], "lq")
        b.load(cst[:], cst_d, ["cst"], "cst")
        b.load(tab[:], tab_d[0, :].partition_broadcast(128), ["tab"], "tab")
        pr = b.sb("lpr", [128, 128])
        b.tt("dve", pr[:, 0:64], lq[:, 0:64], lq[:, 64:128], ALU.mult, ["lq"], ["lpr"])
        b.tt("dve", pr[:, 64:128], lq[:, 128:192], lq[:, 192:256], ALU.mult, ["lq"], ["lpr"])
        ls = b.sb("ls", [128, 4])
        P.op("dve", lambda e: e.tensor_reduce(out=ls[:, 0:1], in_=pr[:, 0:64], axis=AX.X, op=ALU.add), reads=["lpr"], writes=["ls"])
        P.op("dve", lambda e: e.tensor_reduce(out=ls[:, 1:2], in_=pr[:, 64:128], axis=AX.X, op=ALU.add), reads=["lpr"], writes=["ls"])
        b.act(ls[:, 0:2], ls[:, 0:2], AF.Exp, ["ls"], ["ls"])
        b.tt("dve", ls[:, 2:3], ls[:, 0:1], ls[:, 1:2], ALU.subtract, ["ls"], ["ls"])
        b.tt("dve", ls[:, 2:3], ls[:, 2:3], cst[:, 0:1], ALU.add, ["ls", "cst"], ["ls"])
        b.ts("dve", ls[:, 3:4], ls[:, 2:3], -1.0, None, ALU.mult, None, ["ls"], ["ls"])
        dl = b.sb("dl", [128, NBK])
        b.tt("dve", dl[:, 1:NBK], tab[:, 1:NBK], tab[:, 0:NBK - 1], ALU.subtract, ["tab"], ["dl"])
        reli = b.sb("reli", [128, 512], I32)
        relf = b.sb("relf", [128, 512])
        P.op("pool", lambda e: e.iota(reli[:], [[-1, 512]], base=0, channel_multiplier=1), writes=["reli"])
        b.cp("dve", relf[:], reli[:], ["reli"], ["relf"])
        bias6 = b.sb("bias6", [128, 6, 512])
        for dk in range(6):
            off = float((dk - 1) * 128)
            for k in range(1, NBK):
                tmp, tr = b.nx("t")
                b.ts("dve", tmp[:], relf[:], float(T5_BREAKS[k - 1]) - off, dl[:, k:k + 1], ALU.is_ge, ALU.mult,
                     ["relf", "dl"], [tr])
                if k == 1:
                    b.ts("pool", bias6[:, dk, :], tmp[:], tab[:, 0:1], None, ALU.add, None, [tr, "tab"], [("b6", dk)])
                else:
                    b.tt("pool", bias6[:, dk, :], bias6[:, dk, :], tmp[:], ALU.add, [tr, ("b6", dk)], [("b6", dk)])

    scale = (64 ** -0.5) if diff else (192 ** -0.5)
    b.psrot = [0, 1, 2, 3]
    for qt in range(NQ // 512):
        qs = slice(qt * 512, (qt + 1) * 512)
        res_list = []
        for s in range(2 if diff else 1):
            if diff:
                ps_ = slice(s * 64, (s + 1) * 64)
                qlist = [(qa[ps_, qs], "qa")]
                kts = lambda kt, ps_=ps_: [(ka[ps_, kt * 128:(kt + 1) * 128], "ka")]

                def bias_fn(kt, qt=qt):
                    dk = kt - 4 * qt
                    if dk < -1:
                        return ("const", tab[:, 0:1])
                    if dk > 4:
                        return ("const", tab[:, NBK - 1:NBK])
                    return ("tile", (bias6[:, dk + 1, :], ("b6", dk + 1)))
            else:
                qlist = [(qa[:, qs], "qa"), (qb[:, qs], "qb")]
                kts = lambda kt: [(ka[:, kt * 128:(kt + 1) * 128], "ka"), (kb[:, kt * 128:(kt + 1) * 128], "kb")]
            vts = lambda kt: (vv[:, kt, :], "vv")
            if diff:
                res_list.append(attn_core(b, qlist, kts, vts, NKT, scale, 128, bias_fn=bias_fn))
            else:
                res_list.append(attn_core(b, qlist, kts, vts, NKT, scale, 128))
        po, pz = res_list[0]
        rz, rzr = b.nx("t")
        b.recip(rz[:], b.ps[pz][:, :], [("ps", pz)], [rzr])
        o, orr = b.nx("o")
        b.tt("dve", o[:], b.ps[po][:, :], rz[:], ALU.mult, [("ps", po), rzr], [orr])
        if diff:
            po2, pz2 = res_list[1]
            rz2, rz2r = b.nx("t")
            b.recip(rz2[:], b.ps[pz2][:, :], [("ps", pz2)], [rz2r])
            o2, o2r = b.nx("t")
            b.tt("dve", o2[:], b.ps[po2][:, :], rz2[:], ALU.mult, [("ps", po2), rz2r], [o2r])
            b.stt(o[:], o2[:], ls[:, 3:4], o[:], ALU.mult, ALU.add, [o2r, "ls", orr], [orr])
        b.store(o_d[:, qs], o[:], [orr])
    return b.finish()


def _vtiles(v):
    return np.ascontiguousarray(v.reshape(-1, 128, 128).transpose(1, 0, 2))


def prep_attn_diff(inp, l, o_dq, o_dk, o_dv):
    lam_init = 0.8 - 0.6 * math.exp(-0.3 * l)
    maps = []
    for c in range(NCORES):
        h, half = c // 2, c % 2
        rows = slice(h * 128, (h + 1) * 128)
        q, k, v = o_dq[rows], o_dk[rows], o_dv[:, rows]
        tab = inp["rel_bias"][T5_ORDER, h]
        if half == 1:
            q, k, v, tab = q[:, ::-1], k[:, ::-1], v[::-1], tab[::-1]
        cst = np.zeros((128, 2), np.float32)
        cst[:, 0] = lam_init
        maps.append({"qa": np.ascontiguousarray(q[:, :NQ]), "ka": np.ascontiguousarray(k), "v": _vtiles(np.ascontiguousarray(v)),
                     "tab": np.ascontiguousarray(tab.reshape(1, NBK)).astype(np.float32),
                     "lq": np.ascontiguousarray(inp["diff_lambda"][l].reshape(1, 256)), "cst": cst})
    return maps


def post_attn_diff(results):
    out = np.empty((512, SEQ), np.float32)
    for c in range(NCORES):
        h, half = c // 2, c % 2
        o = results[c]["o"]
        if half == 0:
            out[h * 128:(h + 1) * 128, :NQ] = o
        else:
            out[h * 128:(h + 1) * 128, NQ:] = o[:, ::-1]
    return out


def prep_attn_mla(o_mqn, o_mqr, o_mkn, o_mkr, o_mv):
    maps = []
    for c in range(NCORES):
        h, half = c // 2, c % 2
        qs = slice(half * NQ, (half + 1) * NQ)
        maps.append({"qa": np.ascontiguousarray(o_mqn[h * 128:(h + 1) * 128, qs]),
                     "qb": np.ascontiguousarray(o_mqr[h * 64:(h + 1) * 64, qs]),
                     "ka": np.ascontiguousarray(o_mkn[h * 128:(h + 1) * 128]),
                     "kb": np.ascontiguousarray(o_mkr),
                     "v": _vtiles(np.ascontiguousarray(o_mv[:, h * 128:(h + 1) * 128]))})
    return maps


def post_attn_mla(results):
    out = np.empty((512, SEQ), np.float32)
    for c in range(NCORES):
        h, half = c // 2, c % 2
        out[h * 128:(h + 1) * 128, half * NQ:(half + 1) * NQ] = results[c]["o"]
    return out


PC3 = {}
_c = 0
for _n, _w in (("ng", 16), ("gng", 4), ("gnb", 4), ("subg", 1), ("lamf", 1)):
    PC3[_n] = _c
    _c += _w
NPAR3 = _c
NCH3 = 16 + 16 * 4 + 16


def build_p3():
    b = B()
    P = b.P
    xT = b.din("xT", [NT1, 128, 16, TT])
    par_d = b.din("par", [128, NPAR3])
    wA = b.din("wA", [NCH3, 128, 16, 128])
    wB = b.din("wB", [16, 128, 16, 128])
    br_d = b.din("br", [NT1, 128, 6, 4, TT])
    o_x = b.dout("o_x", [NT1, 128, 16, TT])
    setup_consts(b)
    par = b.sb("par", [128, NPAR3])
    b.load(par[:], par_d, ["par"], "par")
    pc = lambda n, i=0: par[:, PC3[n] + i:PC3[n] + i + 1]
    xs = b.sb("xs", [128, 16, TT])
    hT = b.sb("hT", [128, 16, TT], BF16)
    br = b.sb("br", [128, 6, 4, TT])
    yg = b.sb("yg", [128, 16, TT], BF16)
    zT = b.sb("zT", [128, 16, TT], BF16)
    wstA = b.sb("wstA", [128, 2, 16, 128])
    wbfA = b.sb("wbfA", [128, 2, 16, 128], BF16)
    wstB = b.sb("wstB", [128, 2, 16, 128])
    wbfB = b.sb("wbfB", [128, 2, 16, 128], BF16)
    rsx = b.sb("rsx", [128, TT])
    zacc = b.sb("zacc", [128, TT])
    b.ring("t", 10, [128, TT])
    cntA = [0]
    cntB = [0]

    def loadA(ci):
        sl = cntA[0] % 2
        cntA[0] += 1
        b.load(wstA[:, sl], wA[ci], [("wstA", sl)], ("wstA", sl))
        return sl

    def loadB(ci):
        sl = cntB[0] % 2
        cntB[0] += 1
        b.load(wstB[:, sl], wB[ci], [("wstB", sl)], ("wstB", sl))
        return sl

    for tile in range(NT1):
        b.load(xs[:], xT[tile], ["xs"], "xs")
        b.load(br[:], br_d[tile], ["br"], "br", q="sp")
        pendA = loadA(0)
        pi = b.nps()
        for kc in range(16):
            sq, sqr = b.nx("t")
            b.act(sq[:], xs[:, kc, :], AF.Square, ["xs"], [sqr])
            b.mm(pi, 128, TT, b.ones[:], sq[:], kc == 0, kc == 15, [sqr, "ones"])
        ln, lnr = b.nx("t")
        b.act(ln[:], b.ps[pi][:, :], AF.Ln, [("ps", pi)], [lnr], bias=b.epsc[1e-6][:], scale=1.0 / 2048)
        b.act(rsx[:], ln[:], AF.Exp, [lnr], ["rsx"], scale=-0.5)
        for kc in range(16):
            b.stt(hT[:, kc, :], xs[:, kc, :], pc("ng", kc), rsx[:], ALU.mult, ALU.mult, ["xs", "rsx", "par"], ["hT"])
        for kc in range(4):
            ys, ysr = b.nx("t")
            b.tt("pool", ys[:], br[:, 0, kc, :], br[:, 1, kc, :], ALU.add, ["br"], [ysr])
            sq, sqr = b.nx("t")
            b.act(sq[:], ys[:], AF.Square, [ysr], [sqr])
            pm = b.nps()
            b.mm(pm, 128, TT, b.blk[:], ys[:], True, True, [ysr, "blk"])
            pe2 = b.nps()
            b.mm(pe2, 128, TT, b.blk[:], sq[:], True, True, [sqr, "blk"])
            mean, mr = b.nx("t")
            b.act(mean[:], b.ps[pm][:, :], AF.Copy, [("ps", pm)], [mr], scale=1.0 / 64)
            msq, msr = b.nx("t")
            b.act(msq[:], mean[:], AF.Square, [mr], [msr])
            var, vr = b.nx("t")
            b.stt(var[:], b.ps[pe2][:, :], 1.0 / 64, msq[:], ALU.mult, ALU.subtract, [("ps", pe2), msr], [vr])
            b.act(var[:], var[:], AF.Ln, [vr], [vr], bias=b.epsc[64e-5][:])
            b.act(var[:], var[:], AF.Exp, [vr], [vr], scale=-0.5)
            b.tt("pool", ys[:], ys[:], mean[:], ALU.subtract, [ysr, mr], [ysr])
            b.stt(ys[:], ys[:], pc("gng", kc), var[:], ALU.mult, ALU.mult, [ysr, "par", vr], [ysr])
            b.stt(br[:, 0, kc, :], ys[:], pc("gnb", kc), br[:, 2, kc, :], ALU.add, ALU.add, [ysr, "par", "br"], ["br"])
            sq2, sq2r = b.nx("t")
            b.act(sq2[:], br[:, 3, kc, :], AF.Square, ["br"], [sq2r])
            rs, rsr = fm_rstd(b, [(sq2[:], sq2r)], b.ones[:], 128, TT, 1.0 / 128, 1e-6, "ones")
            b.stt(br[:, 3, kc, :], br[:, 3, kc, :], pc("subg"), rs[:], ALU.mult, ALU.mult, ["br", "par", rsr], ["br"])
            b.ts("pool", br[:, 3, kc, :], br[:, 3, kc, :], pc("lamf"), None, ALU.mult, None, ["br", "par"], ["br"])
        ysrc = [0, 3, 4, 5]
        nxt = 1
        for g in range(16):
            sl = pendA
            if nxt < NCH3:
                pendA = loadA(nxt)
                nxt += 1
            b.cp("pool", wbfA[:, sl], wstA[:, sl], [("wstA", sl)], [("wbfA", sl)])
            pi = b.nps()
            for kc in range(16):
                b.mm(pi, 128, TT, wbfA[:, sl, kc, :], hT[:, kc, :], kc == 0, kc == 15, [("wbfA", sl), "hT"])
            sg, sgr = b.nx("t")
            b.act(sg[:], b.ps[pi][:, :], AF.Silu, [("ps", pi)], [sgr])
            bi, kc4 = g // 4, g % 4
            b.tt("dve", yg[:, g, :], br[:, ysrc[bi], kc4, :], sg[:], ALU.mult, ["br", sgr], ["yg"])
        pendB = loadB(0)
        for oc in range(16):
            slB = pendB
            if oc + 1 < 16:
                pendB = loadB(oc + 1)
            b.cp("pool", wbfB[:, slB], wstB[:, slB], [("wstB", slB)], [("wbfB", slB)])
            for bi in range(4):
                sl = pendA
                if nxt < NCH3:
                    pendA = loadA(nxt)
                    nxt += 1
                b.cp("pool", wbfA[:, sl], wstA[:, sl], [("wstA", sl)], [("wbfA", sl)])
                pm = b.nps()
                for kc in range(16):
                    b.mm(pm, 128, TT, wbfA[:, sl, kc, :], hT[:, kc, :], kc == 0, kc == 15, [("wbfA", sl), "hT"])
                pb = b.nps()
                for kc in range(4):
                    b.mm(pb, 128, TT, wbfB[:, slB, bi * 4 + kc, :], yg[:, bi * 4 + kc, :], kc == 0, kc == 3, [("wbfB", slB), "yg"])
                sg, sgr = b.nx("t")
                b.act(sg[:], b.ps[pm][:, :], AF.Sigmoid, [("ps", pm)], [sgr])
                if bi == 0:
                    b.tt("dve", zacc[:], b.ps[pb][:, :], sg[:], ALU.mult, [("ps", pb), sgr], ["zacc"])
                else:
                    tmp, tr = b.nx("t")
                    b.tt("dve", tmp[:], b.ps[pb][:, :], sg[:], ALU.mult, [("ps", pb), sgr], [tr])
                    if bi < 3:
                        b.tt("pool", zacc[:], zacc[:], tmp[:], ALU.add, ["zacc", tr], ["zacc"])
                    else:
                        b.tt("pool", zT[:, oc, :], zacc[:], tmp[:], ALU.add, ["zacc", tr], ["zT"])
        for oc in range(16):
            sl = pendA
            if nxt < NCH3:
                pendA = loadA(nxt)
                nxt += 1
            b.cp("pool", wbfA[:, sl], wstA[:, sl], [("wstA", sl)], [("wbfA", sl)])
            po = b.nps()
            for kc in range(16):
                b.mm(po, 128, TT, wbfA[:, sl, kc, :], zT[:, kc, :], kc == 0, kc == 15, [("wbfA", sl), "zT"])
            xo, xor_ = b.nx("t")
            b.tt("dve", xo[:], b.ps[po][:, :], xs[:, oc, :], ALU.add, [("ps", po), "xs"], [xor_])
            b.store(o_x[tile, :, oc, :], xo[:], [xor_])
    return b.finish()


GM0 = 1792 + 1536 + 384 + 256 + 64 + 512


def prep_p3(inp, l, x_cur, ysf, ysb, bonus, ybr, yc, yd):
    f = np.float32
    w_in = inp["w_in"][l]
    chunks = []
    for g in range(16):
        chunks.append(_fm(w_in[:, GM0 + g * 128:GM0 + (g + 1) * 128], 16))
    M0 = GM0 + 2048
    for oc in range(16):
        for bi in range(4):
            c0 = M0 + bi * 2048 + oc * 128
            chunks.append(_fm(w_in[:, c0:c0 + 128], 16))
    for oc in range(16):
        chunks.append(_fm(inp["w_out"][l][:, oc * 128:(oc + 1) * 128], 16))
    wA = np.stack(chunks)
    wb = inp["w_branch"][l].reshape(2048, 2048)
    wB = np.stack([_fm(wb[:, oc * 128:(oc + 1) * 128], 16) for oc in range(16)])
    par = np.zeros((128, NPAR3), f)
    par[:, PC3["ng"]:PC3["ng"] + 16] = inp["norm_g"][l].reshape(16, 128).T
    par[:, PC3["gng"]:PC3["gng"] + 4] = inp["rw_gn_g"][l].reshape(4, 128).T
    par[:, PC3["gnb"]:PC3["gnb"] + 4] = inp["rw_gn_b"][l].reshape(4, 128).T
    par[:, PC3["subg"]] = inp["diff_sub_g"][l]
    par[:, PC3["lamf"]] = 1.0 - (0.8 - 0.6 * math.exp(-0.3 * l))
    maps = []
    ntok = NT1 * TT
    for c in range(NCORES):
        xt = np.empty((NT1, 128, 16, TT), f)
        brr = np.empty((NT1, 128, 6, 4, TT), f)
        for t in range(NT1):
            n0 = c * ntok + t * TT
            xt[t] = x_cur[n0:n0 + TT].reshape(TT, 16, 128).transpose(2, 1, 0)
            for i, a in enumerate((ysf, ysb, bonus, ybr, yc, yd)):
                brr[t, :, i] = a[:, n0:n0 + TT].reshape(4, 128, TT).transpose(1, 0, 2)
        maps.append({"xT": xt, "par": par, "wA": wA, "wB": wB, "br": brr})
    return maps


def post_p3(results):
    outs = []
    for r in results:
        o = r["o_x"]
        outs.append(o.transpose(0, 3, 2, 1).reshape(NT1 * TT, 2048))
    return np.ascontiguousarray(np.concatenate(outs, 0))


_P1_AXIS = {"o_dv": 0, "o_mv": 0, "o_rw": 2}


def kernel(**inputs):
    inp = {k: np.asarray(v) for k, v in inputs.items()}
    x = np.ascontiguousarray(inp["x"][0], dtype=np.float32)
    for l in range(4):
        r1 = _run("p1", build_p1, prep_p1(inp, l, x))
        o = {k: _cat(r1, k, _P1_AXIS.get(k, 1)) for k in r1[0]}
        del r1
        rs = _run("scan", build_scan, prep_scan(o["o_rw"]))
        ysf, ysb = post_scan(rs)
        del rs
        yb = post_attn_diff(_run("diff", lambda: build_attn("diff"),
                                 prep_attn_diff(inp, l, o["o_dq"], o["o_dk"], o["o_dv"])))
        yc = post_attn_mla(_run("mla", lambda: build_attn("mla"),
                                prep_attn_mla(o["o_mqn"], o["o_mqr"], o["o_mkn"], o["o_mkr"], o["o_mv"])))
        r3 = _run("p3", build_p3, prep_p3(inp, l, x, ysf, ysb, o["o_bonus"], yb, yc, o["o_yd"]))
        x = post_p3(r3)
        del r3, o
    return x[None].astype(np.float32)
```

```python
import math
from contextlib import ExitStack
import numpy as np
import concourse.bass as bass
import concourse.mybir as mybir
from concourse.bass_utils import run_bass_kernel_spmd

F32 = mybir.dt.float32
BF16 = mybir.dt.bfloat16
I32 = mybir.dt.int32
ALU = mybir.AluOpType
AF = mybir.ActivationFunctionType
AX = mybir.AxisListType
ENGS = ("pe", "act", "dve", "pool", "sp")
NCORES = 8


class _Op:
    __slots__ = ("eng", "fn", "waits", "signal", "dma_key", "idx", "sigval")

    def __init__(self, eng, fn, dma_key):
        self.eng = eng
        self.fn = fn
        self.waits = []
        self.signal = False
        self.dma_key = dma_key
        self.idx = None
        self.sigval = None


class _Res:
    __slots__ = ("w", "r")

    def __init__(self):
        self.w = None
        self.r = []


class Prog:
    def __init__(self, nc):
        self.nc = nc
        self.ops = {e: [] for e in ENGS}
        self.res = {}
        self.dma_cnt = {}
        self.dma_last = {}
        self.waited = {e: {} for e in ENGS}

    def _need(self, op, tok, isd):
        if tok is None:
            return
        kind, src, val = tok
        if kind == "e" and src == op.eng and not isd and src == "pe":
            return
        w = self.waited[op.eng]
        k = (kind, src)
        if w.get(k, -1) >= val:
            return
        w[k] = val
        op.waits.append(tok)
        if kind == "e":
            self.ops[src][val].signal = True

    def op(self, eng, fn, reads=(), writes=(), dma=None):
        o = _Op(eng, fn, dma)
        o.idx = len(self.ops[eng])
        isd = dma is not None
        if isd:
            n = self.dma_cnt.get(dma, 0) + 1
            self.dma_cnt[dma] = n
            self._need(o, self.dma_last.get(dma), True)
            tok = ("d", dma, n)
            self.dma_last[dma] = tok
        else:
            tok = ("e", eng, o.idx)
        for r in reads:
            st = self.res.setdefault(r, _Res())
            self._need(o, st.w, isd)
        for r in writes:
            st = self.res.setdefault(r, _Res())
            self._need(o, st.w, isd)
            for t in st.r:
                self._need(o, t, isd)
        for r in reads:
            st = self.res[r]
            st.r.append(tok)
            if len(st.r) > 48:
                st.r = st.r[-48:]
        for r in writes:
            st = self.res[r]
            st.w = tok
            st.r = []
        self.ops[eng].append(o)
        return tok

    def wait_tokens(self, eng, toks):
        o = _Op(eng, None, None)
        o.idx = len(self.ops[eng])
        for t in toks:
            self._need(o, t, True)
        self.ops[eng].append(o)

    def emit(self):
        nc = self.nc
        esem = {e: nc.alloc_semaphore(name=f"s_{e}") for e in ENGS}
        dsem = {k: nc.alloc_semaphore(name=f"d_{i}") for i, k in enumerate(self.dma_cnt)}
        for e in ENGS:
            c = 0
            for o in self.ops[e]:
                if o.signal:
                    c += 1
                    o.sigval = c
        ops = self.ops

        def body(e):
            def f(eng):
                for o in ops[e]:
                    for kind, src, val in o.waits:
                        if kind == "e":
                            eng.wait_ge(esem[src], ops[src][val].sigval)
                        else:
                            eng.wait_ge(dsem[src], 16 * val)
                    if o.fn is None:
                        continue
                    inst = o.fn(eng)
                    if o.dma_key is not None:
                        inst.then_inc(dsem[o.dma_key], 16)
                    elif o.signal:
                        inst.then_inc(esem[e], 1)
            return f

        with nc.Block() as block:
            block.tensor(body("pe"))
            block.scalar(body("act"))
            block.vector(body("dve"))
            block.gpsimd(body("pool"))
            block.sync(body("sp"))


class B:
    def __init__(self):
        self.nc = bass.Bass("TRN2", target_bir_lowering=False)
        self.P = Prog(self.nc)
        self.es = ExitStack()
        self.ps = [self.es.enter_context(self.nc.psum_tensor(f"ps{i}", [128, 512], F32)) for i in range(8)]
        self.psi = 0
        self.psrot = list(range(8))
        self.rings = {}
        self.outtoks = []
        self.ndq = 0
        self.attn_banks = [(4, 5), (6, 7)]
        self.attn_par = 0

    def din(self, name, shape, dt=F32):
        return self.nc.dram_tensor(name, list(shape), dt, kind="ExternalInput").ap()

    def dout(self, name, shape, dt=F32):
        return self.nc.dram_tensor(name, list(shape), dt, kind="ExternalOutput").ap()

    def sb(self, name, shape, dt=F32):
        return self.es.enter_context(self.nc.sbuf_tensor("s_" + name, list(shape), dt))

    def nps(self):
        rot = self.psrot
        i = rot[self.psi % len(rot)]
        self.psi += 1
        return i

    def ring(self, name, n, shape, dt=F32):
        self.rings[name] = [[self.sb(f"{name}{i}", shape, dt) for i in range(n)], 0]

    def nx(self, name):
        r = self.rings[name]
        i = r[1]
        r[1] = (i + 1) % len(r[0])
        return r[0][i], (name, i)

    def mm(self, pi, M, N, lhsT, rhs, st, sp, rd, po=0):
        out = self.ps[pi][po:po + M, 0:N]
        self.P.op("pe", lambda e: e.matmul(out, lhsT, rhs, start=st, stop=sp), reads=rd, writes=[("ps", pi)])

    def act(self, out, in_, func, rd, wr, bias=None, scale=None):
        kw = {}
        if bias is not None:
            kw["bias"] = bias
        if scale is not None:
            kw["scale"] = scale
        self.P.op("act", lambda e: e.activation(out=out, in_=in_, func=func, **kw), reads=rd, writes=wr)

    def stt(self, out, in0, scalar, in1, op0, op1, rd, wr):
        self.P.op("dve", lambda e: e.scalar_tensor_tensor(out=out, in0=in0, scalar=scalar, in1=in1,
                                                          op0=op0, op1=op1), reads=rd, writes=wr)

    def tt(self, eng, out, in0, in1, op, rd, wr):
        self.P.op(eng, lambda e: e.tensor_tensor(out=out, in0=in0, in1=in1, op=op), reads=rd, writes=wr)

    def ts(self, eng, out, in0, s1, s2, op0, op1, rd, wr):
        if op1 is None:
            self.P.op(eng, lambda e: e.tensor_scalar(out=out, in0=in0, scalar1=s1, scalar2=None, op0=op0),
                      reads=rd, writes=wr)
        else:
            self.P.op(eng, lambda e: e.tensor_scalar(out=out, in0=in0, scalar1=s1, scalar2=s2, op0=op0, op1=op1),
                      reads=rd, writes=wr)

    def cp(self, eng, out, in_, rd, wr):
        if eng == "act":
            self.P.op("act", lambda e: e.copy(out=out, in_=in_), reads=rd, writes=wr)
        else:
            self.P.op(eng, lambda e: e.tensor_copy(out=out, in_=in_), reads=rd, writes=wr)

    def recip(self, out, in_, rd, wr):
        self.P.op("dve", lambda e: e.reciprocal(out=out, in_=in_), reads=rd, writes=wr)

    def memset(self, eng, ap, val, wr):
        self.P.op(eng, lambda e: e.memset(ap, val), writes=wr)

    def load(self, out, in_, wr, key, q="sp"):
        return self.P.op(q, lambda e: e.dma_start(out=out, in_=in_), writes=wr, dma=key)

    def store(self, out, in_, rd, q="pool"):
        self.ndq += 1
        key = ("st", self.ndq % 6)
        t = self.P.op(q, lambda e: e.dma_start(out=out, in_=in_), reads=rd, dma=key)
        self.outtoks.append(t)

    def finish(self):
        last = {}
        for t in self.outtoks:
            last[t[1]] = t
        self.P.wait_tokens("pool", list(last.values()))
        self.P.emit()
        self.es.close()
        return self.nc


def fm_rstd(b, sq_list, ones_ap, Pn, N, inv_n, eps, consts_res):
    pi = b.nps()
    for i, (ap, res) in enumerate(sq_list):
        b.mm(pi, Pn, N, ones_ap, ap, i == 0, i == len(sq_list) - 1, [res, consts_res])
    ln, lnr = b.nx("t")
    b.act(ln[0:Pn, 0:N], b.ps[pi][0:Pn, 0:N], AF.Ln, [("ps", pi)], [lnr], bias=b.epsc[eps][0:Pn, :], scale=inv_n)
    rs, rsr = b.nx("t")
    b.act(rs[0:Pn, 0:N], ln[0:Pn, 0:N], AF.Exp, [lnr], [rsr], scale=-0.5)
    return rs, rsr


def setup_consts(b):
    b.ones = b.sb("ones", [128, 128])
    b.blk = b.sb("blk", [128, 128])
    b.memset("pool", b.ones[:], 1.0, ["ones"])
    b.memset("pool", b.blk[:], 0.0, ["blk"])
    b.memset("pool", b.blk[0:64, 0:64], 1.0, ["blk"])
    b.memset("pool", b.blk[64:128, 64:128], 1.0, ["blk"])
    b.onesb = b.sb("onesb", [128, 128], BF16)
    b.memset("pool", b.onesb[:], 1.0, ["onesb"])
    b.epsc = {}
    for i, v in enumerate((1e-6, 64e-5)):
        t = b.sb(f"epsc{i}", [128, 1])
        b.memset("pool", t[:], v, [f"epsc{i}"])
        b.epsc[v] = t
        b.P.res


def attn_core(b, qlist, kts, vts, nkt, scale, out_M, bias_fn=None, tag="a"):
    po, pz = b.attn_banks[b.attn_par]
    b.attn_par ^= 1
    for kt in range(nkt):
        pi = b.nps()
        ks = kts(kt)
        for i, ((qa, qr), (ka, kr)) in enumerate(zip(qlist, ks)):
            b.mm(pi, 128, 512, ka, qa, i == 0, i == len(qlist) - 1, [qr, kr])
        pt, ptr = b.nx("pt")
        if bias_fn is not None and bias_fn(kt) is not None:
            kind, val = bias_fn(kt)
            if kind == "const":
                b.act(pt[:], b.ps[pi][:, :], AF.Exp, [("ps", pi), "tab"], [ptr], bias=val, scale=scale)
            else:
                tmp, tr = b.nx("t")
                bap, bres = val
                b.stt(tmp[:], b.ps[pi][:, :], scale, bap, ALU.mult, ALU.add, [("ps", pi), bres], [tr])
                b.act(pt[:], tmp[:], AF.Exp, [tr], [ptr])
        else:
            b.act(pt[:], b.ps[pi][:, :], AF.Exp, [("ps", pi)], [ptr], scale=scale)
        va, vr = vts(kt)
        b.mm(po, out_M, 512, va, pt[:], kt == 0, kt == nkt - 1, [vr, ptr])
        b.mm(pz, out_M, 512, b.onesb[:, 0:out_M], pt[:], kt == 0, kt == nkt - 1, ["onesb", ptr])
    return po, pz


TT = 512
TH = TT + 2
NT1 = 2
PC = {}
_c = 0
for _n, _w in (("ng", 16), ("sh", 42), ("w0", 8), ("a0", 8), ("kk", 4), ("ka", 4), ("rk", 4), ("dqg", 1), ("dkg", 1),
               ("qlg", 3), ("kvg", 2), ("npg", 2), ("rpg", 2), ("rpgs", 2), ("mqg", 2), ("mng", 16), ("invf", 1),
               ("sgn", 1), ("omka", 4)):
    PC[_n] = _c
    _c += _w
NPAR = _c
RW0 = 0
def _p1_cols():
    cols = []
    cols += [(RW0 + 1536, 128), (RW0 + 1664, 128)]
    for c in range(4):
        cols += [(c * 128, 128), (512 + c * 128, 128), (1024 + c * 128, 128)]
    o = 1792
    cols += [(o + i * 128, 128) for i in range(4)]
    cols += [(o + 512 + i * 128, 128) for i in range(4)]
    o2 = 1792 + 1536
    cols += [(o2 + i * 128, 128) for i in range(3)]
    cols += [(o2 + 384 + i * 128, 128) for i in range(2)]
    cols += [("krope", 128)]
    o3 = o2 + 384 + 256 + 64
    cols += [(o3 + i * 128, 128) for i in range(4)]
    cols += [(1792 + 1024 + i * 128, 128) for i in range(4)]
    return cols
P1COLS = _p1_cols()
NCH1 = len(P1COLS)
KROPE0 = 1792 + 1536 + 384 + 256


def build_p1():
    b = B()
    P = b.P
    xT = b.din("xT", [NT1, 128, 16, TH])
    pos = b.din("pos", [NT1, TH], I32)
    par_d = b.din("par", [128, NPAR])
    w = b.din("w", [NCH1, 128, 16, 128])
    wup_d = b.din("wup", [128, 512])
    aup_d = b.din("aup", [128, 512])
    wuq_d = b.din("wuq", [128, 3, 768])
    wuqs_d = b.din("wuqs", [128, 3, 256])
    wukvk_d = b.din("wukvk", [128, 2, 512])
    wukvv_d = b.din("wukvv", [128, 2, 512])
    memT_d = b.din("memT", [128, 16, 256])
    wkv_d = b.din("wkv", [8, 128, 16, 128])
    o_rw = b.dout("o_rw", [9, 512, NT1 * TT])
    o_bonus = b.dout("o_bonus", [512, NT1 * TT])
    o_dq = b.dout("o_dq", [512, NT1 * TT])
    o_dk = b.dout("o_dk", [512, NT1 * TT])
    o_dv = b.dout("o_dv", [NT1 * TT, 512])
    o_mqn = b.dout("o_mqn", [512, NT1 * TT])
    o_mqr = b.dout("o_mqr", [256, NT1 * TT])
    o_mkn = b.dout("o_mkn", [512, NT1 * TT])
    o_mkr = b.dout("o_mkr", [64, NT1 * TT])
    o_mv = b.dout("o_mv", [NT1 * TT, 512])
    o_yd = b.dout("o_yd", [512, NT1 * TT])

    setup_consts(b)
    par = b.sb("par", [128, NPAR])
    b.load(par[:], par_d, ["par"], "par")
    pc = lambda n, i=0: par[:, PC[n] + i:PC[n] + i + 1]
    b.ts("dve", par[:, PC["omka"]:PC["omka"] + 4], par[:, PC["ka"]:PC["ka"] + 4], -1.0, 1.0, ALU.mult, ALU.add,
         ["par"], ["par"])
    xs = b.sb("xs", [128, 16, TH])
    hT = b.sb("hT", [128, 16, TH], BF16)
    wst = b.sb("wst", [128, 2, 16, 128])
    wbf = b.sb("wbf", [128, 2, 16, 128], BF16)
    stg = b.sb("stg", [128, 3072])
    wup = b.sb("wup_s", [128, 512])
    aup = b.sb("aup_s", [128, 512])
    wuq = b.sb("wuq_s", [128, 3, 768], BF16)
    wuqs = b.sb("wuqs_s", [128, 3, 256], BF16)
    wukvk = b.sb("wukvk_s", [128, 2, 512], BF16)
    wukvv = b.sb("wukvv_s", [128, 2, 512], BF16)
    memn = b.sb("memn", [128, 16, 256], BF16)
    kmem = b.sb("kmem", [128, 4, 256], BF16)
    vmem = b.sb("vmem", [128, 2, 512], BF16)
    b.ring("t", 14, [128, TT])
    b.ring("th", 3, [128, TH])
    b.ring("rkv", 4, [128, TT])
    b.ring("bf", 4, [128, TT], BF16)
    b.ring("pt", 3, [128, TT], BF16)
    twd = b.sb("twd", [128, TT])
    adl = b.sb("adl", [128, TT])
    ropC = b.sb("ropC", [64, TT])
    ropS = b.sb("ropS", [64, TT])
    rsx = b.sb("rsx", [128, TH])
    ql = b.sb("ql", [128, 3, TT])
    qln = b.sb("qln", [128, 3, TT], BF16)
    kvl = b.sb("kvl", [128, 2, TT])
    kvn = b.sb("kvn", [128, 2, TT], BF16)
    posi = b.sb("posi", [64, TH], I32)

    b.load(wup[:], wup_d, ["wup"], "wl0")
    b.load(aup[:], aup_d, ["aup"], "wl1")
    for dst, src, n, nm in ((wuq, wuq_d, 3 * 768, "wuq"), (wuqs, wuqs_d, 3 * 256, "wuqs"),
                            (wukvk, wukvk_d, 1024, "wukvk"), (wukvv, wukvv_d, 1024, "wukvv")):
        b.load(stg[:, 0:n], src.rearrange("p a b -> p (a b)"), ["stg"], "stg")
        b.cp("dve", dst[:].rearrange("p a b -> p (a b)"), stg[:, 0:n], ["stg"], [nm])

    def rmsnorm_cols(src_tile, nkc, N, gname, dst_tile, res_src, res_dst, inv_n):
        pi = b.nps()
        pih = b.nps() if N > 512 else None
        for kc in range(nkc):
            sq, sqr = b.nx("th")
            b.act(sq[:, 0:N], src_tile[:, kc, 0:N], AF.Square, [res_src], [sqr])
            b.mm(pi, 128, min(N, 512), b.ones[:], sq[:, 0:min(N, 512)], kc == 0, kc == nkc - 1, [sqr, "ones"])
            if pih is not None:
                b.mm(pih, 128, N - 512, b.ones[:], sq[:, 512:N], kc == 0, kc == nkc - 1, [sqr, "ones"])
        ln, lnr = b.nx("th")
        b.act(ln[:, 0:min(N, 512)], b.ps[pi][:, 0:min(N, 512)], AF.Ln, [("ps", pi)], [lnr],
              bias=b.epsc[1e-6][:], scale=inv_n)
        if pih is not None:
            b.act(ln[:, 512:N], b.ps[pih][:, 0:N - 512], AF.Ln, [("ps", pih)], [lnr], bias=b.epsc[1e-6][:], scale=inv_n)
        b.act(rsx[:, 0:N], ln[:, 0:N], AF.Exp, [lnr], ["rsx"], scale=-0.5)
        for kc in range(nkc):
            b.stt(dst_tile[:, kc, 0:N], src_tile[:, kc, 0:N], pc(gname, kc), rsx[:, 0:N], ALU.mult, ALU.mult,
                  [res_src, "rsx", "par"], [res_dst])

    b.load(xs[:, :, 0:256], memT_d, ["xs"], "xs")
    rmsnorm_cols(xs, 16, 256, "mng", memn, "xs", "memn", 1.0 / 2048)
    for ci in range(8):
        sl = ci % 2
        b.load(wst[:, sl], wkv_d[ci], [("wst", sl)], ("wst", sl))
        b.cp("pool", wbf[:, sl], wst[:, sl], [("wst", sl)], [("wbf", sl)])
        if ci < 4:
            pi = b.nps()
            for kc in range(16):
                b.mm(pi, 128, 256, wbf[:, sl, kc, :], memn[:, kc, :], kc == 0, kc == 15, [("wbf", sl), "memn"])
            tq, tqr = b.nx("t")
            b.cp("act", tq[:, 0:256], b.ps[pi][:, 0:256], [("ps", pi)], [tqr])
            sq, sqr = b.nx("t")
            b.act(sq[:, 0:256], b.ps[pi][:, 0:256], AF.Square, [("ps", pi)], [sqr])
            rs, rsr = fm_rstd(b, [(sq[:, 0:256], sqr)], b.ones[:], 128, 256, 1.0 / 128, 1e-6, "ones")
            b.stt(kmem[:, ci, :], tq[:, 0:256], pc("mqg", 1), rs[:, 0:256], ALU.mult, ALU.mult, [tqr, rsr, "par"], ["kmem"])
        else:
            for tb in range(2):
                pi = b.nps()
                for kc in range(16):
                    b.mm(pi, 128, 128, memn[:, kc, tb * 128:(tb + 1) * 128], wbf[:, sl, kc, :], kc == 0, kc == 15,
                         [("wbf", sl), "memn"])
                b.cp("act", vmem[:, tb, (ci - 4) * 128:(ci - 3) * 128], b.ps[pi][:, 0:128], [("ps", pi)], ["vmem"])

    def wload(tile, ci):
        sl = (tile * NCH1 + ci) % 2
        b.load(wst[:, sl], w[ci], [("wst", sl)], ("wst", sl))

    def shift(pi, pih, rc, dst, dres):
        u, ur = b.nx("th")
        b.cp("act", u[:, 0:TT], b.ps[pi][:, :], [("ps", pi)], [ur])
        b.cp("act", u[:, TT:TH], b.ps[pih][:, 0:2], [("ps", pih)], [ur])
        s0 = pc("sh", 0 * 14 + rc); s1 = pc("sh", 1 * 14 + rc); s2 = pc("sh", 2 * 14 + rc)
        b.ts("dve", dst[:, :], u[:, 0:TT], s1, None, ALU.mult, None, [ur, "par"], [dres])
        b.stt(dst[:, 1:TT], u[:, 0:TT - 1], s0, dst[:, 1:TT], ALU.mult, ALU.add, [ur, "par", dres], [dres])
        b.stt(dst[:, 0:1], u[:, TT:TT + 1], s0, dst[:, 0:1], ALU.mult, ALU.add, [ur, "par", dres], [dres])
        b.stt(dst[:, 0:TT - 1], u[:, 1:TT], s2, dst[:, 0:TT - 1], ALU.mult, ALU.add, [ur, "par", dres], [dres])
        b.stt(dst[:, TT - 1:TT], u[:, TT + 1:TT + 2], s2, dst[:, TT - 1:TT], ALU.mult, ALU.add, [ur, "par", dres], [dres])

    def head_norm(pi, Pn, ones_ap, inv_n, gcol, dst_ap, dres, N=TT):
        tq, tqr = b.nx("t")
        b.cp("act", tq[0:Pn, 0:N], b.ps[pi][0:Pn, 0:N], [("ps", pi)], [tqr])
        sq, sqr = b.nx("t")
        b.act(sq[0:Pn, 0:N], b.ps[pi][0:Pn, 0:N], AF.Square, [("ps", pi)], [sqr])
        rs, rsr = fm_rstd(b, [(sq[0:Pn, 0:N], sqr)], ones_ap, Pn, N, inv_n, 1e-6, "ones")
        b.stt(dst_ap, tq[0:Pn, 0:N], gcol, rs[0:Pn, 0:N], ALU.mult, ALU.mult, [tqr, rsr, "par"], [dres])
        return tq, tqr, rs, rsr

    for tile in range(NT1):
        t0 = tile * TT
        b.load(xs[:], xT[tile], ["xs"], "xs")
        b.load(posi[:], pos[tile, :].partition_broadcast(64), ["posi"], "posi")
        wload(tile, 0)
        rmsnorm_cols(xs, 16, TH, "ng", hT, "xs", "hT", 1.0 / 2048)
        posf, posr = b.nx("th")
        b.cp("dve", posf[0:64, :], posi[:], ["posi"], [posr])
        for which, dstt, dres in ((0, ropS, "ropS"), (1, ropC, "ropC")):
            a, ar = b.nx("t")
            b.ts("dve", a[0:64, :], posf[0:64, 0:TT], pc("invf")[0:64, :], (math.pi / 2 if which else 0.0),
                 ALU.mult, ALU.add, [posr, "par"], [ar])
            y, yr = b.nx("t")
            b.ts("dve", y[0:64, :], a[0:64, :], 1.0 / (2 * math.pi), None, ALU.mult, None, [ar], [yr])
            ni = b.sb(f"ni{tile}{which}", [64, TT], I32)
            b.cp("dve", ni[:], y[0:64, :], [yr], [f"ni{tile}{which}"])
            nf, nfr = b.nx("t")
            b.cp("dve", nf[0:64, :], ni[:], [f"ni{tile}{which}"], [nfr])
            r, rr = b.nx("t")
            b.stt(r[0:64, :], nf[0:64, :], -2 * math.pi, a[0:64, :], ALU.mult, ALU.add, [nfr, ar], [rr])
            m, mr = b.nx("t")
            b.ts("dve", m[0:64, :], r[0:64, :], math.pi, -2 * math.pi, ALU.is_gt, ALU.mult, [rr], [mr])
            b.tt("dve", r[0:64, :], r[0:64, :], m[0:64, :], ALU.add, [rr, mr], [rr])
            b.ts("dve", m[0:64, :], r[0:64, :], -math.pi, 2 * math.pi, ALU.is_lt, ALU.mult, [rr], [mr])
            b.tt("dve", r[0:64, :], r[0:64, :], m[0:64, :], ALU.add, [rr, mr], [rr])
            if which == 0:
                sn, snr = b.nx("t")
                b.act(sn[0:64, :], r[0:64, :], AF.Sin, [rr], [snr])
                b.ts("dve", ropS[:], sn[0:64, :], pc("sgn")[0:64, :], None, ALU.mult, None, [snr, "par"], ["ropS"])
            else:
                b.act(ropC[:], r[0:64, :], AF.Sin, [rr], ["ropC"])

        def rope_out(t_tq, t_r, sw_tq, sw_r, rs, rsr, gi, dst_dram):
            a, ar = b.nx("t")
            b.stt(a[0:64, :], t_tq[0:64, :], pc("rpg", gi)[0:64, :], ropC[:], ALU.mult, ALU.mult, [t_r, "par", "ropC"], [ar])
            c, cr = b.nx("t")
            b.stt(c[0:64, :], sw_tq[0:64, :], pc("rpgs", gi)[0:64, :], ropS[:], ALU.mult, ALU.mult, [sw_r, "par", "ropS"], [cr])
            b.tt("dve", a[0:64, :], a[0:64, :], c[0:64, :], ALU.add, [ar, cr], [ar])
            b.tt("dve", a[0:64, :], a[0:64, :], rs[0:64, :], ALU.mult, [ar, rsr], [ar])
            b.store(dst_dram, a[0:64, :], [ar])

        rcur = {}
        for ci in range(NCH1):
            sl = (tile * NCH1 + ci) % 2
            if ci + 1 < NCH1:
                wload(tile, ci + 1)
            elif tile + 1 < NT1:
                wload(tile + 1, 0)
            b.cp("pool", wbf[:, sl], wst[:, sl], [("wst", sl)], [("wbf", sl)])
            wr = ("wbf", sl)
            if ci >= 32:
                j = ci - 32
                for tb in range(4):
                    pi = b.nps()
                    for kc in range(16):
                        b.mm(pi, 128, 128, hT[:, kc, tb * 128:(tb + 1) * 128], wbf[:, sl, kc, :], kc == 0, kc == 15, [wr, "hT"])
                    o, orr = b.nx("t")
                    b.cp("act", o[:, 0:128], b.ps[pi][:, 0:128], [("ps", pi)], [orr])
                    b.store(o_dv[t0 + tb * 128:t0 + (tb + 1) * 128, j * 128:(j + 1) * 128], o[:, 0:128], [orr])
                continue
            if ci == 27:
                pis = []
                for hh in range(2):
                    pi = b.nps()
                    for kc in range(16):
                        b.mm(pi, 64, TT, wbf[:, sl, kc, hh * 64:(hh + 1) * 64], hT[:, kc, 0:TT], kc == 0, kc == 15, [wr, "hT"])
                    pis.append(pi)
                tq, tqr = b.nx("t")
                b.cp("act", tq[0:64, :], b.ps[pis[0]][0:64, :], [("ps", pis[0])], [tqr])
                sq, sqr = b.nx("t")
                b.act(sq[0:64, :], b.ps[pis[0]][0:64, :], AF.Square, [("ps", pis[0])], [sqr])
                sw, swr = b.nx("t")
                b.cp("act", sw[0:64, :], b.ps[pis[1]][0:64, :], [("ps", pis[1])], [swr])
                rs, rsr = fm_rstd(b, [(sq[0:64, :], sqr)], b.ones[0:64, 0:64], 64, TT, 1.0 / 64, 1e-6, "ones")
                rope_out(tq, tqr, sw, swr, rs, rsr, 1, o_mkr[:, t0:t0 + TT])
                continue
            pi = b.nps()
            for kc in range(16):
                b.mm(pi, 128, TT, wbf[:, sl, kc, :], hT[:, kc, 0:TT], kc == 0, kc == 15, [wr, "hT"])
            pih = None
            if ci < 14:
                pih = b.nps()
                for kc in range(16):
                    b.mm(pih, 128, 2, wbf[:, sl, kc, :], hT[:, kc, TT:TH], kc == 0, kc == 15, [wr, "hT"])
            if ci == 0:
                tmp, tr = b.nx("t")
                shift(pi, pih, 12, tmp, tr)
                b.act(twd[:], tmp[:], AF.Tanh, [tr], ["twd"])
            elif ci == 1:
                shift(pi, pih, 13, adl, "adl")
            elif ci < 14:
                c = (ci - 2) // 3
                kind = (ci - 2) % 3
                dst, dres = b.nx("rkv")
                shift(pi, pih, kind * 4 + c, dst, dres)
                rcur[kind] = (dst, dres)
                if kind == 0:
                    b.store(o_rw[0, c * 128:(c + 1) * 128, t0:t0 + TT], dst[:], [dres])
                if kind == 2:
                    r_t, r_r = rcur[0]
                    k_t, k_r = rcur[1]
                    v_t, v_r = rcur[2]
                    b.store(o_rw[1, c * 128:(c + 1) * 128, t0:t0 + TT], v_t[:], [v_r])
                    kr_, krr = b.nx("t")
                    b.ts("dve", kr_[:], k_t[:], pc("kk", c), None, ALU.mult, None, [k_r, "par"], [krr])
                    sq, sqr = b.nx("t")
                    b.act(sq[:], kr_[:], AF.Square, [krr], [sqr])
                    pj = b.nps()
                    b.mm(pj, 128, TT, b.blk[:], sq[:], True, True, [sqr, "blk"])
                    nr, nrr = b.nx("t")
                    b.act(nr[:], b.ps[pj][:, :], AF.Sqrt, [("ps", pj)], [nrr])
                    b.ts("dve", nr[:], nr[:], 1e-12, None, ALU.max, None, [nrr], [nrr])
                    b.recip(nr[:], nr[:], [nrr], [nrr])
                    kk_, kkr = b.nx("t")
                    b.tt("dve", kk_[:], kr_[:], nr[:], ALU.mult, [krr, nrr], [kkr])
                    b.store(o_rw[2, c * 128:(c + 1) * 128, t0:t0 + TT], kk_[:], [kkr])
                    kds = []
                    for d in range(2):
                        pw = b.nps()
                        b.mm(pw, 128, TT, wup[d * 64:(d + 1) * 64, c * 128:(c + 1) * 128], twd[d * 64:(d + 1) * 64, :],
                             True, True, ["wup", "twd"])
                        sg, sgr = b.nx("t")
                        b.act(sg[:], b.ps[pw][:, :], AF.Sigmoid, [("ps", pw), "par"], [sgr], bias=pc("w0", d * 4 + c))
                        dec, decr = b.nx("t")
                        b.act(dec[:], sg[:], AF.Copy, [sgr], [decr], scale=-math.exp(-0.5))
                        b.store(o_rw[5 + 3 * d, c * 128:(c + 1) * 128, t0:t0 + TT], dec[:], [decr])
                        pa = b.nps()
                        b.mm(pa, 128, TT, aup[d * 64:(d + 1) * 64, c * 128:(c + 1) * 128], adl[d * 64:(d + 1) * 64, :],
                             True, True, ["aup", "adl"])
                        a_, a_r = b.nx("t")
                        b.act(a_[:], b.ps[pa][:, :], AF.Sigmoid, [("ps", pa), "par"], [a_r], bias=pc("a0", d * 4 + c))
                        nb, nbr = b.nx("t")
                        b.stt(nb[:], kk_[:], -1.0, a_[:], ALU.mult, ALU.mult, [kkr, a_r], [nbr])
                        b.store(o_rw[3 + 3 * d, c * 128:(c + 1) * 128, t0:t0 + TT], nb[:], [nbr])
                        kd, kdr = b.nx("t")
                        b.ts("dve", kd[:], a_[:], pc("ka", c), pc("omka", c), ALU.mult, ALU.add, [a_r, "par"], [kdr])
                        b.tt("dve", kd[:], kd[:], k_t[:], ALU.mult, [kdr, k_r], [kdr])
                        b.store(o_rw[4 + 3 * d, c * 128:(c + 1) * 128, t0:t0 + TT], kd[:], [kdr])
                        kds.append((kd, kdr))
                    s_, s_r = b.nx("t")
                    b.tt("dve", s_[:], kds[0][0][:], kds[1][0][:], ALU.add, [kds[0][1], kds[1][1]], [s_r])
                    b.stt(s_[:], r_t[:], pc("rk", c), s_[:], ALU.mult, ALU.mult, [r_r, "par", s_r], [s_r])
                    pb_ = b.nps()
                    b.mm(pb_, 128, TT, b.blk[:], s_[:], True, True, [s_r, "blk"])
                    bo, bor = b.nx("t")
                    b.tt("dve", bo[:], v_t[:], b.ps[pb_][:, :], ALU.mult, [v_r, ("ps", pb_)], [bor])
                    b.store(o_bonus[c * 128:(c + 1) * 128, t0:t0 + TT], bo[:], [bor])
            elif ci < 22:
                isk = ci >= 18
                j = ci - (18 if isk else 14)
                o, orr = b.nx("t")
                head_norm(pi, 128, b.blk[:], 1.0 / 64, pc("dkg" if isk else "dqg"), o[:], orr)
                b.store((o_dk if isk else o_dq)[j * 128:(j + 1) * 128, t0:t0 + TT], o[:], [orr])
            elif ci < 25:
                j = ci - 22
                b.cp("act", ql[:, j, :], b.ps[pi][:, :], [("ps", pi)], ["ql"])
                if j == 2:
                    rmsnorm_cols(ql, 3, TT, "qlg", qln, "ql", "qln", 1.0 / 384)
                    for h in range(4):
                        pn = b.nps()
                        for kc in range(3):
                            b.mm(pn, 128, TT, wuq[:, kc, h * 192:h * 192 + 128], qln[:, kc, :], kc == 0, kc == 2, ["wuq", "qln"])
                        o, orr = b.nx("t")
                        head_norm(pn, 128, b.ones[:], 1.0 / 128, pc("npg", 0), o[:], orr)
                        b.store(o_mqn[h * 128:(h + 1) * 128, t0:t0 + TT], o[:], [orr])
                        pr = b.nps()
                        for kc in range(3):
                            b.mm(pr, 64, TT, wuq[:, kc, h * 192 + 128:h * 192 + 192], qln[:, kc, :], kc == 0, kc == 2, ["wuq", "qln"])
                        psw = b.nps()
                        for kc in range(3):
                            b.mm(psw, 64, TT, wuqs[:, kc, h * 64:(h + 1) * 64], qln[:, kc, :], kc == 0, kc == 2, ["wuqs", "qln"])
                        tq, tqr = b.nx("t")
                        b.cp("act", tq[0:64, :], b.ps[pr][0:64, :], [("ps", pr)], [tqr])
                        sq, sqr = b.nx("t")
                        b.act(sq[0:64, :], b.ps[pr][0:64, :], AF.Square, [("ps", pr)], [sqr])
                        sw, swr = b.nx("t")
                        b.cp("act", sw[0:64, :], b.ps[psw][0:64, :], [("ps", psw)], [swr])
                        rs, rsr = fm_rstd(b, [(sq[0:64, :], sqr)], b.ones[0:64, 0:64], 64, TT, 1.0 / 64, 1e-6, "ones")
                        rope_out(tq, tqr, sw, swr, rs, rsr, 0, o_mqr[h * 64:(h + 1) * 64, t0:t0 + TT])
            elif ci < 27:
                j = ci - 25
                b.cp("act", kvl[:, j, :], b.ps[pi][:, :], [("ps", pi)], ["kvl"])
                if j == 1:
                    rmsnorm_cols(kvl, 2, TT, "kvg", kvn, "kvl", "kvn", 1.0 / 256)
                    for h in range(4):
                        pn = b.nps()
                        for kc in range(2):
                            b.mm(pn, 128, TT, wukvk[:, kc, h * 128:(h + 1) * 128], kvn[:, kc, :], kc == 0, kc == 1, ["wukvk", "kvn"])
                        o, orr = b.nx("t")
                        head_norm(pn, 128, b.ones[:], 1.0 / 128, pc("npg", 1), o[:], orr)
                        b.store(o_mkn[h * 128:(h + 1) * 128, t0:t0 + TT], o[:], [orr])
                    for tb in range(4):
                        pv = b.nps()
                        for kc in range(2):
                            b.mm(pv, 128, 512, kvn[:, kc, tb * 128:(tb + 1) * 128], wukvv[:, kc, :], kc == 0, kc == 1, ["wukvv", "kvn"])
                        o, orr = b.nx("t")
                        b.cp("act", o[:], b.ps[pv][:, :], [("ps", pv)], [orr])
                        b.store(o_mv[t0 + tb * 128:t0 + (tb + 1) * 128, :], o[:], [orr])
            else:
                j = ci - 28
                qb, qbr = b.nx("bf")
                head_norm(pi, 128, b.ones[:], 1.0 / 128, pc("mqg", 0), qb[:], qbr)
                b.psrot = [0, 1, 2, 3]
                po, pz = attn_core(b, [(qb[:], qbr)],
                                   lambda kt, j=j: [(kmem[:, j, kt * 128:(kt + 1) * 128], "kmem")],
                                   lambda kt, j=j: (vmem[:, kt, j * 128:(j + 1) * 128], "vmem"),
                                   2, 128 ** -0.5, 128)
                b.psrot = list(range(8))
                rz, rzr = b.nx("t")
                b.recip(rz[:], b.ps[pz][:, :], [("ps", pz)], [rzr])
                o, orr = b.nx("t")
                b.tt("dve", o[:], b.ps[po][:, :], rz[:], ALU.mult, [("ps", po), rzr], [orr])
                b.store(o_yd[j * 128:(j + 1) * 128, t0:t0 + TT], o[:], [orr])
    return b.finish()


def _fm(a, nk):
    return np.ascontiguousarray(a.reshape(nk, 128, a.shape[1]).transpose(1, 0, 2))


def _col(par, name, arr, i=0):
    par[:arr.shape[0], PC[name] + i] = arr


def prep_p1(inp, l, x_cur):
    f = np.float32
    w_in = inp["w_in"][l]
    chunks = []
    for col0, n in P1COLS:
        if col0 == "krope":
            idx = [KROPE0 + j for j in range(64)] + [KROPE0 + (j + 32) % 64 for j in range(64)]
            W = w_in[:, idx]
        else:
            W = w_in[:, col0:col0 + 128]
        chunks.append(_fm(W, 16))
    w = np.stack(chunks)
    par = np.zeros((128, NPAR), f)
    par[:, PC["ng"]:PC["ng"] + 16] = inp["norm_g"][l].reshape(16, 128).T
    sh = inp["rw_shift"][l]
    for j in range(3):
        for rc in range(14):
            _col(par, "sh", sh[j, rc * 128:(rc + 1) * 128], j * 14 + rc)
    for d in range(2):
        for c in range(4):
            _col(par, "w0", inp["rw_w0"][l][d, c * 128:(c + 1) * 128], d * 4 + c)
            _col(par, "a0", inp["rw_a0"][l][d, c * 128:(c + 1) * 128], d * 4 + c)
    rk = inp["rw_r_k"][l].reshape(512)
    for c in range(4):
        _col(par, "kk", inp["rw_k_k"][l][c * 128:(c + 1) * 128], c)
        _col(par, "ka", inp["rw_k_a"][l][c * 128:(c + 1) * 128], c)
        _col(par, "rk", rk[c * 128:(c + 1) * 128], c)
    _col(par, "dqg", np.tile(inp["diff_qk_g"][l][0], 2))
    _col(par, "dkg", np.tile(inp["diff_qk_g"][l][1], 2))
    par[:, PC["qlg"]:PC["qlg"] + 3] = inp["mla_q_lat_g"][l].reshape(3, 128).T
    par[:, PC["kvg"]:PC["kvg"] + 2] = inp["mla_kv_lat_g"][l].reshape(2, 128).T
    par[:, PC["npg"]:PC["npg"] + 2] = inp["mla_nope_g"][l].T
    for gi in range(2):
        g = inp["mla_rope_g"][l][gi]
        _col(par, "rpg", g, gi)
        _col(par, "rpgs", np.concatenate([g[32:], g[:32]]), gi)
    par[:, PC["mqg"]:PC["mqg"] + 2] = inp["mem_qk_g"][l].T
    par[:, PC["mng"]:PC["mng"] + 16] = inp["mem_norm_g"][l].reshape(16, 128).T
    invf = (10000.0 ** (-np.arange(0, 64, 2, dtype=np.float32) / 64)).astype(f)
    _col(par, "invf", np.tile(invf, 2))
    _col(par, "sgn", np.concatenate([-np.ones(32, f), np.ones(32, f)]))
    wuq = inp["mla_w_uq"][l]
    sw_idx = [h * 192 + 128 + (j + 32) % 64 for h in range(4) for j in range(64)]
    wukv = inp["mla_w_ukv"][l].reshape(256, 4, 256)
    common = {
        "par": par, "w": w,
        "wup": np.ascontiguousarray(inp["rw_w_up"][l].reshape(128, 512)),
        "aup": np.ascontiguousarray(inp["rw_a_up"][l].reshape(128, 512)),
        "wuq": _fm(wuq, 3), "wuqs": _fm(wuq[:, sw_idx], 3),
        "wukvk": _fm(np.ascontiguousarray(wukv[:, :, :128]).reshape(256, 512), 2),
        "wukvv": _fm(np.ascontiguousarray(wukv[:, :, 128:]).reshape(256, 512), 2),
        "memT": np.ascontiguousarray(inp["mem"][0].reshape(256, 16, 128).transpose(2, 1, 0)),
        "wkv": np.stack([_fm(inp["mem_w_kv"][l][:, ci * 128:(ci + 1) * 128], 16) for ci in range(8)]),
    }
    S = x_cur.shape[0]
    xpad = np.concatenate([np.zeros((1, 2048), f), x_cur, np.zeros((1, 2048), f)], 0)
    posv = inp["positions"][0]
    maps = []
    for c in range(NCORES):
        xt = np.empty((NT1, 128, 16, TH), f)
        pp = np.zeros((NT1, TH), np.int32)
        for t in range(NT1):
            n0 = c * (NT1 * TT) + t * TT
            xe = np.concatenate([xpad[n0 + 1:n0 + 1 + TT], xpad[n0:n0 + 1], xpad[n0 + 1 + TT:n0 + 2 + TT]], 0)
            xt[t] = xe.reshape(TH, 16, 128).transpose(2, 1, 0)
            pp[t, :TT] = posv[n0:n0 + TT]
        m = dict(common)
        m["xT"] = xt
        m["pos"] = pp
        maps.append(m)
    return maps


_NC_CACHE = {}


def _run(name, builder, maps):
    if name not in _NC_CACHE:
        _NC_CACHE[name] = builder()
    res = run_bass_kernel_spmd(_NC_CACHE[name], maps, core_ids=list(range(NCORES)))
    return res.results


def _cat(results, key, axis):
    return np.concatenate([r[key] for r in results], axis=axis)


SEQ = 8192
SC_CH = 32
SC_NV = 5


def build_scan(T=SEQ):
    b = B()
    P = b.P
    bc = b.din("bc", [2, T, SC_NV * 64])
    vcol = b.din("vcol", [128, T])
    y = b.dout("y", [128, T])
    nch = T // SC_CH
    W = SC_NV * 64
    bcb = b.sb("bcb", [128, 2, SC_CH * W])
    vc = b.sb("vc", [128, T])
    yt = b.sb("yt", [128, T])
    S = b.sb("S", [128, 64])
    Sd = b.sb("Sd", [128, 64])
    sa = b.sb("sa", [128, 1])
    b.ring("ja", 2, [128, 64])
    b.ring("jb", 2, [128, 64])
    b.load(vc[:], vcol, ["vc"], "vc")
    b.memset("dve", S[:], 0.0, ["S"])

    def load(c):
        sl = c % 2
        for pr in range(2):
            src = bc[pr, c * SC_CH:(c + 1) * SC_CH, :].rearrange("t f -> (t f)").partition_broadcast(64)
            b.load(bcb[pr * 64:(pr + 1) * 64, sl, :], src, [("bcb", sl, pr)], ("bc", sl, pr))

    load(0)
    for c in range(nch):
        if c + 1 < nch:
            load(c + 1)
        sl = c % 2
        rd = [("bcb", sl, 0), ("bcb", sl, 1)]
        for s in range(SC_CH):
            t = c * SC_CH + s
            o = s * W
            kap = bcb[:, sl, o:o + 64]
            nb = bcb[:, sl, o + 64:o + 128]
            kd = bcb[:, sl, o + 128:o + 192]
            rt = bcb[:, sl, o + 192:o + 256]
            dec = bcb[:, sl, o + 256:o + 320]
            ja, jar = b.nx("ja")
            P.op("dve", lambda e, kap=kap, ja=ja: e.scalar_tensor_tensor(
                out=ja[:], in0=S[:], scalar=1.0, in1=kap, op0=ALU.mult, op1=ALU.mult, accum_out=sa[:]),
                reads=rd + ["S"], writes=[jar, "sa"])
            b.tt("dve", Sd[:], S[:], dec, ALU.mult, rd + ["S"], ["Sd"])
            b.stt(Sd[:], nb, sa[:, 0:1], Sd[:], ALU.mult, ALU.add, rd + ["Sd", "sa"], ["Sd"])
            b.stt(S[:], kd, vc[:, t:t + 1], Sd[:], ALU.mult, ALU.add, rd + ["Sd", "vc"], ["S"])
            jb, jbr = b.nx("jb")
            P.op("dve", lambda e, rt=rt, jb=jb, t=t: e.scalar_tensor_tensor(
                out=jb[:], in0=S[:], scalar=1.0, in1=rt, op0=ALU.mult, op1=ALU.mult, accum_out=yt[:, t:t + 1]),
                reads=rd + ["S"], writes=[jbr, "yt"])
    b.store(y, yt[:], ["yt"])
    return b.finish()


def prep_scan(o_rw, T=SEQ):
    maps = []
    for h in range(NCORES):
        sl = slice(h * 64, (h + 1) * 64)
        bc = np.empty((2, T, SC_NV * 64), np.float32)
        for d in range(2):
            parts = [o_rw[2][sl], o_rw[3 + 3 * d][sl], o_rw[4 + 3 * d][sl], o_rw[0][sl], o_rw[5 + 3 * d][sl]]
            a = np.concatenate([p.T for p in parts], axis=1)
            bc[d] = a if d == 0 else a[::-1]
        v = o_rw[1][sl]
        vcol = np.concatenate([v, v[:, ::-1]], 0)
        maps.append({"bc": bc, "vcol": np.ascontiguousarray(vcol)})
    return maps


def post_scan(results):
    yf = np.concatenate([r["y"][0:64] for r in results], 0)
    yb = np.concatenate([r["y"][64:128][:, ::-1] for r in results], 0)
    return np.ascontiguousarray(yf), np.ascontiguousarray(yb)


NQ = SEQ // 2
NKT = SEQ // 128


def _t5_breaks():
    nb, max_exact = 16, 8
    n = np.arange(0, 1024, dtype=np.int32)
    n_f = np.maximum(n, max_exact).astype(np.float32)
    large = max_exact + (np.log(n_f / np.float32(max_exact)) / np.float32(math.log(128 / max_exact))
                         * np.float32(nb - max_exact)).astype(np.int32)
    large = np.minimum(large, nb - 1)
    f = np.where(n < max_exact, n, large)
    rels = np.arange(-1023, 1024)
    bk = np.where(rels > 0, 16, 0) + f[np.abs(rels)]
    order = [int(bk[0])]
    breaks = []
    for i in range(1, len(rels)):
        if bk[i] != bk[i - 1]:
            order.append(int(bk[i]))
            breaks.append(int(rels[i]))
    return order, breaks


T5_ORDER, T5_BREAKS = _t5_breaks()
NBK = len(T5_ORDER)


def build_attn(kind):
    b = B()
    P = b.P
    diff = kind == "diff"
    qa_d = b.din("qa", [128, NQ])
    ka_d = b.din("ka", [128, SEQ])
    v_d = b.din("v", [128, NKT, 128])
    if diff:
        tab_d = b.din("tab", [1, NBK])
        lq_d = b.din("lq", [1, 256])
        cst_d = b.din("cst", [128, 2])
    else:
        qb_d = b.din("qb", [64, NQ])
        kb_d = b.din("kb", [64, SEQ])
    o_d = b.dout("o", [128, NQ])
    setup_consts(b)
    ka = b.sb("ka", [128, SEQ], BF16)
    qa = b.sb("qa", [128, NQ], BF16)
    vv = b.sb("vv", [128, NKT, 128], BF16)
    b.ring("stg", 2, [128, 2048])
    b.ring("t", 6, [128, 512])
    b.ring("pt", 4, [128, 512], BF16)
    b.ring("o", 2, [128, 512])

    def load_cast(dst, src, Pn, n, res):
        for i in range(0, n, 2048):
            st, sr = b.nx("stg")
            b.load(st[0:Pn, :], src[:, i:i + 2048], [sr], sr)
            b.cp("pool", dst[0:Pn, i:i + 2048], st[0:Pn, :], [sr], [res])

    load_cast(ka, ka_d, 128, SEQ, "ka")
    load_cast(qa, qa_d, 128, NQ, "qa")
    load_cast(vv[:].rearrange("p a b -> p (a b)"), v_d.rearrange("p a b -> p (a b)"), 128, NKT * 128, "vv")
    if not diff:
        kb = b.sb("kb", [64, SEQ], BF16)
        qb = b.sb("qb", [64, NQ], BF16)
        load_cast(kb, kb_d, 64, SEQ, "kb")
        load_cast(qb, qb_d, 64, NQ, "qb")
    else:
        lq = b.sb("lq", [128, 256])
        cst = b.sb("cst", [128, 2])
        tab = b.sb("tab", [128, NBK])
        b.load(lq[:], lq_d[0, :].partition_broadcast(128), ["lq"], "lq")
        b.load(cst[:], cst_d, ["cst"], "cst")
        b.load(tab[:], tab_d[0, :].partition_broadcast(128), ["tab"], "tab")
        pr = b.sb("lpr", [128, 128])
        b.tt("dve", pr[:, 0:64], lq[:, 0:64], lq[:, 64:128], ALU.mult, ["lq"], ["lpr"])
        b.tt("dve", pr[:, 64:128], lq[:, 128:192], lq[:, 192:256], ALU.mult, ["lq"], ["lpr"])
        ls = b.sb("ls", [128, 4])
        P.op("dve", lambda e: e.tensor_reduce(out=ls[:, 0:1], in_=pr[:, 0:64], axis=AX.X, op=ALU.add), reads=["lpr"], writes=["ls"])
        P.op("dve", lambda e: e.tensor_reduce(out=ls[:, 1:2], in_=pr[:, 64:128], axis=AX.X, op=ALU.add), reads=["lpr"], writes=["ls"])
        b.act(ls[:, 0:2], ls[:, 0:2], AF.Exp, ["ls"], ["ls"])
        b.tt("dve", ls[:, 2:3], ls[:, 0:1], ls[:, 1:2], ALU.subtract, ["ls"], ["ls"])
        b.tt("dve", ls[:, 2:3], ls[:, 2:3], cst[:, 0:1], ALU.add, ["ls", "cst"], ["ls"])
        b.ts("dve", ls[:, 3:4], ls[:, 2:3], -1.0, None, ALU.mult, None, ["ls"], ["ls"])
        dl = b.sb("dl", [128, NBK])
        b.tt("dve", dl[:, 1:NBK], tab[:, 1:NBK], tab[:, 0:NBK - 1], ALU.subtract, ["tab"], ["dl"])
        reli = b.sb("reli", [128, 512], I32)
        relf = b.sb("relf", [128, 512])
        P.op("pool", lambda e: e.iota(reli[:], [[-1, 512]], base=0, channel_multiplier=1), writes=["reli"])
        b.cp("dve", relf[:], reli[:], ["reli"], ["relf"])
        bias6 = b.sb("bias6", [128, 6, 512])
        for dk in range(6):
            off = float((dk - 1) * 128)
            for k in range(1, NBK):
                tmp, tr = b.nx("t")
                b.ts("dve", tmp[:], relf[:], float(T5_BREAKS[k - 1]) - off, dl[:, k:k + 1], ALU.is_ge, ALU.mult,
                     ["relf", "dl"], [tr])
                if k == 1:
                    b.ts("pool", bias6[:, dk, :], tmp[:], tab[:, 0:1], None, ALU.add, None, [tr, "tab"], [("b6", dk)])
                else:
                    b.tt("pool", bias6[:, dk, :], bias6[:, dk, :], tmp[:], ALU.add, [tr, ("b6", dk)], [("b6", dk)])

    scale = (64 ** -0.5) if diff else (192 ** -0.5)
    b.psrot = [0, 1, 2, 3]
    for qt in range(NQ // 512):
        qs = slice(qt * 512, (qt + 1) * 512)
        res_list = []
        for s in range(2 if diff else 1):
            if diff:
                ps_ = slice(s * 64, (s + 1) * 64)
                qlist = [(qa[ps_, qs], "qa")]
                kts = lambda kt, ps_=ps_: [(ka[ps_, kt * 128:(kt + 1) * 128], "ka")]

                def bias_fn(kt, qt=qt):
                    dk = kt - 4 * qt
                    if dk < -1:
                        return ("const", tab[:, 0:1])
                    if dk > 4:
                        return ("const", tab[:, NBK - 1:NBK])
                    return ("tile", (bias6[:, dk + 1, :], ("b6", dk + 1)))
            else:
                qlist = [(qa[:, qs], "qa"), (qb[:, qs], "qb")]
                kts = lambda kt: [(ka[:, kt * 128:(kt + 1) * 128], "ka"), (kb[:, kt * 128:(kt + 1) * 128], "kb")]
            vts = lambda kt: (vv[:, kt, :], "vv")
            if diff:
                res_list.append(attn_core(b, qlist, kts, vts, NKT, scale, 128, bias_fn=bias_fn))
            else:
                res_list.append(attn_core(b, qlist, kts, vts, NKT, scale, 128))
        po, pz = res_list[0]
        rz, rzr = b.nx("t")
        b.recip(rz[:], b.ps[pz][:, :], [("ps", pz)], [rzr])
        o, orr = b.nx("o")
        b.tt("dve", o[:], b.ps[po][:, :], rz[:], ALU.mult, [("ps", po), rzr], [orr])
        if diff:
            po2, pz2 = res_list[1]
            rz2, rz2r = b.nx("t")
            b.recip(rz2[:], b.ps[pz2][:, :], [("ps", pz2)], [rz2r])
            o2, o2r = b.nx("t")
            b.tt("dve", o2[:], b.ps[po2][:, :], rz2[:], ALU.mult, [("ps", po2), rz2r], [o2r])
            b.stt(o[:], o2[:], ls[:, 3:4], o[:], ALU.mult, ALU.add, [o2r, "ls", orr], [orr])
        b.store(o_d[:, qs], o[:], [orr])
    return b.finish()


def _vtiles(v):
    return np.ascontiguousarray(v.reshape(-1, 128, 128).transpose(1, 0, 2))


def prep_attn_diff(inp, l, o_dq, o_dk, o_dv):
    lam_init = 0.8 - 0.6 * math.exp(-0.3 * l)
    maps = []
    for c in range(NCORES):
        h, half = c // 2, c % 2
        rows = slice(h * 128, (h + 1) * 128)
        q, k, v = o_dq[rows], o_dk[rows], o_dv[:, rows]
        tab = inp["rel_bias"][T5_ORDER, h]
        if half == 1:
            q, k, v, tab = q[:, ::-1], k[:, ::-1], v[::-1], tab[::-1]
        cst = np.zeros((128, 2), np.float32)
        cst[:, 0] = lam_init
        maps.append({"qa": np.ascontiguousarray(q[:, :NQ]), "ka": np.ascontiguousarray(k), "v": _vtiles(np.ascontiguousarray(v)),
                     "tab": np.ascontiguousarray(tab.reshape(1, NBK)).astype(np.float32),
                     "lq": np.ascontiguousarray(inp["diff_lambda"][l].reshape(1, 256)), "cst": cst})
    return maps


def post_attn_diff(results):
    out = np.empty((512, SEQ), np.float32)
    for c in range(NCORES):
        h, half = c // 2, c % 2
        o = results[c]["o"]
        if half == 0:
            out[h * 128:(h + 1) * 128, :NQ] = o
        else:
            out[h * 128:(h + 1) * 128, NQ:] = o[:, ::-1]
    return out


def prep_attn_mla(o_mqn, o_mqr, o_mkn, o_mkr, o_mv):
    maps = []
    for c in range(NCORES):
        h, half = c // 2, c % 2
        qs = slice(half * NQ, (half + 1) * NQ)
        maps.append({"qa": np.ascontiguousarray(o_mqn[h * 128:(h + 1) * 128, qs]),
                     "qb": np.ascontiguousarray(o_mqr[h * 64:(h + 1) * 64, qs]),
                     "ka": np.ascontiguousarray(o_mkn[h * 128:(h + 1) * 128]),
                     "kb": np.ascontiguousarray(o_mkr),
                     "v": _vtiles(np.ascontiguousarray(o_mv[:, h * 128:(h + 1) * 128]))})
    return maps


def post_attn_mla(results):
    out = np.empty((512, SEQ), np.float32)
    for c in range(NCORES):
        h, half = c // 2, c % 2
        out[h * 128:(h + 1) * 128, half * NQ:(half + 1) * NQ] = results[c]["o"]
    return out


PC3 = {}
_c = 0
for _n, _w in (("ng", 16), ("gng", 4), ("gnb", 4), ("subg", 1), ("lamf", 1)):
    PC3[_n] = _c
    _c += _w
NPAR3 = _c
NCH3 = 16 + 16 * 4 + 16


def build_p3():
    b = B()
    P = b.P
    xT = b.din("xT", [NT1, 128, 16, TT])
    par_d = b.din("par", [128, NPAR3])
    wA = b.din("wA", [NCH3, 128, 16, 128])
    wB = b.din("wB", [16, 128, 16, 128])
    br_d = b.din("br", [NT1, 128, 6, 4, TT])
    o_x = b.dout("o_x", [NT1, 128, 16, TT])
    setup_consts(b)
    par = b.sb("par", [128, NPAR3])
    b.load(par[:], par_d, ["par"], "par")
    pc = lambda n, i=0: par[:, PC3[n] + i:PC3[n] + i + 1]
    xs = b.sb("xs", [128, 16, TT])
    hT = b.sb("hT", [128, 16, TT], BF16)
    br = b.sb("br", [128, 6, 4, TT])
    yg = b.sb("yg", [128, 16, TT], BF16)
    zT = b.sb("zT", [128, 16, TT], BF16)
    wstA = b.sb("wstA", [128, 2, 16, 128])
    wbfA = b.sb("wbfA", [128, 2, 16, 128], BF16)
    wstB = b.sb("wstB", [128, 2, 16, 128])
    wbfB = b.sb("wbfB", [128, 2, 16, 128], BF16)
    rsx = b.sb("rsx", [128, TT])
    zacc = b.sb("zacc", [128, TT])
    b.ring("t", 10, [128, TT])
    cntA = [0]
    cntB = [0]

    def loadA(ci):
        sl = cntA[0] % 2
        cntA[0] += 1
        b.load(wstA[:, sl], wA[ci], [("wstA", sl)], ("wstA", sl))
        return sl

    def loadB(ci):
        sl = cntB[0] % 2
        cntB[0] += 1
        b.load(wstB[:, sl], wB[ci], [("wstB", sl)], ("wstB", sl))
        return sl

    for tile in range(NT1):
        b.load(xs[:], xT[tile], ["xs"], "xs")
        b.load(br[:], br_d[tile], ["br"], "br", q="sp")
        pendA = loadA(0)
        pi = b.nps()
        for kc in range(16):
            sq, sqr = b.nx("t")
            b.act(sq[:], xs[:, kc, :], AF.Square, ["xs"], [sqr])
            b.mm(pi, 128, TT, b.ones[:], sq[:], kc == 0, kc == 15, [sqr, "ones"])
        ln, lnr = b.nx("t")
        b.act(ln[:], b.ps[pi][:, :], AF.Ln, [("ps", pi)], [lnr], bias=b.epsc[1e-6][:], scale=1.0 / 2048)
        b.act(rsx[:], ln[:], AF.Exp, [lnr], ["rsx"], scale=-0.5)
        for kc in range(16):
            b.stt(hT[:, kc, :], xs[:, kc, :], pc("ng", kc), rsx[:], ALU.mult, ALU.mult, ["xs", "rsx", "par"], ["hT"])
        for kc in range(4):
            ys, ysr = b.nx("t")
            b.tt("pool", ys[:], br[:, 0, kc, :], br[:, 1, kc, :], ALU.add, ["br"], [ysr])
            sq, sqr = b.nx("t")
            b.act(sq[:], ys[:], AF.Square, [ysr], [sqr])
            pm = b.nps()
            b.mm(pm, 128, TT, b.blk[:], ys[:], True, True, [ysr, "blk"])
            pe2 = b.nps()
            b.mm(pe2, 128, TT, b.blk[:], sq[:], True, True, [sqr, "blk"])
            mean, mr = b.nx("t")
            b.act(mean[:], b.ps[pm][:, :], AF.Copy, [("ps", pm)], [mr], scale=1.0 / 64)
            msq, msr = b.nx("t")
            b.act(msq[:], mean[:], AF.Square, [mr], [msr])
            var, vr = b.nx("t")
            b.stt(var[:], b.ps[pe2][:, :], 1.0 / 64, msq[:], ALU.mult, ALU.subtract, [("ps", pe2), msr], [vr])
            b.act(var[:], var[:], AF.Ln, [vr], [vr], bias=b.epsc[64e-5][:])
            b.act(var[:], var[:], AF.Exp, [vr], [vr], scale=-0.5)
            b.tt("pool", ys[:], ys[:], mean[:], ALU.subtract, [ysr, mr], [ysr])
            b.stt(ys[:], ys[:], pc("gng", kc), var[:], ALU.mult, ALU.mult, [ysr, "par", vr], [ysr])
            b.stt(br[:, 0, kc, :], ys[:], pc("gnb", kc), br[:, 2, kc, :], ALU.add, ALU.add, [ysr, "par", "br"], ["br"])
            sq2, sq2r = b.nx("t")
            b.act(sq2[:], br[:, 3, kc, :], AF.Square, ["br"], [sq2r])
            rs, rsr = fm_rstd(b, [(sq2[:], sq2r)], b.ones[:], 128, TT, 1.0 / 128, 1e-6, "ones")
            b.stt(br[:, 3, kc, :], br[:, 3, kc, :], pc("subg"), rs[:], ALU.mult, ALU.mult, ["br", "par", rsr], ["br"])
            b.ts("pool", br[:, 3, kc, :], br[:, 3, kc, :], pc("lamf"), None, ALU.mult, None, ["br", "par"], ["br"])
        ysrc = [0, 3, 4, 5]
        nxt = 1
        for g in range(16):
            sl = pendA
            if nxt < NCH3:
                pendA = loadA(nxt)
                nxt += 1
            b.cp("pool", wbfA[:, sl], wstA[:, sl], [("wstA", sl)], [("wbfA", sl)])
            pi = b.nps()
            for kc in range(16):
                b.mm(pi, 128, TT, wbfA[:, sl, kc, :], hT[:, kc, :], kc == 0, kc == 15, [("wbfA", sl), "hT"])
            sg, sgr = b.nx("t")
            b.act(sg[:], b.ps[pi][:, :], AF.Silu, [("ps", pi)], [sgr])
            bi, kc4 = g // 4, g % 4
            b.tt("dve", yg[:, g, :], br[:, ysrc[bi], kc4, :], sg[:], ALU.mult, ["br", sgr], ["yg"])
        pendB = loadB(0)
        for oc in range(16):
            slB = pendB
            if oc + 1 < 16:
                pendB = loadB(oc + 1)
            b.cp("pool", wbfB[:, slB], wstB[:, slB], [("wstB", slB)], [("wbfB", slB)])
            for bi in range(4):
                sl = pendA
                if nxt < NCH3:
                    pendA = loadA(nxt)
                    nxt += 1
                b.cp("pool", wbfA[:, sl], wstA[:, sl], [("wstA", sl)], [("wbfA", sl)])
                pm = b.nps()
                for kc in range(16):
                    b.mm(pm, 128, TT, wbfA[:, sl, kc, :], hT[:, kc, :], kc == 0, kc == 15, [("wbfA", sl), "hT"])
                pb = b.nps()
                for kc in range(4):
                    b.mm(pb, 128, TT, wbfB[:, slB, bi * 4 + kc, :], yg[:, bi * 4 + kc, :], kc == 0, kc == 3, [("wbfB", slB), "yg"])
                sg, sgr = b.nx("t")
                b.act(sg[:], b.ps[pm][:, :], AF.Sigmoid, [("ps", pm)], [sgr])
                if bi == 0:
                    b.tt("dve", zacc[:], b.ps[pb][:, :], sg[:], ALU.mult, [("ps", pb), sgr], ["zacc"])
                else:
                    tmp, tr = b.nx("t")
                    b.tt("dve", tmp[:], b.ps[pb][:, :], sg[:], ALU.mult, [("ps", pb), sgr], [tr])
                    if bi < 3:
                        b.tt("pool", zacc[:], zacc[:], tmp[:], ALU.add, ["zacc", tr], ["zacc"])
                    else:
                        b.tt("pool", zT[:, oc, :], zacc[:], tmp[:], ALU.add, ["zacc", tr], ["zT"])
        for oc in range(16):
            sl = pendA
            if nxt < NCH3:
                pendA = loadA(nxt)
                nxt += 1
            b.cp("pool", wbfA[:, sl], wstA[:, sl], [("wstA", sl)], [("wbfA", sl)])
            po = b.nps()
            for kc in range(16):
                b.mm(po, 128, TT, wbfA[:, sl, kc, :], zT[:, kc, :], kc == 0, kc == 15, [("wbfA", sl), "zT"])
            xo, xor_ = b.nx("t")
            b.tt("dve", xo[:], b.ps[po][:, :], xs[:, oc, :], ALU.add, [("ps", po), "xs"], [xor_])
            b.store(o_x[tile, :, oc, :], xo[:], [xor_])
    return b.finish()


GM0 = 1792 + 1536 + 384 + 256 + 64 + 512


def prep_p3(inp, l, x_cur, ysf, ysb, bonus, ybr, yc, yd):
    f = np.float32
    w_in = inp["w_in"][l]
    chunks = []
    for g in range(16):
        chunks.append(_fm(w_in[:, GM0 + g * 128:GM0 + (g + 1) * 128], 16))
    M0 = GM0 + 2048
    for oc in range(16):
        for bi in range(4):
            c0 = M0 + bi * 2048 + oc * 128
            chunks.append(_fm(w_in[:, c0:c0 + 128], 16))
    for oc in range(16):
        chunks.append(_fm(inp["w_out"][l][:, oc * 128:(oc + 1) * 128], 16))
    wA = np.stack(chunks)
    wb = inp["w_branch"][l].reshape(2048, 2048)
    wB = np.stack([_fm(wb[:, oc * 128:(oc + 1) * 128], 16) for oc in range(16)])
    par = np.zeros((128, NPAR3), f)
    par[:, PC3["ng"]:PC3["ng"] + 16] = inp["norm_g"][l].reshape(16, 128).T
    par[:, PC3["gng"]:PC3["gng"] + 4] = inp["rw_gn_g"][l].reshape(4, 128).T
    par[:, PC3["gnb"]:PC3["gnb"] + 4] = inp["rw_gn_b"][l].reshape(4, 128).T
    par[:, PC3["subg"]] = inp["diff_sub_g"][l]
    par[:, PC3["lamf"]] = 1.0 - (0.8 - 0.6 * math.exp(-0.3 * l))
    maps = []
    ntok = NT1 * TT
    for c in range(NCORES):
        xt = np.empty((NT1, 128, 16, TT), f)
        brr = np.empty((NT1, 128, 6, 4, TT), f)
        for t in range(NT1):
            n0 = c * ntok + t * TT
            xt[t] = x_cur[n0:n0 + TT].reshape(TT, 16, 128).transpose(2, 1, 0)
            for i, a in enumerate((ysf, ysb, bonus, ybr, yc, yd)):
                brr[t, :, i] = a[:, n0:n0 + TT].reshape(4, 128, TT).transpose(1, 0, 2)
        maps.append({"xT": xt, "par": par, "wA": wA, "wB": wB, "br": brr})
    return maps


def post_p3(results):
    outs = []
    for r in results:
        o = r["o_x"]
        outs.append(o.transpose(0, 3, 2, 1).reshape(NT1 * TT, 2048))
    return np.ascontiguousarray(np.concatenate(outs, 0))


_P1_AXIS = {"o_dv": 0, "o_mv": 0, "o_rw": 2}


def kernel(**inputs):
    inp = {k: np.asarray(v) for k, v in inputs.items()}
    x = np.ascontiguousarray(inp["x"][0], dtype=np.float32)
    for l in range(4):
        r1 = _run("p1", build_p1, prep_p1(inp, l, x))
        o = {k: _cat(r1, k, _P1_AXIS.get(k, 1)) for k in r1[0]}
        del r1
        rs = _run("scan2", build_scan2, prep_scan2(o["o_rw"]))
        ysf, ysb = post_scan2(rs)
        del rs
        yb = post_attn_diff(_run("diff", lambda: build_attn("diff"),
                                 prep_attn_diff(inp, l, o["o_dq"], o["o_dk"], o["o_dv"])))
        yc = post_attn_mla(_run("mla", lambda: build_attn("mla"),
                                prep_attn_mla(o["o_mqn"], o["o_mqr"], o["o_mkn"], o["o_mkr"], o["o_mv"])))
        r3 = _run("p3", build_p3, prep_p3(inp, l, x, ysf, ysb, o["o_bonus"], yb, yc, o["o_yd"]))
        x = post_p3(r3)
        del r3, o
    return x[None].astype(np.float32)


SB = 512
SG = 128
SC = 64


def build_scan2(T=SEQ):
    b = B()
    P = b.P
    fm_d = b.din("fm", [64, 2, 5, T])
    v_d = b.din("v", [64, 2, T])
    cst_d = b.din("cst", [128, 4, 128])
    m01_d = b.din("m01", [64, 2 * SB])
    y_d = b.dout("y", [64, 2, T])
    nblk = T // SB
    NI = (SB // SG) * 2
    cst = b.sb("cst", [128, 4, 128])
    m01 = b.sb("m01", [64, 2 * SB])
    b.load(cst[:], cst_d, ["cst"], "cst")
    b.load(m01[:], m01_d, ["m01"], "m01")
    Ml, Mu, MuI, I_ = (cst[:, i, :] for i in range(4))
    fmb = b.sb("fmb", [64, 2, 5, SB])
    vb = b.sb("vb", [64, 2, SB])
    sc = {n: b.sb(n, [64, 2, SB]) for n in ("KT", "NB", "KD", "RT", "NB2", "KD2")}
    b.ring("e", 4, [64, 2, SB])
    gC = b.sb("gC", [64, 2, SB // SC])
    clend = b.sb("clend", [64, 2, SB // SC])
    Hs = b.sb("Hs", [64, 2, SB // SC + 1, 64])
    yb = b.sb("yb", [64, 2, SB])
    it_buf = []
    for i in range(NI):
        d = {}
        for n, shp in (("N0", [128, 128]), ("N1", [128, 128]), ("P0", [128, 128]), ("P1", [128, 128]),
                       ("X0", [128, 128]), ("X1", [128, 128]), ("AkT", [128, 128]), ("BkT", [128, 128]),
                       ("BnbT", [128, 128]), ("NBt", [128, 64]), ("KDt", [128, 64]), ("Vt", [128, 64]),
                       ("NB2t", [128, 64]), ("KD2t", [128, 64]),
                       ("WT", [64, 128]), ("U", [128, 64]), ("G1", [64, 2, 64]), ("G2", [64, 2, 64])):
            d[n] = b.sb(f"i{i}{n}", shp)
        it_buf.append(d)
    b.memset("dve", Hs[:, :, 0, :], 0.0, ["Hs"])

    def r_(i, n):
        return (f"i{i}", n)

    def bulk(blk):
        t0 = blk * SB
        b.load(fmb[:], fm_d[:, :, :, t0:t0 + SB], ["fmb"], "fmb")
        b.load(vb[:], v_d[:, :, t0:t0 + SB], ["vb"], "vb")
        lw = fmb[:, :, 4, :]
        cl, clr = b.nx("e")
        for p in range(2):
            P.op("dve", lambda e, p=p, cl=cl: e.tensor_tensor_scan(
                out=cl[:, p, :], data0=m01[:, 0:SB], data1=fmb[:, p, 4, :], initial=0.0, op0=ALU.mult, op1=ALU.add),
                reads=["fmb", "m01"], writes=[clr])
        for p in range(2):
            b.cp("pool", clend[:, p, :], cl[:, p, SC - 1:SB:SC], [clr], ["clend"])
        e1, e1r = b.nx("e")
        b.tt("pool", e1[:], cl[:], lw, ALU.subtract, [clr, "fmb"], [e1r])
        b.act(e1[:], e1[:], AF.Exp, [e1r], [e1r])
        b.tt("dve", sc["KT"][:], fmb[:, :, 0, :], e1[:], ALU.mult, ["fmb", e1r], ["KT"])
        e2, e2r = b.nx("e")
        b.act(e2[:], cl[:], AF.Exp, [clr], [e2r], scale=-1.0)
        b.tt("pool", sc["NB"][:], fmb[:, :, 1, :], e2[:], ALU.mult, ["fmb", e2r], ["NB"])
        b.tt("dve", sc["KD"][:], fmb[:, :, 2, :], e2[:], ALU.mult, ["fmb", e2r], ["KD"])
        e3, e3r = b.nx("e")
        b.act(e3[:], cl[:], AF.Exp, [clr], [e3r])
        b.tt("pool", sc["RT"][:], fmb[:, :, 3, :], e3[:], ALU.mult, ["fmb", e3r], ["RT"])
        for p in range(2):
            b.cp("pool", gC[:, p, :], e3[:, p, SC - 1:SB:SC], [e3r], ["gC"])
        e4, e4r = b.nx("e")
        for p in range(2):
            for c in range(SB // SC):
                b.act(e4[:, p, c * SC:(c + 1) * SC], cl[:, p, c * SC:(c + 1) * SC], AF.Exp, [clr, "clend"], [e4r],
                      bias=clend[:, p, c:c + 1], scale=-1.0)
        b.tt("dve", sc["NB2"][:], fmb[:, :, 1, :], e4[:], ALU.mult, ["fmb", e4r], ["NB2"])
        b.tt("pool", sc["KD2"][:], fmb[:, :, 2, :], e4[:], ALU.mult, ["fmb", e4r], ["KD2"])

    def evac_act(dst, pi, M, N, wr, scale=None):
        if scale is None:
            b.cp("act", dst, b.ps[pi][0:M, 0:N], [("ps", pi)], wr)
        else:
            b.act(dst, b.ps[pi][0:M, 0:N], AF.Copy, [("ps", pi), "gC"], wr, scale=scale)

    def transpose(pi, in_ap, K, M, rd):
        out = b.ps[pi][0:M, 0:K]
        P.op("pe", lambda e: e.transpose(out, in_ap, I_[0:K, 0:K]), reads=rd + ["cst"], writes=[("ps", pi)])

    def stage1(blk):
        for i in range(NI):
            g, p = i // 2, i % 2
            ts = slice(g * SG, (g + 1) * SG)
            bf = it_buf[i]
            KT, NB, KD, RT = (sc[n][:, p, ts] for n in ("KT", "NB", "KD", "RT"))
            for (la, ln), (ra, rn), msk, dst in (((KT, "KT"), (NB, "NB"), Ml, "N0"), ((NB, "NB"), (KT, "KT"), Mu, "P0"),
                                                 ((KD, "KD"), (KT, "KT"), Mu, "AkT"), ((KD, "KD"), (RT, "RT"), MuI, "BkT"),
                                                 ((NB, "NB"), (RT, "RT"), MuI, "BnbT")):
                pi = b.nps()
                b.mm(pi, 128, 128, la, ra, True, True, [ln, rn])
                b.tt("dve", bf[dst][:], b.ps[pi][:, 0:128], msk, ALU.mult, [("ps", pi), "cst"], [r_(i, dst)])
            for src, sn, dst, dcols in ((sc["KT"], "KT", "X0", slice(0, 64)), (sc["NB"], "NB", "NBt", slice(0, 64)),
                                        (sc["KD"], "KD", "KDt", slice(0, 64)), (vb, "vb", "Vt", slice(0, 64)),
                                        (sc["NB2"], "NB2", "NB2t", slice(0, 64)), (sc["KD2"], "KD2", "KD2t", slice(0, 64))):
                pi = b.nps()
                transpose(pi, src[:, p, ts], 64, 128, [sn])
                evac_act(bf[dst][:, dcols], pi, 128, 64, [r_(i, dst)])
            pi = b.nps()
            b.mm(pi, 128, 64, bf["AkT"][:], bf["Vt"][:], True, True, [r_(i, "AkT"), r_(i, "Vt")])
            evac_act(bf["X0"][:, 64:128], pi, 128, 64, [r_(i, "X0")])

    def stage2(blk):
        for it in range(6):
            cur, nxt = it % 2, (it + 1) % 2
            for i in range(NI):
                bf = it_buf[i]
                Nc, Pc, Xc = bf[f"N{cur}"], bf[f"P{cur}"], bf[f"X{cur}"]
                Nn, Pn, Xn = bf[f"N{nxt}"], bf[f"P{nxt}"], bf[f"X{nxt}"]
                pi = b.nps()
                b.mm(pi, 128, 128, Pc[:], Xc[:], True, True, [r_(i, f"P{cur}"), r_(i, f"X{cur}")])
                b.tt("dve", Xn[:], Xc[:], b.ps[pi][:, 0:128], ALU.add, [("ps", pi), r_(i, f"X{cur}")], [r_(i, f"X{nxt}")])
                if it < 5:
                    pi = b.nps()
                    b.mm(pi, 128, 128, Nc[:], Pc[:], True, True, [r_(i, f"N{cur}"), r_(i, f"P{cur}")])
                    evac_act(Pn[:], pi, 128, 128, [r_(i, f"P{nxt}")])
                if it < 4:
                    pi = b.nps()
                    b.mm(pi, 128, 128, Pc[:], Nc[:], True, True, [r_(i, f"N{cur}"), r_(i, f"P{cur}")])
                    evac_act(Nn[:], pi, 128, 128, [r_(i, f"N{nxt}")])

    def stage3(blk):
        for i in range(NI):
            bf = it_buf[i]
            X = bf["X0"]
            pi = b.nps()
            transpose(pi, X[:, 0:64], 128, 64, [r_(i, "X0")])
            evac_act(bf["WT"][:], pi, 64, 128, [r_(i, "WT")])
            for c in range(2):
                cs = slice(c * SC, (c + 1) * SC)
                pi = b.nps()
                b.mm(pi, 64, 64, X[cs, 0:64], bf["NB2t"][cs, :], True, True, [r_(i, "X0"), r_(i, "NB2t")])
                pdiag, pdr = b.nx("dg")
                g, p = i // 2, i % 2
                cg = g * 2 + c
                b.ts("pool", pdiag[:], I_[0:64, 0:64], gC[:, p, cg:cg + 1], None, ALU.mult, None, ["cst", "gC"], [pdr])
                b.tt("dve", bf["G1"][:, c, :], b.ps[pi][0:64, 0:64], pdiag[:], ALU.add, [("ps", pi), pdr], [r_(i, "G1")])
                pi = b.nps()
                b.mm(pi, 64, 64, bf["NB2t"][cs, :], X[cs, 64:128], True, False, [r_(i, "X0"), r_(i, "NB2t")])
                b.mm(pi, 64, 64, bf["KD2t"][cs, :], bf["Vt"][cs, :], False, True, [r_(i, "KD2t"), r_(i, "Vt")])
                evac_act(bf["G2"][:, c, :], pi, 64, 64, [r_(i, "G2")])

    def stage4(blk):
        nchunk = SB // SC
        for cg in range(nchunk):
            for p in range(2):
                i = (cg // 2) * 2 + p
                c = cg % 2
                bf = it_buf[i]
                pi = b.nps()
                b.mm(pi, 64, 64, bf["G1"][:, c, :], Hs[:, p, cg, :], True, False, [r_(i, "G1"), ("Hs", p)])
                b.mm(pi, 64, 64, I_[0:64, 0:64], bf["G2"][:, c, :], False, True, ["cst", r_(i, "G2")])
                b.cp("act", Hs[:, p, cg + 1, :], b.ps[pi][0:64, 0:64], [("ps", pi)], [("Hs", p)])

    def stage5(blk):
        t0 = blk * SB
        for i in range(NI):
            g, p = i // 2, i % 2
            bf = it_buf[i]
            X = bf["X0"]
            for c in range(2):
                cs = slice(c * SC, (c + 1) * SC)
                cg = g * 2 + c
                pi = b.nps()
                b.mm(pi, 128, 64, bf["WT"][:], Hs[:, p, cg, :], True, True, [r_(i, "WT"), ("Hs", p)])
                b.tt("dve", bf["U"][cs, :], b.ps[pi][cs, 0:64], X[cs, 64:128], ALU.add, [("ps", pi), r_(i, "X0")], [r_(i, "U")])
            pi = b.nps()
            b.mm(pi, 64, 128, bf["Vt"][:], bf["BkT"][:], True, False, [r_(i, "Vt"), r_(i, "BkT")])
            b.mm(pi, 64, 128, bf["U"][:], bf["BnbT"][:], False, False, [r_(i, "U"), r_(i, "BnbT")])
            for c in range(2):
                cg = g * 2 + c
                out = b.ps[pi][0:64, c * SC:(c + 1) * SC]
                lhsT = Hs[:, p, cg, :]
                rhs = sc["RT"][:, p, g * SG + c * SC:g * SG + (c + 1) * SC]
                P.op("pe", lambda e, out=out, lhsT=lhsT, rhs=rhs, c=c: e.matmul(out, lhsT, rhs, start=False, stop=(c == 1)),
                     reads=[("Hs", p), "RT"], writes=[("ps", pi)])
            b.cp("act", yb[:, p, g * SG:(g + 1) * SG], b.ps[pi][0:64, 0:128], [("ps", pi)], ["yb"])
        b.store(y_d[:, :, t0:t0 + SB], yb[:], ["yb"])
        if blk + 1 < nblk:
            b.cp("pool", Hs[:, :, 0, :], Hs[:, :, SB // SC, :], [("Hs", 0), ("Hs", 1)], [("Hs", 0), ("Hs", 1)])

    b.ring("dg", 4, [64, 64])
    for blk in range(nblk):
        bulk(blk)
        stage1(blk)
        stage2(blk)
        stage3(blk)
        stage4(blk)
        stage5(blk)
    return b.finish()


def _scan2_consts():
    G, C = SG, SC
    Ml = np.zeros((G, G), np.float32)
    for t in range(G):
        for s in range(G):
            if t // C == s // C and s < t:
                Ml[t, s] = 1
    cst = np.stack([Ml, Ml.T, Ml.T + np.eye(G, dtype=np.float32), np.eye(G, dtype=np.float32)], 1)
    m01 = np.ones((64, 2 * SB), np.float32)
    m01[:, ::C] = 0
    return np.ascontiguousarray(cst), m01


def prep_scan2(o_rw, T=SEQ):
    cst, m01 = _scan2_consts()
    maps = []
    for h in range(NCORES):
        sl = slice(h * 64, (h + 1) * 64)
        fm = np.empty((64, 2, 5, T), np.float32)
        v = np.empty((64, 2, T), np.float32)
        for d in range(2):
            for k, a in enumerate((o_rw[2][sl], o_rw[3 + 3 * d][sl], o_rw[4 + 3 * d][sl], o_rw[0][sl], o_rw[5 + 3 * d][sl])):
                fm[:, d, k] = a if d == 0 else a[:, ::-1]
            v[:, d] = o_rw[1][sl] if d == 0 else o_rw[1][sl][:, ::-1]
        maps.append({"fm": fm, "v": v, "cst": cst, "m01": m01})
    return maps


def post_scan2(results):
    yf = np.concatenate([r["y"][:, 0] for r in results], 0)
    yb = np.concatenate([r["y"][:, 1][:, ::-1] for r in results], 0)
    return np.ascontiguousarray(yf), np.ascontiguousarray(yb)
```

```python
import math
from contextlib import ExitStack
import numpy as np
import concourse.bass as bass
import concourse.mybir as mybir
from concourse.bass_utils import run_bass_kernel_spmd

F32 = mybir.dt.float32
BF16 = mybir.dt.bfloat16
I32 = mybir.dt.int32
ALU = mybir.AluOpType
AF = mybir.ActivationFunctionType
AX = mybir.AxisListType
ENGS = ("pe", "act", "dve", "pool", "sp")
NCORES = 8


class _Op:
    __slots__ = ("eng", "fn", "waits", "signal", "dma_key", "idx", "sigval")

    def __init__(self, eng, fn, dma_key):
        self.eng = eng
        self.fn = fn
        self.waits = []
        self.signal = False
        self.dma_key = dma_key
        self.idx = None
        self.sigval = None


class _Res:
    __slots__ = ("w", "r")

    def __init__(self):
        self.w = None
        self.r = []


class Prog:
    def __init__(self, nc):
        self.nc = nc
        self.ops = {e: [] for e in ENGS}
        self.res = {}
        self.dma_cnt = {}
        self.dma_last = {}
        self.waited = {e: {} for e in ENGS}

    def _need(self, op, tok, isd):
        if tok is None:
            return
        kind, src, val = tok
        if kind == "e" and src == op.eng and not isd and src == "pe":
            return
        w = self.waited[op.eng]
        k = (kind, src)
        if w.get(k, -1) >= val:
            return
        w[k] = val
        op.waits.append(tok)
        if kind == "e":
            self.ops[src][val].signal = True

    def op(self, eng, fn, reads=(), writes=(), dma=None):
        o = _Op(eng, fn, dma)
        o.idx = len(self.ops[eng])
        isd = dma is not None
        if isd:
            n = self.dma_cnt.get(dma, 0) + 1
            self.dma_cnt[dma] = n
            self._need(o, self.dma_last.get(dma), True)
            tok = ("d", dma, n)
            self.dma_last[dma] = tok
        else:
            tok = ("e", eng, o.idx)
        for r in reads:
            st = self.res.setdefault(r, _Res())
            self._need(o, st.w, isd)
        for r in writes:
            st = self.res.setdefault(r, _Res())
            self._need(o, st.w, isd)
            for t in st.r:
                self._need(o, t, isd)
        for r in reads:
            st = self.res[r]
            st.r.append(tok)
            if len(st.r) > 48:
                st.r = st.r[-48:]
        for r in writes:
            st = self.res[r]
            st.w = tok
            st.r = []
        self.ops[eng].append(o)
        return tok

    def wait_tokens(self, eng, toks):
        o = _Op(eng, None, None)
        o.idx = len(self.ops[eng])
        for t in toks:
            self._need(o, t, True)
        self.ops[eng].append(o)

    def emit(self):
        nc = self.nc
        esem = {e: nc.alloc_semaphore(name=f"s_{e}") for e in ENGS}
        dsem = {k: nc.alloc_semaphore(name=f"d_{i}") for i, k in enumerate(self.dma_cnt)}
        for e in ENGS:
            c = 0
            for o in self.ops[e]:
                if o.signal:
                    c += 1
                    o.sigval = c
        ops = self.ops

        def body(e):
            def f(eng):
                for o in ops[e]:
                    for kind, src, val in o.waits:
                        if kind == "e":
                            eng.wait_ge(esem[src], ops[src][val].sigval)
                        else:
                            eng.wait_ge(dsem[src], 16 * val)
                    if o.fn is None:
                        continue
                    inst = o.fn(eng)
                    if o.dma_key is not None:
                        inst.then_inc(dsem[o.dma_key], 16)
                    elif o.signal:
                        inst.then_inc(esem[e], 1)
            return f

        with nc.Block() as block:
            block.tensor(body("pe"))
            block.scalar(body("act"))
            block.vector(body("dve"))
            block.gpsimd(body("pool"))
            block.sync(body("sp"))


class B:
    def __init__(self):
        self.nc = bass.Bass("TRN2", target_bir_lowering=False)
        self.P = Prog(self.nc)
        self.es = ExitStack()
        self.ps = [self.es.enter_context(self.nc.psum_tensor(f"ps{i}", [128, 512], F32)) for i in range(8)]
        self.psi = 0
        self.psrot = list(range(8))
        self.rings = {}
        self.outtoks = []
        self.ndq = 0
        self.attn_banks = [(4, 5), (6, 7)]
        self.attn_par = 0

    def din(self, name, shape, dt=F32):
        return self.nc.dram_tensor(name, list(shape), dt, kind="ExternalInput").ap()

    def dout(self, name, shape, dt=F32):
        return self.nc.dram_tensor(name, list(shape), dt, kind="ExternalOutput").ap()

    def sb(self, name, shape, dt=F32):
        return self.es.enter_context(self.nc.sbuf_tensor("s_" + name, list(shape), dt))

    def nps(self):
        rot = self.psrot
        i = rot[self.psi % len(rot)]
        self.psi += 1
        return i

    def ring(self, name, n, shape, dt=F32):
        self.rings[name] = [[self.sb(f"{name}{i}", shape, dt) for i in range(n)], 0]

    def nx(self, name):
        r = self.rings[name]
        i = r[1]
        r[1] = (i + 1) % len(r[0])
        return r[0][i], (name, i)

    def mm(self, pi, M, N, lhsT, rhs, st, sp, rd, po=0):
        out = self.ps[pi][po:po + M, 0:N]
        self.P.op("pe", lambda e: e.matmul(out, lhsT, rhs, start=st, stop=sp), reads=rd, writes=[("ps", pi)])

    def act(self, out, in_, func, rd, wr, bias=None, scale=None):
        kw = {}
        if bias is not None:
            kw["bias"] = bias
        if scale is not None:
            kw["scale"] = scale
        self.P.op("act", lambda e: e.activation(out=out, in_=in_, func=func, **kw), reads=rd, writes=wr)

    def stt(self, out, in0, scalar, in1, op0, op1, rd, wr):
        self.P.op("dve", lambda e: e.scalar_tensor_tensor(out=out, in0=in0, scalar=scalar, in1=in1,
                                                          op0=op0, op1=op1), reads=rd, writes=wr)

    def tt(self, eng, out, in0, in1, op, rd, wr):
        self.P.op(eng, lambda e: e.tensor_tensor(out=out, in0=in0, in1=in1, op=op), reads=rd, writes=wr)

    def ts(self, eng, out, in0, s1, s2, op0, op1, rd, wr):
        if op1 is None:
            self.P.op(eng, lambda e: e.tensor_scalar(out=out, in0=in0, scalar1=s1, scalar2=None, op0=op0),
                      reads=rd, writes=wr)
        else:
            self.P.op(eng, lambda e: e.tensor_scalar(out=out, in0=in0, scalar1=s1, scalar2=s2, op0=op0, op1=op1),
                      reads=rd, writes=wr)

    def cp(self, eng, out, in_, rd, wr):
        if eng == "act":
            self.P.op("act", lambda e: e.copy(out=out, in_=in_), reads=rd, writes=wr)
        else:
            self.P.op(eng, lambda e: e.tensor_copy(out=out, in_=in_), reads=rd, writes=wr)

    def wcast(self, wbf, wst, sl, rn, wn):
        self.cp("dve", wbf[:, sl, 0:8], wst[:, sl, 0:8], [(rn, sl)], [(wn, sl)])
        self.cp("act", wbf[:, sl, 8:16], wst[:, sl, 8:16], [(rn, sl)], [(wn, sl)])

    def recip(self, out, in_, rd, wr):
        self.P.op("dve", lambda e: e.reciprocal(out=out, in_=in_), reads=rd, writes=wr)

    def memset(self, eng, ap, val, wr):
        self.P.op(eng, lambda e: e.memset(ap, val), writes=wr)

    def load(self, out, in_, wr, key, q="sp"):
        return self.P.op(q, lambda e: e.dma_start(out=out, in_=in_), writes=wr, dma=key)

    def store(self, out, in_, rd, q="pool"):
        self.ndq += 1
        key = ("st", self.ndq % 6)
        t = self.P.op(q, lambda e: e.dma_start(out=out, in_=in_), reads=rd, dma=key)
        self.outtoks.append(t)

    def finish(self):
        last = {}
        for t in self.outtoks:
            last[t[1]] = t
        self.P.wait_tokens("pool", list(last.values()))
        self.P.emit()
        self.es.close()
        return self.nc


def fm_rstd(b, sq_list, ones_ap, Pn, N, inv_n, eps, consts_res):
    pi = b.nps()
    for i, (ap, res) in enumerate(sq_list):
        b.mm(pi, Pn, N, ones_ap, ap, i == 0, i == len(sq_list) - 1, [res, consts_res])
    ln, lnr = b.nx("t")
    b.act(ln[0:Pn, 0:N], b.ps[pi][0:Pn, 0:N], AF.Ln, [("ps", pi)], [lnr], bias=b.epsc[eps][0:Pn, :], scale=inv_n)
    rs, rsr = b.nx("t")
    b.act(rs[0:Pn, 0:N], ln[0:Pn, 0:N], AF.Exp, [lnr], [rsr], scale=-0.5)
    return rs, rsr


def setup_consts(b):
    b.ones = b.sb("ones", [128, 128])
    b.blk = b.sb("blk", [128, 128])
    b.memset("pool", b.ones[:], 1.0, ["ones"])
    b.memset("pool", b.blk[:], 0.0, ["blk"])
    b.memset("pool", b.blk[0:64, 0:64], 1.0, ["blk"])
    b.memset("pool", b.blk[64:128, 64:128], 1.0, ["blk"])
    b.onesb = b.sb("onesb", [128, 128], BF16)
    b.memset("pool", b.onesb[:], 1.0, ["onesb"])
    b.epsc = {}
    for i, v in enumerate((1e-6, 64e-5)):
        t = b.sb(f"epsc{i}", [128, 1])
        b.memset("pool", t[:], v, [f"epsc{i}"])
        b.epsc[v] = t
        b.P.res


def attn_core(b, qlist, kts, vts, nkt, scale, out_M, bias_fn=None, tag="a"):
    po, pz = b.attn_banks[b.attn_par]
    b.attn_par ^= 1
    acc, accr = b.nx("zacc")
    acc2, acc2r = b.nx("zacc2")
    pend = None

    def pv(kt, pt, ptr):
        va, vr = vts(kt)
        b.mm(po, out_M, 512, va, pt[:], kt == 0, kt == nkt - 1, [vr, ptr])

    for kt in range(nkt):
        pi = b.nps()
        ks = kts(kt)
        for i, ((qa, qr), (ka, kr)) in enumerate(zip(qlist, ks)):
            b.mm(pi, 128, 512, ka, qa, i == 0, i == len(qlist) - 1, [qr, kr])
        if pend is not None:
            pv(*pend)
        pt, ptr = b.nx("pt")
        bf = bias_fn(kt) if bias_fn is not None else None
        if bf is not None:
            kind, val = bf
            if kind == "const":
                b.act(pt[:], b.ps[pi][:, :], AF.Exp, [("ps", pi), "tab"], [ptr], bias=val, scale=scale)
            else:
                tmp, tr = b.nx("t")
                bap, bres = val
                b.stt(tmp[:], b.ps[pi][:, :], scale, bap, ALU.mult, ALU.add, [("ps", pi), bres], [tr])
                b.act(pt[:], tmp[:], AF.Exp, [tr], [ptr])
        else:
            b.act(pt[:], b.ps[pi][:, :], AF.Exp, [("ps", pi)], [ptr], scale=scale)
        eng = "pool" if kt % 3 == 0 else "dve"
        a_, a_r = (acc, accr) if eng == "pool" else (acc2, acc2r)
        if kt < 2 and (kt == 0 or eng == "dve"):
            b.cp(eng, a_[:], pt[:], [ptr], [a_r])
        else:
            b.tt(eng, a_[:], a_[:], pt[:], ALU.add, [ptr, a_r], [a_r])
        pend = (kt, pt, ptr)
    pv(*pend)
    b.mm(pz, out_M, 512, b.ones[:, 0:out_M], acc[:], True, False, ["ones", accr])
    b.mm(pz, out_M, 512, b.ones[:, 0:out_M], acc2[:], False, True, ["ones", acc2r])
    return po, pz


TT = 512
TH = TT + 2
NT1 = 2
PC = {}
_c = 0
for _n, _w in (("ng", 16), ("sh", 42), ("w0", 8), ("a0", 8), ("kk", 4), ("ka", 4), ("rk", 4), ("dqg", 1), ("dkg", 1),
               ("qlg", 3), ("kvg", 2), ("npg", 2), ("rpg", 2), ("rpgs", 2), ("mqg", 2), ("mng", 16), ("invf", 1),
               ("sgn", 1), ("omka", 4)):
    PC[_n] = _c
    _c += _w
NPAR = _c
RW0 = 0
def _p1_cols():
    cols = []
    cols += [(RW0 + 1536, 128), (RW0 + 1664, 128)]
    for c in range(4):
        cols += [(c * 128, 128), (512 + c * 128, 128), (1024 + c * 128, 128)]
    o = 1792
    cols += [(o + i * 128, 128) for i in range(4)]
    cols += [(o + 512 + i * 128, 128) for i in range(4)]
    o2 = 1792 + 1536
    cols += [(o2 + i * 128, 128) for i in range(3)]
    cols += [(o2 + 384 + i * 128, 128) for i in range(2)]
    cols += [("krope", 128)]
    o3 = o2 + 384 + 256 + 64
    cols += [(o3 + i * 128, 128) for i in range(4)]
    cols += [(1792 + 1024 + i * 128, 128) for i in range(4)]
    return cols
P1COLS = _p1_cols()
NCH1 = len(P1COLS)
KROPE0 = 1792 + 1536 + 384 + 256


def build_p1():
    b = B()
    P = b.P
    xT = b.din("xT", [NT1, 128, 16, TH])
    pos = b.din("pos", [NT1, TH], I32)
    par_d = b.din("par", [128, NPAR])
    w = b.din("w", [NCH1, 128, 16, 128])
    wup_d = b.din("wup", [128, 512])
    aup_d = b.din("aup", [128, 512])
    wuq_d = b.din("wuq", [128, 3, 768])
    wuqs_d = b.din("wuqs", [128, 3, 256])
    wukvk_d = b.din("wukvk", [128, 2, 512])
    wukvv_d = b.din("wukvv", [128, 2, 512])
    memT_d = b.din("memT", [128, 16, 256])
    wkv_d = b.din("wkv", [8, 128, 16, 128])
    o_rw = b.dout("o_rw", [9, 512, NT1 * TT])
    o_bonus = b.dout("o_bonus", [512, NT1 * TT])
    o_dq = b.dout("o_dq", [512, NT1 * TT])
    o_dk = b.dout("o_dk", [512, NT1 * TT])
    o_dv = b.dout("o_dv", [NT1 * TT, 512])
    o_mqn = b.dout("o_mqn", [512, NT1 * TT])
    o_mqr = b.dout("o_mqr", [256, NT1 * TT])
    o_mkn = b.dout("o_mkn", [512, NT1 * TT])
    o_mkr = b.dout("o_mkr", [64, NT1 * TT])
    o_mv = b.dout("o_mv", [NT1 * TT, 512])
    o_yd = b.dout("o_yd", [512, NT1 * TT])

    setup_consts(b)
    par = b.sb("par", [128, NPAR])
    b.load(par[:], par_d, ["par"], "par")
    pc = lambda n, i=0: par[:, PC[n] + i:PC[n] + i + 1]
    b.ts("dve", par[:, PC["omka"]:PC["omka"] + 4], par[:, PC["ka"]:PC["ka"] + 4], -1.0, 1.0, ALU.mult, ALU.add,
         ["par"], ["par"])
    xs = b.sb("xs", [128, 16, TH])
    hT = b.sb("hT", [128, 16, TH], BF16)
    wst = b.sb("wst", [128, 2, 16, 128])
    wbf = b.sb("wbf", [128, 2, 16, 128], BF16)
    stg = b.sb("stg", [128, 3072])
    wup = b.sb("wup_s", [128, 512])
    aup = b.sb("aup_s", [128, 512])
    wuq = b.sb("wuq_s", [128, 3, 768], BF16)
    wuqs = b.sb("wuqs_s", [128, 3, 256], BF16)
    wukvk = b.sb("wukvk_s", [128, 2, 512], BF16)
    wukvv = b.sb("wukvv_s", [128, 2, 512], BF16)
    memn = b.sb("memn", [128, 16, 256], BF16)
    kmem = b.sb("kmem", [128, 4, 256], BF16)
    vmem = b.sb("vmem", [128, 2, 512], BF16)
    b.ring("t", 14, [128, TT])
    b.ring("th", 3, [128, TH])
    b.ring("rkv", 4, [128, TT])
    b.ring("bf", 4, [128, TT], BF16)
    b.ring("pt", 3, [128, TT], BF16)
    b.ring("zacc", 1, [128, TT])
    b.ring("zacc2", 1, [128, TT])
    twd = b.sb("twd", [128, TT])
    adl = b.sb("adl", [128, TT])
    ropC = b.sb("ropC", [64, TT])
    ropS = b.sb("ropS", [64, TT])
    rsx = b.sb("rsx", [128, TH])
    ql = b.sb("ql", [128, 3, TT])
    qln = b.sb("qln", [128, 3, TT], BF16)
    kvl = b.sb("kvl", [128, 2, TT])
    kvn = b.sb("kvn", [128, 2, TT], BF16)
    posi = b.sb("posi", [64, TH], I32)

    b.load(wup[:], wup_d, ["wup"], "wl0")
    b.load(aup[:], aup_d, ["aup"], "wl1")
    for dst, src, n, nm in ((wuq, wuq_d, 3 * 768, "wuq"), (wuqs, wuqs_d, 3 * 256, "wuqs"),
                            (wukvk, wukvk_d, 1024, "wukvk"), (wukvv, wukvv_d, 1024, "wukvv")):
        b.load(stg[:, 0:n], src.rearrange("p a b -> p (a b)"), ["stg"], "stg")
        b.cp("dve", dst[:].rearrange("p a b -> p (a b)"), stg[:, 0:n], ["stg"], [nm])

    def rmsnorm_cols(src_tile, nkc, N, gname, dst_tile, res_src, res_dst, inv_n):
        pi = b.nps()
        pih = b.nps() if N > 512 else None
        for kc in range(nkc):
            sq, sqr = b.nx("th")
            b.act(sq[:, 0:N], src_tile[:, kc, 0:N], AF.Square, [res_src], [sqr])
            b.mm(pi, 128, min(N, 512), b.ones[:], sq[:, 0:min(N, 512)], kc == 0, kc == nkc - 1, [sqr, "ones"])
            if pih is not None:
                b.mm(pih, 128, N - 512, b.ones[:], sq[:, 512:N], kc == 0, kc == nkc - 1, [sqr, "ones"])
        ln, lnr = b.nx("th")
        b.act(ln[:, 0:min(N, 512)], b.ps[pi][:, 0:min(N, 512)], AF.Ln, [("ps", pi)], [lnr],
              bias=b.epsc[1e-6][:], scale=inv_n)
        if pih is not None:
            b.act(ln[:, 512:N], b.ps[pih][:, 0:N - 512], AF.Ln, [("ps", pih)], [lnr], bias=b.epsc[1e-6][:], scale=inv_n)
        b.act(rsx[:, 0:N], ln[:, 0:N], AF.Exp, [lnr], ["rsx"], scale=-0.5)
        for kc in range(nkc):
            b.stt(dst_tile[:, kc, 0:N], src_tile[:, kc, 0:N], pc(gname, kc), rsx[:, 0:N], ALU.mult, ALU.mult,
                  [res_src, "rsx", "par"], [res_dst])

    b.load(xs[:, :, 0:256], memT_d, ["xs"], "xs")
    rmsnorm_cols(xs, 16, 256, "mng", memn, "xs", "memn", 1.0 / 2048)
    for ci in range(8):
        sl = ci % 2
        b.load(wst[:, sl], wkv_d[ci], [("wst", sl)], ("wst", sl))
        b.wcast(wbf, wst, sl, "wst", "wbf")
        if ci < 4:
            pi = b.nps()
            for kc in range(16):
                b.mm(pi, 128, 256, wbf[:, sl, kc, :], memn[:, kc, :], kc == 0, kc == 15, [("wbf", sl), "memn"])
            tq, tqr = b.nx("t")
            b.cp("act", tq[:, 0:256], b.ps[pi][:, 0:256], [("ps", pi)], [tqr])
            sq, sqr = b.nx("t")
            b.act(sq[:, 0:256], b.ps[pi][:, 0:256], AF.Square, [("ps", pi)], [sqr])
            rs, rsr = fm_rstd(b, [(sq[:, 0:256], sqr)], b.ones[:], 128, 256, 1.0 / 128, 1e-6, "ones")
            b.stt(kmem[:, ci, :], tq[:, 0:256], pc("mqg", 1), rs[:, 0:256], ALU.mult, ALU.mult, [tqr, rsr, "par"], ["kmem"])
        else:
            for tb in range(2):
                pi = b.nps()
                for kc in range(16):
                    b.mm(pi, 128, 128, memn[:, kc, tb * 128:(tb + 1) * 128], wbf[:, sl, kc, :], kc == 0, kc == 15,
                         [("wbf", sl), "memn"])
                b.cp("act", vmem[:, tb, (ci - 4) * 128:(ci - 3) * 128], b.ps[pi][:, 0:128], [("ps", pi)], ["vmem"])

    def wload(tile, ci):
        sl = (tile * NCH1 + ci) % 2
        b.load(wst[:, sl], w[ci], [("wst", sl)], ("wst", sl))

    def shift(pi, pih, rc, dst, dres):
        u, ur = b.nx("th")
        b.cp("act", u[:, 0:TT], b.ps[pi][:, :], [("ps", pi)], [ur])
        b.cp("act", u[:, TT:TH], b.ps[pih][:, 0:2], [("ps", pih)], [ur])
        s0 = pc("sh", 0 * 14 + rc); s1 = pc("sh", 1 * 14 + rc); s2 = pc("sh", 2 * 14 + rc)
        b.ts("dve", dst[:, :], u[:, 0:TT], s1, None, ALU.mult, None, [ur, "par"], [dres])
        b.stt(dst[:, 1:TT], u[:, 0:TT - 1], s0, dst[:, 1:TT], ALU.mult, ALU.add, [ur, "par", dres], [dres])
        b.stt(dst[:, 0:1], u[:, TT:TT + 1], s0, dst[:, 0:1], ALU.mult, ALU.add, [ur, "par", dres], [dres])
        b.stt(dst[:, 0:TT - 1], u[:, 1:TT], s2, dst[:, 0:TT - 1], ALU.mult, ALU.add, [ur, "par", dres], [dres])
        b.stt(dst[:, TT - 1:TT], u[:, TT + 1:TT + 2], s2, dst[:, TT - 1:TT], ALU.mult, ALU.add, [ur, "par", dres], [dres])

    def head_norm(pi, Pn, ones_ap, inv_n, gcol, dst_ap, dres, N=TT):
        tq, tqr = b.nx("t")
        b.cp("act", tq[0:Pn, 0:N], b.ps[pi][0:Pn, 0:N], [("ps", pi)], [tqr])
        sq, sqr = b.nx("t")
        b.act(sq[0:Pn, 0:N], b.ps[pi][0:Pn, 0:N], AF.Square, [("ps", pi)], [sqr])
        rs, rsr = fm_rstd(b, [(sq[0:Pn, 0:N], sqr)], ones_ap, Pn, N, inv_n, 1e-6, "ones")
        b.stt(dst_ap, tq[0:Pn, 0:N], gcol, rs[0:Pn, 0:N], ALU.mult, ALU.mult, [tqr, rsr, "par"], [dres])
        return tq, tqr, rs, rsr

    for tile in range(NT1):
        t0 = tile * TT
        b.load(xs[:], xT[tile], ["xs"], "xs")
        b.load(posi[:], pos[tile, :].partition_broadcast(64), ["posi"], "posi")
        wload(tile, 0)
        rmsnorm_cols(xs, 16, TH, "ng", hT, "xs", "hT", 1.0 / 2048)
        posf, posr = b.nx("th")
        b.cp("dve", posf[0:64, :], posi[:], ["posi"], [posr])
        for which, dstt, dres in ((0, ropS, "ropS"), (1, ropC, "ropC")):
            a, ar = b.nx("t")
            b.ts("dve", a[0:64, :], posf[0:64, 0:TT], pc("invf")[0:64, :], (math.pi / 2 if which else 0.0),
                 ALU.mult, ALU.add, [posr, "par"], [ar])
            y, yr = b.nx("t")
            b.ts("dve", y[0:64, :], a[0:64, :], 1.0 / (2 * math.pi), None, ALU.mult, None, [ar], [yr])
            ni = b.sb(f"ni{tile}{which}", [64, TT], I32)
            b.cp("dve", ni[:], y[0:64, :], [yr], [f"ni{tile}{which}"])
            nf, nfr = b.nx("t")
            b.cp("dve", nf[0:64, :], ni[:], [f"ni{tile}{which}"], [nfr])
            r, rr = b.nx("t")
            b.stt(r[0:64, :], nf[0:64, :], -2 * math.pi, a[0:64, :], ALU.mult, ALU.add, [nfr, ar], [rr])
            m, mr = b.nx("t")
            b.ts("dve", m[0:64, :], r[0:64, :], math.pi, -2 * math.pi, ALU.is_gt, ALU.mult, [rr], [mr])
            b.tt("dve", r[0:64, :], r[0:64, :], m[0:64, :], ALU.add, [rr, mr], [rr])
            b.ts("dve", m[0:64, :], r[0:64, :], -math.pi, 2 * math.pi, ALU.is_lt, ALU.mult, [rr], [mr])
            b.tt("dve", r[0:64, :], r[0:64, :], m[0:64, :], ALU.add, [rr, mr], [rr])
            if which == 0:
                sn, snr = b.nx("t")
                b.act(sn[0:64, :], r[0:64, :], AF.Sin, [rr], [snr])
                b.ts("dve", ropS[:], sn[0:64, :], pc("sgn")[0:64, :], None, ALU.mult, None, [snr, "par"], ["ropS"])
            else:
                b.act(ropC[:], r[0:64, :], AF.Sin, [rr], ["ropC"])

        def rope_out(t_tq, t_r, sw_tq, sw_r, rs, rsr, gi, dst_dram):
            a, ar = b.nx("t")
            b.stt(a[0:64, :], t_tq[0:64, :], pc("rpg", gi)[0:64, :], ropC[:], ALU.mult, ALU.mult, [t_r, "par", "ropC"], [ar])
            c, cr = b.nx("t")
            b.stt(c[0:64, :], sw_tq[0:64, :], pc("rpgs", gi)[0:64, :], ropS[:], ALU.mult, ALU.mult, [sw_r, "par", "ropS"], [cr])
            b.tt("dve", a[0:64, :], a[0:64, :], c[0:64, :], ALU.add, [ar, cr], [ar])
            b.tt("dve", a[0:64, :], a[0:64, :], rs[0:64, :], ALU.mult, [ar, rsr], [ar])
            b.store(dst_dram, a[0:64, :], [ar])

        rcur = {}
        for ci in range(NCH1):
            sl = (tile * NCH1 + ci) % 2
            if ci + 1 < NCH1:
                wload(tile, ci + 1)
            elif tile + 1 < NT1:
                wload(tile + 1, 0)
            b.wcast(wbf, wst, sl, "wst", "wbf")
            wr = ("wbf", sl)
            if ci >= 32:
                j = ci - 32
                for tb in range(4):
                    pi = b.nps()
                    for kc in range(16):
                        b.mm(pi, 128, 128, hT[:, kc, tb * 128:(tb + 1) * 128], wbf[:, sl, kc, :], kc == 0, kc == 15, [wr, "hT"])
                    o, orr = b.nx("t")
                    b.cp("act", o[:, 0:128], b.ps[pi][:, 0:128], [("ps", pi)], [orr])
                    b.store(o_dv[t0 + tb * 128:t0 + (tb + 1) * 128, j * 128:(j + 1) * 128], o[:, 0:128], [orr])
                continue
            if ci == 27:
                pis = []
                for hh in range(2):
                    pi = b.nps()
                    for kc in range(16):
                        b.mm(pi, 64, TT, wbf[:, sl, kc, hh * 64:(hh + 1) * 64], hT[:, kc, 0:TT], kc == 0, kc == 15, [wr, "hT"])
                    pis.append(pi)
                tq, tqr = b.nx("t")
                b.cp("act", tq[0:64, :], b.ps[pis[0]][0:64, :], [("ps", pis[0])], [tqr])
                sq, sqr = b.nx("t")
                b.act(sq[0:64, :], b.ps[pis[0]][0:64, :], AF.Square, [("ps", pis[0])], [sqr])
                sw, swr = b.nx("t")
                b.cp("act", sw[0:64, :], b.ps[pis[1]][0:64, :], [("ps", pis[1])], [swr])
                rs, rsr = fm_rstd(b, [(sq[0:64, :], sqr)], b.ones[0:64, 0:64], 64, TT, 1.0 / 64, 1e-6, "ones")
                rope_out(tq, tqr, sw, swr, rs, rsr, 1, o_mkr[:, t0:t0 + TT])
                continue
            pi = b.nps()
            for kc in range(16):
                b.mm(pi, 128, TT, wbf[:, sl, kc, :], hT[:, kc, 0:TT], kc == 0, kc == 15, [wr, "hT"])
            pih = None
            if ci < 14:
                pih = b.nps()
                for kc in range(16):
                    b.mm(pih, 128, 2, wbf[:, sl, kc, :], hT[:, kc, TT:TH], kc == 0, kc == 15, [wr, "hT"])
            if ci == 0:
                tmp, tr = b.nx("t")
                shift(pi, pih, 12, tmp, tr)
                b.act(twd[:], tmp[:], AF.Tanh, [tr], ["twd"])
            elif ci == 1:
                shift(pi, pih, 13, adl, "adl")
            elif ci < 14:
                c = (ci - 2) // 3
                kind = (ci - 2) % 3
                dst, dres = b.nx("rkv")
                shift(pi, pih, kind * 4 + c, dst, dres)
                rcur[kind] = (dst, dres)
                if kind == 0:
                    b.store(o_rw[0, c * 128:(c + 1) * 128, t0:t0 + TT], dst[:], [dres])
                if kind == 2:
                    r_t, r_r = rcur[0]
                    k_t, k_r = rcur[1]
                    v_t, v_r = rcur[2]
                    b.store(o_rw[1, c * 128:(c + 1) * 128, t0:t0 + TT], v_t[:], [v_r])
                    kr_, krr = b.nx("t")
                    b.ts("dve", kr_[:], k_t[:], pc("kk", c), None, ALU.mult, None, [k_r, "par"], [krr])
                    sq, sqr = b.nx("t")
                    b.act(sq[:], kr_[:], AF.Square, [krr], [sqr])
                    pj = b.nps()
                    b.mm(pj, 128, TT, b.blk[:], sq[:], True, True, [sqr, "blk"])
                    nr, nrr = b.nx("t")
                    b.act(nr[:], b.ps[pj][:, :], AF.Sqrt, [("ps", pj)], [nrr])
                    b.ts("dve", nr[:], nr[:], 1e-12, None, ALU.max, None, [nrr], [nrr])
                    b.recip(nr[:], nr[:], [nrr], [nrr])
                    kk_, kkr = b.nx("t")
                    b.tt("dve", kk_[:], kr_[:], nr[:], ALU.mult, [krr, nrr], [kkr])
                    b.store(o_rw[2, c * 128:(c + 1) * 128, t0:t0 + TT], kk_[:], [kkr])
                    kds = []
                    for d in range(2):
                        pw = b.nps()
                        b.mm(pw, 128, TT, wup[d * 64:(d + 1) * 64, c * 128:(c + 1) * 128], twd[d * 64:(d + 1) * 64, :],
                             True, True, ["wup", "twd"])
                        sg, sgr = b.nx("t")
                        b.act(sg[:], b.ps[pw][:, :], AF.Sigmoid, [("ps", pw), "par"], [sgr], bias=pc("w0", d * 4 + c))
                        dec, decr = b.nx("t")
                        b.act(dec[:], sg[:], AF.Copy, [sgr], [decr], scale=-math.exp(-0.5))
                        b.store(o_rw[5 + 3 * d, c * 128:(c + 1) * 128, t0:t0 + TT], dec[:], [decr])
                        pa = b.nps()
                        b.mm(pa, 128, TT, aup[d * 64:(d + 1) * 64, c * 128:(c + 1) * 128], adl[d * 64:(d + 1) * 64, :],
                             True, True, ["aup", "adl"])
                        a_, a_r = b.nx("t")
                        b.act(a_[:], b.ps[pa][:, :], AF.Sigmoid, [("ps", pa), "par"], [a_r], bias=pc("a0", d * 4 + c))
                        nb, nbr = b.nx("t")
                        b.stt(nb[:], kk_[:], -1.0, a_[:], ALU.mult, ALU.mult, [kkr, a_r], [nbr])
                        b.store(o_rw[3 + 3 * d, c * 128:(c + 1) * 128, t0:t0 + TT], nb[:], [nbr])
                        kd, kdr = b.nx("t")
                        b.ts("dve", kd[:], a_[:], pc("ka", c), pc("omka", c), ALU.mult, ALU.add, [a_r, "par"], [kdr])
                        b.tt("dve", kd[:], kd[:], k_t[:], ALU.mult, [kdr, k_r], [kdr])
                        b.store(o_rw[4 + 3 * d, c * 128:(c + 1) * 128, t0:t0 + TT], kd[:], [kdr])
                        kds.append((kd, kdr))
                    s_, s_r = b.nx("t")
                    b.tt("dve", s_[:], kds[0][0][:], kds[1][0][:], ALU.add, [kds[0][1], kds[1][1]], [s_r])
                    b.stt(s_[:], r_t[:], pc("rk", c), s_[:], ALU.mult, ALU.mult, [r_r, "par", s_r], [s_r])
                    pb_ = b.nps()
                    b.mm(pb_, 128, TT, b.blk[:], s_[:], True, True, [s_r, "blk"])
                    bo, bor = b.nx("t")
                    b.tt("dve", bo[:], v_t[:], b.ps[pb_][:, :], ALU.mult, [v_r, ("ps", pb_)], [bor])
                    b.store(o_bonus[c * 128:(c + 1) * 128, t0:t0 + TT], bo[:], [bor])
            elif ci < 22:
                isk = ci >= 18
                j = ci - (18 if isk else 14)
                o, orr = b.nx("t")
                head_norm(pi, 128, b.blk[:], 1.0 / 64, pc("dkg" if isk else "dqg"), o[:], orr)
                b.store((o_dk if isk else o_dq)[j * 128:(j + 1) * 128, t0:t0 + TT], o[:], [orr])
            elif ci < 25:
                j = ci - 22
                b.cp("act", ql[:, j, :], b.ps[pi][:, :], [("ps", pi)], ["ql"])
                if j == 2:
                    rmsnorm_cols(ql, 3, TT, "qlg", qln, "ql", "qln", 1.0 / 384)
                    for h in range(4):
                        pn = b.nps()
                        for kc in range(3):
                            b.mm(pn, 128, TT, wuq[:, kc, h * 192:h * 192 + 128], qln[:, kc, :], kc == 0, kc == 2, ["wuq", "qln"])
                        o, orr = b.nx("t")
                        head_norm(pn, 128, b.ones[:], 1.0 / 128, pc("npg", 0), o[:], orr)
                        b.store(o_mqn[h * 128:(h + 1) * 128, t0:t0 + TT], o[:], [orr])
                        pr = b.nps()
                        for kc in range(3):
                            b.mm(pr, 64, TT, wuq[:, kc, h * 192 + 128:h * 192 + 192], qln[:, kc, :], kc == 0, kc == 2, ["wuq", "qln"])
                        psw = b.nps()
                        for kc in range(3):
                            b.mm(psw, 64, TT, wuqs[:, kc, h * 64:(h + 1) * 64], qln[:, kc, :], kc == 0, kc == 2, ["wuqs", "qln"])
                        tq, tqr = b.nx("t")
                        b.cp("act", tq[0:64, :], b.ps[pr][0:64, :], [("ps", pr)], [tqr])
                        sq, sqr = b.nx("t")
                        b.act(sq[0:64, :], b.ps[pr][0:64, :], AF.Square, [("ps", pr)], [sqr])
                        sw, swr = b.nx("t")
                        b.cp("act", sw[0:64, :], b.ps[psw][0:64, :], [("ps", psw)], [swr])
                        rs, rsr = fm_rstd(b, [(sq[0:64, :], sqr)], b.ones[0:64, 0:64], 64, TT, 1.0 / 64, 1e-6, "ones")
                        rope_out(tq, tqr, sw, swr, rs, rsr, 0, o_mqr[h * 64:(h + 1) * 64, t0:t0 + TT])
            elif ci < 27:
                j = ci - 25
                b.cp("act", kvl[:, j, :], b.ps[pi][:, :], [("ps", pi)], ["kvl"])
                if j == 1:
                    rmsnorm_cols(kvl, 2, TT, "kvg", kvn, "kvl", "kvn", 1.0 / 256)
                    for h in range(4):
                        pn = b.nps()
                        for kc in range(2):
                            b.mm(pn, 128, TT, wukvk[:, kc, h * 128:(h + 1) * 128], kvn[:, kc, :], kc == 0, kc == 1, ["wukvk", "kvn"])
                        o, orr = b.nx("t")
                        head_norm(pn, 128, b.ones[:], 1.0 / 128, pc("npg", 1), o[:], orr)
                        b.store(o_mkn[h * 128:(h + 1) * 128, t0:t0 + TT], o[:], [orr])
                    for tb in range(4):
                        pv = b.nps()
                        for kc in range(2):
                            b.mm(pv, 128, 512, kvn[:, kc, tb * 128:(tb + 1) * 128], wukvv[:, kc, :], kc == 0, kc == 1, ["wukvv", "kvn"])
                        o, orr = b.nx("t")
                        b.cp("act", o[:], b.ps[pv][:, :], [("ps", pv)], [orr])
                        b.store(o_mv[t0 + tb * 128:t0 + (tb + 1) * 128, :], o[:], [orr])
            else:
                j = ci - 28
                qb, qbr = b.nx("bf")
                head_norm(pi, 128, b.ones[:], 1.0 / 128, pc("mqg", 0), qb[:], qbr)
                b.psrot = [0, 1, 2, 3]
                po, pz = attn_core(b, [(qb[:], qbr)],
                                   lambda kt, j=j: [(kmem[:, j, kt * 128:(kt + 1) * 128], "kmem")],
                                   lambda kt, j=j: (vmem[:, kt, j * 128:(j + 1) * 128], "vmem"),
                                   2, 128 ** -0.5, 128)
                b.psrot = list(range(8))
                rz, rzr = b.nx("t")
                b.recip(rz[:], b.ps[pz][:, :], [("ps", pz)], [rzr])
                o, orr = b.nx("t")
                b.tt("dve", o[:], b.ps[po][:, :], rz[:], ALU.mult, [("ps", po), rzr], [orr])
                b.store(o_yd[j * 128:(j + 1) * 128, t0:t0 + TT], o[:], [orr])
    return b.finish()


def _fm(a, nk):
    return np.ascontiguousarray(a.reshape(nk, 128, a.shape[1]).transpose(1, 0, 2))


def _col(par, name, arr, i=0):
    par[:arr.shape[0], PC[name] + i] = arr


def prep_p1(inp, l, x_cur):
    f = np.float32
    w_in = inp["w_in"][l]
    chunks = []
    for col0, n in P1COLS:
        if col0 == "krope":
            idx = [KROPE0 + j for j in range(64)] + [KROPE0 + (j + 32) % 64 for j in range(64)]
            W = w_in[:, idx]
        else:
            W = w_in[:, col0:col0 + 128]
        chunks.append(_fm(W, 16))
    w = np.stack(chunks)
    par = np.zeros((128, NPAR), f)
    par[:, PC["ng"]:PC["ng"] + 16] = inp["norm_g"][l].reshape(16, 128).T
    sh = inp["rw_shift"][l]
    for j in range(3):
        for rc in range(14):
            _col(par, "sh", sh[j, rc * 128:(rc + 1) * 128], j * 14 + rc)
    for d in range(2):
        for c in range(4):
            _col(par, "w0", inp["rw_w0"][l][d, c * 128:(c + 1) * 128], d * 4 + c)
            _col(par, "a0", inp["rw_a0"][l][d, c * 128:(c + 1) * 128], d * 4 + c)
    rk = inp["rw_r_k"][l].reshape(512)
    for c in range(4):
        _col(par, "kk", inp["rw_k_k"][l][c * 128:(c + 1) * 128], c)
        _col(par, "ka", inp["rw_k_a"][l][c * 128:(c + 1) * 128], c)
        _col(par, "rk", rk[c * 128:(c + 1) * 128], c)
    _col(par, "dqg", np.tile(inp["diff_qk_g"][l][0], 2))
    _col(par, "dkg", np.tile(inp["diff_qk_g"][l][1], 2))
    par[:, PC["qlg"]:PC["qlg"] + 3] = inp["mla_q_lat_g"][l].reshape(3, 128).T
    par[:, PC["kvg"]:PC["kvg"] + 2] = inp["mla_kv_lat_g"][l].reshape(2, 128).T
    par[:, PC["npg"]:PC["npg"] + 2] = inp["mla_nope_g"][l].T
    for gi in range(2):
        g = inp["mla_rope_g"][l][gi]
        _col(par, "rpg", g, gi)
        _col(par, "rpgs", np.concatenate([g[32:], g[:32]]), gi)
    par[:, PC["mqg"]:PC["mqg"] + 2] = inp["mem_qk_g"][l].T
    par[:, PC["mng"]:PC["mng"] + 16] = inp["mem_norm_g"][l].reshape(16, 128).T
    invf = (10000.0 ** (-np.arange(0, 64, 2, dtype=np.float32) / 64)).astype(f)
    _col(par, "invf", np.tile(invf, 2))
    _col(par, "sgn", np.concatenate([-np.ones(32, f), np.ones(32, f)]))
    wuq = inp["mla_w_uq"][l]
    sw_idx = [h * 192 + 128 + (j + 32) % 64 for h in range(4) for j in range(64)]
    wukv = inp["mla_w_ukv"][l].reshape(256, 4, 256)
    common = {
        "par": par, "w": w,
        "wup": np.ascontiguousarray(inp["rw_w_up"][l].reshape(128, 512)),
        "aup": np.ascontiguousarray(inp["rw_a_up"][l].reshape(128, 512)),
        "wuq": _fm(wuq, 3), "wuqs": _fm(wuq[:, sw_idx], 3),
        "wukvk": _fm(np.ascontiguousarray(wukv[:, :, :128]).reshape(256, 512), 2),
        "wukvv": _fm(np.ascontiguousarray(wukv[:, :, 128:]).reshape(256, 512), 2),
        "memT": np.ascontiguousarray(inp["mem"][0].reshape(256, 16, 128).transpose(2, 1, 0)),
        "wkv": np.stack([_fm(inp["mem_w_kv"][l][:, ci * 128:(ci + 1) * 128], 16) for ci in range(8)]),
    }
    S = x_cur.shape[0]
    xpad = np.concatenate([np.zeros((1, 2048), f), x_cur, np.zeros((1, 2048), f)], 0)
    posv = inp["positions"][0]
    maps = []
    for c in range(NCORES):
        xt = np.empty((NT1, 128, 16, TH), f)
        pp = np.zeros((NT1, TH), np.int32)
        for t in range(NT1):
            n0 = c * (NT1 * TT) + t * TT
            xe = np.concatenate([xpad[n0 + 1:n0 + 1 + TT], xpad[n0:n0 + 1], xpad[n0 + 1 + TT:n0 + 2 + TT]], 0)
            xt[t] = xe.reshape(TH, 16, 128).transpose(2, 1, 0)
            pp[t, :TT] = posv[n0:n0 + TT]
        m = dict(common)
        m["xT"] = xt
        m["pos"] = pp
        maps.append(m)
    return maps


_NC_CACHE = {}


def _run(name, builder, maps):
    if name not in _NC_CACHE:
        _NC_CACHE[name] = builder()
    res = run_bass_kernel_spmd(_NC_CACHE[name], maps, core_ids=list(range(NCORES)))
    return res.results


def _cat(results, key, axis):
    return np.concatenate([r[key] for r in results], axis=axis)


SEQ = 8192
SC_CH = 32
SC_NV = 5


def build_scan(T=SEQ):
    b = B()
    P = b.P
    bc = b.din("bc", [2, T, SC_NV * 64])
    vcol = b.din("vcol", [128, T])
    y = b.dout("y", [128, T])
    nch = T // SC_CH
    W = SC_NV * 64
    bcb = b.sb("bcb", [128, 2, SC_CH * W])
    vc = b.sb("vc", [128, T])
    yt = b.sb("yt", [128, T])
    S = b.sb("S", [128, 64])
    Sd = b.sb("Sd", [128, 64])
    sa = b.sb("sa", [128, 1])
    b.ring("ja", 2, [128, 64])
    b.ring("jb", 2, [128, 64])
    b.load(vc[:], vcol, ["vc"], "vc")
    b.memset("dve", S[:], 0.0, ["S"])

    def load(c):
        sl = c % 2
        for pr in range(2):
            src = bc[pr, c * SC_CH:(c + 1) * SC_CH, :].rearrange("t f -> (t f)").partition_broadcast(64)
            b.load(bcb[pr * 64:(pr + 1) * 64, sl, :], src, [("bcb", sl, pr)], ("bc", sl, pr))

    load(0)
    for c in range(nch):
        if c + 1 < nch:
            load(c + 1)
        sl = c % 2
        rd = [("bcb", sl, 0), ("bcb", sl, 1)]
        for s in range(SC_CH):
            t = c * SC_CH + s
            o = s * W
            kap = bcb[:, sl, o:o + 64]
            nb = bcb[:, sl, o + 64:o + 128]
            kd = bcb[:, sl, o + 128:o + 192]
            rt = bcb[:, sl, o + 192:o + 256]
            dec = bcb[:, sl, o + 256:o + 320]
            ja, jar = b.nx("ja")
            P.op("dve", lambda e, kap=kap, ja=ja: e.scalar_tensor_tensor(
                out=ja[:], in0=S[:], scalar=1.0, in1=kap, op0=ALU.mult, op1=ALU.mult, accum_out=sa[:]),
                reads=rd + ["S"], writes=[jar, "sa"])
            b.tt("dve", Sd[:], S[:], dec, ALU.mult, rd + ["S"], ["Sd"])
            b.stt(Sd[:], nb, sa[:, 0:1], Sd[:], ALU.mult, ALU.add, rd + ["Sd", "sa"], ["Sd"])
            b.stt(S[:], kd, vc[:, t:t + 1], Sd[:], ALU.mult, ALU.add, rd + ["Sd", "vc"], ["S"])
            jb, jbr = b.nx("jb")
            P.op("dve", lambda e, rt=rt, jb=jb, t=t: e.scalar_tensor_tensor(
                out=jb[:], in0=S[:], scalar=1.0, in1=rt, op0=ALU.mult, op1=ALU.mult, accum_out=yt[:, t:t + 1]),
                reads=rd + ["S"], writes=[jbr, "yt"])
    b.store(y, yt[:], ["yt"])
    return b.finish()


def prep_scan(o_rw, T=SEQ):
    maps = []
    for h in range(NCORES):
        sl = slice(h * 64, (h + 1) * 64)
        bc = np.empty((2, T, SC_NV * 64), np.float32)
        for d in range(2):
            parts = [o_rw[2][sl], o_rw[3 + 3 * d][sl], o_rw[4 + 3 * d][sl], o_rw[0][sl], o_rw[5 + 3 * d][sl]]
            a = np.concatenate([p.T for p in parts], axis=1)
            bc[d] = a if d == 0 else a[::-1]
        v = o_rw[1][sl]
        vcol = np.concatenate([v, v[:, ::-1]], 0)
        maps.append({"bc": bc, "vcol": np.ascontiguousarray(vcol)})
    return maps


def post_scan(results):
    yf = np.concatenate([r["y"][0:64] for r in results], 0)
    yb = np.concatenate([r["y"][64:128][:, ::-1] for r in results], 0)
    return np.ascontiguousarray(yf), np.ascontiguousarray(yb)


NQ = SEQ // 2
NKT = SEQ // 128


def _t5_breaks():
    nb, max_exact = 16, 8
    n = np.arange(0, 1024, dtype=np.int32)
    n_f = np.maximum(n, max_exact).astype(np.float32)
    large = max_exact + (np.log(n_f / np.float32(max_exact)) / np.float32(math.log(128 / max_exact))
                         * np.float32(nb - max_exact)).astype(np.int32)
    large = np.minimum(large, nb - 1)
    f = np.where(n < max_exact, n, large)
    rels = np.arange(-1023, 1024)
    bk = np.where(rels > 0, 16, 0) + f[np.abs(rels)]
    order = [int(bk[0])]
    breaks = []
    for i in range(1, len(rels)):
        if bk[i] != bk[i - 1]:
            order.append(int(bk[i]))
            breaks.append(int(rels[i]))
    return order, breaks


T5_ORDER, T5_BREAKS = _t5_breaks()
NBK = len(T5_ORDER)


def build_attn(kind):
    b = B()
    P = b.P
    diff = kind == "diff"
    qa_d = b.din("qa", [128, NQ])
    ka_d = b.din("ka", [128, SEQ])
    v_d = b.din("v", [128, NKT, 128])
    if diff:
        tab_d = b.din("tab", [1, NBK])
        lq_d = b.din("lq", [1, 256])
        cst_d = b.din("cst", [128, 2])
    else:
        qb_d = b.din("qb", [64, NQ])
        kb_d = b.din("kb", [64, SEQ])
    o_d = b.dout("o", [128, NQ])
    setup_consts(b)
    ka = b.sb("ka", [128, SEQ], BF16)
    qa = b.sb("qa", [128, NQ], BF16)
    vv = b.sb("vv", [128, NKT, 128], BF16)
    b.ring("stg", 2, [128, 2048])
    b.ring("t", 6, [128, 512])
    b.ring("pt", 6, [128, 512], BF16)
    b.ring("zacc", 2, [128, 512])
    b.ring("zacc2", 2, [128, 512])
    b.ring("o", 2, [128, 512])

    def load_cast(dst, src, Pn, n, res):
        for i in range(0, n, 2048):
            st, sr = b.nx("stg")
            b.load(st[0:Pn, :], src[:, i:i + 2048], [sr], sr)
            b.cp("pool", dst[0:Pn, i:i + 2048], st[0:Pn, :], [sr], [res])

    load_cast(ka, ka_d, 128, SEQ, "ka")
    load_cast(qa, qa_d, 128, NQ, "qa")
    load_cast(vv[:].rearrange("p a b -> p (a b)"), v_d.rearrange("p a b -> p (a b)"), 128, NKT * 128, "vv")
    if not diff:
        kb = b.sb("kb", [64, SEQ], BF16)
        qb = b.sb("qb", [64, NQ], BF16)
        load_cast(kb, kb_d, 64, SEQ, "kb")
        load_cast(qb, qb_d, 64, NQ, "qb")
    else:
        lq = b.sb("lq", [128, 256])
        cst = b.sb("cst", [128, 2])
        tab = b.sb("tab", [128, NBK])
        b.load(lq[:], lq_d[0, :].partition_broadcast(128), ["lq"], "lq")
        b.load(cst[:], cst_d, ["cst"], "cst")
        b.load(tab[:], tab_d[0, :].partition_broadcast(128), ["tab"], "tab")
        pr = b.sb("lpr", [128, 128])
        b.tt("dve", pr[:, 0:64], lq[:, 0:64], lq[:, 64:128], ALU.mult, ["lq"], ["lpr"])
        b.tt("dve", pr[:, 64:128], lq[:, 128:192], lq[:, 192:256], ALU.mult, ["lq"], ["lpr"])
        ls = b.sb("ls", [128, 4])
        P.op("dve", lambda e: e.tensor_reduce(out=ls[:, 0:1], in_=pr[:, 0:64], axis=AX.X, op=ALU.add), reads=["lpr"], writes=["ls"])
        P.op("dve", lambda e: e.tensor_reduce(out=ls[:, 1:2], in_=pr[:, 64:128], axis=AX.X, op=ALU.add), reads=["lpr"], writes=["ls"])
        b.act(ls[:, 0:2], ls[:, 0:2], AF.Exp, ["ls"], ["ls"])
        b.tt("dve", ls[:, 2:3], ls[:, 0:1], ls[:, 1:2], ALU.subtract, ["ls"], ["ls"])
        b.tt("dve", ls[:, 2:3], ls[:, 2:3], cst[:, 0:1], ALU.add, ["ls", "cst"], ["ls"])
        b.ts("dve", ls[:, 3:4], ls[:, 2:3], -1.0, None, ALU.mult, None, ["ls"], ["ls"])
        dl = b.sb("dl", [128, NBK])
        b.tt("dve", dl[:, 1:NBK], tab[:, 1:NBK], tab[:, 0:NBK - 1], ALU.subtract, ["tab"], ["dl"])
        reli = b.sb("reli", [128, 512], I32)
        relf = b.sb("relf", [128, 512])
        P.op("pool", lambda e: e.iota(reli[:], [[-1, 512]], base=0, channel_multiplier=1), writes=["reli"])
        b.cp("dve", relf[:], reli[:], ["reli"], ["relf"])
        bias6 = b.sb("bias6", [128, 6, 512])
        for dk in range(6):
            off = float((dk - 1) * 128)
            for k in range(1, NBK):
                tmp, tr = b.nx("t")
                b.ts("dve", tmp[:], relf[:], float(T5_BREAKS[k - 1]) - off, dl[:, k:k + 1], ALU.is_ge, ALU.mult,
                     ["relf", "dl"], [tr])
                if k == 1:
                    b.ts("pool", bias6[:, dk, :], tmp[:], tab[:, 0:1], None, ALU.add, None, [tr, "tab"], [("b6", dk)])
                else:
                    b.tt("pool", bias6[:, dk, :], bias6[:, dk, :], tmp[:], ALU.add, [tr, ("b6", dk)], [("b6", dk)])

    scale = (64 ** -0.5) if diff else (192 ** -0.5)
    b.psrot = [0, 1, 2, 3]
    for qt in range(NQ // 512):
        qs = slice(qt * 512, (qt + 1) * 512)
        res_list = []
        for s in range(2 if diff else 1):
            if diff:
                ps_ = slice(s * 64, (s + 1) * 64)
                qlist = [(qa[ps_, qs], "qa")]
                kts = lambda kt, ps_=ps_: [(ka[ps_, kt * 128:(kt + 1) * 128], "ka")]

                def bias_fn(kt, qt=qt):
                    dk = kt - 4 * qt
                    if dk < -1:
                        return ("const", tab[:, 0:1])
                    if dk > 4:
                        return ("const", tab[:, NBK - 1:NBK])
                    return ("tile", (bias6[:, dk + 1, :], ("b6", dk + 1)))
            else:
                qlist = [(qa[:, qs], "qa"), (qb[:, qs], "qb")]
                kts = lambda kt: [(ka[:, kt * 128:(kt + 1) * 128], "ka"), (kb[:, kt * 128:(kt + 1) * 128], "kb")]
            vts = lambda kt: (vv[:, kt, :], "vv")
            if diff:
                res_list.append(attn_core(b, qlist, kts, vts, NKT, scale, 128, bias_fn=bias_fn))
            else:
                res_list.append(attn_core(b, qlist, kts, vts, NKT, scale, 128))
        po, pz = res_list[0]
        rz, rzr = b.nx("t")
        b.recip(rz[:], b.ps[pz][:, :], [("ps", pz)], [rzr])
        o, orr = b.nx("o")
        b.tt("dve", o[:], b.ps[po][:, :], rz[:], ALU.mult, [("ps", po), rzr], [orr])
        if diff:
            po2, pz2 = res_list[1]
            rz2, rz2r = b.nx("t")
            b.recip(rz2[:], b.ps[pz2][:, :], [("ps", pz2)], [rz2r])
            o2, o2r = b.nx("t")
            b.tt("dve", o2[:], b.ps[po2][:, :], rz2[:], ALU.mult, [("ps", po2), rz2r], [o2r])
            b.stt(o[:], o2[:], ls[:, 3:4], o[:], ALU.mult, ALU.add, [o2r, "ls", orr], [orr])
        b.store(o_d[:, qs], o[:], [orr])
    return b.finish()


def _vtiles(v):
    return np.ascontiguousarray(v.reshape(-1, 128, 128).transpose(1, 0, 2))


def prep_attn_diff(inp, l, o_dq, o_dk, o_dv):
    lam_init = 0.8 - 0.6 * math.exp(-0.3 * l)
    maps = []
    for c in range(NCORES):
        h, half = c // 2, c % 2
        rows = slice(h * 128, (h + 1) * 128)
        q, k, v = o_dq[rows], o_dk[rows], o_dv[:, rows]
        tab = inp["rel_bias"][T5_ORDER, h]
        if half == 1:
            q, k, v, tab = q[:, ::-1], k[:, ::-1], v[::-1], tab[::-1]
        cst = np.zeros((128, 2), np.float32)
        cst[:, 0] = lam_init
        maps.append({"qa": np.ascontiguousarray(q[:, :NQ]), "ka": np.ascontiguousarray(k), "v": _vtiles(np.ascontiguousarray(v)),
                     "tab": np.ascontiguousarray(tab.reshape(1, NBK)).astype(np.float32),
                     "lq": np.ascontiguousarray(inp["diff_lambda"][l].reshape(1, 256)), "cst": cst})
    return maps


def post_attn_diff(results):
    out = np.empty((512, SEQ), np.float32)
    for c in range(NCORES):
        h, half = c // 2, c % 2
        o = results[c]["o"]
        if half == 0:
            out[h * 128:(h + 1) * 128, :NQ] = o
        else:
            out[h * 128:(h + 1) * 128, NQ:] = o[:, ::-1]
    return out


def prep_attn_mla(o_mqn, o_mqr, o_mkn, o_mkr, o_mv):
    maps = []
    for c in range(NCORES):
        h, half = c // 2, c % 2
        qs = slice(half * NQ, (half + 1) * NQ)
        maps.append({"qa": np.ascontiguousarray(o_mqn[h * 128:(h + 1) * 128, qs]),
                     "qb": np.ascontiguousarray(o_mqr[h * 64:(h + 1) * 64, qs]),
                     "ka": np.ascontiguousarray(o_mkn[h * 128:(h + 1) * 128]),
                     "kb": np.ascontiguousarray(o_mkr),
                     "v": _vtiles(np.ascontiguousarray(o_mv[:, h * 128:(h + 1) * 128]))})
    return maps


def post_attn_mla(results):
    out = np.empty((512, SEQ), np.float32)
    for c in range(NCORES):
        h, half = c // 2, c % 2
        out[h * 128:(h + 1) * 128, half * NQ:(half + 1) * NQ] = results[c]["o"]
    return out


PC3 = {}
_c = 0
for _n, _w in (("ng", 16), ("gng", 4), ("gnb", 4), ("subg", 1), ("lamf", 1)):
    PC3[_n] = _c
    _c += _w
NPAR3 = _c
NCH3 = 16 + 16 * 4 + 16


def build_p3():
    b = B()
    P = b.P
    xT = b.din("xT", [NT1, 128, 16, TT])
    par_d = b.din("par", [128, NPAR3])
    wA = b.din("wA", [NCH3, 128, 16, 128])
    wB = b.din("wB", [16, 128, 16, 128])
    br_d = b.din("br", [NT1, 128, 6, 4, TT])
    o_x = b.dout("o_x", [NT1, 128, 16, TT])
    setup_consts(b)
    par = b.sb("par", [128, NPAR3])
    b.load(par[:], par_d, ["par"], "par")
    pc = lambda n, i=0: par[:, PC3[n] + i:PC3[n] + i + 1]
    xs = b.sb("xs", [128, 16, TT])
    hT = b.sb("hT", [128, 16, TT], BF16)
    br = b.sb("br", [128, 6, 4, TT])
    yg = b.sb("yg", [128, 16, TT], BF16)
    zT = b.sb("zT", [128, 16, TT], BF16)
    wstA = b.sb("wstA", [128, 2, 16, 128])
    wbfA = b.sb("wbfA", [128, 2, 16, 128], BF16)
    wstB = b.sb("wstB", [128, 2, 16, 128])
    wbfB = b.sb("wbfB", [128, 2, 16, 128], BF16)
    rsx = b.sb("rsx", [128, TT])
    zacc = b.sb("zacc", [128, TT])
    b.ring("t", 10, [128, TT])
    cntA = [0]
    cntB = [0]

    def loadA(ci):
        sl = cntA[0] % 2
        cntA[0] += 1
        b.load(wstA[:, sl], wA[ci], [("wstA", sl)], ("wstA", sl))
        return sl

    def loadB(ci):
        sl = cntB[0] % 2
        cntB[0] += 1
        b.load(wstB[:, sl], wB[ci], [("wstB", sl)], ("wstB", sl))
        return sl

    for tile in range(NT1):
        b.load(xs[:], xT[tile], ["xs"], "xs")
        b.load(br[:], br_d[tile], ["br"], "br", q="sp")
        pendA = loadA(0)
        pi = b.nps()
        for kc in range(16):
            sq, sqr = b.nx("t")
            b.act(sq[:], xs[:, kc, :], AF.Square, ["xs"], [sqr])
            b.mm(pi, 128, TT, b.ones[:], sq[:], kc == 0, kc == 15, [sqr, "ones"])
        ln, lnr = b.nx("t")
        b.act(ln[:], b.ps[pi][:, :], AF.Ln, [("ps", pi)], [lnr], bias=b.epsc[1e-6][:], scale=1.0 / 2048)
        b.act(rsx[:], ln[:], AF.Exp, [lnr], ["rsx"], scale=-0.5)
        for kc in range(16):
            b.stt(hT[:, kc, :], xs[:, kc, :], pc("ng", kc), rsx[:], ALU.mult, ALU.mult, ["xs", "rsx", "par"], ["hT"])
        for kc in range(4):
            ys, ysr = b.nx("t")
            b.tt("pool", ys[:], br[:, 0, kc, :], br[:, 1, kc, :], ALU.add, ["br"], [ysr])
            sq, sqr = b.nx("t")
            b.act(sq[:], ys[:], AF.Square, [ysr], [sqr])
            pm = b.nps()
            b.mm(pm, 128, TT, b.blk[:], ys[:], True, True, [ysr, "blk"])
            pe2 = b.nps()
            b.mm(pe2, 128, TT, b.blk[:], sq[:], True, True, [sqr, "blk"])
            mean, mr = b.nx("t")
            b.act(mean[:], b.ps[pm][:, :], AF.Copy, [("ps", pm)], [mr], scale=1.0 / 64)
            msq, msr = b.nx("t")
            b.act(msq[:], mean[:], AF.Square, [mr], [msr])
            var, vr = b.nx("t")
            b.stt(var[:], b.ps[pe2][:, :], 1.0 / 64, msq[:], ALU.mult, ALU.subtract, [("ps", pe2), msr], [vr])
            b.act(var[:], var[:], AF.Ln, [vr], [vr], bias=b.epsc[64e-5][:])
            b.act(var[:], var[:], AF.Exp, [vr], [vr], scale=-0.5)
            b.tt("pool", ys[:], ys[:], mean[:], ALU.subtract, [ysr, mr], [ysr])
            b.stt(ys[:], ys[:], pc("gng", kc), var[:], ALU.mult, ALU.mult, [ysr, "par", vr], [ysr])
            b.stt(br[:, 0, kc, :], ys[:], pc("gnb", kc), br[:, 2, kc, :], ALU.add, ALU.add, [ysr, "par", "br"], ["br"])
            sq2, sq2r = b.nx("t")
            b.act(sq2[:], br[:, 3, kc, :], AF.Square, ["br"], [sq2r])
            rs, rsr = fm_rstd(b, [(sq2[:], sq2r)], b.ones[:], 128, TT, 1.0 / 128, 1e-6, "ones")
            b.stt(br[:, 3, kc, :], br[:, 3, kc, :], pc("subg"), rs[:], ALU.mult, ALU.mult, ["br", "par", rsr], ["br"])
            b.ts("pool", br[:, 3, kc, :], br[:, 3, kc, :], pc("lamf"), None, ALU.mult, None, ["br", "par"], ["br"])
        ysrc = [0, 3, 4, 5]
        nxt = 1
        for g in range(16):
            sl = pendA
            if nxt < NCH3:
                pendA = loadA(nxt)
                nxt += 1
            b.wcast(wbfA, wstA, sl, "wstA", "wbfA")
            pi = b.nps()
            for kc in range(16):
                b.mm(pi, 128, TT, wbfA[:, sl, kc, :], hT[:, kc, :], kc == 0, kc == 15, [("wbfA", sl), "hT"])
            sg, sgr = b.nx("t")
            b.act(sg[:], b.ps[pi][:, :], AF.Silu, [("ps", pi)], [sgr])
            bi, kc4 = g // 4, g % 4
            b.tt("dve", yg[:, g, :], br[:, ysrc[bi], kc4, :], sg[:], ALU.mult, ["br", sgr], ["yg"])
        pendB = loadB(0)
        for oc in range(16):
            slB = pendB
            if oc + 1 < 16:
                pendB = loadB(oc + 1)
            b.wcast(wbfB, wstB, slB, "wstB", "wbfB")
            for bi in range(4):
                sl = pendA
                if nxt < NCH3:
                    pendA = loadA(nxt)
                    nxt += 1
                b.wcast(wbfA, wstA, sl, "wstA", "wbfA")
                pm = b.nps()
                for kc in range(16):
                    b.mm(pm, 128, TT, wbfA[:, sl, kc, :], hT[:, kc, :], kc == 0, kc == 15, [("wbfA", sl), "hT"])
                pb = b.nps()
                for kc in range(4):
                    b.mm(pb, 128, TT, wbfB[:, slB, bi * 4 + kc, :], yg[:, bi * 4 + kc, :], kc == 0, kc == 3, [("wbfB", slB), "yg"])
                sg, sgr = b.nx("t")
                b.act(sg[:], b.ps[pm][:, :], AF.Sigmoid, [("ps", pm)], [sgr])
                if bi == 0:
                    b.tt("dve", zacc[:], b.ps[pb][:, :], sg[:], ALU.mult, [("ps", pb), sgr], ["zacc"])
                else:
                    tmp, tr = b.nx("t")
                    b.tt("dve", tmp[:], b.ps[pb][:, :], sg[:], ALU.mult, [("ps", pb), sgr], [tr])
                    if bi < 3:
                        b.tt("pool", zacc[:], zacc[:], tmp[:], ALU.add, ["zacc", tr], ["zacc"])
                    else:
                        b.tt("pool", zT[:, oc, :], zacc[:], tmp[:], ALU.add, ["zacc", tr], ["zT"])
        for oc in range(16):
            sl = pendA
            if nxt < NCH3:
                pendA = loadA(nxt)
                nxt += 1
            b.wcast(wbfA, wstA, sl, "wstA", "wbfA")
            po = b.nps()
            for kc in range(16):
                b.mm(po, 128, TT, wbfA[:, sl, kc, :], zT[:, kc, :], kc == 0, kc == 15, [("wbfA", sl), "zT"])
            xo, xor_ = b.nx("t")
            b.tt("dve", xo[:], b.ps[po][:, :], xs[:, oc, :], ALU.add, [("ps", po), "xs"], [xor_])
            b.store(o_x[tile, :, oc, :], xo[:], [xor_])
    return b.finish()


GM0 = 1792 + 1536 + 384 + 256 + 64 + 512


def prep_p3(inp, l, x_cur, ysf, ysb, bonus, ybr, yc, yd):
    f = np.float32
    w_in = inp["w_in"][l]
    chunks = []
    for g in range(16):
        chunks.append(_fm(w_in[:, GM0 + g * 128:GM0 + (g + 1) * 128], 16))
    M0 = GM0 + 2048
    for oc in range(16):
        for bi in range(4):
            c0 = M0 + bi * 2048 + oc * 128
            chunks.append(_fm(w_in[:, c0:c0 + 128], 16))
    for oc in range(16):
        chunks.append(_fm(inp["w_out"][l][:, oc * 128:(oc + 1) * 128], 16))
    wA = np.stack(chunks)
    wb = inp["w_branch"][l].reshape(2048, 2048)
    wB = np.stack([_fm(wb[:, oc * 128:(oc + 1) * 128], 16) for oc in range(16)])
    par = np.zeros((128, NPAR3), f)
    par[:, PC3["ng"]:PC3["ng"] + 16] = inp["norm_g"][l].reshape(16, 128).T
    par[:, PC3["gng"]:PC3["gng"] + 4] = inp["rw_gn_g"][l].reshape(4, 128).T
    par[:, PC3["gnb"]:PC3["gnb"] + 4] = inp["rw_gn_b"][l].reshape(4, 128).T
    par[:, PC3["subg"]] = inp["diff_sub_g"][l]
    par[:, PC3["lamf"]] = 1.0 - (0.8 - 0.6 * math.exp(-0.3 * l))
    maps = []
    ntok = NT1 * TT
    for c in range(NCORES):
        xt = np.empty((NT1, 128, 16, TT), f)
        brr = np.empty((NT1, 128, 6, 4, TT), f)
        for t in range(NT1):
            n0 = c * ntok + t * TT
            xt[t] = x_cur[n0:n0 + TT].reshape(TT, 16, 128).transpose(2, 1, 0)
            for i, a in enumerate((ysf, ysb, bonus, ybr, yc, yd)):
                brr[t, :, i] = a[:, n0:n0 + TT].reshape(4, 128, TT).transpose(1, 0, 2)
        maps.append({"xT": xt, "par": par, "wA": wA, "wB": wB, "br": brr})
    return maps


def post_p3(results):
    outs = []
    for r in results:
        o = r["o_x"]
        outs.append(o.transpose(0, 3, 2, 1).reshape(NT1 * TT, 2048))
    return np.ascontiguousarray(np.concatenate(outs, 0))


_P1_AXIS = {"o_dv": 0, "o_mv": 0, "o_rw": 2}


def kernel(**inputs):
    inp = {k: np.asarray(v) for k, v in inputs.items()}
    x = np.ascontiguousarray(inp["x"][0], dtype=np.float32)
    for l in range(4):
        r1 = _run("p1", build_p1, prep_p1(inp, l, x))
        o = {k: _cat(r1, k, _P1_AXIS.get(k, 1)) for k in r1[0]}
        del r1
        rs = _run("scan2", build_scan2, prep_scan2(o["o_rw"]))
        ysf, ysb = post_scan2(rs)
        del rs
        yb = post_attn_diff(_run("diff", lambda: build_attn("diff"),
                                 prep_attn_diff(inp, l, o["o_dq"], o["o_dk"], o["o_dv"])))
        yc = post_attn_mla(_run("mla", lambda: build_attn("mla"),
                                prep_attn_mla(o["o_mqn"], o["o_mqr"], o["o_mkn"], o["o_mkr"], o["o_mv"])))
        r3 = _run("p3", build_p3, prep_p3(inp, l, x, ysf, ysb, o["o_bonus"], yb, yc, o["o_yd"]))
        x = post_p3(r3)
        del r3, o
    return x[None].astype(np.float32)


SB = 512
SG = 128
SC = 64


def build_scan2(T=SEQ):
    b = B()
    P = b.P
    fm_d = b.din("fm", [64, 2, 5, T])
    v_d = b.din("v", [64, 2, T])
    cst_d = b.din("cst", [128, 4, 128])
    m01_d = b.din("m01", [64, 2 * SB])
    y_d = b.dout("y", [64, 2, T])
    nblk = T // SB
    NI = (SB // SG) * 2
    cst = b.sb("cst", [128, 4, 128])
    m01 = b.sb("m01", [64, 2 * SB])
    b.load(cst[:], cst_d, ["cst"], "cst")
    b.load(m01[:], m01_d, ["m01"], "m01")
    Ml, Mu, MuI, I_ = (cst[:, i, :] for i in range(4))
    fmb = b.sb("fmb", [64, 2, 5, SB])
    vb = b.sb("vb", [64, 2, SB])
    sc = {n: b.sb(n, [64, 2, SB]) for n in ("KT", "NB", "KD", "RT", "NB2", "KD2")}
    b.ring("e", 4, [64, 2, SB])
    gC = b.sb("gC", [64, 2, SB // SC])
    clend = b.sb("clend", [64, 2, SB // SC])
    Hs = b.sb("Hs", [64, 2, SB // SC + 1, 64])
    yb = b.sb("yb", [64, 2, SB])
    it_buf = []
    for i in range(NI):
        d = {}
        for n, shp in (("N0", [128, 128]), ("N1", [128, 128]), ("P0", [128, 128]), ("P1", [128, 128]),
                       ("X0", [128, 128]), ("X1", [128, 128]), ("AkT", [128, 128]), ("BkT", [128, 128]),
                       ("BnbT", [128, 128]), ("NBt", [128, 64]), ("KDt", [128, 64]), ("Vt", [128, 64]),
                       ("NB2t", [128, 64]), ("KD2t", [128, 64]),
                       ("WT", [64, 128]), ("U", [128, 64]), ("G1", [64, 2, 64]), ("G2", [64, 2, 64])):
            d[n] = b.sb(f"i{i}{n}", shp)
        it_buf.append(d)
    b.memset("dve", Hs[:, :, 0, :], 0.0, ["Hs"])

    def r_(i, n):
        return (f"i{i}", n)

    def bulk(blk):
        t0 = blk * SB
        b.load(fmb[:], fm_d[:, :, :, t0:t0 + SB], ["fmb"], "fmb")
        b.load(vb[:], v_d[:, :, t0:t0 + SB], ["vb"], "vb")
        lw = fmb[:, :, 4, :]
        cl, clr = b.nx("e")
        for p in range(2):
            P.op("dve", lambda e, p=p, cl=cl: e.tensor_tensor_scan(
                out=cl[:, p, :], data0=m01[:, 0:SB], data1=fmb[:, p, 4, :], initial=0.0, op0=ALU.mult, op1=ALU.add),
                reads=["fmb", "m01"], writes=[clr])
        for p in range(2):
            b.cp("pool", clend[:, p, :], cl[:, p, SC - 1:SB:SC], [clr], ["clend"])
        e1, e1r = b.nx("e")
        b.tt("pool", e1[:], cl[:], lw, ALU.subtract, [clr, "fmb"], [e1r])
        b.act(e1[:], e1[:], AF.Exp, [e1r], [e1r])
        b.tt("dve", sc["KT"][:], fmb[:, :, 0, :], e1[:], ALU.mult, ["fmb", e1r], ["KT"])
        e2, e2r = b.nx("e")
        b.act(e2[:], cl[:], AF.Exp, [clr], [e2r], scale=-1.0)
        b.tt("pool", sc["NB"][:], fmb[:, :, 1, :], e2[:], ALU.mult, ["fmb", e2r], ["NB"])
        b.tt("dve", sc["KD"][:], fmb[:, :, 2, :], e2[:], ALU.mult, ["fmb", e2r], ["KD"])
        e3, e3r = b.nx("e")
        b.act(e3[:], cl[:], AF.Exp, [clr], [e3r])
        b.tt("pool", sc["RT"][:], fmb[:, :, 3, :], e3[:], ALU.mult, ["fmb", e3r], ["RT"])
        for p in range(2):
            b.cp("pool", gC[:, p, :], e3[:, p, SC - 1:SB:SC], [e3r], ["gC"])
        e4, e4r = b.nx("e")
        for p in range(2):
            for c in range(SB // SC):
                b.act(e4[:, p, c * SC:(c + 1) * SC], cl[:, p, c * SC:(c + 1) * SC], AF.Exp, [clr, "clend"], [e4r],
                      bias=clend[:, p, c:c + 1], scale=-1.0)
        b.tt("dve", sc["NB2"][:], fmb[:, :, 1, :], e4[:], ALU.mult, ["fmb", e4r], ["NB2"])
        b.tt("pool", sc["KD2"][:], fmb[:, :, 2, :], e4[:], ALU.mult, ["fmb", e4r], ["KD2"])

    def evac_act(dst, pi, M, N, wr, scale=None):
        if scale is None:
            b.cp("act", dst, b.ps[pi][0:M, 0:N], [("ps", pi)], wr)
        else:
            b.act(dst, b.ps[pi][0:M, 0:N], AF.Copy, [("ps", pi), "gC"], wr, scale=scale)

    def transpose(pi, in_ap, K, M, rd):
        out = b.ps[pi][0:M, 0:K]
        P.op("pe", lambda e: e.transpose(out, in_ap, I_[0:K, 0:K]), reads=rd + ["cst"], writes=[("ps", pi)])

    def stage1(blk):
        for i in range(NI):
            g, p = i // 2, i % 2
            ts = slice(g * SG, (g + 1) * SG)
            bf = it_buf[i]
            KT, NB, KD, RT = (sc[n][:, p, ts] for n in ("KT", "NB", "KD", "RT"))
            for (la, ln), (ra, rn), msk, dst in (((KT, "KT"), (NB, "NB"), Ml, "N0"), ((NB, "NB"), (KT, "KT"), Mu, "P0"),
                                                 ((KD, "KD"), (KT, "KT"), Mu, "AkT"), ((KD, "KD"), (RT, "RT"), MuI, "BkT"),
                                                 ((NB, "NB"), (RT, "RT"), MuI, "BnbT")):
                pi = b.nps()
                b.mm(pi, 128, 128, la, ra, True, True, [ln, rn])
                b.tt("dve", bf[dst][:], b.ps[pi][:, 0:128], msk, ALU.mult, [("ps", pi), "cst"], [r_(i, dst)])
            for src, sn, dst, dcols in ((sc["KT"], "KT", "X0", slice(0, 64)), (sc["NB"], "NB", "NBt", slice(0, 64)),
                                        (sc["KD"], "KD", "KDt", slice(0, 64)), (vb, "vb", "Vt", slice(0, 64)),
                                        (sc["NB2"], "NB2", "NB2t", slice(0, 64)), (sc["KD2"], "KD2", "KD2t", slice(0, 64))):
                pi = b.nps()
                transpose(pi, src[:, p, ts], 64, 128, [sn])
                evac_act(bf[dst][:, dcols], pi, 128, 64, [r_(i, dst)])
            pi = b.nps()
            b.mm(pi, 128, 64, bf["AkT"][:], bf["Vt"][:], True, True, [r_(i, "AkT"), r_(i, "Vt")])
            evac_act(bf["X0"][:, 64:128], pi, 128, 64, [r_(i, "X0")])

    def stage2(blk):
        for it in range(6):
            cur, nxt = it % 2, (it + 1) % 2
            for i in range(NI):
                bf = it_buf[i]
                Nc, Pc, Xc = bf[f"N{cur}"], bf[f"P{cur}"], bf[f"X{cur}"]
                Nn, Pn, Xn = bf[f"N{nxt}"], bf[f"P{nxt}"], bf[f"X{nxt}"]
                pi = b.nps()
                b.mm(pi, 128, 128, Pc[:], Xc[:], True, True, [r_(i, f"P{cur}"), r_(i, f"X{cur}")])
                b.tt("dve", Xn[:], Xc[:], b.ps[pi][:, 0:128], ALU.add, [("ps", pi), r_(i, f"X{cur}")], [r_(i, f"X{nxt}")])
                if it < 5:
                    pi = b.nps()
                    b.mm(pi, 128, 128, Nc[:], Pc[:], True, True, [r_(i, f"N{cur}"), r_(i, f"P{cur}")])
                    evac_act(Pn[:], pi, 128, 128, [r_(i, f"P{nxt}")])
                if it < 4:
                    pi = b.nps()
                    b.mm(pi, 128, 128, Pc[:], Nc[:], True, True, [r_(i, f"N{cur}"), r_(i, f"P{cur}")])
                    evac_act(Nn[:], pi, 128, 128, [r_(i, f"N{nxt}")])

    def stage3(blk):
        for i in range(NI):
            bf = it_buf[i]
            X = bf["X0"]
            pi = b.nps()
            transpose(pi, X[:, 0:64], 128, 64, [r_(i, "X0")])
            evac_act(bf["WT"][:], pi, 64, 128, [r_(i, "WT")])
            for c in range(2):
                cs = slice(c * SC, (c + 1) * SC)
                pi = b.nps()
                b.mm(pi, 64, 64, X[cs, 0:64], bf["NB2t"][cs, :], True, True, [r_(i, "X0"), r_(i, "NB2t")])
                pdiag, pdr = b.nx("dg")
                g, p = i // 2, i % 2
                cg = g * 2 + c
                b.ts("pool", pdiag[:], I_[0:64, 0:64], gC[:, p, cg:cg + 1], None, ALU.mult, None, ["cst", "gC"], [pdr])
                b.tt("dve", bf["G1"][:, c, :], b.ps[pi][0:64, 0:64], pdiag[:], ALU.add, [("ps", pi), pdr], [r_(i, "G1")])
                pi = b.nps()
                b.mm(pi, 64, 64, bf["NB2t"][cs, :], X[cs, 64:128], True, False, [r_(i, "X0"), r_(i, "NB2t")])
                b.mm(pi, 64, 64, bf["KD2t"][cs, :], bf["Vt"][cs, :], False, True, [r_(i, "KD2t"), r_(i, "Vt")])
                evac_act(bf["G2"][:, c, :], pi, 64, 64, [r_(i, "G2")])

    def stage4(blk):
        nchunk = SB // SC
        for cg in range(nchunk):
            for p in range(2):
                i = (cg // 2) * 2 + p
                c = cg % 2
                bf = it_buf[i]
                pi = b.nps()
                b.mm(pi, 64, 64, bf["G1"][:, c, :], Hs[:, p, cg, :], True, False, [r_(i, "G1"), ("Hs", p)])
                b.mm(pi, 64, 64, I_[0:64, 0:64], bf["G2"][:, c, :], False, True, ["cst", r_(i, "G2")])
                b.cp("act", Hs[:, p, cg + 1, :], b.ps[pi][0:64, 0:64], [("ps", pi)], [("Hs", p)])

    def stage5(blk):
        t0 = blk * SB
        for i in range(NI):
            g, p = i // 2, i % 2
            bf = it_buf[i]
            X = bf["X0"]
            for c in range(2):
                cs = slice(c * SC, (c + 1) * SC)
                cg = g * 2 + c
                pi = b.nps()
                b.mm(pi, 128, 64, bf["WT"][:], Hs[:, p, cg, :], True, True, [r_(i, "WT"), ("Hs", p)])
                b.tt("dve", bf["U"][cs, :], b.ps[pi][cs, 0:64], X[cs, 64:128], ALU.add, [("ps", pi), r_(i, "X0")], [r_(i, "U")])
            pi = b.nps()
            b.mm(pi, 64, 128, bf["Vt"][:], bf["BkT"][:], True, False, [r_(i, "Vt"), r_(i, "BkT")])
            b.mm(pi, 64, 128, bf["U"][:], bf["BnbT"][:], False, False, [r_(i, "U"), r_(i, "BnbT")])
            for c in range(2):
                cg = g * 2 + c
                out = b.ps[pi][0:64, c * SC:(c + 1) * SC]
                lhsT = Hs[:, p, cg, :]
                rhs = sc["RT"][:, p, g * SG + c * SC:g * SG + (c + 1) * SC]
                P.op("pe", lambda e, out=out, lhsT=lhsT, rhs=rhs, c=c: e.matmul(out, lhsT, rhs, start=False, stop=(c == 1)),
                     reads=[("Hs", p), "RT"], writes=[("ps", pi)])
            b.cp("act", yb[:, p, g * SG:(g + 1) * SG], b.ps[pi][0:64, 0:128], [("ps", pi)], ["yb"])
        b.store(y_d[:, :, t0:t0 + SB], yb[:], ["yb"])
        if blk + 1 < nblk:
            b.cp("pool", Hs[:, :, 0, :], Hs[:, :, SB // SC, :], [("Hs", 0), ("Hs", 1)], [("Hs", 0), ("Hs", 1)])

    b.ring("dg", 4, [64, 64])
    for blk in range(nblk):
        bulk(blk)
        stage1(blk)
        stage2(blk)
        stage3(blk)
        stage4(blk)
        stage5(blk)
    return b.finish()


def _scan2_consts():
    G, C = SG, SC
    Ml = np.zeros((G, G), np.float32)
    for t in range(G):
        for s in range(G):
            if t // C == s // C and s < t:
                Ml[t, s] = 1
    cst = np.stack([Ml, Ml.T, Ml.T + np.eye(G, dtype=np.float32), np.eye(G, dtype=np.float32)], 1)
    m01 = np.ones((64, 2 * SB), np.float32)
    m01[:, ::C] = 0
    return np.ascontiguousarray(cst), m01


def prep_scan2(o_rw, T=SEQ):
    cst, m01 = _scan2_consts()
    maps = []
    for h in range(NCORES):
        sl = slice(h * 64, (h + 1) * 64)
        fm = np.empty((64, 2, 5, T), np.float32)
        v = np.empty((64, 2, T), np.float32)
        for d in range(2):
            for k, a in enumerate((o_rw[2][sl], o_rw[3 + 3 * d][sl], o_rw[4 + 3 * d][sl], o_rw[0][sl], o_rw[5 + 3 * d][sl])):
                fm[:, d, k] = a if d == 0 else a[:, ::-1]
            v[:, d] = o_rw[1][sl] if d == 0 else o_rw[1][sl][:, ::-1]
        maps.append({"fm": fm, "v": v, "cst": cst, "m01": m01})
    return maps


def post_scan2(results):
    yf = np.concatenate([r["y"][:, 0] for r in results], 0)
    yb = np.concatenate([r["y"][:, 1][:, ::-1] for r in results], 0)
    return np.ascontiguousarray(yf), np.ascontiguousarray(yb)
```

```python
import math
from contextlib import ExitStack
import numpy as np
import concourse.bass as bass
import concourse.mybir as mybir
from concourse.bass_utils import run_bass_kernel_spmd

F32 = mybir.dt.float32
BF16 = mybir.dt.bfloat16
I32 = mybir.dt.int32
ALU = mybir.AluOpType
AF = mybir.ActivationFunctionType
AX = mybir.AxisListType
ENGS = ("pe", "act", "dve", "pool", "sp")
NCORES = 8


class _Op:
    __slots__ = ("eng", "fn", "waits", "signal", "dma_key", "idx", "sigval")

    def __init__(self, eng, fn, dma_key):
        self.eng = eng
        self.fn = fn
        self.waits = []
        self.signal = False
        self.dma_key = dma_key
        self.idx = None
        self.sigval = None


class _Res:
    __slots__ = ("w", "r")

    def __init__(self):
        self.w = None
        self.r = []


class Prog:
    def __init__(self, nc):
        self.nc = nc
        self.ops = {e: [] for e in ENGS}
        self.res = {}
        self.dma_cnt = {}
        self.dma_last = {}
        self.waited = {e: {} for e in ENGS}

    def _need(self, op, tok, isd):
        if tok is None:
            return
        kind, src, val = tok
        if kind == "e" and src == op.eng and not isd and src == "pe":
            return
        w = self.waited[op.eng]
        k = (kind, src)
        if w.get(k, -1) >= val:
            return
        w[k] = val
        op.waits.append(tok)
        if kind == "e":
            self.ops[src][val].signal = True

    def op(self, eng, fn, reads=(), writes=(), dma=None):
        o = _Op(eng, fn, dma)
        o.idx = len(self.ops[eng])
        isd = dma is not None
        if isd:
            n = self.dma_cnt.get(dma, 0) + 1
            self.dma_cnt[dma] = n
            self._need(o, self.dma_last.get(dma), True)
            tok = ("d", dma, n)
            self.dma_last[dma] = tok
        else:
            tok = ("e", eng, o.idx)
        for r in reads:
            st = self.res.setdefault(r, _Res())
            self._need(o, st.w, isd)
        for r in writes:
            st = self.res.setdefault(r, _Res())
            self._need(o, st.w, isd)
            for t in st.r:
                self._need(o, t, isd)
        for r in reads:
            st = self.res[r]
            st.r.append(tok)
            if len(st.r) > 48:
                st.r = st.r[-48:]
        for r in writes:
            st = self.res[r]
            st.w = tok
            st.r = []
        self.ops[eng].append(o)
        return tok

    def wait_tokens(self, eng, toks):
        o = _Op(eng, None, None)
        o.idx = len(self.ops[eng])
        for t in toks:
            self._need(o, t, True)
        self.ops[eng].append(o)

    def emit(self):
        nc = self.nc
        esem = {e: nc.alloc_semaphore(name=f"s_{e}") for e in ENGS}
        dsem = {k: nc.alloc_semaphore(name=f"d_{i}") for i, k in enumerate(self.dma_cnt)}
        for e in ENGS:
            c = 0
            for o in self.ops[e]:
                if o.signal:
                    c += 1
                    o.sigval = c
        ops = self.ops

        def body(e):
            def f(eng):
                for o in ops[e]:
                    for kind, src, val in o.waits:
                        if kind == "e":
                            eng.wait_ge(esem[src], ops[src][val].sigval)
                        else:
                            eng.wait_ge(dsem[src], 16 * val)
                    if o.fn is None:
                        continue
                    inst = o.fn(eng)
                    if o.dma_key is not None:
                        inst.then_inc(dsem[o.dma_key], 16)
                    elif o.signal:
                        inst.then_inc(esem[e], 1)
            return f

        with nc.Block() as block:
            block.tensor(body("pe"))
            block.scalar(body("act"))
            block.vector(body("dve"))
            block.gpsimd(body("pool"))
            block.sync(body("sp"))


class B:
    def __init__(self):
        self.nc = bass.Bass("TRN2", target_bir_lowering=False)
        self.P = Prog(self.nc)
        self.es = ExitStack()
        self.ps = [self.es.enter_context(self.nc.psum_tensor(f"ps{i}", [128, 512], F32)) for i in range(8)]
        self.psi = 0
        self.psrot = list(range(8))
        self.rings = {}
        self.outtoks = []
        self.ndq = 0
        self.attn_banks = [(4, 5), (6, 7)]
        self.attn_par = 0

    def din(self, name, shape, dt=F32):
        return self.nc.dram_tensor(name, list(shape), dt, kind="ExternalInput").ap()

    def dout(self, name, shape, dt=F32):
        return self.nc.dram_tensor(name, list(shape), dt, kind="ExternalOutput").ap()

    def sb(self, name, shape, dt=F32):
        return self.es.enter_context(self.nc.sbuf_tensor("s_" + name, list(shape), dt))

    def nps(self):
        rot = self.psrot
        i = rot[self.psi % len(rot)]
        self.psi += 1
        return i

    def ring(self, name, n, shape, dt=F32):
        self.rings[name] = [[self.sb(f"{name}{i}", shape, dt) for i in range(n)], 0]

    def nx(self, name):
        r = self.rings[name]
        i = r[1]
        r[1] = (i + 1) % len(r[0])
        return r[0][i], (name, i)

    def mm(self, pi, M, N, lhsT, rhs, st, sp, rd, po=0):
        out = self.ps[pi][po:po + M, 0:N]
        self.P.op("pe", lambda e: e.matmul(out, lhsT, rhs, start=st, stop=sp), reads=rd, writes=[("ps", pi)])

    def act(self, out, in_, func, rd, wr, bias=None, scale=None):
        kw = {}
        if bias is not None:
            kw["bias"] = bias
        if scale is not None:
            kw["scale"] = scale
        self.P.op("act", lambda e: e.activation(out=out, in_=in_, func=func, **kw), reads=rd, writes=wr)

    def stt(self, out, in0, scalar, in1, op0, op1, rd, wr):
        self.P.op("dve", lambda e: e.scalar_tensor_tensor(out=out, in0=in0, scalar=scalar, in1=in1,
                                                          op0=op0, op1=op1), reads=rd, writes=wr)

    def tt(self, eng, out, in0, in1, op, rd, wr):
        self.P.op(eng, lambda e: e.tensor_tensor(out=out, in0=in0, in1=in1, op=op), reads=rd, writes=wr)

    def ts(self, eng, out, in0, s1, s2, op0, op1, rd, wr):
        if op1 is None:
            self.P.op(eng, lambda e: e.tensor_scalar(out=out, in0=in0, scalar1=s1, scalar2=None, op0=op0),
                      reads=rd, writes=wr)
        else:
            self.P.op(eng, lambda e: e.tensor_scalar(out=out, in0=in0, scalar1=s1, scalar2=s2, op0=op0, op1=op1),
                      reads=rd, writes=wr)

    def cp(self, eng, out, in_, rd, wr):
        if eng == "act":
            self.P.op("act", lambda e: e.copy(out=out, in_=in_), reads=rd, writes=wr)
        else:
            self.P.op(eng, lambda e: e.tensor_copy(out=out, in_=in_), reads=rd, writes=wr)

    def wcast(self, wbf, wst, sl, rn, wn, dsl=None):
        dsl = sl if dsl is None else dsl
        self.cp("dve", wbf[:, dsl, 0:8], wst[:, sl, 0:8], [(rn, sl)], [(wn, dsl)])
        self.cp("act", wbf[:, dsl, 8:16], wst[:, sl, 8:16], [(rn, sl)], [(wn, dsl)])

    def recip(self, out, in_, rd, wr):
        self.P.op("dve", lambda e: e.reciprocal(out=out, in_=in_), reads=rd, writes=wr)

    def memset(self, eng, ap, val, wr):
        self.P.op(eng, lambda e: e.memset(ap, val), writes=wr)

    def load(self, out, in_, wr, key, q="sp"):
        return self.P.op(q, lambda e: e.dma_start(out=out, in_=in_), writes=wr, dma=key)

    def store(self, out, in_, rd, q="pool"):
        self.ndq += 1
        key = ("st", self.ndq % 6)
        t = self.P.op(q, lambda e: e.dma_start(out=out, in_=in_), reads=rd, dma=key)
        self.outtoks.append(t)

    def finish(self):
        last = {}
        for t in self.outtoks:
            last[t[1]] = t
        self.P.wait_tokens("pool", list(last.values()))
        self.P.emit()
        self.es.close()
        return self.nc


def fm_rstd(b, sq_list, ones_ap, Pn, N, inv_n, eps, consts_res):
    pi = b.nps()
    for i, (ap, res) in enumerate(sq_list):
        b.mm(pi, Pn, N, ones_ap, ap, i == 0, i == len(sq_list) - 1, [res, consts_res])
    ln, lnr = b.nx("t")
    b.act(ln[0:Pn, 0:N], b.ps[pi][0:Pn, 0:N], AF.Ln, [("ps", pi)], [lnr], bias=b.epsc[eps][0:Pn, :], scale=inv_n)
    rs, rsr = b.nx("t")
    b.act(rs[0:Pn, 0:N], ln[0:Pn, 0:N], AF.Exp, [lnr], [rsr], scale=-0.5)
    return rs, rsr


def setup_consts(b):
    b.ones = b.sb("ones", [128, 128])
    b.blk = b.sb("blk", [128, 128])
    b.memset("pool", b.ones[:], 1.0, ["ones"])
    b.memset("pool", b.blk[:], 0.0, ["blk"])
    b.memset("pool", b.blk[0:64, 0:64], 1.0, ["blk"])
    b.memset("pool", b.blk[64:128, 64:128], 1.0, ["blk"])
    b.onesb = b.sb("onesb", [128, 128], BF16)
    b.memset("pool", b.onesb[:], 1.0, ["onesb"])
    b.epsc = {}
    for i, v in enumerate((1e-6, 64e-5)):
        t = b.sb(f"epsc{i}", [128, 1])
        b.memset("pool", t[:], v, [f"epsc{i}"])
        b.epsc[v] = t
        b.P.res


def attn_core(b, qlist, kts, vts, nkt, scale, out_M, bias_fn=None, tag="a"):
    po, pz = b.attn_banks[b.attn_par]
    b.attn_par ^= 1
    acc, accr = b.nx("zacc")
    acc2, acc2r = b.nx("zacc2")
    pend = None

    def pv(kt, pt, ptr):
        va, vr = vts(kt)
        b.mm(po, out_M, 512, va, pt[:], kt == 0, kt == nkt - 1, [vr, ptr])

    for kt in range(nkt):
        pi = b.nps()
        ks = kts(kt)
        for i, ((qa, qr), (ka, kr)) in enumerate(zip(qlist, ks)):
            b.mm(pi, 128, 512, ka, qa, i == 0, i == len(qlist) - 1, [qr, kr])
        if pend is not None:
            pv(*pend)
        pt, ptr = b.nx("pt")
        bf = bias_fn(kt) if bias_fn is not None else None
        if bf is not None:
            kind, val = bf
            if kind == "const":
                b.act(pt[:], b.ps[pi][:, :], AF.Exp, [("ps", pi), "tab"], [ptr], bias=val, scale=scale)
            else:
                tmp, tr = b.nx("t")
                bap, bres = val
                b.stt(tmp[:], b.ps[pi][:, :], scale, bap, ALU.mult, ALU.add, [("ps", pi), bres], [tr])
                b.act(pt[:], tmp[:], AF.Exp, [tr], [ptr])
        else:
            b.act(pt[:], b.ps[pi][:, :], AF.Exp, [("ps", pi)], [ptr], scale=scale)
        eng = "pool" if kt % 3 == 0 else "dve"
        a_, a_r = (acc, accr) if eng == "pool" else (acc2, acc2r)
        if kt < 2 and (kt == 0 or eng == "dve"):
            b.cp(eng, a_[:], pt[:], [ptr], [a_r])
        else:
            b.tt(eng, a_[:], a_[:], pt[:], ALU.add, [ptr, a_r], [a_r])
        pend = (kt, pt, ptr)
    pv(*pend)
    b.mm(pz, out_M, 512, b.ones[:, 0:out_M], acc[:], True, False, ["ones", accr])
    b.mm(pz, out_M, 512, b.ones[:, 0:out_M], acc2[:], False, True, ["ones", acc2r])
    return po, pz


TT = 512
TH = TT + 2
NT1 = 2
PC = {}
_c = 0
for _n, _w in (("ng", 16), ("sh", 42), ("w0", 8), ("a0", 8), ("kk", 4), ("ka", 4), ("rk", 4), ("dqg", 1), ("dkg", 1),
               ("qlg", 3), ("kvg", 2), ("npg", 2), ("rpg", 2), ("rpgs", 2), ("mqg", 2), ("mng", 16), ("invf", 1),
               ("sgn", 1), ("omka", 4)):
    PC[_n] = _c
    _c += _w
NPAR = _c
RW0 = 0
def _p1_cols():
    cols = []
    cols += [(RW0 + 1536, 128), (RW0 + 1664, 128)]
    for c in range(4):
        cols += [(c * 128, 128), (512 + c * 128, 128), (1024 + c * 128, 128)]
    o = 1792
    cols += [(o + i * 128, 128) for i in range(4)]
    cols += [(o + 512 + i * 128, 128) for i in range(4)]
    o2 = 1792 + 1536
    cols += [(o2 + i * 128, 128) for i in range(3)]
    cols += [(o2 + 384 + i * 128, 128) for i in range(2)]
    cols += [("krope", 128)]
    o3 = o2 + 384 + 256 + 64
    cols += [(o3 + i * 128, 128) for i in range(4)]
    cols += [(1792 + 1024 + i * 128, 128) for i in range(4)]
    return cols
P1COLS = _p1_cols()
NCH1 = len(P1COLS)
KROPE0 = 1792 + 1536 + 384 + 256


def build_p1():
    b = B()
    P = b.P
    xT = b.din("xT", [NT1, 128, 16, TH])
    pos = b.din("pos", [NT1, TH], I32)
    par_d = b.din("par", [128, NPAR])
    w = b.din("w", [NCH1, 128, 16, 128])
    wup_d = b.din("wup", [128, 512])
    aup_d = b.din("aup", [128, 512])
    wuq_d = b.din("wuq", [128, 3, 768])
    wuqs_d = b.din("wuqs", [128, 3, 256])
    wukvk_d = b.din("wukvk", [128, 2, 512])
    wukvv_d = b.din("wukvv", [128, 2, 512])
    memT_d = b.din("memT", [128, 16, 256])
    wkv_d = b.din("wkv", [8, 128, 16, 128])
    o_rw = b.dout("o_rw", [9, 512, NT1 * TT])
    o_bonus = b.dout("o_bonus", [512, NT1 * TT])
    o_dq = b.dout("o_dq", [512, NT1 * TT])
    o_dk = b.dout("o_dk", [512, NT1 * TT])
    o_dv = b.dout("o_dv", [NT1 * TT, 512])
    o_mqn = b.dout("o_mqn", [512, NT1 * TT])
    o_mqr = b.dout("o_mqr", [256, NT1 * TT])
    o_mkn = b.dout("o_mkn", [512, NT1 * TT])
    o_mkr = b.dout("o_mkr", [64, NT1 * TT])
    o_mv = b.dout("o_mv", [NT1 * TT, 512])
    o_yd = b.dout("o_yd", [512, NT1 * TT])

    setup_consts(b)
    par = b.sb("par", [128, NPAR])
    b.load(par[:], par_d, ["par"], "par")
    pc = lambda n, i=0: par[:, PC[n] + i:PC[n] + i + 1]
    b.ts("dve", par[:, PC["omka"]:PC["omka"] + 4], par[:, PC["ka"]:PC["ka"] + 4], -1.0, 1.0, ALU.mult, ALU.add,
         ["par"], ["par"])
    xs = b.sb("xs", [128, 16, TH])
    hT = b.sb("hT", [128, 16, TH], BF16)
    wst = b.sb("wst", [128, 2, 16, 128])
    wbf = b.sb("wbf", [128, 2, 16, 128], BF16)
    stg = b.sb("stg", [128, 3072])
    wup = b.sb("wup_s", [128, 512])
    aup = b.sb("aup_s", [128, 512])
    wuq = b.sb("wuq_s", [128, 3, 768], BF16)
    wuqs = b.sb("wuqs_s", [128, 3, 256], BF16)
    wukvk = b.sb("wukvk_s", [128, 2, 512], BF16)
    wukvv = b.sb("wukvv_s", [128, 2, 512], BF16)
    memn = b.sb("memn", [128, 16, 256], BF16)
    kmem = b.sb("kmem", [128, 4, 256], BF16)
    vmem = b.sb("vmem", [128, 2, 512], BF16)
    b.ring("t", 14, [128, TT])
    b.ring("th", 3, [128, TH])
    b.ring("rkv", 4, [128, TT])
    b.ring("bf", 4, [128, TT], BF16)
    b.ring("pt", 3, [128, TT], BF16)
    b.ring("zacc", 1, [128, TT])
    b.ring("zacc2", 1, [128, TT])
    twd = b.sb("twd", [128, TT])
    adl = b.sb("adl", [128, TT])
    ropC = b.sb("ropC", [64, TT])
    ropS = b.sb("ropS", [64, TT])
    rsx = b.sb("rsx", [128, TH])
    ql = b.sb("ql", [128, 3, TT])
    qln = b.sb("qln", [128, 3, TT], BF16)
    kvl = b.sb("kvl", [128, 2, TT])
    kvn = b.sb("kvn", [128, 2, TT], BF16)
    posi = b.sb("posi", [64, TH], I32)

    b.load(wup[:], wup_d, ["wup"], "wl0")
    b.load(aup[:], aup_d, ["aup"], "wl1")
    for dst, src, n, nm in ((wuq, wuq_d, 3 * 768, "wuq"), (wuqs, wuqs_d, 3 * 256, "wuqs"),
                            (wukvk, wukvk_d, 1024, "wukvk"), (wukvv, wukvv_d, 1024, "wukvv")):
        b.load(stg[:, 0:n], src.rearrange("p a b -> p (a b)"), ["stg"], "stg")
        b.cp("dve", dst[:].rearrange("p a b -> p (a b)"), stg[:, 0:n], ["stg"], [nm])

    def rmsnorm_cols(src_tile, nkc, N, gname, dst_tile, res_src, res_dst, inv_n):
        pi = b.nps()
        pih = b.nps() if N > 512 else None
        for kc in range(nkc):
            sq, sqr = b.nx("th")
            b.act(sq[:, 0:N], src_tile[:, kc, 0:N], AF.Square, [res_src], [sqr])
            b.mm(pi, 128, min(N, 512), b.ones[:], sq[:, 0:min(N, 512)], kc == 0, kc == nkc - 1, [sqr, "ones"])
            if pih is not None:
                b.mm(pih, 128, N - 512, b.ones[:], sq[:, 512:N], kc == 0, kc == nkc - 1, [sqr, "ones"])
        ln, lnr = b.nx("th")
        b.act(ln[:, 0:min(N, 512)], b.ps[pi][:, 0:min(N, 512)], AF.Ln, [("ps", pi)], [lnr],
              bias=b.epsc[1e-6][:], scale=inv_n)
        if pih is not None:
            b.act(ln[:, 512:N], b.ps[pih][:, 0:N - 512], AF.Ln, [("ps", pih)], [lnr], bias=b.epsc[1e-6][:], scale=inv_n)
        b.act(rsx[:, 0:N], ln[:, 0:N], AF.Exp, [lnr], ["rsx"], scale=-0.5)
        for kc in range(nkc):
            b.stt(dst_tile[:, kc, 0:N], src_tile[:, kc, 0:N], pc(gname, kc), rsx[:, 0:N], ALU.mult, ALU.mult,
                  [res_src, "rsx", "par"], [res_dst])

    b.load(xs[:, :, 0:256], memT_d, ["xs"], "xs")
    rmsnorm_cols(xs, 16, 256, "mng", memn, "xs", "memn", 1.0 / 2048)
    for ci in range(8):
        sl = ci % 2
        b.load(wst[:, sl], wkv_d[ci], [("wst", sl)], ("wst", sl))
        b.wcast(wbf, wst, sl, "wst", "wbf")
        if ci < 4:
            pi = b.nps()
            for kc in range(16):
                b.mm(pi, 128, 256, wbf[:, sl, kc, :], memn[:, kc, :], kc == 0, kc == 15, [("wbf", sl), "memn"])
            tq, tqr = b.nx("t")
            b.cp("act", tq[:, 0:256], b.ps[pi][:, 0:256], [("ps", pi)], [tqr])
            sq, sqr = b.nx("t")
            b.act(sq[:, 0:256], b.ps[pi][:, 0:256], AF.Square, [("ps", pi)], [sqr])
            rs, rsr = fm_rstd(b, [(sq[:, 0:256], sqr)], b.ones[:], 128, 256, 1.0 / 128, 1e-6, "ones")
            b.stt(kmem[:, ci, :], tq[:, 0:256], pc("mqg", 1), rs[:, 0:256], ALU.mult, ALU.mult, [tqr, rsr, "par"], ["kmem"])
        else:
            for tb in range(2):
                pi = b.nps()
                for kc in range(16):
                    b.mm(pi, 128, 128, memn[:, kc, tb * 128:(tb + 1) * 128], wbf[:, sl, kc, :], kc == 0, kc == 15,
                         [("wbf", sl), "memn"])
                b.cp("act", vmem[:, tb, (ci - 4) * 128:(ci - 3) * 128], b.ps[pi][:, 0:128], [("ps", pi)], ["vmem"])

    def wload(tile, ci):
        sl = (tile * NCH1 + ci) % 2
        b.load(wst[:, sl], w[ci], [("wst", sl)], ("wst", sl))

    def shift(pi, pih, rc, dst, dres):
        u, ur = b.nx("th")
        b.cp("act", u[:, 0:TT], b.ps[pi][:, :], [("ps", pi)], [ur])
        b.cp("act", u[:, TT:TH], b.ps[pih][:, 0:2], [("ps", pih)], [ur])
        s0 = pc("sh", 0 * 14 + rc); s1 = pc("sh", 1 * 14 + rc); s2 = pc("sh", 2 * 14 + rc)
        b.ts("dve", dst[:, :], u[:, 0:TT], s1, None, ALU.mult, None, [ur, "par"], [dres])
        b.stt(dst[:, 1:TT], u[:, 0:TT - 1], s0, dst[:, 1:TT], ALU.mult, ALU.add, [ur, "par", dres], [dres])
        b.stt(dst[:, 0:1], u[:, TT:TT + 1], s0, dst[:, 0:1], ALU.mult, ALU.add, [ur, "par", dres], [dres])
        b.stt(dst[:, 0:TT - 1], u[:, 1:TT], s2, dst[:, 0:TT - 1], ALU.mult, ALU.add, [ur, "par", dres], [dres])
        b.stt(dst[:, TT - 1:TT], u[:, TT + 1:TT + 2], s2, dst[:, TT - 1:TT], ALU.mult, ALU.add, [ur, "par", dres], [dres])

    def head_norm(pi, Pn, ones_ap, inv_n, gcol, dst_ap, dres, N=TT):
        tq, tqr = b.nx("t")
        b.cp("act", tq[0:Pn, 0:N], b.ps[pi][0:Pn, 0:N], [("ps", pi)], [tqr])
        sq, sqr = b.nx("t")
        b.act(sq[0:Pn, 0:N], b.ps[pi][0:Pn, 0:N], AF.Square, [("ps", pi)], [sqr])
        rs, rsr = fm_rstd(b, [(sq[0:Pn, 0:N], sqr)], ones_ap, Pn, N, inv_n, 1e-6, "ones")
        b.stt(dst_ap, tq[0:Pn, 0:N], gcol, rs[0:Pn, 0:N], ALU.mult, ALU.mult, [tqr, rsr, "par"], [dres])
        return tq, tqr, rs, rsr

    for tile in range(NT1):
        t0 = tile * TT
        b.load(xs[:], xT[tile], ["xs"], "xs")
        b.load(posi[:], pos[tile, :].partition_broadcast(64), ["posi"], "posi")
        wload(tile, 0)
        rmsnorm_cols(xs, 16, TH, "ng", hT, "xs", "hT", 1.0 / 2048)
        posf, posr = b.nx("th")
        b.cp("dve", posf[0:64, :], posi[:], ["posi"], [posr])
        for which, dstt, dres in ((0, ropS, "ropS"), (1, ropC, "ropC")):
            a, ar = b.nx("t")
            b.ts("dve", a[0:64, :], posf[0:64, 0:TT], pc("invf")[0:64, :], (math.pi / 2 if which else 0.0),
                 ALU.mult, ALU.add, [posr, "par"], [ar])
            y, yr = b.nx("t")
            b.ts("dve", y[0:64, :], a[0:64, :], 1.0 / (2 * math.pi), None, ALU.mult, None, [ar], [yr])
            ni = b.sb(f"ni{tile}{which}", [64, TT], I32)
            b.cp("dve", ni[:], y[0:64, :], [yr], [f"ni{tile}{which}"])
            nf, nfr = b.nx("t")
            b.cp("dve", nf[0:64, :], ni[:], [f"ni{tile}{which}"], [nfr])
            r, rr = b.nx("t")
            b.stt(r[0:64, :], nf[0:64, :], -2 * math.pi, a[0:64, :], ALU.mult, ALU.add, [nfr, ar], [rr])
            m, mr = b.nx("t")
            b.ts("dve", m[0:64, :], r[0:64, :], math.pi, -2 * math.pi, ALU.is_gt, ALU.mult, [rr], [mr])
            b.tt("dve", r[0:64, :], r[0:64, :], m[0:64, :], ALU.add, [rr, mr], [rr])
            b.ts("dve", m[0:64, :], r[0:64, :], -math.pi, 2 * math.pi, ALU.is_lt, ALU.mult, [rr], [mr])
            b.tt("dve", r[0:64, :], r[0:64, :], m[0:64, :], ALU.add, [rr, mr], [rr])
            if which == 0:
                sn, snr = b.nx("t")
                b.act(sn[0:64, :], r[0:64, :], AF.Sin, [rr], [snr])
                b.ts("dve", ropS[:], sn[0:64, :], pc("sgn")[0:64, :], None, ALU.mult, None, [snr, "par"], ["ropS"])
            else:
                b.act(ropC[:], r[0:64, :], AF.Sin, [rr], ["ropC"])

        def rope_out(t_tq, t_r, sw_tq, sw_r, rs, rsr, gi, dst_dram):
            a, ar = b.nx("t")
            b.stt(a[0:64, :], t_tq[0:64, :], pc("rpg", gi)[0:64, :], ropC[:], ALU.mult, ALU.mult, [t_r, "par", "ropC"], [ar])
            c, cr = b.nx("t")
            b.stt(c[0:64, :], sw_tq[0:64, :], pc("rpgs", gi)[0:64, :], ropS[:], ALU.mult, ALU.mult, [sw_r, "par", "ropS"], [cr])
            b.tt("dve", a[0:64, :], a[0:64, :], c[0:64, :], ALU.add, [ar, cr], [ar])
            b.tt("dve", a[0:64, :], a[0:64, :], rs[0:64, :], ALU.mult, [ar, rsr], [ar])
            b.store(dst_dram, a[0:64, :], [ar])

        rcur = {}
        for ci in range(NCH1):
            sl = (tile * NCH1 + ci) % 2
            if ci + 1 < NCH1:
                wload(tile, ci + 1)
            elif tile + 1 < NT1:
                wload(tile + 1, 0)
            b.wcast(wbf, wst, sl, "wst", "wbf")
            wr = ("wbf", sl)
            if ci >= 32:
                j = ci - 32
                for tb in range(4):
                    pi = b.nps()
                    for kc in range(16):
                        b.mm(pi, 128, 128, hT[:, kc, tb * 128:(tb + 1) * 128], wbf[:, sl, kc, :], kc == 0, kc == 15, [wr, "hT"])
                    o, orr = b.nx("t")
                    b.cp("act", o[:, 0:128], b.ps[pi][:, 0:128], [("ps", pi)], [orr])
                    b.store(o_dv[t0 + tb * 128:t0 + (tb + 1) * 128, j * 128:(j + 1) * 128], o[:, 0:128], [orr])
                continue
            if ci == 27:
                pis = []
                for hh in range(2):
                    pi = b.nps()
                    for kc in range(16):
                        b.mm(pi, 64, TT, wbf[:, sl, kc, hh * 64:(hh + 1) * 64], hT[:, kc, 0:TT], kc == 0, kc == 15, [wr, "hT"])
                    pis.append(pi)
                tq, tqr = b.nx("t")
                b.cp("act", tq[0:64, :], b.ps[pis[0]][0:64, :], [("ps", pis[0])], [tqr])
                sq, sqr = b.nx("t")
                b.act(sq[0:64, :], b.ps[pis[0]][0:64, :], AF.Square, [("ps", pis[0])], [sqr])
                sw, swr = b.nx("t")
                b.cp("act", sw[0:64, :], b.ps[pis[1]][0:64, :], [("ps", pis[1])], [swr])
                rs, rsr = fm_rstd(b, [(sq[0:64, :], sqr)], b.ones[0:64, 0:64], 64, TT, 1.0 / 64, 1e-6, "ones")
                rope_out(tq, tqr, sw, swr, rs, rsr, 1, o_mkr[:, t0:t0 + TT])
                continue
            pi = b.nps()
            for kc in range(16):
                b.mm(pi, 128, TT, wbf[:, sl, kc, :], hT[:, kc, 0:TT], kc == 0, kc == 15, [wr, "hT"])
            pih = None
            if ci < 14:
                pih = b.nps()
                for kc in range(16):
                    b.mm(pih, 128, 2, wbf[:, sl, kc, :], hT[:, kc, TT:TH], kc == 0, kc == 15, [wr, "hT"])
            if ci == 0:
                tmp, tr = b.nx("t")
                shift(pi, pih, 12, tmp, tr)
                b.act(twd[:], tmp[:], AF.Tanh, [tr], ["twd"])
            elif ci == 1:
                shift(pi, pih, 13, adl, "adl")
            elif ci < 14:
                c = (ci - 2) // 3
                kind = (ci - 2) % 3
                dst, dres = b.nx("rkv")
                shift(pi, pih, kind * 4 + c, dst, dres)
                rcur[kind] = (dst, dres)
                if kind == 0:
                    b.store(o_rw[0, c * 128:(c + 1) * 128, t0:t0 + TT], dst[:], [dres])
                if kind == 2:
                    r_t, r_r = rcur[0]
                    k_t, k_r = rcur[1]
                    v_t, v_r = rcur[2]
                    b.store(o_rw[1, c * 128:(c + 1) * 128, t0:t0 + TT], v_t[:], [v_r])
                    kr_, krr = b.nx("t")
                    b.ts("dve", kr_[:], k_t[:], pc("kk", c), None, ALU.mult, None, [k_r, "par"], [krr])
                    sq, sqr = b.nx("t")
                    b.act(sq[:], kr_[:], AF.Square, [krr], [sqr])
                    pj = b.nps()
                    b.mm(pj, 128, TT, b.blk[:], sq[:], True, True, [sqr, "blk"])
                    nr, nrr = b.nx("t")
                    b.act(nr[:], b.ps[pj][:, :], AF.Sqrt, [("ps", pj)], [nrr])
                    b.ts("dve", nr[:], nr[:], 1e-12, None, ALU.max, None, [nrr], [nrr])
                    b.recip(nr[:], nr[:], [nrr], [nrr])
                    kk_, kkr = b.nx("t")
                    b.tt("dve", kk_[:], kr_[:], nr[:], ALU.mult, [krr, nrr], [kkr])
                    b.store(o_rw[2, c * 128:(c + 1) * 128, t0:t0 + TT], kk_[:], [kkr])
                    kds = []
                    for d in range(2):
                        pw = b.nps()
                        b.mm(pw, 128, TT, wup[d * 64:(d + 1) * 64, c * 128:(c + 1) * 128], twd[d * 64:(d + 1) * 64, :],
                             True, True, ["wup", "twd"])
                        sg, sgr = b.nx("t")
                        b.act(sg[:], b.ps[pw][:, :], AF.Sigmoid, [("ps", pw), "par"], [sgr], bias=pc("w0", d * 4 + c))
                        dec, decr = b.nx("t")
                        b.act(dec[:], sg[:], AF.Copy, [sgr], [decr], scale=-math.exp(-0.5))
                        b.store(o_rw[5 + 3 * d, c * 128:(c + 1) * 128, t0:t0 + TT], dec[:], [decr])
                        pa = b.nps()
                        b.mm(pa, 128, TT, aup[d * 64:(d + 1) * 64, c * 128:(c + 1) * 128], adl[d * 64:(d + 1) * 64, :],
                             True, True, ["aup", "adl"])
                        a_, a_r = b.nx("t")
                        b.act(a_[:], b.ps[pa][:, :], AF.Sigmoid, [("ps", pa), "par"], [a_r], bias=pc("a0", d * 4 + c))
                        nb, nbr = b.nx("t")
                        b.stt(nb[:], kk_[:], -1.0, a_[:], ALU.mult, ALU.mult, [kkr, a_r], [nbr])
                        b.store(o_rw[3 + 3 * d, c * 128:(c + 1) * 128, t0:t0 + TT], nb[:], [nbr])
                        kd, kdr = b.nx("t")
                        b.ts("dve", kd[:], a_[:], pc("ka", c), pc("omka", c), ALU.mult, ALU.add, [a_r, "par"], [kdr])
                        b.tt("dve", kd[:], kd[:], k_t[:], ALU.mult, [kdr, k_r], [kdr])
                        b.store(o_rw[4 + 3 * d, c * 128:(c + 1) * 128, t0:t0 + TT], kd[:], [kdr])
                        kds.append((kd, kdr))
                    s_, s_r = b.nx("t")
                    b.tt("dve", s_[:], kds[0][0][:], kds[1][0][:], ALU.add, [kds[0][1], kds[1][1]], [s_r])
                    b.stt(s_[:], r_t[:], pc("rk", c), s_[:], ALU.mult, ALU.mult, [r_r, "par", s_r], [s_r])
                    pb_ = b.nps()
                    b.mm(pb_, 128, TT, b.blk[:], s_[:], True, True, [s_r, "blk"])
                    bo, bor = b.nx("t")
                    b.tt("dve", bo[:], v_t[:], b.ps[pb_][:, :], ALU.mult, [v_r, ("ps", pb_)], [bor])
                    b.store(o_bonus[c * 128:(c + 1) * 128, t0:t0 + TT], bo[:], [bor])
            elif ci < 22:
                isk = ci >= 18
                j = ci - (18 if isk else 14)
                o, orr = b.nx("t")
                head_norm(pi, 128, b.blk[:], 1.0 / 64, pc("dkg" if isk else "dqg"), o[:], orr)
                b.store((o_dk if isk else o_dq)[j * 128:(j + 1) * 128, t0:t0 + TT], o[:], [orr])
            elif ci < 25:
                j = ci - 22
                b.cp("act", ql[:, j, :], b.ps[pi][:, :], [("ps", pi)], ["ql"])
                if j == 2:
                    rmsnorm_cols(ql, 3, TT, "qlg", qln, "ql", "qln", 1.0 / 384)
                    for h in range(4):
                        pn = b.nps()
                        for kc in range(3):
                            b.mm(pn, 128, TT, wuq[:, kc, h * 192:h * 192 + 128], qln[:, kc, :], kc == 0, kc == 2, ["wuq", "qln"])
                        o, orr = b.nx("t")
                        head_norm(pn, 128, b.ones[:], 1.0 / 128, pc("npg", 0), o[:], orr)
                        b.store(o_mqn[h * 128:(h + 1) * 128, t0:t0 + TT], o[:], [orr])
                        pr = b.nps()
                        for kc in range(3):
                            b.mm(pr, 64, TT, wuq[:, kc, h * 192 + 128:h * 192 + 192], qln[:, kc, :], kc == 0, kc == 2, ["wuq", "qln"])
                        psw = b.nps()
                        for kc in range(3):
                            b.mm(psw, 64, TT, wuqs[:, kc, h * 64:(h + 1) * 64], qln[:, kc, :], kc == 0, kc == 2, ["wuqs", "qln"])
                        tq, tqr = b.nx("t")
                        b.cp("act", tq[0:64, :], b.ps[pr][0:64, :], [("ps", pr)], [tqr])
                        sq, sqr = b.nx("t")
                        b.act(sq[0:64, :], b.ps[pr][0:64, :], AF.Square, [("ps", pr)], [sqr])
                        sw, swr = b.nx("t")
                        b.cp("act", sw[0:64, :], b.ps[psw][0:64, :], [("ps", psw)], [swr])
                        rs, rsr = fm_rstd(b, [(sq[0:64, :], sqr)], b.ones[0:64, 0:64], 64, TT, 1.0 / 64, 1e-6, "ones")
                        rope_out(tq, tqr, sw, swr, rs, rsr, 0, o_mqr[h * 64:(h + 1) * 64, t0:t0 + TT])
            elif ci < 27:
                j = ci - 25
                b.cp("act", kvl[:, j, :], b.ps[pi][:, :], [("ps", pi)], ["kvl"])
                if j == 1:
                    rmsnorm_cols(kvl, 2, TT, "kvg", kvn, "kvl", "kvn", 1.0 / 256)
                    for h in range(4):
                        pn = b.nps()
                        for kc in range(2):
                            b.mm(pn, 128, TT, wukvk[:, kc, h * 128:(h + 1) * 128], kvn[:, kc, :], kc == 0, kc == 1, ["wukvk", "kvn"])
                        o, orr = b.nx("t")
                        head_norm(pn, 128, b.ones[:], 1.0 / 128, pc("npg", 1), o[:], orr)
                        b.store(o_mkn[h * 128:(h + 1) * 128, t0:t0 + TT], o[:], [orr])
                    for tb in range(4):
                        pv = b.nps()
                        for kc in range(2):
                            b.mm(pv, 128, 512, kvn[:, kc, tb * 128:(tb + 1) * 128], wukvv[:, kc, :], kc == 0, kc == 1, ["wukvv", "kvn"])
                        o, orr = b.nx("t")
                        b.cp("act", o[:], b.ps[pv][:, :], [("ps", pv)], [orr])
                        b.store(o_mv[t0 + tb * 128:t0 + (tb + 1) * 128, :], o[:], [orr])
            else:
                j = ci - 28
                qb, qbr = b.nx("bf")
                head_norm(pi, 128, b.ones[:], 1.0 / 128, pc("mqg", 0), qb[:], qbr)
                b.psrot = [0, 1, 2, 3]
                po, pz = attn_core(b, [(qb[:], qbr)],
                                   lambda kt, j=j: [(kmem[:, j, kt * 128:(kt + 1) * 128], "kmem")],
                                   lambda kt, j=j: (vmem[:, kt, j * 128:(j + 1) * 128], "vmem"),
                                   2, 128 ** -0.5, 128)
                b.psrot = list(range(8))
                rz, rzr = b.nx("t")
                b.recip(rz[:], b.ps[pz][:, :], [("ps", pz)], [rzr])
                o, orr = b.nx("t")
                b.tt("dve", o[:], b.ps[po][:, :], rz[:], ALU.mult, [("ps", po), rzr], [orr])
                b.store(o_yd[j * 128:(j + 1) * 128, t0:t0 + TT], o[:], [orr])
    return b.finish()


def _fm(a, nk):
    return np.ascontiguousarray(a.reshape(nk, 128, a.shape[1]).transpose(1, 0, 2))


def _col(par, name, arr, i=0):
    par[:arr.shape[0], PC[name] + i] = arr


def prep_p1(inp, l, x_cur):
    f = np.float32
    w_in = inp["w_in"][l]
    chunks = []
    for col0, n in P1COLS:
        if col0 == "krope":
            idx = [KROPE0 + j for j in range(64)] + [KROPE0 + (j + 32) % 64 for j in range(64)]
            W = w_in[:, idx]
        else:
            W = w_in[:, col0:col0 + 128]
        chunks.append(_fm(W, 16))
    w = np.stack(chunks)
    par = np.zeros((128, NPAR), f)
    par[:, PC["ng"]:PC["ng"] + 16] = inp["norm_g"][l].reshape(16, 128).T
    sh = inp["rw_shift"][l]
    for j in range(3):
        for rc in range(14):
            _col(par, "sh", sh[j, rc * 128:(rc + 1) * 128], j * 14 + rc)
    for d in range(2):
        for c in range(4):
            _col(par, "w0", inp["rw_w0"][l][d, c * 128:(c + 1) * 128], d * 4 + c)
            _col(par, "a0", inp["rw_a0"][l][d, c * 128:(c + 1) * 128], d * 4 + c)
    rk = inp["rw_r_k"][l].reshape(512)
    for c in range(4):
        _col(par, "kk", inp["rw_k_k"][l][c * 128:(c + 1) * 128], c)
        _col(par, "ka", inp["rw_k_a"][l][c * 128:(c + 1) * 128], c)
        _col(par, "rk", rk[c * 128:(c + 1) * 128], c)
    _col(par, "dqg", np.tile(inp["diff_qk_g"][l][0], 2))
    _col(par, "dkg", np.tile(inp["diff_qk_g"][l][1], 2))
    par[:, PC["qlg"]:PC["qlg"] + 3] = inp["mla_q_lat_g"][l].reshape(3, 128).T
    par[:, PC["kvg"]:PC["kvg"] + 2] = inp["mla_kv_lat_g"][l].reshape(2, 128).T
    par[:, PC["npg"]:PC["npg"] + 2] = inp["mla_nope_g"][l].T
    for gi in range(2):
        g = inp["mla_rope_g"][l][gi]
        _col(par, "rpg", g, gi)
        _col(par, "rpgs", np.concatenate([g[32:], g[:32]]), gi)
    par[:, PC["mqg"]:PC["mqg"] + 2] = inp["mem_qk_g"][l].T
    par[:, PC["mng"]:PC["mng"] + 16] = inp["mem_norm_g"][l].reshape(16, 128).T
    invf = (10000.0 ** (-np.arange(0, 64, 2, dtype=np.float32) / 64)).astype(f)
    _col(par, "invf", np.tile(invf, 2))
    _col(par, "sgn", np.concatenate([-np.ones(32, f), np.ones(32, f)]))
    wuq = inp["mla_w_uq"][l]
    sw_idx = [h * 192 + 128 + (j + 32) % 64 for h in range(4) for j in range(64)]
    wukv = inp["mla_w_ukv"][l].reshape(256, 4, 256)
    common = {
        "par": par, "w": w,
        "wup": np.ascontiguousarray(inp["rw_w_up"][l].reshape(128, 512)),
        "aup": np.ascontiguousarray(inp["rw_a_up"][l].reshape(128, 512)),
        "wuq": _fm(wuq, 3), "wuqs": _fm(wuq[:, sw_idx], 3),
        "wukvk": _fm(np.ascontiguousarray(wukv[:, :, :128]).reshape(256, 512), 2),
        "wukvv": _fm(np.ascontiguousarray(wukv[:, :, 128:]).reshape(256, 512), 2),
        "memT": np.ascontiguousarray(inp["mem"][0].reshape(256, 16, 128).transpose(2, 1, 0)),
        "wkv": np.stack([_fm(inp["mem_w_kv"][l][:, ci * 128:(ci + 1) * 128], 16) for ci in range(8)]),
    }
    S = x_cur.shape[0]
    xpad = np.concatenate([np.zeros((1, 2048), f), x_cur, np.zeros((1, 2048), f)], 0)
    posv = inp["positions"][0]
    maps = []
    for c in range(NCORES):
        xt = np.empty((NT1, 128, 16, TH), f)
        pp = np.zeros((NT1, TH), np.int32)
        for t in range(NT1):
            n0 = c * (NT1 * TT) + t * TT
            xe = np.concatenate([xpad[n0 + 1:n0 + 1 + TT], xpad[n0:n0 + 1], xpad[n0 + 1 + TT:n0 + 2 + TT]], 0)
            xt[t] = xe.reshape(TH, 16, 128).transpose(2, 1, 0)
            pp[t, :TT] = posv[n0:n0 + TT]
        m = dict(common)
        m["xT"] = xt
        m["pos"] = pp
        maps.append(m)
    return maps


_NC_CACHE = {}


def _run(name, builder, maps):
    if name not in _NC_CACHE:
        _NC_CACHE[name] = builder()
    res = run_bass_kernel_spmd(_NC_CACHE[name], maps, core_ids=list(range(NCORES)))
    return res.results


def _cat(results, key, axis):
    return np.concatenate([r[key] for r in results], axis=axis)


SEQ = 8192
SC_CH = 32
SC_NV = 5


def build_scan(T=SEQ):
    b = B()
    P = b.P
    bc = b.din("bc", [2, T, SC_NV * 64])
    vcol = b.din("vcol", [128, T])
    y = b.dout("y", [128, T])
    nch = T // SC_CH
    W = SC_NV * 64
    bcb = b.sb("bcb", [128, 2, SC_CH * W])
    vc = b.sb("vc", [128, T])
    yt = b.sb("yt", [128, T])
    S = b.sb("S", [128, 64])
    Sd = b.sb("Sd", [128, 64])
    sa = b.sb("sa", [128, 1])
    b.ring("ja", 2, [128, 64])
    b.ring("jb", 2, [128, 64])
    b.load(vc[:], vcol, ["vc"], "vc")
    b.memset("dve", S[:], 0.0, ["S"])

    def load(c):
        sl = c % 2
        for pr in range(2):
            src = bc[pr, c * SC_CH:(c + 1) * SC_CH, :].rearrange("t f -> (t f)").partition_broadcast(64)
            b.load(bcb[pr * 64:(pr + 1) * 64, sl, :], src, [("bcb", sl, pr)], ("bc", sl, pr))

    load(0)
    for c in range(nch):
        if c + 1 < nch:
            load(c + 1)
        sl = c % 2
        rd = [("bcb", sl, 0), ("bcb", sl, 1)]
        for s in range(SC_CH):
            t = c * SC_CH + s
            o = s * W
            kap = bcb[:, sl, o:o + 64]
            nb = bcb[:, sl, o + 64:o + 128]
            kd = bcb[:, sl, o + 128:o + 192]
            rt = bcb[:, sl, o + 192:o + 256]
            dec = bcb[:, sl, o + 256:o + 320]
            ja, jar = b.nx("ja")
            P.op("dve", lambda e, kap=kap, ja=ja: e.scalar_tensor_tensor(
                out=ja[:], in0=S[:], scalar=1.0, in1=kap, op0=ALU.mult, op1=ALU.mult, accum_out=sa[:]),
                reads=rd + ["S"], writes=[jar, "sa"])
            b.tt("dve", Sd[:], S[:], dec, ALU.mult, rd + ["S"], ["Sd"])
            b.stt(Sd[:], nb, sa[:, 0:1], Sd[:], ALU.mult, ALU.add, rd + ["Sd", "sa"], ["Sd"])
            b.stt(S[:], kd, vc[:, t:t + 1], Sd[:], ALU.mult, ALU.add, rd + ["Sd", "vc"], ["S"])
            jb, jbr = b.nx("jb")
            P.op("dve", lambda e, rt=rt, jb=jb, t=t: e.scalar_tensor_tensor(
                out=jb[:], in0=S[:], scalar=1.0, in1=rt, op0=ALU.mult, op1=ALU.mult, accum_out=yt[:, t:t + 1]),
                reads=rd + ["S"], writes=[jbr, "yt"])
    b.store(y, yt[:], ["yt"])
    return b.finish()


def prep_scan(o_rw, T=SEQ):
    maps = []
    for h in range(NCORES):
        sl = slice(h * 64, (h + 1) * 64)
        bc = np.empty((2, T, SC_NV * 64), np.float32)
        for d in range(2):
            parts = [o_rw[2][sl], o_rw[3 + 3 * d][sl], o_rw[4 + 3 * d][sl], o_rw[0][sl], o_rw[5 + 3 * d][sl]]
            a = np.concatenate([p.T for p in parts], axis=1)
            bc[d] = a if d == 0 else a[::-1]
        v = o_rw[1][sl]
        vcol = np.concatenate([v, v[:, ::-1]], 0)
        maps.append({"bc": bc, "vcol": np.ascontiguousarray(vcol)})
    return maps


def post_scan(results):
    yf = np.concatenate([r["y"][0:64] for r in results], 0)
    yb = np.concatenate([r["y"][64:128][:, ::-1] for r in results], 0)
    return np.ascontiguousarray(yf), np.ascontiguousarray(yb)


NQ = SEQ // 2
NKT = SEQ // 128


def _t5_breaks():
    nb, max_exact = 16, 8
    n = np.arange(0, 1024, dtype=np.int32)
    n_f = np.maximum(n, max_exact).astype(np.float32)
    large = max_exact + (np.log(n_f / np.float32(max_exact)) / np.float32(math.log(128 / max_exact))
                         * np.float32(nb - max_exact)).astype(np.int32)
    large = np.minimum(large, nb - 1)
    f = np.where(n < max_exact, n, large)
    rels = np.arange(-1023, 1024)
    bk = np.where(rels > 0, 16, 0) + f[np.abs(rels)]
    order = [int(bk[0])]
    breaks = []
    for i in range(1, len(rels)):
        if bk[i] != bk[i - 1]:
            order.append(int(bk[i]))
            breaks.append(int(rels[i]))
    return order, breaks


T5_ORDER, T5_BREAKS = _t5_breaks()
NBK = len(T5_ORDER)


def build_attn(kind):
    b = B()
    P = b.P
    diff = kind == "diff"
    qa_d = b.din("qa", [128, NQ])
    ka_d = b.din("ka", [128, SEQ])
    v_d = b.din("v", [128, NKT, 128])
    if diff:
        tab_d = b.din("tab", [1, NBK])
        lq_d = b.din("lq", [1, 256])
        cst_d = b.din("cst", [128, 2])
    else:
        qb_d = b.din("qb", [64, NQ])
        kb_d = b.din("kb", [64, SEQ])
    o_d = b.dout("o", [128, NQ])
    setup_consts(b)
    ka = b.sb("ka", [128, SEQ], BF16)
    qa = b.sb("qa", [128, NQ], BF16)
    vv = b.sb("vv", [128, NKT, 128], BF16)
    b.ring("stg", 2, [128, 2048])
    b.ring("t", 6, [128, 512])
    b.ring("pt", 6, [128, 512], BF16)
    b.ring("zacc", 2, [128, 512])
    b.ring("zacc2", 2, [128, 512])
    b.ring("o", 2, [128, 512])

    def load_cast(dst, src, Pn, n, res):
        for i in range(0, n, 2048):
            st, sr = b.nx("stg")
            b.load(st[0:Pn, :], src[:, i:i + 2048], [sr], sr)
            b.cp("pool", dst[0:Pn, i:i + 2048], st[0:Pn, :], [sr], [res])

    load_cast(ka, ka_d, 128, SEQ, "ka")
    load_cast(qa, qa_d, 128, NQ, "qa")
    load_cast(vv[:].rearrange("p a b -> p (a b)"), v_d.rearrange("p a b -> p (a b)"), 128, NKT * 128, "vv")
    if not diff:
        kb = b.sb("kb", [64, SEQ], BF16)
        qb = b.sb("qb", [64, NQ], BF16)
        load_cast(kb, kb_d, 64, SEQ, "kb")
        load_cast(qb, qb_d, 64, NQ, "qb")
    else:
        lq = b.sb("lq", [128, 256])
        cst = b.sb("cst", [128, 2])
        tab = b.sb("tab", [128, NBK])
        b.load(lq[:], lq_d[0, :].partition_broadcast(128), ["lq"], "lq")
        b.load(cst[:], cst_d, ["cst"], "cst")
        b.load(tab[:], tab_d[0, :].partition_broadcast(128), ["tab"], "tab")
        pr = b.sb("lpr", [128, 128])
        b.tt("dve", pr[:, 0:64], lq[:, 0:64], lq[:, 64:128], ALU.mult, ["lq"], ["lpr"])
        b.tt("dve", pr[:, 64:128], lq[:, 128:192], lq[:, 192:256], ALU.mult, ["lq"], ["lpr"])
        ls = b.sb("ls", [128, 4])
        P.op("dve", lambda e: e.tensor_reduce(out=ls[:, 0:1], in_=pr[:, 0:64], axis=AX.X, op=ALU.add), reads=["lpr"], writes=["ls"])
        P.op("dve", lambda e: e.tensor_reduce(out=ls[:, 1:2], in_=pr[:, 64:128], axis=AX.X, op=ALU.add), reads=["lpr"], writes=["ls"])
        b.act(ls[:, 0:2], ls[:, 0:2], AF.Exp, ["ls"], ["ls"])
        b.tt("dve", ls[:, 2:3], ls[:, 0:1], ls[:, 1:2], ALU.subtract, ["ls"], ["ls"])
        b.tt("dve", ls[:, 2:3], ls[:, 2:3], cst[:, 0:1], ALU.add, ["ls", "cst"], ["ls"])
        b.ts("dve", ls[:, 3:4], ls[:, 2:3], -1.0, None, ALU.mult, None, ["ls"], ["ls"])
        dl = b.sb("dl", [128, NBK])
        b.tt("dve", dl[:, 1:NBK], tab[:, 1:NBK], tab[:, 0:NBK - 1], ALU.subtract, ["tab"], ["dl"])
        reli = b.sb("reli", [128, 512], I32)
        relf = b.sb("relf", [128, 512])
        P.op("pool", lambda e: e.iota(reli[:], [[-1, 512]], base=0, channel_multiplier=1), writes=["reli"])
        b.cp("dve", relf[:], reli[:], ["reli"], ["relf"])
        bias6 = b.sb("bias6", [128, 6, 512])
        for dk in range(6):
            off = float((dk - 1) * 128)
            for k in range(1, NBK):
                tmp, tr = b.nx("t")
                b.ts("dve", tmp[:], relf[:], float(T5_BREAKS[k - 1]) - off, dl[:, k:k + 1], ALU.is_ge, ALU.mult,
                     ["relf", "dl"], [tr])
                if k == 1:
                    b.ts("pool", bias6[:, dk, :], tmp[:], tab[:, 0:1], None, ALU.add, None, [tr, "tab"], [("b6", dk)])
                else:
                    b.tt("pool", bias6[:, dk, :], bias6[:, dk, :], tmp[:], ALU.add, [tr, ("b6", dk)], [("b6", dk)])

    scale = (64 ** -0.5) if diff else (192 ** -0.5)
    b.psrot = [0, 1, 2, 3]
    for qt in range(NQ // 512):
        qs = slice(qt * 512, (qt + 1) * 512)
        res_list = []
        for s in range(2 if diff else 1):
            if diff:
                ps_ = slice(s * 64, (s + 1) * 64)
                qlist = [(qa[ps_, qs], "qa")]
                kts = lambda kt, ps_=ps_: [(ka[ps_, kt * 128:(kt + 1) * 128], "ka")]

                def bias_fn(kt, qt=qt):
                    dk = kt - 4 * qt
                    if dk < -1:
                        return ("const", tab[:, 0:1])
                    if dk > 4:
                        return ("const", tab[:, NBK - 1:NBK])
                    return ("tile", (bias6[:, dk + 1, :], ("b6", dk + 1)))
            else:
                qlist = [(qa[:, qs], "qa"), (qb[:, qs], "qb")]
                kts = lambda kt: [(ka[:, kt * 128:(kt + 1) * 128], "ka"), (kb[:, kt * 128:(kt + 1) * 128], "kb")]
            vts = lambda kt: (vv[:, kt, :], "vv")
            if diff:
                res_list.append(attn_core(b, qlist, kts, vts, NKT, scale, 128, bias_fn=bias_fn))
            else:
                res_list.append(attn_core(b, qlist, kts, vts, NKT, scale, 128))
        po, pz = res_list[0]
        rz, rzr = b.nx("t")
        b.recip(rz[:], b.ps[pz][:, :], [("ps", pz)], [rzr])
        o, orr = b.nx("o")
        b.tt("dve", o[:], b.ps[po][:, :], rz[:], ALU.mult, [("ps", po), rzr], [orr])
        if diff:
            po2, pz2 = res_list[1]
            rz2, rz2r = b.nx("t")
            b.recip(rz2[:], b.ps[pz2][:, :], [("ps", pz2)], [rz2r])
            o2, o2r = b.nx("t")
            b.tt("dve", o2[:], b.ps[po2][:, :], rz2[:], ALU.mult, [("ps", po2), rz2r], [o2r])
            b.stt(o[:], o2[:], ls[:, 3:4], o[:], ALU.mult, ALU.add, [o2r, "ls", orr], [orr])
        b.store(o_d[:, qs], o[:], [orr])
    return b.finish()


def _vtiles(v):
    return np.ascontiguousarray(v.reshape(-1, 128, 128).transpose(1, 0, 2))


def prep_attn_diff(inp, l, o_dq, o_dk, o_dv):
    lam_init = 0.8 - 0.6 * math.exp(-0.3 * l)
    maps = []
    for c in range(NCORES):
        h, half = c // 2, c % 2
        rows = slice(h * 128, (h + 1) * 128)
        q, k, v = o_dq[rows], o_dk[rows], o_dv[:, rows]
        tab = inp["rel_bias"][T5_ORDER, h]
        if half == 1:
            q, k, v, tab = q[:, ::-1], k[:, ::-1], v[::-1], tab[::-1]
        cst = np.zeros((128, 2), np.float32)
        cst[:, 0] = lam_init
        maps.append({"qa": np.ascontiguousarray(q[:, :NQ]), "ka": np.ascontiguousarray(k), "v": _vtiles(np.ascontiguousarray(v)),
                     "tab": np.ascontiguousarray(tab.reshape(1, NBK)).astype(np.float32),
                     "lq": np.ascontiguousarray(inp["diff_lambda"][l].reshape(1, 256)), "cst": cst})
    return maps


def post_attn_diff(results):
    out = np.empty((512, SEQ), np.float32)
    for c in range(NCORES):
        h, half = c // 2, c % 2
        o = results[c]["o"]
        if half == 0:
            out[h * 128:(h + 1) * 128, :NQ] = o
        else:
            out[h * 128:(h + 1) * 128, NQ:] = o[:, ::-1]
    return out


def prep_attn_mla(o_mqn, o_mqr, o_mkn, o_mkr, o_mv):
    maps = []
    for c in range(NCORES):
        h, half = c // 2, c % 2
        qs = slice(half * NQ, (half + 1) * NQ)
        maps.append({"qa": np.ascontiguousarray(o_mqn[h * 128:(h + 1) * 128, qs]),
                     "qb": np.ascontiguousarray(o_mqr[h * 64:(h + 1) * 64, qs]),
                     "ka": np.ascontiguousarray(o_mkn[h * 128:(h + 1) * 128]),
                     "kb": np.ascontiguousarray(o_mkr),
                     "v": _vtiles(np.ascontiguousarray(o_mv[:, h * 128:(h + 1) * 128]))})
    return maps


def post_attn_mla(results):
    out = np.empty((512, SEQ), np.float32)
    for c in range(NCORES):
        h, half = c // 2, c % 2
        out[h * 128:(h + 1) * 128, half * NQ:(half + 1) * NQ] = results[c]["o"]
    return out


PC3 = {}
_c = 0
for _n, _w in (("ng", 16), ("gng", 4), ("gnb", 4), ("subg", 1), ("lamf", 1)):
    PC3[_n] = _c
    _c += _w
NPAR3 = _c
NCH3 = 16 + 16 * 4 + 16


def build_p3():
    b = B()
    P = b.P
    xT = b.din("xT", [NT1, 128, 16, TT])
    par_d = b.din("par", [128, NPAR3])
    wA = b.din("wA", [NCH3, 128, 16, 128])
    wB = b.din("wB", [16, 128, 16, 128])
    br_d = b.din("br", [NT1, 128, 6, 4, TT])
    o_x = b.dout("o_x", [NT1, 128, 16, TT])
    setup_consts(b)
    par = b.sb("par", [128, NPAR3])
    b.load(par[:], par_d, ["par"], "par")
    pc = lambda n, i=0: par[:, PC3[n] + i:PC3[n] + i + 1]
    xz = b.sb("xz", [128, 16, TT])
    xs = xz
    zT = xz[:].rearrange("p a t -> p (a t)").bitcast(BF16).rearrange("p (n a t) -> p n a t", n=NT1, a=16)
    hT = b.sb("hT", [128, NT1, 16, TT], BF16)
    yg = b.sb("yg", [128, NT1, 16, TT], BF16)
    b.ring("brk", 2, [128, 6, TT])
    wstA = b.sb("wstA", [128, 2, 16, 128])
    wbfA = b.sb("wbfA", [128, 2, 16, 128], BF16)
    wstB = b.sb("wstB", [128, 1, 16, 128])
    wbfB = b.sb("wbfB", [128, 2, 16, 128], BF16)
    rsx = b.sb("rsx", [128, TT])
    zacc = b.sb("zacc", [128, NT1, TT])
    b.ring("t", 10, [128, TT])
    cntA = [0]

    def loadA(ci):
        sl = cntA[0] % 2
        cntA[0] += 1
        b.load(wstA[:, sl], wA[ci], [("wstA", sl)], ("wstA", sl))
        return sl

    ysrc = [0, 3, 4, 5]
    for tile in range(NT1):
        b.load(xs[:], xT[tile], ["xz"], "xz")
        pendA = loadA(0)
        pi = b.nps()
        for kc in range(16):
            sq, sqr = b.nx("t")
            b.act(sq[:], xs[:, kc, :], AF.Square, ["xz"], [sqr])
            b.mm(pi, 128, TT, b.ones[:], sq[:], kc == 0, kc == 15, [sqr, "ones"])
        ln, lnr = b.nx("t")
        b.act(ln[:], b.ps[pi][:, :], AF.Ln, [("ps", pi)], [lnr], bias=b.epsc[1e-6][:], scale=1.0 / 2048)
        b.act(rsx[:], ln[:], AF.Exp, [lnr], ["rsx"], scale=-0.5)
        for kc in range(16):
            b.stt(hT[:, tile, kc, :], xs[:, kc, :], pc("ng", kc), rsx[:], ALU.mult, ALU.mult, ["xz", "rsx", "par"], [("hT", tile)])
        for g in range(16):
            sl = pendA
            if g + 1 < 16:
                pendA = loadA(g + 1)
            b.wcast(wbfA, wstA, sl, "wstA", "wbfA")
            pi = b.nps()
            for kc in range(16):
                b.mm(pi, 128, TT, wbfA[:, sl, kc, :], hT[:, tile, kc, :], kc == 0, kc == 15, [("wbfA", sl), ("hT", tile)])
            b.act(yg[:, tile, g, :], b.ps[pi][:, :], AF.Silu, [("ps", pi)], [("yg", tile, g)])
        for kc in range(4):
            brk, brr = b.nx("brk")
            b.load(brk[:], br_d[tile, :, :, kc, :], [brr], brr)
            ys, ysr = b.nx("t")
            b.tt("pool", ys[:], brk[:, 0, :], brk[:, 1, :], ALU.add, [brr], [ysr])
            sq, sqr = b.nx("t")
            b.act(sq[:], ys[:], AF.Square, [ysr], [sqr])
            pm = b.nps()
            b.mm(pm, 128, TT, b.blk[:], ys[:], True, True, [ysr, "blk"])
            pe2 = b.nps()
            b.mm(pe2, 128, TT, b.blk[:], sq[:], True, True, [sqr, "blk"])
            mean, mr = b.nx("t")
            b.act(mean[:], b.ps[pm][:, :], AF.Copy, [("ps", pm)], [mr], scale=1.0 / 64)
            msq, msr = b.nx("t")
            b.act(msq[:], mean[:], AF.Square, [mr], [msr])
            var, vr = b.nx("t")
            b.stt(var[:], b.ps[pe2][:, :], 1.0 / 64, msq[:], ALU.mult, ALU.subtract, [("ps", pe2), msr], [vr])
            b.act(var[:], var[:], AF.Ln, [vr], [vr], bias=b.epsc[64e-5][:])
            b.act(var[:], var[:], AF.Exp, [vr], [vr], scale=-0.5)
            b.tt("pool", ys[:], ys[:], mean[:], ALU.subtract, [ysr, mr], [ysr])
            b.stt(ys[:], ys[:], pc("gng", kc), var[:], ALU.mult, ALU.mult, [ysr, "par", vr], [ysr])
            b.stt(ys[:], ys[:], pc("gnb", kc), brk[:, 2, :], ALU.add, ALU.add, [ysr, "par", brr], [ysr])
            b.tt("dve", yg[:, tile, 0 * 4 + kc, :], yg[:, tile, 0 * 4 + kc, :], ys[:], ALU.mult, [ysr, ("yg", tile, kc)], [("yg", tile, kc)])
            sq2, sq2r = b.nx("t")
            b.act(sq2[:], brk[:, 3, :], AF.Square, [brr], [sq2r])
            rs, rsr = fm_rstd(b, [(sq2[:], sq2r)], b.ones[:], 128, TT, 1.0 / 128, 1e-6, "ones")
            yb_, ybr_ = b.nx("t")
            b.stt(yb_[:], brk[:, 3, :], pc("subg"), rs[:], ALU.mult, ALU.mult, [brr, "par", rsr], [ybr_])
            b.stt(yg[:, tile, 4 + kc, :], yb_[:], pc("lamf"), yg[:, tile, 4 + kc, :], ALU.mult, ALU.mult,
                  [ybr_, "par", ("yg", tile, 4 + kc)], [("yg", tile, 4 + kc)])
            b.tt("pool", yg[:, tile, 8 + kc, :], yg[:, tile, 8 + kc, :], brk[:, 4, :], ALU.mult, [brr, ("yg", tile, 8 + kc)], [("yg", tile, 8 + kc)])
            b.tt("pool", yg[:, tile, 12 + kc, :], yg[:, tile, 12 + kc, :], brk[:, 5, :], ALU.mult, [brr, ("yg", tile, 12 + kc)], [("yg", tile, 12 + kc)])
    ygr = lambda t: [("yg", t, g) for g in range(16)]
    nxt = 16
    pendA = loadA(nxt)
    nxt += 1
    for oc in range(16):
        slB = oc % 2
        b.load(wstB[:, 0], wB[oc], [("wstB", 0)], ("wstB", 0))
        b.wcast(wbfB, wstB, 0, "wstB", "wbfB", dsl=slB)
        for bi in range(4):
            sl = pendA
            if nxt < NCH3:
                pendA = loadA(nxt)
                nxt += 1
            b.wcast(wbfA, wstA, sl, "wstA", "wbfA")
            for tile in range(NT1):
                pm = b.nps()
                for kc in range(16):
                    b.mm(pm, 128, TT, wbfA[:, sl, kc, :], hT[:, tile, kc, :], kc == 0, kc == 15, [("wbfA", sl), ("hT", tile)])
                pb = b.nps()
                for kc in range(4):
                    b.mm(pb, 128, TT, wbfB[:, slB, bi * 4 + kc, :], yg[:, tile, bi * 4 + kc, :], kc == 0, kc == 3,
                         [("wbfB", slB), ("yg", tile, bi * 4 + kc)])
                sg, sgr = b.nx("t")
                b.act(sg[:], b.ps[pm][:, :], AF.Sigmoid, [("ps", pm)], [sgr])
                if bi == 0:
                    b.tt("dve", zacc[:, tile, :], b.ps[pb][:, :], sg[:], ALU.mult, [("ps", pb), sgr], [("zacc", tile)])
                else:
                    tmp, tr = b.nx("t")
                    b.tt("dve", tmp[:], b.ps[pb][:, :], sg[:], ALU.mult, [("ps", pb), sgr], [tr])
                    if bi < 3:
                        b.tt("pool", zacc[:, tile, :], zacc[:, tile, :], tmp[:], ALU.add, [("zacc", tile), tr], [("zacc", tile)])
                    else:
                        b.tt("pool", zT[:, tile, oc, :], zacc[:, tile, :], tmp[:], ALU.add, [("zacc", tile), tr], ["xz"])
    for oc in range(16):
        sl = pendA
        if nxt < NCH3:
            pendA = loadA(nxt)
            nxt += 1
        b.wcast(wbfA, wstA, sl, "wstA", "wbfA")
        for tile in range(NT1):
            po = b.nps()
            for kc in range(16):
                b.mm(po, 128, TT, wbfA[:, sl, kc, :], zT[:, tile, kc, :], kc == 0, kc == 15, [("wbfA", sl), "xz"])
            xr, xrr = b.nx("t")
            b.load(xr[:], xT[tile, :, oc, :], [xrr], xrr)
            xo, xor_ = b.nx("t")
            b.tt("dve", xo[:], b.ps[po][:, :], xr[:], ALU.add, [("ps", po), xrr], [xor_])
            b.store(o_x[tile, :, oc, :], xo[:], [xor_])
    return b.finish()


GM0 = 1792 + 1536 + 384 + 256 + 64 + 512


def prep_p3(inp, l, x_cur, ysf, ysb, bonus, ybr, yc, yd):
    f = np.float32
    w_in = inp["w_in"][l]
    chunks = []
    for g in range(16):
        chunks.append(_fm(w_in[:, GM0 + g * 128:GM0 + (g + 1) * 128], 16))
    M0 = GM0 + 2048
    for oc in range(16):
        for bi in range(4):
            c0 = M0 + bi * 2048 + oc * 128
            chunks.append(_fm(w_in[:, c0:c0 + 128], 16))
    for oc in range(16):
        chunks.append(_fm(inp["w_out"][l][:, oc * 128:(oc + 1) * 128], 16))
    wA = np.stack(chunks)
    wb = inp["w_branch"][l].reshape(2048, 2048)
    wB = np.stack([_fm(wb[:, oc * 128:(oc + 1) * 128], 16) for oc in range(16)])
    par = np.zeros((128, NPAR3), f)
    par[:, PC3["ng"]:PC3["ng"] + 16] = inp["norm_g"][l].reshape(16, 128).T
    par[:, PC3["gng"]:PC3["gng"] + 4] = inp["rw_gn_g"][l].reshape(4, 128).T
    par[:, PC3["gnb"]:PC3["gnb"] + 4] = inp["rw_gn_b"][l].reshape(4, 128).T
    par[:, PC3["subg"]] = inp["diff_sub_g"][l]
    par[:, PC3["lamf"]] = 1.0 - (0.8 - 0.6 * math.exp(-0.3 * l))
    maps = []
    ntok = NT1 * TT
    for c in range(NCORES):
        xt = np.empty((NT1, 128, 16, TT), f)
        brr = np.empty((NT1, 128, 6, 4, TT), f)
        for t in range(NT1):
            n0 = c * ntok + t * TT
            xt[t] = x_cur[n0:n0 + TT].reshape(TT, 16, 128).transpose(2, 1, 0)
            for i, a in enumerate((ysf, ysb, bonus, ybr, yc, yd)):
                brr[t, :, i] = a[:, n0:n0 + TT].reshape(4, 128, TT).transpose(1, 0, 2)
        maps.append({"xT": xt, "par": par, "wA": wA, "wB": wB, "br": brr})
    return maps


def post_p3(results):
    outs = []
    for r in results:
        o = r["o_x"]
        outs.append(o.transpose(0, 3, 2, 1).reshape(NT1 * TT, 2048))
    return np.ascontiguousarray(np.concatenate(outs, 0))


_P1_AXIS = {"o_dv": 0, "o_mv": 0, "o_rw": 2}


def kernel(**inputs):
    inp = {k: np.asarray(v) for k, v in inputs.items()}
    x = np.ascontiguousarray(inp["x"][0], dtype=np.float32)
    for l in range(4):
        r1 = _run("p1", build_p1, prep_p1(inp, l, x))
        o = {k: _cat(r1, k, _P1_AXIS.get(k, 1)) for k in r1[0]}
        del r1
        rs = _run("scan2", build_scan2, prep_scan2(o["o_rw"]))
        ysf, ysb = post_scan2(rs)
        del rs
        yb = post_attn_diff(_run("diff", lambda: build_attn("diff"),
                                 prep_attn_diff(inp, l, o["o_dq"], o["o_dk"], o["o_dv"])))
        yc = post_attn_mla(_run("mla", lambda: build_attn("mla"),
                                prep_attn_mla(o["o_mqn"], o["o_mqr"], o["o_mkn"], o["o_mkr"], o["o_mv"])))
        r3 = _run("p3", build_p3, prep_p3(inp, l, x, ysf, ysb, o["o_bonus"], yb, yc, o["o_yd"]))
        x = post_p3(r3)
        del r3, o
    return x[None].astype(np.float32)


SB = 512
SG = 128
SC = 64


def build_scan2(T=SEQ):
    b = B()
    P = b.P
    fm_d = b.din("fm", [64, 2, 5, T])
    v_d = b.din("v", [64, 2, T])
    cst_d = b.din("cst", [128, 4, 128])
    m01_d = b.din("m01", [64, 2 * SB])
    y_d = b.dout("y", [64, 2, T])
    nblk = T // SB
    NI = (SB // SG) * 2
    cst = b.sb("cst", [128, 4, 128])
    m01 = b.sb("m01", [64, 2 * SB])
    b.load(cst[:], cst_d, ["cst"], "cst")
    b.load(m01[:], m01_d, ["m01"], "m01")
    Ml, Mu, MuI, I_ = (cst[:, i, :] for i in range(4))
    fmb = b.sb("fmb", [64, 2, 5, SB])
    vb = b.sb("vb", [64, 2, SB])
    sc = {n: b.sb(n, [64, 2, SB]) for n in ("KT", "NB", "KD", "RT", "NB2", "KD2")}
    b.ring("e", 4, [64, 2, SB])
    gC = b.sb("gC", [64, 2, SB // SC])
    clend = b.sb("clend", [64, 2, SB // SC])
    Hs = b.sb("Hs", [64, 2, SB // SC + 1, 64])
    yb = b.sb("yb", [64, 2, SB])
    it_buf = []
    for i in range(NI):
        d = {}
        for n, shp in (("N0", [128, 128]), ("N1", [128, 128]), ("P0", [128, 128]), ("P1", [128, 128]),
                       ("X0", [128, 128]), ("X1", [128, 128]), ("AkT", [128, 128]), ("BkT", [128, 128]),
                       ("BnbT", [128, 128]), ("NBt", [128, 64]), ("KDt", [128, 64]), ("Vt", [128, 64]),
                       ("NB2t", [128, 64]), ("KD2t", [128, 64]),
                       ("WT", [64, 128]), ("U", [128, 64]), ("G1", [64, 2, 64]), ("G2", [64, 2, 64])):
            d[n] = b.sb(f"i{i}{n}", shp)
        it_buf.append(d)
    b.memset("dve", Hs[:, :, 0, :], 0.0, ["Hs"])

    def r_(i, n):
        return (f"i{i}", n)

    def bulk(blk):
        t0 = blk * SB
        b.load(fmb[:], fm_d[:, :, :, t0:t0 + SB], ["fmb"], "fmb")
        b.load(vb[:], v_d[:, :, t0:t0 + SB], ["vb"], "vb")
        lw = fmb[:, :, 4, :]
        cl, clr = b.nx("e")
        for p in range(2):
            P.op("dve", lambda e, p=p, cl=cl: e.tensor_tensor_scan(
                out=cl[:, p, :], data0=m01[:, 0:SB], data1=fmb[:, p, 4, :], initial=0.0, op0=ALU.mult, op1=ALU.add),
                reads=["fmb", "m01"], writes=[clr])
        for p in range(2):
            b.cp("pool", clend[:, p, :], cl[:, p, SC - 1:SB:SC], [clr], ["clend"])
        e1, e1r = b.nx("e")
        b.tt("pool", e1[:], cl[:], lw, ALU.subtract, [clr, "fmb"], [e1r])
        b.act(e1[:], e1[:], AF.Exp, [e1r], [e1r])
        b.tt("dve", sc["KT"][:], fmb[:, :, 0, :], e1[:], ALU.mult, ["fmb", e1r], ["KT"])
        e2, e2r = b.nx("e")
        b.act(e2[:], cl[:], AF.Exp, [clr], [e2r], scale=-1.0)
        b.tt("pool", sc["NB"][:], fmb[:, :, 1, :], e2[:], ALU.mult, ["fmb", e2r], ["NB"])
        b.tt("dve", sc["KD"][:], fmb[:, :, 2, :], e2[:], ALU.mult, ["fmb", e2r], ["KD"])
        e3, e3r = b.nx("e")
        b.act(e3[:], cl[:], AF.Exp, [clr], [e3r])
        b.tt("pool", sc["RT"][:], fmb[:, :, 3, :], e3[:], ALU.mult, ["fmb", e3r], ["RT"])
        for p in range(2):
            b.cp("pool", gC[:, p, :], e3[:, p, SC - 1:SB:SC], [e3r], ["gC"])
        e4, e4r = b.nx("e")
        for p in range(2):
            for c in range(SB // SC):
                b.act(e4[:, p, c * SC:(c + 1) * SC], cl[:, p, c * SC:(c + 1) * SC], AF.Exp, [clr, "clend"], [e4r],
                      bias=clend[:, p, c:c + 1], scale=-1.0)
        b.tt("dve", sc["NB2"][:], fmb[:, :, 1, :], e4[:], ALU.mult, ["fmb", e4r], ["NB2"])
        b.tt("pool", sc["KD2"][:], fmb[:, :, 2, :], e4[:], ALU.mult, ["fmb", e4r], ["KD2"])

    def evac_act(dst, pi, M, N, wr, scale=None):
        if scale is None:
            b.cp("act", dst, b.ps[pi][0:M, 0:N], [("ps", pi)], wr)
        else:
            b.act(dst, b.ps[pi][0:M, 0:N], AF.Copy, [("ps", pi), "gC"], wr, scale=scale)

    def transpose(pi, in_ap, K, M, rd):
        out = b.ps[pi][0:M, 0:K]
        P.op("pe", lambda e: e.transpose(out, in_ap, I_[0:K, 0:K]), reads=rd + ["cst"], writes=[("ps", pi)])

    def stage1(blk):
        for i in range(NI):
            g, p = i // 2, i % 2
            ts = slice(g * SG, (g + 1) * SG)
            bf = it_buf[i]
            KT, NB, KD, RT = (sc[n][:, p, ts] for n in ("KT", "NB", "KD", "RT"))
            for (la, ln), (ra, rn), msk, dst in (((KT, "KT"), (NB, "NB"), Ml, "N0"), ((NB, "NB"), (KT, "KT"), Mu, "P0"),
                                                 ((KD, "KD"), (KT, "KT"), Mu, "AkT"), ((KD, "KD"), (RT, "RT"), MuI, "BkT"),
                                                 ((NB, "NB"), (RT, "RT"), MuI, "BnbT")):
                pi = b.nps()
                b.mm(pi, 128, 128, la, ra, True, True, [ln, rn])
                b.tt("dve", bf[dst][:], b.ps[pi][:, 0:128], msk, ALU.mult, [("ps", pi), "cst"], [r_(i, dst)])
            for src, sn, dst, dcols in ((sc["KT"], "KT", "X0", slice(0, 64)), (sc["NB"], "NB", "NBt", slice(0, 64)),
                                        (sc["KD"], "KD", "KDt", slice(0, 64)), (vb, "vb", "Vt", slice(0, 64)),
                                        (sc["NB2"], "NB2", "NB2t", slice(0, 64)), (sc["KD2"], "KD2", "KD2t", slice(0, 64))):
                pi = b.nps()
                transpose(pi, src[:, p, ts], 64, 128, [sn])
                evac_act(bf[dst][:, dcols], pi, 128, 64, [r_(i, dst)])
            pi = b.nps()
            b.mm(pi, 128, 64, bf["AkT"][:], bf["Vt"][:], True, True, [r_(i, "AkT"), r_(i, "Vt")])
            evac_act(bf["X0"][:, 64:128], pi, 128, 64, [r_(i, "X0")])

    def stage2(blk):
        for it in range(6):
            cur, nxt = it % 2, (it + 1) % 2
            for i in range(NI):
                bf = it_buf[i]
                Nc, Pc, Xc = bf[f"N{cur}"], bf[f"P{cur}"], bf[f"X{cur}"]
                Nn, Pn, Xn = bf[f"N{nxt}"], bf[f"P{nxt}"], bf[f"X{nxt}"]
                pi = b.nps()
                b.mm(pi, 128, 128, Pc[:], Xc[:], True, True, [r_(i, f"P{cur}"), r_(i, f"X{cur}")])
                b.tt("dve", Xn[:], Xc[:], b.ps[pi][:, 0:128], ALU.add, [("ps", pi), r_(i, f"X{cur}")], [r_(i, f"X{nxt}")])
                if it < 5:
                    pi = b.nps()
                    b.mm(pi, 128, 128, Nc[:], Pc[:], True, True, [r_(i, f"N{cur}"), r_(i, f"P{cur}")])
                    evac_act(Pn[:], pi, 128, 128, [r_(i, f"P{nxt}")])
                if it < 4:
                    pi = b.nps()
                    b.mm(pi, 128, 128, Pc[:], Nc[:], True, True, [r_(i, f"N{cur}"), r_(i, f"P{cur}")])
                    evac_act(Nn[:], pi, 128, 128, [r_(i, f"N{nxt}")])

    def stage3(blk):
        for i in range(NI):
            bf = it_buf[i]
            X = bf["X0"]
            pi = b.nps()
            transpose(pi, X[:, 0:64], 128, 64, [r_(i, "X0")])
            evac_act(bf["WT"][:], pi, 64, 128, [r_(i, "WT")])
            for c in range(2):
                cs = slice(c * SC, (c + 1) * SC)
                pi = b.nps()
                b.mm(pi, 64, 64, X[cs, 0:64], bf["NB2t"][cs, :], True, True, [r_(i, "X0"), r_(i, "NB2t")])
                pdiag, pdr = b.nx("dg")
                g, p = i // 2, i % 2
                cg = g * 2 + c
                b.ts("pool", pdiag[:], I_[0:64, 0:64], gC[:, p, cg:cg + 1], None, ALU.mult, None, ["cst", "gC"], [pdr])
                b.tt("dve", bf["G1"][:, c, :], b.ps[pi][0:64, 0:64], pdiag[:], ALU.add, [("ps", pi), pdr], [r_(i, "G1")])
                pi = b.nps()
                b.mm(pi, 64, 64, bf["NB2t"][cs, :], X[cs, 64:128], True, False, [r_(i, "X0"), r_(i, "NB2t")])
                b.mm(pi, 64, 64, bf["KD2t"][cs, :], bf["Vt"][cs, :], False, True, [r_(i, "KD2t"), r_(i, "Vt")])
                evac_act(bf["G2"][:, c, :], pi, 64, 64, [r_(i, "G2")])

    def stage4(blk):
        nchunk = SB // SC
        for cg in range(nchunk):
            for p in range(2):
                i = (cg // 2) * 2 + p
                c = cg % 2
                bf = it_buf[i]
                pi = b.nps()
                b.mm(pi, 64, 64, bf["G1"][:, c, :], Hs[:, p, cg, :], True, False, [r_(i, "G1"), ("Hs", p)])
                b.mm(pi, 64, 64, I_[0:64, 0:64], bf["G2"][:, c, :], False, True, ["cst", r_(i, "G2")])
                b.cp("act", Hs[:, p, cg + 1, :], b.ps[pi][0:64, 0:64], [("ps", pi)], [("Hs", p)])

    def stage5(blk):
        t0 = blk * SB
        for i in range(NI):
            g, p = i // 2, i % 2
            bf = it_buf[i]
            X = bf["X0"]
            for c in range(2):
                cs = slice(c * SC, (c + 1) * SC)
                cg = g * 2 + c
                pi = b.nps()
                b.mm(pi, 128, 64, bf["WT"][:], Hs[:, p, cg, :], True, True, [r_(i, "WT"), ("Hs", p)])
                b.tt("dve", bf["U"][cs, :], b.ps[pi][cs, 0:64], X[cs, 64:128], ALU.add, [("ps", pi), r_(i, "X0")], [r_(i, "U")])
            pi = b.nps()
            b.mm(pi, 64, 128, bf["Vt"][:], bf["BkT"][:], True, False, [r_(i, "Vt"), r_(i, "BkT")])
            b.mm(pi, 64, 128, bf["U"][:], bf["BnbT"][:], False, False, [r_(i, "U"), r_(i, "BnbT")])
            for c in range(2):
                cg = g * 2 + c
                out = b.ps[pi][0:64, c * SC:(c + 1) * SC]
                lhsT = Hs[:, p, cg, :]
                rhs = sc["RT"][:, p, g * SG + c * SC:g * SG + (c + 1) * SC]
                P.op("pe", lambda e, out=out, lhsT=lhsT, rhs=rhs, c=c: e.matmul(out, lhsT, rhs, start=False, stop=(c == 1)),
                     reads=[("Hs", p), "RT"], writes=[("ps", pi)])
            b.cp("act", yb[:, p, g * SG:(g + 1) * SG], b.ps[pi][0:64, 0:128], [("ps", pi)], ["yb"])
        b.store(y_d[:, :, t0:t0 + SB], yb[:], ["yb"])
        if blk + 1 < nblk:
            b.cp("pool", Hs[:, :, 0, :], Hs[:, :, SB // SC, :], [("Hs", 0), ("Hs", 1)], [("Hs", 0), ("Hs", 1)])

    b.ring("dg", 4, [64, 64])
    for blk in range(nblk):
        bulk(blk)
        stage1(blk)
        stage2(blk)
        stage3(blk)
        stage4(blk)
        stage5(blk)
    return b.finish()


def _scan2_consts():
    G, C = SG, SC
    Ml = np.zeros((G, G), np.float32)
    for t in range(G):
        for s in range(G):
            if t // C == s // C and s < t:
                Ml[t, s] = 1
    cst = np.stack([Ml, Ml.T, Ml.T + np.eye(G, dtype=np.float32), np.eye(G, dtype=np.float32)], 1)
    m01 = np.ones((64, 2 * SB), np.float32)
    m01[:, ::C] = 0
    return np.ascontiguousarray(cst), m01


def prep_scan2(o_rw, T=SEQ):
    cst, m01 = _scan2_consts()
    maps = []
    for h in range(NCORES):
        sl = slice(h * 64, (h + 1) * 64)
        fm = np.empty((64, 2, 5, T), np.float32)
        v = np.empty((64, 2, T), np.float32)
        for d in range(2):
            for k, a in enumerate((o_rw[2][sl], o_rw[3 + 3 * d][sl], o_rw[4 + 3 * d][sl], o_rw[0][sl], o_rw[5 + 3 * d][sl])):
                fm[:, d, k] = a if d == 0 else a[:, ::-1]
            v[:, d] = o_rw[1][sl] if d == 0 else o_rw[1][sl][:, ::-1]
        maps.append({"fm": fm, "v": v, "cst": cst, "m01": m01})
    return maps


def post_scan2(results):
    yf = np.concatenate([r["y"][:, 0] for r in results], 0)
    yb = np.concatenate([r["y"][:, 1][:, ::-1] for r in results], 0)
    return np.ascontiguousarray(yf), np.ascontiguousarray(yb)
```

```python
import math
from contextlib import ExitStack
import numpy as np
import concourse.bass as bass
import concourse.mybir as mybir
from concourse.bass_utils import run_bass_kernel_spmd

F32 = mybir.dt.float32
BF16 = mybir.dt.bfloat16
F32R = mybir.dt.float32r
I32 = mybir.dt.int32
ALU = mybir.AluOpType
AF = mybir.ActivationFunctionType
AX = mybir.AxisListType
ENGS = ("pe", "act", "dve", "pool", "sp")
NCORES = 8


class _Op:
    __slots__ = ("eng", "fn", "waits", "signal", "dma_key", "idx", "sigval")

    def __init__(self, eng, fn, dma_key):
        self.eng = eng
        self.fn = fn
        self.waits = []
        self.signal = False
        self.dma_key = dma_key
        self.idx = None
        self.sigval = None


class _Res:
    __slots__ = ("w", "r")

    def __init__(self):
        self.w = None
        self.r = []


class Prog:
    def __init__(self, nc):
        self.nc = nc
        self.ops = {e: [] for e in ENGS}
        self.res = {}
        self.dma_cnt = {}
        self.dma_last = {}
        self.waited = {e: {} for e in ENGS}

    def _need(self, op, tok, isd):
        if tok is None:
            return
        kind, src, val = tok
        if kind == "e" and src == op.eng and not isd and src == "pe":
            return
        w = self.waited[op.eng]
        k = (kind, src)
        if w.get(k, -1) >= val:
            return
        w[k] = val
        op.waits.append(tok)
        if kind == "e":
            self.ops[src][val].signal = True

    def op(self, eng, fn, reads=(), writes=(), dma=None):
        o = _Op(eng, fn, dma)
        o.idx = len(self.ops[eng])
        isd = dma is not None
        if isd:
            n = self.dma_cnt.get(dma, 0) + 1
            self.dma_cnt[dma] = n
            self._need(o, self.dma_last.get(dma), True)
            tok = ("d", dma, n)
            self.dma_last[dma] = tok
        else:
            tok = ("e", eng, o.idx)
        for r in reads:
            st = self.res.setdefault(r, _Res())
            self._need(o, st.w, isd)
        for r in writes:
            st = self.res.setdefault(r, _Res())
            self._need(o, st.w, isd)
            for t in st.r:
                self._need(o, t, isd)
        for r in reads:
            st = self.res[r]
            st.r.append(tok)
            if len(st.r) > 48:
                st.r = st.r[-48:]
        for r in writes:
            st = self.res[r]
            st.w = tok
            st.r = []
        self.ops[eng].append(o)
        return tok

    def wait_tokens(self, eng, toks):
        o = _Op(eng, None, None)
        o.idx = len(self.ops[eng])
        for t in toks:
            self._need(o, t, True)
        self.ops[eng].append(o)

    def emit(self):
        nc = self.nc
        esem = {e: nc.alloc_semaphore(name=f"s_{e}") for e in ENGS}
        dsem = {k: nc.alloc_semaphore(name=f"d_{i}") for i, k in enumerate(self.dma_cnt)}
        for e in ENGS:
            c = 0
            for o in self.ops[e]:
                if o.signal:
                    c += 1
                    o.sigval = c
        ops = self.ops

        def body(e):
            def f(eng):
                for o in ops[e]:
                    for kind, src, val in o.waits:
                        if kind == "e":
                            eng.wait_ge(esem[src], ops[src][val].sigval)
                        else:
                            eng.wait_ge(dsem[src], 16 * val)
                    if o.fn is None:
                        continue
                    inst = o.fn(eng)
                    if o.dma_key is not None:
                        inst.then_inc(dsem[o.dma_key], 16)
                    elif o.signal:
                        inst.then_inc(esem[e], 1)
            return f

        with nc.Block() as block:
            block.tensor(body("pe"))
            block.scalar(body("act"))
            block.vector(body("dve"))
            block.gpsimd(body("pool"))
            block.sync(body("sp"))


class B:
    def __init__(self):
        self.nc = bass.Bass("TRN2", target_bir_lowering=False)
        self.P = Prog(self.nc)
        self.es = ExitStack()
        self.ps = [self.es.enter_context(self.nc.psum_tensor(f"ps{i}", [128, 512], F32)) for i in range(8)]
        self.psi = 0
        self.psrot = list(range(8))
        self.rings = {}
        self.outtoks = []
        self.ndq = 0
        self.attn_banks = [(4, 5), (6, 7)]
        self.attn_par = 0

    def din(self, name, shape, dt=F32):
        return self.nc.dram_tensor(name, list(shape), dt, kind="ExternalInput").ap()

    def dout(self, name, shape, dt=F32):
        return self.nc.dram_tensor(name, list(shape), dt, kind="ExternalOutput").ap()

    def sb(self, name, shape, dt=F32):
        return self.es.enter_context(self.nc.sbuf_tensor("s_" + name, list(shape), dt))

    def nps(self):
        rot = self.psrot
        i = rot[self.psi % len(rot)]
        self.psi += 1
        return i

    def ring(self, name, n, shape, dt=F32):
        self.rings[name] = [[self.sb(f"{name}{i}", shape, dt) for i in range(n)], 0]

    def nx(self, name):
        r = self.rings[name]
        i = r[1]
        r[1] = (i + 1) % len(r[0])
        return r[0][i], (name, i)

    def mm(self, pi, M, N, lhsT, rhs, st, sp, rd, po=0):
        out = self.ps[pi][po:po + M, 0:N]
        if getattr(self, "f32r", False) and lhsT.dtype == F32 and rhs.dtype == F32:
            lhsT = lhsT.bitcast(F32R)
            rhs = rhs.bitcast(F32R)
        self.P.op("pe", lambda e: e.matmul(out, lhsT, rhs, start=st, stop=sp), reads=rd, writes=[("ps", pi)])

    def act(self, out, in_, func, rd, wr, bias=None, scale=None):
        kw = {}
        if bias is not None:
            kw["bias"] = bias
        if scale is not None:
            kw["scale"] = scale
        self.P.op("act", lambda e: e.activation(out=out, in_=in_, func=func, **kw), reads=rd, writes=wr)

    def stt(self, out, in0, scalar, in1, op0, op1, rd, wr):
        self.P.op("dve", lambda e: e.scalar_tensor_tensor(out=out, in0=in0, scalar=scalar, in1=in1,
                                                          op0=op0, op1=op1), reads=rd, writes=wr)

    def tt(self, eng, out, in0, in1, op, rd, wr):
        self.P.op(eng, lambda e: e.tensor_tensor(out=out, in0=in0, in1=in1, op=op), reads=rd, writes=wr)

    def ts(self, eng, out, in0, s1, s2, op0, op1, rd, wr):
        if op1 is None:
            self.P.op(eng, lambda e: e.tensor_scalar(out=out, in0=in0, scalar1=s1, scalar2=None, op0=op0),
                      reads=rd, writes=wr)
        else:
            self.P.op(eng, lambda e: e.tensor_scalar(out=out, in0=in0, scalar1=s1, scalar2=s2, op0=op0, op1=op1),
                      reads=rd, writes=wr)

    def cp(self, eng, out, in_, rd, wr):
        if eng == "act":
            self.P.op("act", lambda e: e.copy(out=out, in_=in_), reads=rd, writes=wr)
        else:
            self.P.op(eng, lambda e: e.tensor_copy(out=out, in_=in_), reads=rd, writes=wr)

    def wcast(self, wbf, wst, sl, rn, wn, dsl=None):
        dsl = sl if dsl is None else dsl
        self.cp("dve", wbf[:, dsl, 0:8], wst[:, sl, 0:8], [(rn, sl)], [(wn, dsl)])
        self.cp("act", wbf[:, dsl, 8:16], wst[:, sl, 8:16], [(rn, sl)], [(wn, dsl)])

    def recip(self, out, in_, rd, wr):
        self.P.op("dve", lambda e: e.reciprocal(out=out, in_=in_), reads=rd, writes=wr)

    def memset(self, eng, ap, val, wr):
        self.P.op(eng, lambda e: e.memset(ap, val), writes=wr)

    def load(self, out, in_, wr, key, q="sp"):
        return self.P.op(q, lambda e: e.dma_start(out=out, in_=in_), writes=wr, dma=key)

    def store(self, out, in_, rd, q="pool"):
        self.ndq += 1
        key = ("st", self.ndq % 6)
        t = self.P.op(q, lambda e: e.dma_start(out=out, in_=in_), reads=rd, dma=key)
        self.outtoks.append(t)

    def finish(self):
        last = {}
        for t in self.outtoks:
            last[t[1]] = t
        self.P.wait_tokens("pool", list(last.values()))
        self.P.emit()
        self.es.close()
        return self.nc


def fm_rstd(b, sq_list, ones_ap, Pn, N, inv_n, eps, consts_res):
    pi = b.nps()
    for i, (ap, res) in enumerate(sq_list):
        b.mm(pi, Pn, N, ones_ap, ap, i == 0, i == len(sq_list) - 1, [res, consts_res])
    ln, lnr = b.nx("t")
    b.act(ln[0:Pn, 0:N], b.ps[pi][0:Pn, 0:N], AF.Ln, [("ps", pi)], [lnr], bias=b.epsc[eps][0:Pn, :], scale=inv_n)
    rs, rsr = b.nx("t")
    b.act(rs[0:Pn, 0:N], ln[0:Pn, 0:N], AF.Exp, [lnr], [rsr], scale=-0.5)
    return rs, rsr


def setup_consts(b):
    b.ones = b.sb("ones", [128, 128])
    b.blk = b.sb("blk", [128, 128])
    b.memset("pool", b.ones[:], 1.0, ["ones"])
    b.memset("pool", b.blk[:], 0.0, ["blk"])
    b.memset("pool", b.blk[0:64, 0:64], 1.0, ["blk"])
    b.memset("pool", b.blk[64:128, 64:128], 1.0, ["blk"])
    b.onesb = b.sb("onesb", [128, 128], BF16)
    b.memset("pool", b.onesb[:], 1.0, ["onesb"])
    b.epsc = {}
    for i, v in enumerate((1e-6, 64e-5)):
        t = b.sb(f"epsc{i}", [128, 1])
        b.memset("pool", t[:], v, [f"epsc{i}"])
        b.epsc[v] = t
        b.P.res


def attn_core(b, qlist, kts, vts, nkt, scale, out_M, bias_fn=None, tag="a"):
    po, pz = b.attn_banks[b.attn_par]
    b.attn_par ^= 1
    acc, accr = b.nx("zacc")
    acc2, acc2r = b.nx("zacc2")
    pend = []
    zq = []
    acc_init = [False]
    LOOK = 2

    def pv(kt, pt, ptr):
        va, vr = vts(kt)
        b.mm(po, out_M, 512, va, pt[:], kt == 0, kt == nkt - 1, [vr, ptr])
        if kt % 3 == 0:
            b.mm(pz, out_M, 512, b.onesb[:, 0:out_M], pt[:], kt == 0, False, ["onesb", ptr])

    for kt in range(nkt):
        pi = b.nps()
        ks = kts(kt)
        for i, ((qa, qr), (ka, kr)) in enumerate(zip(qlist, ks)):
            b.mm(pi, 128, 512, ka, qa, i == 0, i == len(qlist) - 1, [qr, kr])
        if len(pend) >= LOOK:
            pv(*pend.pop(0))
        pt, ptr = b.nx("pt")
        bf = bias_fn(kt) if bias_fn is not None else None
        if bf is not None:
            kind, val = bf
            if kind == "const":
                b.act(pt[:], b.ps[pi][:, :], AF.Exp, [("ps", pi), "tab"], [ptr], bias=val, scale=scale)
            else:
                tmp, tr = b.nx("t")
                bap, bres = val
                b.stt(tmp[:], b.ps[pi][:, :], scale, bap, ALU.mult, ALU.add, [("ps", pi), bres], [tr])
                b.act(pt[:], tmp[:], AF.Exp, [tr], [ptr])
        else:
            b.act(pt[:], b.ps[pi][:, :], AF.Exp, [("ps", pi)], [ptr], scale=scale)
        if kt % 3 == 0:
            zq.append((kt, pt, ptr))
        elif not acc_init[0]:
            b.cp("dve", acc2[:], pt[:], [ptr], [acc2r])
            acc_init[0] = True
        else:
            b.tt("dve", acc2[:], acc2[:], pt[:], ALU.add, [ptr, acc2r], [acc2r])
        pend.append((kt, pt, ptr))
    for pp in pend:
        pv(*pp)
    b.mm(pz, out_M, 512, b.ones[:, 0:out_M], acc2[:], False, True, ["ones", acc2r])
    return po, pz


TT = 512
TH = TT + 2
NT1 = 2
PC = {}
_c = 0
for _n, _w in (("ng", 16), ("sh", 42), ("w0", 8), ("a0", 8), ("kk", 4), ("ka", 4), ("rk", 4), ("dqg", 1), ("dkg", 1),
               ("qlg", 3), ("kvg", 2), ("npg", 2), ("rpg", 2), ("rpgs", 2), ("mqg", 2), ("mng", 16), ("invf", 1),
               ("sgn", 1), ("omka", 4)):
    PC[_n] = _c
    _c += _w
NPAR = _c
RW0 = 0
def _p1_cols():
    cols = []
    cols += [(RW0 + 1536, 128), (RW0 + 1664, 128)]
    for c in range(4):
        cols += [(c * 128, 128), (512 + c * 128, 128), (1024 + c * 128, 128)]
    o = 1792
    cols += [(o + i * 128, 128) for i in range(4)]
    cols += [(o + 512 + i * 128, 128) for i in range(4)]
    o2 = 1792 + 1536
    cols += [(o2 + i * 128, 128) for i in range(3)]
    cols += [(o2 + 384 + i * 128, 128) for i in range(2)]
    cols += [("krope", 128)]
    o3 = o2 + 384 + 256 + 64
    cols += [(o3 + i * 128, 128) for i in range(4)]
    cols += [(1792 + 1024 + i * 128, 128) for i in range(4)]
    return cols
P1COLS = _p1_cols()
NCH1 = len(P1COLS)
KROPE0 = 1792 + 1536 + 384 + 256


def build_p1():
    b = B()
    P = b.P
    xT = b.din("xT", [NT1, 128, 16, TH])
    pos = b.din("pos", [NT1, TH], I32)
    par_d = b.din("par", [128, NPAR])
    w = b.din("w", [NCH1, 128, 16, 128])
    wup_d = b.din("wup", [128, 512])
    aup_d = b.din("aup", [128, 512])
    wuq_d = b.din("wuq", [128, 3, 768])
    wuqs_d = b.din("wuqs", [128, 3, 256])
    wukvk_d = b.din("wukvk", [128, 2, 512])
    wukvv_d = b.din("wukvv", [128, 2, 512])
    memT_d = b.din("memT", [128, 16, 256])
    wkv_d = b.din("wkv", [8, 128, 16, 128])
    o_rw = b.dout("o_rw", [9, 512, NT1 * TT])
    o_bonus = b.dout("o_bonus", [512, NT1 * TT])
    o_dq = b.dout("o_dq", [512, NT1 * TT])
    o_dk = b.dout("o_dk", [512, NT1 * TT])
    o_dv = b.dout("o_dv", [NT1 * TT, 512])
    o_mqn = b.dout("o_mqn", [512, NT1 * TT])
    o_mqr = b.dout("o_mqr", [256, NT1 * TT])
    o_mkn = b.dout("o_mkn", [512, NT1 * TT])
    o_mkr = b.dout("o_mkr", [64, NT1 * TT])
    o_mv = b.dout("o_mv", [NT1 * TT, 512])
    o_yd = b.dout("o_yd", [512, NT1 * TT])

    setup_consts(b)
    par = b.sb("par", [128, NPAR])
    b.load(par[:], par_d, ["par"], "par")
    pc = lambda n, i=0: par[:, PC[n] + i:PC[n] + i + 1]
    b.ts("dve", par[:, PC["omka"]:PC["omka"] + 4], par[:, PC["ka"]:PC["ka"] + 4], -1.0, 1.0, ALU.mult, ALU.add,
         ["par"], ["par"])
    xs = b.sb("xs", [128, 16, TH])
    hT = b.sb("hT", [128, 16, TH], BF16)
    wst = b.sb("wst", [128, 2, 16, 128])
    wbf = b.sb("wbf", [128, 2, 16, 128], BF16)
    stg = b.sb("stg", [128, 3072])
    wup = b.sb("wup_s", [128, 512])
    aup = b.sb("aup_s", [128, 512])
    wuq = b.sb("wuq_s", [128, 3, 768], BF16)
    wuqs = b.sb("wuqs_s", [128, 3, 256], BF16)
    wukvk = b.sb("wukvk_s", [128, 2, 512], BF16)
    wukvv = b.sb("wukvv_s", [128, 2, 512], BF16)
    memn = b.sb("memn", [128, 16, 256], BF16)
    kmem = b.sb("kmem", [128, 4, 256], BF16)
    vmem = b.sb("vmem", [128, 2, 512], BF16)
    b.ring("t", 14, [128, TT])
    b.ring("th", 3, [128, TH])
    b.ring("rkv", 4, [128, TT])
    b.ring("bf", 4, [128, TT], BF16)
    b.ring("pt", 3, [128, TT], BF16)
    b.ring("zacc", 1, [128, TT])
    b.ring("zacc2", 1, [128, TT])
    twd = b.sb("twd", [128, TT])
    adl = b.sb("adl", [128, TT])
    ropC = b.sb("ropC", [64, TT])
    ropS = b.sb("ropS", [64, TT])
    rsx = b.sb("rsx", [128, TH])
    ql = b.sb("ql", [128, 3, TT])
    qln = b.sb("qln", [128, 3, TT], BF16)
    kvl = b.sb("kvl", [128, 2, TT])
    kvn = b.sb("kvn", [128, 2, TT], BF16)
    posi = b.sb("posi", [64, TH], I32)

    b.load(wup[:], wup_d, ["wup"], "wl0")
    b.load(aup[:], aup_d, ["aup"], "wl1")
    for dst, src, n, nm in ((wuq, wuq_d, 3 * 768, "wuq"), (wuqs, wuqs_d, 3 * 256, "wuqs"),
                            (wukvk, wukvk_d, 1024, "wukvk"), (wukvv, wukvv_d, 1024, "wukvv")):
        b.load(stg[:, 0:n], src.rearrange("p a b -> p (a b)"), ["stg"], "stg")
        b.cp("dve", dst[:].rearrange("p a b -> p (a b)"), stg[:, 0:n], ["stg"], [nm])

    def rmsnorm_cols(src_tile, nkc, N, gname, dst_tile, res_src, res_dst, inv_n):
        pi = b.nps()
        pih = b.nps() if N > 512 else None
        for kc in range(nkc):
            sq, sqr = b.nx("th")
            b.act(sq[:, 0:N], src_tile[:, kc, 0:N], AF.Square, [res_src], [sqr])
            b.mm(pi, 128, min(N, 512), b.ones[:], sq[:, 0:min(N, 512)], kc == 0, kc == nkc - 1, [sqr, "ones"])
            if pih is not None:
                b.mm(pih, 128, N - 512, b.ones[:], sq[:, 512:N], kc == 0, kc == nkc - 1, [sqr, "ones"])
        ln, lnr = b.nx("th")
        b.act(ln[:, 0:min(N, 512)], b.ps[pi][:, 0:min(N, 512)], AF.Ln, [("ps", pi)], [lnr],
              bias=b.epsc[1e-6][:], scale=inv_n)
        if pih is not None:
            b.act(ln[:, 512:N], b.ps[pih][:, 0:N - 512], AF.Ln, [("ps", pih)], [lnr], bias=b.epsc[1e-6][:], scale=inv_n)
        b.act(rsx[:, 0:N], ln[:, 0:N], AF.Exp, [lnr], ["rsx"], scale=-0.5)
        for kc in range(nkc):
            b.stt(dst_tile[:, kc, 0:N], src_tile[:, kc, 0:N], pc(gname, kc), rsx[:, 0:N], ALU.mult, ALU.mult,
                  [res_src, "rsx", "par"], [res_dst])

    b.load(xs[:, :, 0:256], memT_d, ["xs"], "xs")
    rmsnorm_cols(xs, 16, 256, "mng", memn, "xs", "memn", 1.0 / 2048)
    for ci in range(8):
        sl = ci % 2
        b.load(wst[:, sl], wkv_d[ci], [("wst", sl)], ("wst", sl))
        b.wcast(wbf, wst, sl, "wst", "wbf")
        if ci < 4:
            pi = b.nps()
            for kc in range(16):
                b.mm(pi, 128, 256, wbf[:, sl, kc, :], memn[:, kc, :], kc == 0, kc == 15, [("wbf", sl), "memn"])
            tq, tqr = b.nx("t")
            b.cp("act", tq[:, 0:256], b.ps[pi][:, 0:256], [("ps", pi)], [tqr])
            sq, sqr = b.nx("t")
            b.act(sq[:, 0:256], b.ps[pi][:, 0:256], AF.Square, [("ps", pi)], [sqr])
            rs, rsr = fm_rstd(b, [(sq[:, 0:256], sqr)], b.ones[:], 128, 256, 1.0 / 128, 1e-6, "ones")
            b.stt(kmem[:, ci, :], tq[:, 0:256], pc("mqg", 1), rs[:, 0:256], ALU.mult, ALU.mult, [tqr, rsr, "par"], ["kmem"])
        else:
            for tb in range(2):
                pi = b.nps()
                for kc in range(16):
                    b.mm(pi, 128, 128, memn[:, kc, tb * 128:(tb + 1) * 128], wbf[:, sl, kc, :], kc == 0, kc == 15,
                         [("wbf", sl), "memn"])
                b.cp("act", vmem[:, tb, (ci - 4) * 128:(ci - 3) * 128], b.ps[pi][:, 0:128], [("ps", pi)], ["vmem"])

    def wload(tile, ci):
        sl = (tile * NCH1 + ci) % 2
        b.load(wst[:, sl], w[ci], [("wst", sl)], ("wst", sl))

    def shift(pi, pih, rc, dst, dres):
        u, ur = b.nx("th")
        b.cp("act", u[:, 0:TT], b.ps[pi][:, :], [("ps", pi)], [ur])
        b.cp("act", u[:, TT:TH], b.ps[pih][:, 0:2], [("ps", pih)], [ur])
        s0 = pc("sh", 0 * 14 + rc); s1 = pc("sh", 1 * 14 + rc); s2 = pc("sh", 2 * 14 + rc)
        b.ts("dve", dst[:, :], u[:, 0:TT], s1, None, ALU.mult, None, [ur, "par"], [dres])
        b.stt(dst[:, 1:TT], u[:, 0:TT - 1], s0, dst[:, 1:TT], ALU.mult, ALU.add, [ur, "par", dres], [dres])
        b.stt(dst[:, 0:1], u[:, TT:TT + 1], s0, dst[:, 0:1], ALU.mult, ALU.add, [ur, "par", dres], [dres])
        b.stt(dst[:, 0:TT - 1], u[:, 1:TT], s2, dst[:, 0:TT - 1], ALU.mult, ALU.add, [ur, "par", dres], [dres])
        b.stt(dst[:, TT - 1:TT], u[:, TT + 1:TT + 2], s2, dst[:, TT - 1:TT], ALU.mult, ALU.add, [ur, "par", dres], [dres])

    def head_norm(pi, Pn, ones_ap, inv_n, gcol, dst_ap, dres, N=TT):
        tq, tqr = b.nx("t")
        b.cp("act", tq[0:Pn, 0:N], b.ps[pi][0:Pn, 0:N], [("ps", pi)], [tqr])
        sq, sqr = b.nx("t")
        b.act(sq[0:Pn, 0:N], b.ps[pi][0:Pn, 0:N], AF.Square, [("ps", pi)], [sqr])
        rs, rsr = fm_rstd(b, [(sq[0:Pn, 0:N], sqr)], ones_ap, Pn, N, inv_n, 1e-6, "ones")
        b.stt(dst_ap, tq[0:Pn, 0:N], gcol, rs[0:Pn, 0:N], ALU.mult, ALU.mult, [tqr, rsr, "par"], [dres])
        return tq, tqr, rs, rsr

    for tile in range(NT1):
        t0 = tile * TT
        b.load(xs[:], xT[tile], ["xs"], "xs")
        b.load(posi[:], pos[tile, :].partition_broadcast(64), ["posi"], "posi")
        wload(tile, 0)
        rmsnorm_cols(xs, 16, TH, "ng", hT, "xs", "hT", 1.0 / 2048)
        posf, posr = b.nx("th")
        b.cp("dve", posf[0:64, :], posi[:], ["posi"], [posr])
        for which, dstt, dres in ((0, ropS, "ropS"), (1, ropC, "ropC")):
            a, ar = b.nx("t")
            b.ts("dve", a[0:64, :], posf[0:64, 0:TT], pc("invf")[0:64, :], (math.pi / 2 if which else 0.0),
                 ALU.mult, ALU.add, [posr, "par"], [ar])
            y, yr = b.nx("t")
            b.ts("dve", y[0:64, :], a[0:64, :], 1.0 / (2 * math.pi), None, ALU.mult, None, [ar], [yr])
            ni = b.sb(f"ni{tile}{which}", [64, TT], I32)
            b.cp("dve", ni[:], y[0:64, :], [yr], [f"ni{tile}{which}"])
            nf, nfr = b.nx("t")
            b.cp("dve", nf[0:64, :], ni[:], [f"ni{tile}{which}"], [nfr])
            r, rr = b.nx("t")
            b.stt(r[0:64, :], nf[0:64, :], -2 * math.pi, a[0:64, :], ALU.mult, ALU.add, [nfr, ar], [rr])
            m, mr = b.nx("t")
            b.ts("dve", m[0:64, :], r[0:64, :], math.pi, -2 * math.pi, ALU.is_gt, ALU.mult, [rr], [mr])
            b.tt("dve", r[0:64, :], r[0:64, :], m[0:64, :], ALU.add, [rr, mr], [rr])
            b.ts("dve", m[0:64, :], r[0:64, :], -math.pi, 2 * math.pi, ALU.is_lt, ALU.mult, [rr], [mr])
            b.tt("dve", r[0:64, :], r[0:64, :], m[0:64, :], ALU.add, [rr, mr], [rr])
            if which == 0:
                sn, snr = b.nx("t")
                b.act(sn[0:64, :], r[0:64, :], AF.Sin, [rr], [snr])
                b.ts("dve", ropS[:], sn[0:64, :], pc("sgn")[0:64, :], None, ALU.mult, None, [snr, "par"], ["ropS"])
            else:
                b.act(ropC[:], r[0:64, :], AF.Sin, [rr], ["ropC"])

        def rope_out(t_tq, t_r, sw_tq, sw_r, rs, rsr, gi, dst_dram):
            a, ar = b.nx("t")
            b.stt(a[0:64, :], t_tq[0:64, :], pc("rpg", gi)[0:64, :], ropC[:], ALU.mult, ALU.mult, [t_r, "par", "ropC"], [ar])
            c, cr = b.nx("t")
            b.stt(c[0:64, :], sw_tq[0:64, :], pc("rpgs", gi)[0:64, :], ropS[:], ALU.mult, ALU.mult, [sw_r, "par", "ropS"], [cr])
            b.tt("dve", a[0:64, :], a[0:64, :], c[0:64, :], ALU.add, [ar, cr], [ar])
            b.tt("dve", a[0:64, :], a[0:64, :], rs[0:64, :], ALU.mult, [ar, rsr], [ar])
            b.store(dst_dram, a[0:64, :], [ar])

        rcur = {}
        for ci in range(NCH1):
            sl = (tile * NCH1 + ci) % 2
            if ci + 1 < NCH1:
                wload(tile, ci + 1)
            elif tile + 1 < NT1:
                wload(tile + 1, 0)
            b.wcast(wbf, wst, sl, "wst", "wbf")
            wr = ("wbf", sl)
            if ci >= 32:
                j = ci - 32
                for tb in range(4):
                    pi = b.nps()
                    for kc in range(16):
                        b.mm(pi, 128, 128, hT[:, kc, tb * 128:(tb + 1) * 128], wbf[:, sl, kc, :], kc == 0, kc == 15, [wr, "hT"])
                    o, orr = b.nx("t")
                    b.cp("act", o[:, 0:128], b.ps[pi][:, 0:128], [("ps", pi)], [orr])
                    b.store(o_dv[t0 + tb * 128:t0 + (tb + 1) * 128, j * 128:(j + 1) * 128], o[:, 0:128], [orr])
                continue
            if ci == 27:
                pis = []
                for hh in range(2):
                    pi = b.nps()
                    for kc in range(16):
                        b.mm(pi, 64, TT, wbf[:, sl, kc, hh * 64:(hh + 1) * 64], hT[:, kc, 0:TT], kc == 0, kc == 15, [wr, "hT"])
                    pis.append(pi)
                tq, tqr = b.nx("t")
                b.cp("act", tq[0:64, :], b.ps[pis[0]][0:64, :], [("ps", pis[0])], [tqr])
                sq, sqr = b.nx("t")
                b.act(sq[0:64, :], b.ps[pis[0]][0:64, :], AF.Square, [("ps", pis[0])], [sqr])
                sw, swr = b.nx("t")
                b.cp("act", sw[0:64, :], b.ps[pis[1]][0:64, :], [("ps", pis[1])], [swr])
                rs, rsr = fm_rstd(b, [(sq[0:64, :], sqr)], b.ones[0:64, 0:64], 64, TT, 1.0 / 64, 1e-6, "ones")
                rope_out(tq, tqr, sw, swr, rs, rsr, 1, o_mkr[:, t0:t0 + TT])
                continue
            pi = b.nps()
            for kc in range(16):
                b.mm(pi, 128, TT, wbf[:, sl, kc, :], hT[:, kc, 0:TT], kc == 0, kc == 15, [wr, "hT"])
            pih = None
            if ci < 14:
                pih = b.nps()
                for kc in range(16):
                    b.mm(pih, 128, 2, wbf[:, sl, kc, :], hT[:, kc, TT:TH], kc == 0, kc == 15, [wr, "hT"])
            if ci == 0:
                tmp, tr = b.nx("t")
                shift(pi, pih, 12, tmp, tr)
                b.act(twd[:], tmp[:], AF.Tanh, [tr], ["twd"])
            elif ci == 1:
                shift(pi, pih, 13, adl, "adl")
            elif ci < 14:
                c = (ci - 2) // 3
                kind = (ci - 2) % 3
                dst, dres = b.nx("rkv")
                shift(pi, pih, kind * 4 + c, dst, dres)
                rcur[kind] = (dst, dres)
                if kind == 0:
                    b.store(o_rw[0, c * 128:(c + 1) * 128, t0:t0 + TT], dst[:], [dres])
                if kind == 2:
                    r_t, r_r = rcur[0]
                    k_t, k_r = rcur[1]
                    v_t, v_r = rcur[2]
                    b.store(o_rw[1, c * 128:(c + 1) * 128, t0:t0 + TT], v_t[:], [v_r])
                    kr_, krr = b.nx("t")
                    b.ts("dve", kr_[:], k_t[:], pc("kk", c), None, ALU.mult, None, [k_r, "par"], [krr])
                    sq, sqr = b.nx("t")
                    b.act(sq[:], kr_[:], AF.Square, [krr], [sqr])
                    pj = b.nps()
                    b.mm(pj, 128, TT, b.blk[:], sq[:], True, True, [sqr, "blk"])
                    nr, nrr = b.nx("t")
                    b.act(nr[:], b.ps[pj][:, :], AF.Sqrt, [("ps", pj)], [nrr])
                    b.ts("dve", nr[:], nr[:], 1e-12, None, ALU.max, None, [nrr], [nrr])
                    b.recip(nr[:], nr[:], [nrr], [nrr])
                    kk_, kkr = b.nx("t")
                    b.tt("dve", kk_[:], kr_[:], nr[:], ALU.mult, [krr, nrr], [kkr])
                    b.store(o_rw[2, c * 128:(c + 1) * 128, t0:t0 + TT], kk_[:], [kkr])
                    kds = []
                    for d in range(2):
                        pw = b.nps()
                        b.mm(pw, 128, TT, wup[d * 64:(d + 1) * 64, c * 128:(c + 1) * 128], twd[d * 64:(d + 1) * 64, :],
                             True, True, ["wup", "twd"])
                        sg, sgr = b.nx("t")
                        b.act(sg[:], b.ps[pw][:, :], AF.Sigmoid, [("ps", pw), "par"], [sgr], bias=pc("w0", d * 4 + c))
                        dec, decr = b.nx("t")
                        b.act(dec[:], sg[:], AF.Copy, [sgr], [decr], scale=-math.exp(-0.5))
                        b.store(o_rw[5 + 3 * d, c * 128:(c + 1) * 128, t0:t0 + TT], dec[:], [decr])
                        pa = b.nps()
                        b.mm(pa, 128, TT, aup[d * 64:(d + 1) * 64, c * 128:(c + 1) * 128], adl[d * 64:(d + 1) * 64, :],
                             True, True, ["aup", "adl"])
                        a_, a_r = b.nx("t")
                        b.act(a_[:], b.ps[pa][:, :], AF.Sigmoid, [("ps", pa), "par"], [a_r], bias=pc("a0", d * 4 + c))
                        nb, nbr = b.nx("t")
                        b.stt(nb[:], kk_[:], -1.0, a_[:], ALU.mult, ALU.mult, [kkr, a_r], [nbr])
                        b.store(o_rw[3 + 3 * d, c * 128:(c + 1) * 128, t0:t0 + TT], nb[:], [nbr])
                        kd, kdr = b.nx("t")
                        b.ts("dve", kd[:], a_[:], pc("ka", c), pc("omka", c), ALU.mult, ALU.add, [a_r, "par"], [kdr])
                        b.tt("dve", kd[:], kd[:], k_t[:], ALU.mult, [kdr, k_r], [kdr])
                        b.store(o_rw[4 + 3 * d, c * 128:(c + 1) * 128, t0:t0 + TT], kd[:], [kdr])
                        kds.append((kd, kdr))
                    s_, s_r = b.nx("t")
                    b.tt("dve", s_[:], kds[0][0][:], kds[1][0][:], ALU.add, [kds[0][1], kds[1][1]], [s_r])
                    b.stt(s_[:], r_t[:], pc("rk", c), s_[:], ALU.mult, ALU.mult, [r_r, "par", s_r], [s_r])
                    pb_ = b.nps()
                    b.mm(pb_, 128, TT, b.blk[:], s_[:], True, True, [s_r, "blk"])
                    bo, bor = b.nx("t")
                    b.tt("dve", bo[:], v_t[:], b.ps[pb_][:, :], ALU.mult, [v_r, ("ps", pb_)], [bor])
                    b.store(o_bonus[c * 128:(c + 1) * 128, t0:t0 + TT], bo[:], [bor])
            elif ci < 22:
                isk = ci >= 18
                j = ci - (18 if isk else 14)
                o, orr = b.nx("t")
                head_norm(pi, 128, b.blk[:], 1.0 / 64, pc("dkg" if isk else "dqg"), o[:], orr)
                b.store((o_dk if isk else o_dq)[j * 128:(j + 1) * 128, t0:t0 + TT], o[:], [orr])
            elif ci < 25:
                j = ci - 22
                b.cp("act", ql[:, j, :], b.ps[pi][:, :], [("ps", pi)], ["ql"])
                if j == 2:
                    rmsnorm_cols(ql, 3, TT, "qlg", qln, "ql", "qln", 1.0 / 384)
                    for h in range(4):
                        pn = b.nps()
                        for kc in range(3):
                            b.mm(pn, 128, TT, wuq[:, kc, h * 192:h * 192 + 128], qln[:, kc, :], kc == 0, kc == 2, ["wuq", "qln"])
                        o, orr = b.nx("t")
                        head_norm(pn, 128, b.ones[:], 1.0 / 128, pc("npg", 0), o[:], orr)
                        b.store(o_mqn[h * 128:(h + 1) * 128, t0:t0 + TT], o[:], [orr])
                        pr = b.nps()
                        for kc in range(3):
                            b.mm(pr, 64, TT, wuq[:, kc, h * 192 + 128:h * 192 + 192], qln[:, kc, :], kc == 0, kc == 2, ["wuq", "qln"])
                        psw = b.nps()
                        for kc in range(3):
                            b.mm(psw, 64, TT, wuqs[:, kc, h * 64:(h + 1) * 64], qln[:, kc, :], kc == 0, kc == 2, ["wuqs", "qln"])
                        tq, tqr = b.nx("t")
                        b.cp("act", tq[0:64, :], b.ps[pr][0:64, :], [("ps", pr)], [tqr])
                        sq, sqr = b.nx("t")
                        b.act(sq[0:64, :], b.ps[pr][0:64, :], AF.Square, [("ps", pr)], [sqr])
                        sw, swr = b.nx("t")
                        b.cp("act", sw[0:64, :], b.ps[psw][0:64, :], [("ps", psw)], [swr])
                        rs, rsr = fm_rstd(b, [(sq[0:64, :], sqr)], b.ones[0:64, 0:64], 64, TT, 1.0 / 64, 1e-6, "ones")
                        rope_out(tq, tqr, sw, swr, rs, rsr, 0, o_mqr[h * 64:(h + 1) * 64, t0:t0 + TT])
            elif ci < 27:
                j = ci - 25
                b.cp("act", kvl[:, j, :], b.ps[pi][:, :], [("ps", pi)], ["kvl"])
                if j == 1:
                    rmsnorm_cols(kvl, 2, TT, "kvg", kvn, "kvl", "kvn", 1.0 / 256)
                    for h in range(4):
                        pn = b.nps()
                        for kc in range(2):
                            b.mm(pn, 128, TT, wukvk[:, kc, h * 128:(h + 1) * 128], kvn[:, kc, :], kc == 0, kc == 1, ["wukvk", "kvn"])
                        o, orr = b.nx("t")
                        head_norm(pn, 128, b.ones[:], 1.0 / 128, pc("npg", 1), o[:], orr)
                        b.store(o_mkn[h * 128:(h + 1) * 128, t0:t0 + TT], o[:], [orr])
                    for tb in range(4):
                        pv = b.nps()
                        for kc in range(2):
                            b.mm(pv, 128, 512, kvn[:, kc, tb * 128:(tb + 1) * 128], wukvv[:, kc, :], kc == 0, kc == 1, ["wukvv", "kvn"])
                        o, orr = b.nx("t")
                        b.cp("act", o[:], b.ps[pv][:, :], [("ps", pv)], [orr])
                        b.store(o_mv[t0 + tb * 128:t0 + (tb + 1) * 128, :], o[:], [orr])
            else:
                j = ci - 28
                qb, qbr = b.nx("bf")
                head_norm(pi, 128, b.ones[:], 1.0 / 128, pc("mqg", 0), qb[:], qbr)
                b.psrot = [0, 1, 2, 3]
                po, pz = attn_core(b, [(qb[:], qbr)],
                                   lambda kt, j=j: [(kmem[:, j, kt * 128:(kt + 1) * 128], "kmem")],
                                   lambda kt, j=j: (vmem[:, kt, j * 128:(j + 1) * 128], "vmem"),
                                   2, 128 ** -0.5, 128)
                b.psrot = list(range(8))
                rz, rzr = b.nx("t")
                b.recip(rz[:], b.ps[pz][:, :], [("ps", pz)], [rzr])
                o, orr = b.nx("t")
                b.tt("dve", o[:], b.ps[po][:, :], rz[:], ALU.mult, [("ps", po), rzr], [orr])
                b.store(o_yd[j * 128:(j + 1) * 128, t0:t0 + TT], o[:], [orr])
    return b.finish()


def _fm(a, nk):
    return np.ascontiguousarray(a.reshape(nk, 128, a.shape[1]).transpose(1, 0, 2))


def _col(par, name, arr, i=0):
    par[:arr.shape[0], PC[name] + i] = arr


def prep_p1(inp, l, x_cur):
    f = np.float32
    w_in = inp["w_in"][l]
    chunks = []
    for col0, n in P1COLS:
        if col0 == "krope":
            idx = [KROPE0 + j for j in range(64)] + [KROPE0 + (j + 32) % 64 for j in range(64)]
            W = w_in[:, idx]
        else:
            W = w_in[:, col0:col0 + 128]
        chunks.append(_fm(W, 16))
    w = np.stack(chunks)
    par = np.zeros((128, NPAR), f)
    par[:, PC["ng"]:PC["ng"] + 16] = inp["norm_g"][l].reshape(16, 128).T
    sh = inp["rw_shift"][l]
    for j in range(3):
        for rc in range(14):
            _col(par, "sh", sh[j, rc * 128:(rc + 1) * 128], j * 14 + rc)
    for d in range(2):
        for c in range(4):
            _col(par, "w0", inp["rw_w0"][l][d, c * 128:(c + 1) * 128], d * 4 + c)
            _col(par, "a0", inp["rw_a0"][l][d, c * 128:(c + 1) * 128], d * 4 + c)
    rk = inp["rw_r_k"][l].reshape(512)
    for c in range(4):
        _col(par, "kk", inp["rw_k_k"][l][c * 128:(c + 1) * 128], c)
        _col(par, "ka", inp["rw_k_a"][l][c * 128:(c + 1) * 128], c)
        _col(par, "rk", rk[c * 128:(c + 1) * 128], c)
    _col(par, "dqg", np.tile(inp["diff_qk_g"][l][0], 2))
    _col(par, "dkg", np.tile(inp["diff_qk_g"][l][1], 2))
    par[:, PC["qlg"]:PC["qlg"] + 3] = inp["mla_q_lat_g"][l].reshape(3, 128).T
    par[:, PC["kvg"]:PC["kvg"] + 2] = inp["mla_kv_lat_g"][l].reshape(2, 128).T
    par[:, PC["npg"]:PC["npg"] + 2] = inp["mla_nope_g"][l].T
    for gi in range(2):
        g = inp["mla_rope_g"][l][gi]
        _col(par, "rpg", g, gi)
        _col(par, "rpgs", np.concatenate([g[32:], g[:32]]), gi)
    par[:, PC["mqg"]:PC["mqg"] + 2] = inp["mem_qk_g"][l].T
    par[:, PC["mng"]:PC["mng"] + 16] = inp["mem_norm_g"][l].reshape(16, 128).T
    invf = (10000.0 ** (-np.arange(0, 64, 2, dtype=np.float32) / 64)).astype(f)
    _col(par, "invf", np.tile(invf, 2))
    _col(par, "sgn", np.concatenate([-np.ones(32, f), np.ones(32, f)]))
    wuq = inp["mla_w_uq"][l]
    sw_idx = [h * 192 + 128 + (j + 32) % 64 for h in range(4) for j in range(64)]
    wukv = inp["mla_w_ukv"][l].reshape(256, 4, 256)
    common = {
        "par": par, "w": w,
        "wup": np.ascontiguousarray(inp["rw_w_up"][l].reshape(128, 512)),
        "aup": np.ascontiguousarray(inp["rw_a_up"][l].reshape(128, 512)),
        "wuq": _fm(wuq, 3), "wuqs": _fm(wuq[:, sw_idx], 3),
        "wukvk": _fm(np.ascontiguousarray(wukv[:, :, :128]).reshape(256, 512), 2),
        "wukvv": _fm(np.ascontiguousarray(wukv[:, :, 128:]).reshape(256, 512), 2),
        "memT": np.ascontiguousarray(inp["mem"][0].reshape(256, 16, 128).transpose(2, 1, 0)),
        "wkv": np.stack([_fm(inp["mem_w_kv"][l][:, ci * 128:(ci + 1) * 128], 16) for ci in range(8)]),
    }
    S = x_cur.shape[0]
    xpad = np.concatenate([np.zeros((1, 2048), f), x_cur, np.zeros((1, 2048), f)], 0)
    posv = inp["positions"][0]
    maps = []
    for c in range(NCORES):
        xt = np.empty((NT1, 128, 16, TH), f)
        pp = np.zeros((NT1, TH), np.int32)
        for t in range(NT1):
            n0 = c * (NT1 * TT) + t * TT
            xe = np.concatenate([xpad[n0 + 1:n0 + 1 + TT], xpad[n0:n0 + 1], xpad[n0 + 1 + TT:n0 + 2 + TT]], 0)
            xt[t] = xe.reshape(TH, 16, 128).transpose(2, 1, 0)
            pp[t, :TT] = posv[n0:n0 + TT]
        m = dict(common)
        m["xT"] = xt
        m["pos"] = pp
        maps.append(m)
    return maps


_NC_CACHE = {}


def _run(name, builder, maps):
    if name not in _NC_CACHE:
        _NC_CACHE[name] = builder()
    res = run_bass_kernel_spmd(_NC_CACHE[name], maps, core_ids=list(range(NCORES)))
    return res.results


def _cat(results, key, axis):
    return np.concatenate([r[key] for r in results], axis=axis)


SEQ = 8192
SC_CH = 32
SC_NV = 5


def build_scan(T=SEQ):
    b = B()
    P = b.P
    bc = b.din("bc", [2, T, SC_NV * 64])
    vcol = b.din("vcol", [128, T])
    y = b.dout("y", [128, T])
    nch = T // SC_CH
    W = SC_NV * 64
    bcb = b.sb("bcb", [128, 2, SC_CH * W])
    vc = b.sb("vc", [128, T])
    yt = b.sb("yt", [128, T])
    S = b.sb("S", [128, 64])
    Sd = b.sb("Sd", [128, 64])
    sa = b.sb("sa", [128, 1])
    b.ring("ja", 2, [128, 64])
    b.ring("jb", 2, [128, 64])
    b.load(vc[:], vcol, ["vc"], "vc")
    b.memset("dve", S[:], 0.0, ["S"])

    def load(c):
        sl = c % 2
        for pr in range(2):
            src = bc[pr, c * SC_CH:(c + 1) * SC_CH, :].rearrange("t f -> (t f)").partition_broadcast(64)
            b.load(bcb[pr * 64:(pr + 1) * 64, sl, :], src, [("bcb", sl, pr)], ("bc", sl, pr))

    load(0)
    for c in range(nch):
        if c + 1 < nch:
            load(c + 1)
        sl = c % 2
        rd = [("bcb", sl, 0), ("bcb", sl, 1)]
        for s in range(SC_CH):
            t = c * SC_CH + s
            o = s * W
            kap = bcb[:, sl, o:o + 64]
            nb = bcb[:, sl, o + 64:o + 128]
            kd = bcb[:, sl, o + 128:o + 192]
            rt = bcb[:, sl, o + 192:o + 256]
            dec = bcb[:, sl, o + 256:o + 320]
            ja, jar = b.nx("ja")
            P.op("dve", lambda e, kap=kap, ja=ja: e.scalar_tensor_tensor(
                out=ja[:], in0=S[:], scalar=1.0, in1=kap, op0=ALU.mult, op1=ALU.mult, accum_out=sa[:]),
                reads=rd + ["S"], writes=[jar, "sa"])
            b.tt("dve", Sd[:], S[:], dec, ALU.mult, rd + ["S"], ["Sd"])
            b.stt(Sd[:], nb, sa[:, 0:1], Sd[:], ALU.mult, ALU.add, rd + ["Sd", "sa"], ["Sd"])
            b.stt(S[:], kd, vc[:, t:t + 1], Sd[:], ALU.mult, ALU.add, rd + ["Sd", "vc"], ["S"])
            jb, jbr = b.nx("jb")
            P.op("dve", lambda e, rt=rt, jb=jb, t=t: e.scalar_tensor_tensor(
                out=jb[:], in0=S[:], scalar=1.0, in1=rt, op0=ALU.mult, op1=ALU.mult, accum_out=yt[:, t:t + 1]),
                reads=rd + ["S"], writes=[jbr, "yt"])
    b.store(y, yt[:], ["yt"])
    return b.finish()


def prep_scan(o_rw, T=SEQ):
    maps = []
    for h in range(NCORES):
        sl = slice(h * 64, (h + 1) * 64)
        bc = np.empty((2, T, SC_NV * 64), np.float32)
        for d in range(2):
            parts = [o_rw[2][sl], o_rw[3 + 3 * d][sl], o_rw[4 + 3 * d][sl], o_rw[0][sl], o_rw[5 + 3 * d][sl]]
            a = np.concatenate([p.T for p in parts], axis=1)
            bc[d] = a if d == 0 else a[::-1]
        v = o_rw[1][sl]
        vcol = np.concatenate([v, v[:, ::-1]], 0)
        maps.append({"bc": bc, "vcol": np.ascontiguousarray(vcol)})
    return maps


def post_scan(results):
    yf = np.concatenate([r["y"][0:64] for r in results], 0)
    yb = np.concatenate([r["y"][64:128][:, ::-1] for r in results], 0)
    return np.ascontiguousarray(yf), np.ascontiguousarray(yb)


NQ = SEQ // 2
NKT = SEQ // 128


def _t5_breaks():
    nb, max_exact = 16, 8
    n = np.arange(0, 1024, dtype=np.int32)
    n_f = np.maximum(n, max_exact).astype(np.float32)
    large = max_exact + (np.log(n_f / np.float32(max_exact)) / np.float32(math.log(128 / max_exact))
                         * np.float32(nb - max_exact)).astype(np.int32)
    large = np.minimum(large, nb - 1)
    f = np.where(n < max_exact, n, large)
    rels = np.arange(-1023, 1024)
    bk = np.where(rels > 0, 16, 0) + f[np.abs(rels)]
    order = [int(bk[0])]
    breaks = []
    for i in range(1, len(rels)):
        if bk[i] != bk[i - 1]:
            order.append(int(bk[i]))
            breaks.append(int(rels[i]))
    return order, breaks


T5_ORDER, T5_BREAKS = _t5_breaks()
NBK = len(T5_ORDER)


def build_attn(kind):
    b = B()
    P = b.P
    diff = kind == "diff"
    qa_d = b.din("qa", [128, NQ])
    ka_d = b.din("ka", [128, SEQ])
    v_d = b.din("v", [128, NKT, 128])
    if diff:
        tab_d = b.din("tab", [1, NBK])
        lq_d = b.din("lq", [1, 256])
        cst_d = b.din("cst", [128, 2])
    else:
        qb_d = b.din("qb", [64, NQ])
        kb_d = b.din("kb", [64, SEQ])
    o_d = b.dout("o", [128, NQ])
    setup_consts(b)
    ka = b.sb("ka", [128, SEQ], BF16)
    qa = b.sb("qa", [128, NQ], BF16)
    vv = b.sb("vv", [128, NKT, 128], BF16)
    b.ring("stg", 2, [128, 2048])
    b.ring("t", 6, [128, 512])
    b.ring("pt", 6, [128, 512], BF16)
    b.ring("zacc", 2, [128, 512])
    b.ring("zacc2", 2, [128, 512])
    b.ring("o", 2, [128, 512])

    def load_cast(dst, src, Pn, n, res):
        for i in range(0, n, 2048):
            st, sr = b.nx("stg")
            b.load(st[0:Pn, :], src[:, i:i + 2048], [sr], sr)
            b.cp("pool", dst[0:Pn, i:i + 2048], st[0:Pn, :], [sr], [res])

    load_cast(ka, ka_d, 128, SEQ, "ka")
    load_cast(qa, qa_d, 128, NQ, "qa")
    load_cast(vv[:].rearrange("p a b -> p (a b)"), v_d.rearrange("p a b -> p (a b)"), 128, NKT * 128, "vv")
    if not diff:
        kb = b.sb("kb", [64, SEQ], BF16)
        qb = b.sb("qb", [64, NQ], BF16)
        load_cast(kb, kb_d, 64, SEQ, "kb")
        load_cast(qb, qb_d, 64, NQ, "qb")
    else:
        lq = b.sb("lq", [128, 256])
        cst = b.sb("cst", [128, 2])
        tab = b.sb("tab", [128, NBK])
        b.load(lq[:], lq_d[0, :].partition_broadcast(128), ["lq"], "lq")
        b.load(cst[:], cst_d, ["cst"], "cst")
        b.load(tab[:], tab_d[0, :].partition_broadcast(128), ["tab"], "tab")
        pr = b.sb("lpr", [128, 128])
        b.tt("dve", pr[:, 0:64], lq[:, 0:64], lq[:, 64:128], ALU.mult, ["lq"], ["lpr"])
        b.tt("dve", pr[:, 64:128], lq[:, 128:192], lq[:, 192:256], ALU.mult, ["lq"], ["lpr"])
        ls = b.sb("ls", [128, 4])
        P.op("dve", lambda e: e.tensor_reduce(out=ls[:, 0:1], in_=pr[:, 0:64], axis=AX.X, op=ALU.add), reads=["lpr"], writes=["ls"])
        P.op("dve", lambda e: e.tensor_reduce(out=ls[:, 1:2], in_=pr[:, 64:128], axis=AX.X, op=ALU.add), reads=["lpr"], writes=["ls"])
        b.act(ls[:, 0:2], ls[:, 0:2], AF.Exp, ["ls"], ["ls"])
        b.tt("dve", ls[:, 2:3], ls[:, 0:1], ls[:, 1:2], ALU.subtract, ["ls"], ["ls"])
        b.tt("dve", ls[:, 2:3], ls[:, 2:3], cst[:, 0:1], ALU.add, ["ls", "cst"], ["ls"])
        b.ts("dve", ls[:, 3:4], ls[:, 2:3], -1.0, None, ALU.mult, None, ["ls"], ["ls"])
        dl = b.sb("dl", [128, NBK])
        b.tt("dve", dl[:, 1:NBK], tab[:, 1:NBK], tab[:, 0:NBK - 1], ALU.subtract, ["tab"], ["dl"])
        SW = 1152
        reli = b.sb("reli", [128, SW], I32)
        relf = b.sb("relf", [128, SW])
        P.op("pool", lambda e: e.iota(reli[:], [[-1, SW]], base=512, channel_multiplier=1), writes=["reli"])
        b.cp("dve", relf[:], reli[:], ["reli"], ["relf"])
        strip = b.sb("strip", [128, SW])
        b.ring("stmp", 2, [128, SW])
        for k in range(1, NBK):
            tmp, tr = b.nx("stmp")
            b.ts("dve", tmp[:], relf[:], float(T5_BREAKS[k - 1]), dl[:, k:k + 1], ALU.is_ge, ALU.mult, ["relf", "dl"], [tr])
            if k == 1:
                b.ts("pool", strip[:], tmp[:], tab[:, 0:1], None, ALU.add, None, [tr, "tab"], ["strip"])
            else:
                b.tt("pool", strip[:], strip[:], tmp[:], ALU.add, [tr, "strip"], ["strip"])

    scale = (64 ** -0.5) if diff else (192 ** -0.5)
    b.psrot = [0, 1, 2, 3]
    for qt in range(NQ // 512):
        qs = slice(qt * 512, (qt + 1) * 512)
        res_list = []
        for s in range(2 if diff else 1):
            if diff:
                ps_ = slice(s * 64, (s + 1) * 64)
                qlist = [(qa[ps_, qs], "qa")]
                kts = lambda kt, ps_=ps_: [(ka[ps_, kt * 128:(kt + 1) * 128], "ka")]

                def bias_fn(kt, qt=qt):
                    dk = kt - 4 * qt
                    if dk < -1:
                        return ("const", tab[:, 0:1])
                    if dk > 4:
                        return ("const", tab[:, NBK - 1:NBK])
                    return ("tile", (strip[:, (4 - dk) * 128:(4 - dk) * 128 + 512], "strip"))
            else:
                qlist = [(qa[:, qs], "qa"), (qb[:, qs], "qb")]
                kts = lambda kt: [(ka[:, kt * 128:(kt + 1) * 128], "ka"), (kb[:, kt * 128:(kt + 1) * 128], "kb")]
            vts = lambda kt: (vv[:, kt, :], "vv")
            if diff:
                res_list.append(attn_core(b, qlist, kts, vts, NKT, scale, 128, bias_fn=bias_fn))
            else:
                res_list.append(attn_core(b, qlist, kts, vts, NKT, scale, 128))
        po, pz = res_list[0]
        rz, rzr = b.nx("t")
        b.recip(rz[:], b.ps[pz][:, :], [("ps", pz)], [rzr])
        o, orr = b.nx("o")
        b.tt("dve", o[:], b.ps[po][:, :], rz[:], ALU.mult, [("ps", po), rzr], [orr])
        if diff:
            po2, pz2 = res_list[1]
            rz2, rz2r = b.nx("t")
            b.recip(rz2[:], b.ps[pz2][:, :], [("ps", pz2)], [rz2r])
            o2, o2r = b.nx("t")
            b.tt("dve", o2[:], b.ps[po2][:, :], rz2[:], ALU.mult, [("ps", po2), rz2r], [o2r])
            b.stt(o[:], o2[:], ls[:, 3:4], o[:], ALU.mult, ALU.add, [o2r, "ls", orr], [orr])
        b.store(o_d[:, qs], o[:], [orr])
    return b.finish()


def _vtiles(v):
    return np.ascontiguousarray(v.reshape(-1, 128, 128).transpose(1, 0, 2))


def prep_attn_diff(inp, l, o_dq, o_dk, o_dv):
    lam_init = 0.8 - 0.6 * math.exp(-0.3 * l)
    maps = []
    for c in range(NCORES):
        h, half = c // 2, c % 2
        rows = slice(h * 128, (h + 1) * 128)
        q, k, v = o_dq[rows], o_dk[rows], o_dv[:, rows]
        tab = inp["rel_bias"][T5_ORDER, h]
        if half == 1:
            q, k, v, tab = q[:, ::-1], k[:, ::-1], v[::-1], tab[::-1]
        cst = np.zeros((128, 2), np.float32)
        cst[:, 0] = lam_init
        maps.append({"qa": np.ascontiguousarray(q[:, :NQ]), "ka": np.ascontiguousarray(k), "v": _vtiles(np.ascontiguousarray(v)),
                     "tab": np.ascontiguousarray(tab.reshape(1, NBK)).astype(np.float32),
                     "lq": np.ascontiguousarray(inp["diff_lambda"][l].reshape(1, 256)), "cst": cst})
    return maps


def post_attn_diff(results):
    out = np.empty((512, SEQ), np.float32)
    for c in range(NCORES):
        h, half = c // 2, c % 2
        o = results[c]["o"]
        if half == 0:
            out[h * 128:(h + 1) * 128, :NQ] = o
        else:
            out[h * 128:(h + 1) * 128, NQ:] = o[:, ::-1]
    return out


def prep_attn_mla(o_mqn, o_mqr, o_mkn, o_mkr, o_mv):
    maps = []
    for c in range(NCORES):
        h, half = c // 2, c % 2
        qs = slice(half * NQ, (half + 1) * NQ)
        maps.append({"qa": np.ascontiguousarray(o_mqn[h * 128:(h + 1) * 128, qs]),
                     "qb": np.ascontiguousarray(o_mqr[h * 64:(h + 1) * 64, qs]),
                     "ka": np.ascontiguousarray(o_mkn[h * 128:(h + 1) * 128]),
                     "kb": np.ascontiguousarray(o_mkr),
                     "v": _vtiles(np.ascontiguousarray(o_mv[:, h * 128:(h + 1) * 128]))})
    return maps


def post_attn_mla(results):
    out = np.empty((512, SEQ), np.float32)
    for c in range(NCORES):
        h, half = c // 2, c % 2
        out[h * 128:(h + 1) * 128, half * NQ:(half + 1) * NQ] = results[c]["o"]
    return out


PC3 = {}
_c = 0
for _n, _w in (("ng", 16), ("gng", 4), ("gnb", 4), ("subg", 1), ("lamf", 1)):
    PC3[_n] = _c
    _c += _w
NPAR3 = _c
NCH3 = 16 + 16 * 4 + 16


def build_p3():
    b = B()
    P = b.P
    xT = b.din("xT", [NT1, 128, 16, TT])
    par_d = b.din("par", [128, NPAR3])
    wA = b.din("wA", [NCH3, 128, 16, 128])
    wB = b.din("wB", [16, 128, 16, 128])
    br_d = b.din("br", [NT1, 128, 6, 4, TT])
    o_x = b.dout("o_x", [NT1, 128, 16, TT])
    setup_consts(b)
    par = b.sb("par", [128, NPAR3])
    b.load(par[:], par_d, ["par"], "par")
    pc = lambda n, i=0: par[:, PC3[n] + i:PC3[n] + i + 1]
    xz = b.sb("xz", [128, 16, TT])
    xs = xz
    zT = xz[:].rearrange("p a t -> p (a t)").bitcast(BF16).rearrange("p (n a t) -> p n a t", n=NT1, a=16)
    hT = b.sb("hT", [128, NT1, 16, TT], BF16)
    yg = b.sb("yg", [128, NT1, 16, TT], BF16)
    b.ring("brk", 2, [128, 6, TT])
    wstA = b.sb("wstA", [128, 2, 16, 128])
    wbfA = b.sb("wbfA", [128, 2, 16, 128], BF16)
    wstB = b.sb("wstB", [128, 1, 16, 128])
    wbfB = b.sb("wbfB", [128, 2, 16, 128], BF16)
    rsx = b.sb("rsx", [128, TT])
    zacc = b.sb("zacc", [128, NT1, TT])
    b.ring("t", 10, [128, TT])
    cntA = [0]

    def loadA(ci):
        sl = cntA[0] % 2
        cntA[0] += 1
        b.load(wstA[:, sl], wA[ci], [("wstA", sl)], ("wstA", sl))
        return sl

    ysrc = [0, 3, 4, 5]
    for tile in range(NT1):
        b.load(xs[:], xT[tile], ["xz"], "xz")
        pendA = loadA(0)
        pi = b.nps()
        for kc in range(16):
            sq, sqr = b.nx("t")
            b.act(sq[:], xs[:, kc, :], AF.Square, ["xz"], [sqr])
            b.mm(pi, 128, TT, b.ones[:], sq[:], kc == 0, kc == 15, [sqr, "ones"])
        ln, lnr = b.nx("t")
        b.act(ln[:], b.ps[pi][:, :], AF.Ln, [("ps", pi)], [lnr], bias=b.epsc[1e-6][:], scale=1.0 / 2048)
        b.act(rsx[:], ln[:], AF.Exp, [lnr], ["rsx"], scale=-0.5)
        for kc in range(16):
            b.stt(hT[:, tile, kc, :], xs[:, kc, :], pc("ng", kc), rsx[:], ALU.mult, ALU.mult, ["xz", "rsx", "par"], [("hT", tile)])
        for g in range(16):
            sl = pendA
            if g + 1 < 16:
                pendA = loadA(g + 1)
            b.wcast(wbfA, wstA, sl, "wstA", "wbfA")
            pi = b.nps()
            for kc in range(16):
                b.mm(pi, 128, TT, wbfA[:, sl, kc, :], hT[:, tile, kc, :], kc == 0, kc == 15, [("wbfA", sl), ("hT", tile)])
            b.act(yg[:, tile, g, :], b.ps[pi][:, :], AF.Silu, [("ps", pi)], [("yg", tile, g)])
        for kc in range(4):
            brk, brr = b.nx("brk")
            b.load(brk[:], br_d[tile, :, :, kc, :], [brr], brr)
            ys, ysr = b.nx("t")
            b.tt("pool", ys[:], brk[:, 0, :], brk[:, 1, :], ALU.add, [brr], [ysr])
            sq, sqr = b.nx("t")
            b.act(sq[:], ys[:], AF.Square, [ysr], [sqr])
            pm = b.nps()
            b.mm(pm, 128, TT, b.blk[:], ys[:], True, True, [ysr, "blk"])
            pe2 = b.nps()
            b.mm(pe2, 128, TT, b.blk[:], sq[:], True, True, [sqr, "blk"])
            mean, mr = b.nx("t")
            b.act(mean[:], b.ps[pm][:, :], AF.Copy, [("ps", pm)], [mr], scale=1.0 / 64)
            msq, msr = b.nx("t")
            b.act(msq[:], mean[:], AF.Square, [mr], [msr])
            var, vr = b.nx("t")
            b.stt(var[:], b.ps[pe2][:, :], 1.0 / 64, msq[:], ALU.mult, ALU.subtract, [("ps", pe2), msr], [vr])
            b.act(var[:], var[:], AF.Ln, [vr], [vr], bias=b.epsc[64e-5][:])
            b.act(var[:], var[:], AF.Exp, [vr], [vr], scale=-0.5)
            b.tt("pool", ys[:], ys[:], mean[:], ALU.subtract, [ysr, mr], [ysr])
            b.stt(ys[:], ys[:], pc("gng", kc), var[:], ALU.mult, ALU.mult, [ysr, "par", vr], [ysr])
            b.stt(ys[:], ys[:], pc("gnb", kc), brk[:, 2, :], ALU.add, ALU.add, [ysr, "par", brr], [ysr])
            b.tt("dve", yg[:, tile, 0 * 4 + kc, :], yg[:, tile, 0 * 4 + kc, :], ys[:], ALU.mult, [ysr, ("yg", tile, kc)], [("yg", tile, kc)])
            sq2, sq2r = b.nx("t")
            b.act(sq2[:], brk[:, 3, :], AF.Square, [brr], [sq2r])
            rs, rsr = fm_rstd(b, [(sq2[:], sq2r)], b.ones[:], 128, TT, 1.0 / 128, 1e-6, "ones")
            yb_, ybr_ = b.nx("t")
            b.stt(yb_[:], brk[:, 3, :], pc("subg"), rs[:], ALU.mult, ALU.mult, [brr, "par", rsr], [ybr_])
            b.stt(yg[:, tile, 4 + kc, :], yb_[:], pc("lamf"), yg[:, tile, 4 + kc, :], ALU.mult, ALU.mult,
                  [ybr_, "par", ("yg", tile, 4 + kc)], [("yg", tile, 4 + kc)])
            b.tt("pool", yg[:, tile, 8 + kc, :], yg[:, tile, 8 + kc, :], brk[:, 4, :], ALU.mult, [brr, ("yg", tile, 8 + kc)], [("yg", tile, 8 + kc)])
            b.tt("pool", yg[:, tile, 12 + kc, :], yg[:, tile, 12 + kc, :], brk[:, 5, :], ALU.mult, [brr, ("yg", tile, 12 + kc)], [("yg", tile, 12 + kc)])
    ygr = lambda t: [("yg", t, g) for g in range(16)]
    nxt = 16
    pendA = loadA(nxt)
    nxt += 1
    for oc in range(16):
        slB = oc % 2
        b.load(wstB[:, 0], wB[oc], [("wstB", 0)], ("wstB", 0))
        b.wcast(wbfB, wstB, 0, "wstB", "wbfB", dsl=slB)
        for bi in range(4):
            sl = pendA
            if nxt < NCH3:
                pendA = loadA(nxt)
                nxt += 1
            b.wcast(wbfA, wstA, sl, "wstA", "wbfA")
            for tile in range(NT1):
                pm = b.nps()
                for kc in range(16):
                    b.mm(pm, 128, TT, wbfA[:, sl, kc, :], hT[:, tile, kc, :], kc == 0, kc == 15, [("wbfA", sl), ("hT", tile)])
                pb = b.nps()
                for kc in range(4):
                    b.mm(pb, 128, TT, wbfB[:, slB, bi * 4 + kc, :], yg[:, tile, bi * 4 + kc, :], kc == 0, kc == 3,
                         [("wbfB", slB), ("yg", tile, bi * 4 + kc)])
                sg, sgr = b.nx("t")
                b.act(sg[:], b.ps[pm][:, :], AF.Sigmoid, [("ps", pm)], [sgr])
                if bi == 0:
                    b.tt("dve", zacc[:, tile, :], b.ps[pb][:, :], sg[:], ALU.mult, [("ps", pb), sgr], [("zacc", tile)])
                else:
                    tmp, tr = b.nx("t")
                    b.tt("dve", tmp[:], b.ps[pb][:, :], sg[:], ALU.mult, [("ps", pb), sgr], [tr])
                    if bi < 3:
                        b.tt("pool", zacc[:, tile, :], zacc[:, tile, :], tmp[:], ALU.add, [("zacc", tile), tr], [("zacc", tile)])
                    else:
                        b.tt("pool", zT[:, tile, oc, :], zacc[:, tile, :], tmp[:], ALU.add, [("zacc", tile), tr], ["xz"])
    for oc in range(16):
        sl = pendA
        if nxt < NCH3:
            pendA = loadA(nxt)
            nxt += 1
        b.wcast(wbfA, wstA, sl, "wstA", "wbfA")
        for tile in range(NT1):
            po = b.nps()
            for kc in range(16):
                b.mm(po, 128, TT, wbfA[:, sl, kc, :], zT[:, tile, kc, :], kc == 0, kc == 15, [("wbfA", sl), "xz"])
            xr, xrr = b.nx("t")
            b.load(xr[:], xT[tile, :, oc, :], [xrr], xrr)
            xo, xor_ = b.nx("t")
            b.tt("dve", xo[:], b.ps[po][:, :], xr[:], ALU.add, [("ps", po), xrr], [xor_])
            b.store(o_x[tile, :, oc, :], xo[:], [xor_])
    return b.finish()


GM0 = 1792 + 1536 + 384 + 256 + 64 + 512


def prep_p3(inp, l, x_cur, ysf, ysb, bonus, ybr, yc, yd):
    f = np.float32
    w_in = inp["w_in"][l]
    chunks = []
    for g in range(16):
        chunks.append(_fm(w_in[:, GM0 + g * 128:GM0 + (g + 1) * 128], 16))
    M0 = GM0 + 2048
    for oc in range(16):
        for bi in range(4):
            c0 = M0 + bi * 2048 + oc * 128
            chunks.append(_fm(w_in[:, c0:c0 + 128], 16))
    for oc in range(16):
        chunks.append(_fm(inp["w_out"][l][:, oc * 128:(oc + 1) * 128], 16))
    wA = np.stack(chunks)
    wb = inp["w_branch"][l].reshape(2048, 2048)
    wB = np.stack([_fm(wb[:, oc * 128:(oc + 1) * 128], 16) for oc in range(16)])
    par = np.zeros((128, NPAR3), f)
    par[:, PC3["ng"]:PC3["ng"] + 16] = inp["norm_g"][l].reshape(16, 128).T
    par[:, PC3["gng"]:PC3["gng"] + 4] = inp["rw_gn_g"][l].reshape(4, 128).T
    par[:, PC3["gnb"]:PC3["gnb"] + 4] = inp["rw_gn_b"][l].reshape(4, 128).T
    par[:, PC3["subg"]] = inp["diff_sub_g"][l]
    par[:, PC3["lamf"]] = 1.0 - (0.8 - 0.6 * math.exp(-0.3 * l))
    maps = []
    ntok = NT1 * TT
    for c in range(NCORES):
        xt = np.empty((NT1, 128, 16, TT), f)
        brr = np.empty((NT1, 128, 6, 4, TT), f)
        for t in range(NT1):
            n0 = c * ntok + t * TT
            xt[t] = x_cur[n0:n0 + TT].reshape(TT, 16, 128).transpose(2, 1, 0)
            for i, a in enumerate((ysf, ysb, bonus, ybr, yc, yd)):
                brr[t, :, i] = a[:, n0:n0 + TT].reshape(4, 128, TT).transpose(1, 0, 2)
        maps.append({"xT": xt, "par": par, "wA": wA, "wB": wB, "br": brr})
    return maps


def post_p3(results):
    outs = []
    for r in results:
        o = r["o_x"]
        outs.append(o.transpose(0, 3, 2, 1).reshape(NT1 * TT, 2048))
    return np.ascontiguousarray(np.concatenate(outs, 0))


_P1_AXIS = {"o_dv": 0, "o_mv": 0, "o_rw": 2}


def kernel(**inputs):
    inp = {k: np.asarray(v) for k, v in inputs.items()}
    x = np.ascontiguousarray(inp["x"][0], dtype=np.float32)
    for l in range(4):
        r1 = _run("p1", build_p1, prep_p1(inp, l, x))
        o = {k: _cat(r1, k, _P1_AXIS.get(k, 1)) for k in r1[0]}
        del r1
        rs = _run("scan2", build_scan2, prep_scan2(o["o_rw"]))
        ysf, ysb = post_scan2(rs)
        del rs
        yb = post_attn_diff(_run("diff", lambda: build_attn("diff"),
                                 prep_attn_diff(inp, l, o["o_dq"], o["o_dk"], o["o_dv"])))
        yc = post_attn_mla(_run("mla", lambda: build_attn("mla"),
                                prep_attn_mla(o["o_mqn"], o["o_mqr"], o["o_mkn"], o["o_mkr"], o["o_mv"])))
        r3 = _run("p3", build_p3, prep_p3(inp, l, x, ysf, ysb, o["o_bonus"], yb, yc, o["o_yd"]))
        x = post_p3(r3)
        del r3, o
    return x[None].astype(np.float32)


SB = 512
SG = 128
SC = 64


def build_scan2(T=SEQ, f32r=False):
    b = B()
    b.f32r = f32r
    P = b.P
    fm_d = b.din("fm", [64, 2, 5, T])
    v_d = b.din("v", [64, 2, T])
    cst_d = b.din("cst", [128, 4, 128])
    m01_d = b.din("m01", [64, 2 * SB])
    y_d = b.dout("y", [64, 2, T])
    nblk = T // SB
    NI = (SB // SG) * 2
    cst = b.sb("cst", [128, 4, 128])
    m01 = b.sb("m01", [64, 2 * SB])
    b.load(cst[:], cst_d, ["cst"], "cst")
    b.load(m01[:], m01_d, ["m01"], "m01")
    Ml, Mu, MuI, I_ = (cst[:, i, :] for i in range(4))
    fmb = b.sb("fmb", [64, 2, 5, SB])
    vb = b.sb("vb", [64, 2, SB])
    sc = {n: b.sb(n, [64, 2, SB]) for n in ("KT", "NB", "KD", "RT", "NB2", "KD2")}
    b.ring("e", 4, [64, 2, SB])
    gC = b.sb("gC", [64, 2, SB // SC])
    clend = b.sb("clend", [64, 2, SB // SC])
    Hs = b.sb("Hs", [64, 2, SB // SC + 1, 64])
    yb = b.sb("yb", [64, 2, SB])
    it_buf = []
    for i in range(NI):
        d = {}
        for n, shp in (("N0", [128, 128]), ("N1", [128, 128]), ("P0", [128, 128]), ("P1", [128, 128]),
                       ("X0", [128, 128]), ("X1", [128, 128]), ("AkT", [128, 128]), ("BkT", [128, 128]),
                       ("BnbT", [128, 128]), ("NBt", [128, 64]), ("KDt", [128, 64]), ("Vt", [128, 64]),
                       ("NB2t", [128, 64]), ("KD2t", [128, 64]),
                       ("WT", [64, 128]), ("U", [128, 64]), ("G1", [64, 2, 64]), ("G2", [64, 2, 64])):
            d[n] = b.sb(f"i{i}{n}", shp)
        it_buf.append(d)
    b.memset("dve", Hs[:, :, 0, :], 0.0, ["Hs"])

    def r_(i, n):
        return (f"i{i}", n)

    def bulk(blk):
        t0 = blk * SB
        b.load(fmb[:], fm_d[:, :, :, t0:t0 + SB], ["fmb"], "fmb")
        b.load(vb[:], v_d[:, :, t0:t0 + SB], ["vb"], "vb")
        lw = fmb[:, :, 4, :]
        cl, clr = b.nx("e")
        for p in range(2):
            P.op("dve", lambda e, p=p, cl=cl: e.tensor_tensor_scan(
                out=cl[:, p, :], data0=m01[:, 0:SB], data1=fmb[:, p, 4, :], initial=0.0, op0=ALU.mult, op1=ALU.add),
                reads=["fmb", "m01"], writes=[clr])
        for p in range(2):
            b.cp("pool", clend[:, p, :], cl[:, p, SC - 1:SB:SC], [clr], ["clend"])
        e1, e1r = b.nx("e")
        b.tt("pool", e1[:], cl[:], lw, ALU.subtract, [clr, "fmb"], [e1r])
        b.act(e1[:], e1[:], AF.Exp, [e1r], [e1r])
        b.tt("dve", sc["KT"][:], fmb[:, :, 0, :], e1[:], ALU.mult, ["fmb", e1r], ["KT"])
        e2, e2r = b.nx("e")
        b.act(e2[:], cl[:], AF.Exp, [clr], [e2r], scale=-1.0)
        b.tt("pool", sc["NB"][:], fmb[:, :, 1, :], e2[:], ALU.mult, ["fmb", e2r], ["NB"])
        b.tt("dve", sc["KD"][:], fmb[:, :, 2, :], e2[:], ALU.mult, ["fmb", e2r], ["KD"])
        e3, e3r = b.nx("e")
        b.act(e3[:], cl[:], AF.Exp, [clr], [e3r])
        b.tt("pool", sc["RT"][:], fmb[:, :, 3, :], e3[:], ALU.mult, ["fmb", e3r], ["RT"])
        for p in range(2):
            b.cp("pool", gC[:, p, :], e3[:, p, SC - 1:SB:SC], [e3r], ["gC"])
        e4, e4r = b.nx("e")
        for p in range(2):
            for c in range(SB // SC):
                b.act(e4[:, p, c * SC:(c + 1) * SC], cl[:, p, c * SC:(c + 1) * SC], AF.Exp, [clr, "clend"], [e4r],
                      bias=clend[:, p, c:c + 1], scale=-1.0)
        b.tt("dve", sc["NB2"][:], fmb[:, :, 1, :], e4[:], ALU.mult, ["fmb", e4r], ["NB2"])
        b.tt("pool", sc["KD2"][:], fmb[:, :, 2, :], e4[:], ALU.mult, ["fmb", e4r], ["KD2"])

    def evac_act(dst, pi, M, N, wr, scale=None):
        if scale is None:
            b.cp("act", dst, b.ps[pi][0:M, 0:N], [("ps", pi)], wr)
        else:
            b.act(dst, b.ps[pi][0:M, 0:N], AF.Copy, [("ps", pi), "gC"], wr, scale=scale)

    def transpose(pi, in_ap, K, M, rd):
        out = b.ps[pi][0:M, 0:K]
        P.op("pe", lambda e: e.transpose(out, in_ap, I_[0:K, 0:K]), reads=rd + ["cst"], writes=[("ps", pi)])

    def stage1(blk):
        for i in range(NI):
            g, p = i // 2, i % 2
            ts = slice(g * SG, (g + 1) * SG)
            bf = it_buf[i]
            KT, NB, KD, RT = (sc[n][:, p, ts] for n in ("KT", "NB", "KD", "RT"))
            for (la, ln), (ra, rn), msk, dst in (((KT, "KT"), (NB, "NB"), Ml, "N0"), ((NB, "NB"), (KT, "KT"), Mu, "P0"),
                                                 ((KD, "KD"), (KT, "KT"), Mu, "AkT"), ((KD, "KD"), (RT, "RT"), MuI, "BkT"),
                                                 ((NB, "NB"), (RT, "RT"), MuI, "BnbT")):
                pi = b.nps()
                b.mm(pi, 128, 128, la, ra, True, True, [ln, rn])
                b.tt("dve", bf[dst][:], b.ps[pi][:, 0:128], msk, ALU.mult, [("ps", pi), "cst"], [r_(i, dst)])
            for src, sn, dst, dcols in ((sc["KT"], "KT", "X0", slice(0, 64)), (sc["NB"], "NB", "NBt", slice(0, 64)),
                                        (sc["KD"], "KD", "KDt", slice(0, 64)), (vb, "vb", "Vt", slice(0, 64)),
                                        (sc["NB2"], "NB2", "NB2t", slice(0, 64)), (sc["KD2"], "KD2", "KD2t", slice(0, 64))):
                pi = b.nps()
                transpose(pi, src[:, p, ts], 64, 128, [sn])
                evac_act(bf[dst][:, dcols], pi, 128, 64, [r_(i, dst)])
            pi = b.nps()
            b.mm(pi, 128, 64, bf["AkT"][:], bf["Vt"][:], True, True, [r_(i, "AkT"), r_(i, "Vt")])
            evac_act(bf["X0"][:, 64:128], pi, 128, 64, [r_(i, "X0")])

    def stage2(blk):
        for it in range(6):
            cur, nxt = it % 2, (it + 1) % 2
            for i in range(NI):
                bf = it_buf[i]
                Nc, Pc, Xc = bf[f"N{cur}"], bf[f"P{cur}"], bf[f"X{cur}"]
                Nn, Pn, Xn = bf[f"N{nxt}"], bf[f"P{nxt}"], bf[f"X{nxt}"]
                pi = b.nps()
                b.mm(pi, 128, 128, Pc[:], Xc[:], True, True, [r_(i, f"P{cur}"), r_(i, f"X{cur}")])
                b.tt("dve", Xn[:], Xc[:], b.ps[pi][:, 0:128], ALU.add, [("ps", pi), r_(i, f"X{cur}")], [r_(i, f"X{nxt}")])
                if it < 5:
                    pi = b.nps()
                    b.mm(pi, 128, 128, Nc[:], Pc[:], True, True, [r_(i, f"N{cur}"), r_(i, f"P{cur}")])
                    evac_act(Pn[:], pi, 128, 128, [r_(i, f"P{nxt}")])
                if it < 4:
                    pi = b.nps()
                    b.mm(pi, 128, 128, Pc[:], Nc[:], True, True, [r_(i, f"N{cur}"), r_(i, f"P{cur}")])
                    evac_act(Nn[:], pi, 128, 128, [r_(i, f"N{nxt}")])

    def stage3(blk):
        for i in range(NI):
            bf = it_buf[i]
            X = bf["X0"]
            pi = b.nps()
            transpose(pi, X[:, 0:64], 128, 64, [r_(i, "X0")])
            evac_act(bf["WT"][:], pi, 64, 128, [r_(i, "WT")])
            for c in range(2):
                cs = slice(c * SC, (c + 1) * SC)
                pi = b.nps()
                b.mm(pi, 64, 64, X[cs, 0:64], bf["NB2t"][cs, :], True, True, [r_(i, "X0"), r_(i, "NB2t")])
                pdiag, pdr = b.nx("dg")
                g, p = i // 2, i % 2
                cg = g * 2 + c
                b.ts("pool", pdiag[:], I_[0:64, 0:64], gC[:, p, cg:cg + 1], None, ALU.mult, None, ["cst", "gC"], [pdr])
                b.tt("dve", bf["G1"][:, c, :], b.ps[pi][0:64, 0:64], pdiag[:], ALU.add, [("ps", pi), pdr], [r_(i, "G1")])
                pi = b.nps()
                b.mm(pi, 64, 64, bf["NB2t"][cs, :], X[cs, 64:128], True, False, [r_(i, "X0"), r_(i, "NB2t")])
                b.mm(pi, 64, 64, bf["KD2t"][cs, :], bf["Vt"][cs, :], False, True, [r_(i, "KD2t"), r_(i, "Vt")])
                evac_act(bf["G2"][:, c, :], pi, 64, 64, [r_(i, "G2")])

    def stage4(blk):
        nchunk = SB // SC
        for cg in range(nchunk):
            for p in range(2):
                i = (cg // 2) * 2 + p
                c = cg % 2
                bf = it_buf[i]
                pi = b.nps()
                b.mm(pi, 64, 64, bf["G1"][:, c, :], Hs[:, p, cg, :], True, False, [r_(i, "G1"), ("Hs", p)])
                b.mm(pi, 64, 64, I_[0:64, 0:64], bf["G2"][:, c, :], False, True, ["cst", r_(i, "G2")])
                b.cp("act", Hs[:, p, cg + 1, :], b.ps[pi][0:64, 0:64], [("ps", pi)], [("Hs", p)])

    def stage5(blk):
        t0 = blk * SB
        for i in range(NI):
            g, p = i // 2, i % 2
            bf = it_buf[i]
            X = bf["X0"]
            for c in range(2):
                cs = slice(c * SC, (c + 1) * SC)
                cg = g * 2 + c
                pi = b.nps()
                b.mm(pi, 128, 64, bf["WT"][:], Hs[:, p, cg, :], True, True, [r_(i, "WT"), ("Hs", p)])
                b.tt("dve", bf["U"][cs, :], b.ps[pi][cs, 0:64], X[cs, 64:128], ALU.add, [("ps", pi), r_(i, "X0")], [r_(i, "U")])
            pi = b.nps()
            b.mm(pi, 64, 128, bf["Vt"][:], bf["BkT"][:], True, False, [r_(i, "Vt"), r_(i, "BkT")])
            b.mm(pi, 64, 128, bf["U"][:], bf["BnbT"][:], False, False, [r_(i, "U"), r_(i, "BnbT")])
            for c in range(2):
                cg = g * 2 + c
                out = b.ps[pi][0:64, c * SC:(c + 1) * SC]
                lhsT = Hs[:, p, cg, :]
                rhs = sc["RT"][:, p, g * SG + c * SC:g * SG + (c + 1) * SC]
                P.op("pe", lambda e, out=out, lhsT=lhsT, rhs=rhs, c=c: e.matmul(out, lhsT, rhs, start=False, stop=(c == 1)),
                     reads=[("Hs", p), "RT"], writes=[("ps", pi)])
            b.cp("act", yb[:, p, g * SG:(g + 1) * SG], b.ps[pi][0:64, 0:128], [("ps", pi)], ["yb"])
        b.store(y_d[:, :, t0:t0 + SB], yb[:], ["yb"])
        if blk + 1 < nblk:
            b.cp("pool", Hs[:, :, 0, :], Hs[:, :, SB // SC, :], [("Hs", 0), ("Hs", 1)], [("Hs", 0), ("Hs", 1)])

    b.ring("dg", 4, [64, 64])
    for blk in range(nblk):
        bulk(blk)
        stage1(blk)
        stage2(blk)
        stage3(blk)
        stage4(blk)
        stage5(blk)
    return b.finish()


def _scan2_consts():
    G, C = SG, SC
    Ml = np.zeros((G, G), np.float32)
    for t in range(G):
        for s in range(G):
            if t // C == s // C and s < t:
                Ml[t, s] = 1
    cst = np.stack([Ml, Ml.T, Ml.T + np.eye(G, dtype=np.float32), np.eye(G, dtype=np.float32)], 1)
    m01 = np.ones((64, 2 * SB), np.float32)
    m01[:, ::C] = 0
    return np.ascontiguousarray(cst), m01


def prep_scan2(o_rw, T=SEQ):
    cst, m01 = _scan2_consts()
    maps = []
    for h in range(NCORES):
        sl = slice(h * 64, (h + 1) * 64)
        fm = np.empty((64, 2, 5, T), np.float32)
        v = np.empty((64, 2, T), np.float32)
        for d in range(2):
            for k, a in enumerate((o_rw[2][sl], o_rw[3 + 3 * d][sl], o_rw[4 + 3 * d][sl], o_rw[0][sl], o_rw[5 + 3 * d][sl])):
                fm[:, d, k] = a if d == 0 else a[:, ::-1]
            v[:, d] = o_rw[1][sl] if d == 0 else o_rw[1][sl][:, ::-1]
        maps.append({"fm": fm, "v": v, "cst": cst, "m01": m01})
    return maps


def post_scan2(results):
    yf = np.concatenate([r["y"][:, 0] for r in results], 0)
    yb = np.concatenate([r["y"][:, 1][:, ::-1] for r in results], 0)
    return np.ascontiguousarray(yf), np.ascontiguousarray(yb)
```

```python
import math
from contextlib import ExitStack
import numpy as np
import concourse.bass as bass
import concourse.mybir as mybir
from concourse.bass_utils import run_bass_kernel_spmd

F32 = mybir.dt.float32
BF16 = mybir.dt.bfloat16
F32R = mybir.dt.float32r
I32 = mybir.dt.int32
ALU = mybir.AluOpType
AF = mybir.ActivationFunctionType
AX = mybir.AxisListType
ENGS = ("pe", "act", "dve", "pool", "sp")
NCORES = 8


class _Op:
    __slots__ = ("eng", "fn", "waits", "signal", "dma_key", "idx", "sigval")

    def __init__(self, eng, fn, dma_key):
        self.eng = eng
        self.fn = fn
        self.waits = []
        self.signal = False
        self.dma_key = dma_key
        self.idx = None
        self.sigval = None


class _Res:
    __slots__ = ("w", "r")

    def __init__(self):
        self.w = None
        self.r = []


class Prog:
    def __init__(self, nc):
        self.nc = nc
        self.ops = {e: [] for e in ENGS}
        self.res = {}
        self.dma_cnt = {}
        self.dma_last = {}
        self.waited = {e: {} for e in ENGS}

    def _need(self, op, tok, isd):
        if tok is None:
            return
        kind, src, val = tok
        if kind == "e" and src == op.eng and not isd and src == "pe":
            return
        w = self.waited[op.eng]
        k = (kind, src)
        if w.get(k, -1) >= val:
            return
        w[k] = val
        op.waits.append(tok)
        if kind == "e":
            self.ops[src][val].signal = True

    def op(self, eng, fn, reads=(), writes=(), dma=None):
        o = _Op(eng, fn, dma)
        o.idx = len(self.ops[eng])
        isd = dma is not None
        if isd:
            n = self.dma_cnt.get(dma, 0) + 1
            self.dma_cnt[dma] = n
            self._need(o, self.dma_last.get(dma), True)
            tok = ("d", dma, n)
            self.dma_last[dma] = tok
        else:
            tok = ("e", eng, o.idx)
        for r in reads:
            st = self.res.setdefault(r, _Res())
            self._need(o, st.w, isd)
        for r in writes:
            st = self.res.setdefault(r, _Res())
            self._need(o, st.w, isd)
            for t in st.r:
                self._need(o, t, isd)
        for r in reads:
            st = self.res[r]
            st.r.append(tok)
            if len(st.r) > 48:
                st.r = st.r[-48:]
        for r in writes:
            st = self.res[r]
            st.w = tok
            st.r = []
        self.ops[eng].append(o)
        return tok

    def wait_tokens(self, eng, toks):
        o = _Op(eng, None, None)
        o.idx = len(self.ops[eng])
        for t in toks:
            self._need(o, t, True)
        self.ops[eng].append(o)

    def emit(self):
        nc = self.nc
        esem = {e: nc.alloc_semaphore(name=f"s_{e}") for e in ENGS}
        dsem = {k: nc.alloc_semaphore(name=f"d_{i}") for i, k in enumerate(self.dma_cnt)}
        for e in ENGS:
            c = 0
            for o in self.ops[e]:
                if o.signal:
                    c += 1
                    o.sigval = c
        ops = self.ops

        def body(e):
            def f(eng):
                for o in ops[e]:
                    for kind, src, val in o.waits:
                        if kind == "e":
                            eng.wait_ge(esem[src], ops[src][val].sigval)
                        else:
                            eng.wait_ge(dsem[src], 16 * val)
                    if o.fn is None:
                        continue
                    inst = o.fn(eng)
                    if o.dma_key is not None:
                        inst.then_inc(dsem[o.dma_key], 16)
                    elif o.signal:
                        inst.then_inc(esem[e], 1)
            return f

        with nc.Block() as block:
            block.tensor(body("pe"))
            block.scalar(body("act"))
            block.vector(body("dve"))
            block.gpsimd(body("pool"))
            block.sync(body("sp"))


class B:
    def __init__(self):
        self.nc = bass.Bass("TRN2", target_bir_lowering=False)
        self.P = Prog(self.nc)
        self.es = ExitStack()
        self.ps = [self.es.enter_context(self.nc.psum_tensor(f"ps{i}", [128, 512], F32)) for i in range(8)]
        self.psi = 0
        self.psrot = list(range(8))
        self.rings = {}
        self.outtoks = []
        self.ndq = 0
        self.attn_banks = [(4, 5), (6, 7)]
        self.zmod = 3
        self.attn_par = 0

    def din(self, name, shape, dt=F32):
        return self.nc.dram_tensor(name, list(shape), dt, kind="ExternalInput").ap()

    def dout(self, name, shape, dt=F32):
        return self.nc.dram_tensor(name, list(shape), dt, kind="ExternalOutput").ap()

    def sb(self, name, shape, dt=F32):
        return self.es.enter_context(self.nc.sbuf_tensor("s_" + name, list(shape), dt))

    def nps(self):
        rot = self.psrot
        i = rot[self.psi % len(rot)]
        self.psi += 1
        return i

    def ring(self, name, n, shape, dt=F32):
        self.rings[name] = [[self.sb(f"{name}{i}", shape, dt) for i in range(n)], 0]

    def nx(self, name):
        r = self.rings[name]
        i = r[1]
        r[1] = (i + 1) % len(r[0])
        return r[0][i], (name, i)

    def mm(self, pi, M, N, lhsT, rhs, st, sp, rd, po=0):
        out = self.ps[pi][po:po + M, 0:N]
        if getattr(self, "f32r", False) and lhsT.dtype == F32 and rhs.dtype == F32:
            lhsT = lhsT.bitcast(F32R)
            rhs = rhs.bitcast(F32R)
        self.P.op("pe", lambda e: e.matmul(out, lhsT, rhs, start=st, stop=sp), reads=rd, writes=[("ps", pi)])

    def act(self, out, in_, func, rd, wr, bias=None, scale=None):
        kw = {}
        if bias is not None:
            kw["bias"] = bias
        if scale is not None:
            kw["scale"] = scale
        self.P.op("act", lambda e: e.activation(out=out, in_=in_, func=func, **kw), reads=rd, writes=wr)

    def stt(self, out, in0, scalar, in1, op0, op1, rd, wr):
        self.P.op("dve", lambda e: e.scalar_tensor_tensor(out=out, in0=in0, scalar=scalar, in1=in1,
                                                          op0=op0, op1=op1), reads=rd, writes=wr)

    def tt(self, eng, out, in0, in1, op, rd, wr):
        self.P.op(eng, lambda e: e.tensor_tensor(out=out, in0=in0, in1=in1, op=op), reads=rd, writes=wr)

    def ts(self, eng, out, in0, s1, s2, op0, op1, rd, wr):
        if op1 is None:
            self.P.op(eng, lambda e: e.tensor_scalar(out=out, in0=in0, scalar1=s1, scalar2=None, op0=op0),
                      reads=rd, writes=wr)
        else:
            self.P.op(eng, lambda e: e.tensor_scalar(out=out, in0=in0, scalar1=s1, scalar2=s2, op0=op0, op1=op1),
                      reads=rd, writes=wr)

    def cp(self, eng, out, in_, rd, wr):
        if eng == "act":
            self.P.op("act", lambda e: e.copy(out=out, in_=in_), reads=rd, writes=wr)
        else:
            self.P.op(eng, lambda e: e.tensor_copy(out=out, in_=in_), reads=rd, writes=wr)

    def wcast(self, wbf, wst, sl, rn, wn, dsl=None):
        dsl = sl if dsl is None else dsl
        self.cp("dve", wbf[:, dsl, 0:8], wst[:, sl, 0:8], [(rn, sl)], [(wn, dsl)])
        self.cp("act", wbf[:, dsl, 8:16], wst[:, sl, 8:16], [(rn, sl)], [(wn, dsl)])

    def recip(self, out, in_, rd, wr):
        self.P.op("dve", lambda e: e.reciprocal(out=out, in_=in_), reads=rd, writes=wr)

    def memset(self, eng, ap, val, wr):
        self.P.op(eng, lambda e: e.memset(ap, val), writes=wr)

    def load(self, out, in_, wr, key, q="sp"):
        return self.P.op(q, lambda e: e.dma_start(out=out, in_=in_), writes=wr, dma=key)

    def store(self, out, in_, rd, q="pool"):
        self.ndq += 1
        key = ("st", self.ndq % 6)
        t = self.P.op(q, lambda e: e.dma_start(out=out, in_=in_), reads=rd, dma=key)
        self.outtoks.append(t)

    def finish(self):
        last = {}
        for t in self.outtoks:
            last[t[1]] = t
        self.P.wait_tokens("pool", list(last.values()))
        self.P.emit()
        self.es.close()
        return self.nc


def fm_rstd(b, sq_list, ones_ap, Pn, N, inv_n, eps, consts_res):
    pi = b.nps()
    for i, (ap, res) in enumerate(sq_list):
        b.mm(pi, Pn, N, ones_ap, ap, i == 0, i == len(sq_list) - 1, [res, consts_res])
    ln, lnr = b.nx("t")
    b.act(ln[0:Pn, 0:N], b.ps[pi][0:Pn, 0:N], AF.Ln, [("ps", pi)], [lnr], bias=b.epsc[eps][0:Pn, :], scale=inv_n)
    rs, rsr = b.nx("t")
    b.act(rs[0:Pn, 0:N], ln[0:Pn, 0:N], AF.Exp, [lnr], [rsr], scale=-0.5)
    return rs, rsr


def setup_consts(b):
    b.ones = b.sb("ones", [128, 128])
    b.blk = b.sb("blk", [128, 128])
    b.memset("pool", b.ones[:], 1.0, ["ones"])
    b.memset("pool", b.blk[:], 0.0, ["blk"])
    b.memset("pool", b.blk[0:64, 0:64], 1.0, ["blk"])
    b.memset("pool", b.blk[64:128, 64:128], 1.0, ["blk"])
    b.onesb = b.sb("onesb", [128, 128], BF16)
    b.memset("pool", b.onesb[:], 1.0, ["onesb"])
    b.epsc = {}
    for i, v in enumerate((1e-6, 64e-5)):
        t = b.sb(f"epsc{i}", [128, 1])
        b.memset("pool", t[:], v, [f"epsc{i}"])
        b.epsc[v] = t
        b.P.res


def attn_core(b, qlist, kts, vts, nkt, scale, out_M, bias_fn=None, tag="a"):
    po, pz = b.attn_banks[b.attn_par]
    b.attn_par ^= 1
    acc, accr = b.nx("zacc")
    acc2, acc2r = b.nx("zacc2")
    pend = []
    zq = []
    acc_init = [False]
    LOOK = 2
    ZMOD = b.zmod if nkt > 2 else 2

    def pv(kt, pt, ptr):
        va, vr = vts(kt)
        b.mm(po, out_M, 512, va, pt[:], kt == 0, kt == nkt - 1, [vr, ptr])
        if kt % ZMOD == 0:
            b.mm(pz, out_M, 512, b.onesb[:, 0:out_M], pt[:], kt == 0, (ZMOD == 1 and kt == nkt - 1), ["onesb", ptr])

    for kt in range(nkt):
        pi = b.nps()
        ks = kts(kt)
        for i, ((qa, qr), (ka, kr)) in enumerate(zip(qlist, ks)):
            b.mm(pi, 128, 512, ka, qa, i == 0, i == len(qlist) - 1, [qr, kr])
        if len(pend) >= LOOK:
            pv(*pend.pop(0))
        pt, ptr = b.nx("pt")
        bf = bias_fn(kt) if bias_fn is not None else None
        if bf is not None:
            kind, val = bf
            if kind == "const":
                b.act(pt[:], b.ps[pi][:, :], AF.Exp, [("ps", pi), "tab"], [ptr], bias=val, scale=scale)
            else:
                tmp, tr = b.nx("t")
                bap, bres = val
                b.stt(tmp[:], b.ps[pi][:, :], scale, bap, ALU.mult, ALU.add, [("ps", pi), bres], [tr])
                b.act(pt[:], tmp[:], AF.Exp, [tr], [ptr])
        else:
            b.act(pt[:], b.ps[pi][:, :], AF.Exp, [("ps", pi)], [ptr], scale=scale)
        if kt % ZMOD == 0:
            zq.append((kt, pt, ptr))
        elif not acc_init[0]:
            b.cp("dve", acc2[:], pt[:], [ptr], [acc2r])
            acc_init[0] = True
        else:
            b.tt("dve", acc2[:], acc2[:], pt[:], ALU.add, [ptr, acc2r], [acc2r])
        pend.append((kt, pt, ptr))
    for pp in pend:
        pv(*pp)
    if ZMOD > 1:
        b.mm(pz, out_M, 512, b.ones[:, 0:out_M], acc2[:], False, True, ["ones", acc2r])
    return po, pz


TT = 512
TH = TT + 2
NT1 = 2
PC = {}
_c = 0
for _n, _w in (("ng", 16), ("sh", 42), ("w0", 8), ("a0", 8), ("kk", 4), ("ka", 4), ("rk", 4), ("dqg", 1), ("dkg", 1),
               ("qlg", 3), ("kvg", 2), ("npg", 2), ("rpg", 2), ("rpgs", 2), ("mqg", 2), ("mng", 16), ("invf", 1),
               ("sgn", 1), ("omka", 4)):
    PC[_n] = _c
    _c += _w
NPAR = _c
RW0 = 0
def _p1_cols():
    cols = []
    cols += [(RW0 + 1536, 128), (RW0 + 1664, 128)]
    for c in range(4):
        cols += [(c * 128, 128), (512 + c * 128, 128), (1024 + c * 128, 128)]
    o = 1792
    cols += [(o + i * 128, 128) for i in range(4)]
    cols += [(o + 512 + i * 128, 128) for i in range(4)]
    o2 = 1792 + 1536
    cols += [(o2 + i * 128, 128) for i in range(3)]
    cols += [(o2 + 384 + i * 128, 128) for i in range(2)]
    cols += [("krope", 128)]
    o3 = o2 + 384 + 256 + 64
    cols += [(o3 + i * 128, 128) for i in range(4)]
    cols += [(1792 + 1024 + i * 128, 128) for i in range(4)]
    return cols
P1COLS = _p1_cols()
NCH1 = len(P1COLS)
KROPE0 = 1792 + 1536 + 384 + 256


def build_p1():
    b = B()
    P = b.P
    xT = b.din("xT", [NT1, 128, 16, TH])
    pos = b.din("pos", [NT1, TH], I32)
    par_d = b.din("par", [128, NPAR])
    w = b.din("w", [NCH1, 128, 16, 128])
    wup_d = b.din("wup", [128, 512])
    aup_d = b.din("aup", [128, 512])
    wuq_d = b.din("wuq", [128, 3, 768])
    wuqs_d = b.din("wuqs", [128, 3, 256])
    wukvk_d = b.din("wukvk", [128, 2, 512])
    wukvv_d = b.din("wukvv", [128, 2, 512])
    memT_d = b.din("memT", [128, 16, 256])
    wkv_d = b.din("wkv", [8, 128, 16, 128])
    o_rw = b.dout("o_rw", [9, 512, NT1 * TT])
    o_bonus = b.dout("o_bonus", [512, NT1 * TT])
    o_dq = b.dout("o_dq", [512, NT1 * TT])
    o_dk = b.dout("o_dk", [512, NT1 * TT])
    o_dv = b.dout("o_dv", [NT1 * TT, 512])
    o_mqn = b.dout("o_mqn", [512, NT1 * TT])
    o_mqr = b.dout("o_mqr", [256, NT1 * TT])
    o_mkn = b.dout("o_mkn", [512, NT1 * TT])
    o_mkr = b.dout("o_mkr", [64, NT1 * TT])
    o_mv = b.dout("o_mv", [NT1 * TT, 512])
    o_yd = b.dout("o_yd", [512, NT1 * TT])

    setup_consts(b)
    par = b.sb("par", [128, NPAR])
    b.load(par[:], par_d, ["par"], "par")
    pc = lambda n, i=0: par[:, PC[n] + i:PC[n] + i + 1]
    b.ts("dve", par[:, PC["omka"]:PC["omka"] + 4], par[:, PC["ka"]:PC["ka"] + 4], -1.0, 1.0, ALU.mult, ALU.add,
         ["par"], ["par"])
    xs = b.sb("xs", [128, 16, TH])
    hT = b.sb("hT", [128, 16, TH], BF16)
    wst = b.sb("wst", [128, 2, 16, 128])
    wbf = b.sb("wbf", [128, 2, 16, 128], BF16)
    stg = b.sb("stg", [128, 3072])
    wup = b.sb("wup_s", [128, 512])
    aup = b.sb("aup_s", [128, 512])
    wuq = b.sb("wuq_s", [128, 3, 768], BF16)
    wuqs = b.sb("wuqs_s", [128, 3, 256], BF16)
    wukvk = b.sb("wukvk_s", [128, 2, 512], BF16)
    wukvv = b.sb("wukvv_s", [128, 2, 512], BF16)
    memn = b.sb("memn", [128, 16, 256], BF16)
    kmem = b.sb("kmem", [128, 4, 256], BF16)
    vmem = b.sb("vmem", [128, 2, 512], BF16)
    b.ring("t", 14, [128, TT])
    b.ring("th", 3, [128, TH])
    b.ring("rkv", 4, [128, TT])
    b.ring("bf", 4, [128, TT], BF16)
    b.ring("pt", 3, [128, TT], BF16)
    b.ring("zacc", 1, [128, TT])
    b.ring("zacc2", 1, [128, TT])
    twd = b.sb("twd", [128, TT])
    adl = b.sb("adl", [128, TT])
    ropC = b.sb("ropC", [64, TT])
    ropS = b.sb("ropS", [64, TT])
    rsx = b.sb("rsx", [128, TH])
    ql = b.sb("ql", [128, 3, TT])
    qln = b.sb("qln", [128, 3, TT], BF16)
    kvl = b.sb("kvl", [128, 2, TT])
    kvn = b.sb("kvn", [128, 2, TT], BF16)
    posi = b.sb("posi", [64, TH], I32)

    b.load(wup[:], wup_d, ["wup"], "wl0")
    b.load(aup[:], aup_d, ["aup"], "wl1")
    for dst, src, n, nm in ((wuq, wuq_d, 3 * 768, "wuq"), (wuqs, wuqs_d, 3 * 256, "wuqs"),
                            (wukvk, wukvk_d, 1024, "wukvk"), (wukvv, wukvv_d, 1024, "wukvv")):
        b.load(stg[:, 0:n], src.rearrange("p a b -> p (a b)"), ["stg"], "stg")
        b.cp("dve", dst[:].rearrange("p a b -> p (a b)"), stg[:, 0:n], ["stg"], [nm])

    def rmsnorm_cols(src_tile, nkc, N, gname, dst_tile, res_src, res_dst, inv_n):
        pi = b.nps()
        pih = b.nps() if N > 512 else None
        for kc in range(nkc):
            sq, sqr = b.nx("th")
            b.act(sq[:, 0:N], src_tile[:, kc, 0:N], AF.Square, [res_src], [sqr])
            b.mm(pi, 128, min(N, 512), b.ones[:], sq[:, 0:min(N, 512)], kc == 0, kc == nkc - 1, [sqr, "ones"])
            if pih is not None:
                b.mm(pih, 128, N - 512, b.ones[:], sq[:, 512:N], kc == 0, kc == nkc - 1, [sqr, "ones"])
        ln, lnr = b.nx("th")
        b.act(ln[:, 0:min(N, 512)], b.ps[pi][:, 0:min(N, 512)], AF.Ln, [("ps", pi)], [lnr],
              bias=b.epsc[1e-6][:], scale=inv_n)
        if pih is not None:
            b.act(ln[:, 512:N], b.ps[pih][:, 0:N - 512], AF.Ln, [("ps", pih)], [lnr], bias=b.epsc[1e-6][:], scale=inv_n)
        b.act(rsx[:, 0:N], ln[:, 0:N], AF.Exp, [lnr], ["rsx"], scale=-0.5)
        for kc in range(nkc):
            b.stt(dst_tile[:, kc, 0:N], src_tile[:, kc, 0:N], pc(gname, kc), rsx[:, 0:N], ALU.mult, ALU.mult,
                  [res_src, "rsx", "par"], [res_dst])

    b.load(xs[:, :, 0:256], memT_d, ["xs"], "xs")
    rmsnorm_cols(xs, 16, 256, "mng", memn, "xs", "memn", 1.0 / 2048)
    for ci in range(8):
        sl = ci % 2
        b.load(wst[:, sl], wkv_d[ci], [("wst", sl)], ("wst", sl))
        b.wcast(wbf, wst, sl, "wst", "wbf")
        if ci < 4:
            pi = b.nps()
            for kc in range(16):
                b.mm(pi, 128, 256, wbf[:, sl, kc, :], memn[:, kc, :], kc == 0, kc == 15, [("wbf", sl), "memn"])
            tq, tqr = b.nx("t")
            b.cp("act", tq[:, 0:256], b.ps[pi][:, 0:256], [("ps", pi)], [tqr])
            sq, sqr = b.nx("t")
            b.act(sq[:, 0:256], b.ps[pi][:, 0:256], AF.Square, [("ps", pi)], [sqr])
            rs, rsr = fm_rstd(b, [(sq[:, 0:256], sqr)], b.ones[:], 128, 256, 1.0 / 128, 1e-6, "ones")
            b.stt(kmem[:, ci, :], tq[:, 0:256], pc("mqg", 1), rs[:, 0:256], ALU.mult, ALU.mult, [tqr, rsr, "par"], ["kmem"])
        else:
            for tb in range(2):
                pi = b.nps()
                for kc in range(16):
                    b.mm(pi, 128, 128, memn[:, kc, tb * 128:(tb + 1) * 128], wbf[:, sl, kc, :], kc == 0, kc == 15,
                         [("wbf", sl), "memn"])
                b.cp("act", vmem[:, tb, (ci - 4) * 128:(ci - 3) * 128], b.ps[pi][:, 0:128], [("ps", pi)], ["vmem"])

    def wload(tile, ci):
        sl = (tile * NCH1 + ci) % 2
        b.load(wst[:, sl], w[ci], [("wst", sl)], ("wst", sl))

    def shift(pi, pih, rc, dst, dres):
        u, ur = b.nx("th")
        b.cp("act", u[:, 0:TT], b.ps[pi][:, :], [("ps", pi)], [ur])
        b.cp("act", u[:, TT:TH], b.ps[pih][:, 0:2], [("ps", pih)], [ur])
        s0 = pc("sh", 0 * 14 + rc); s1 = pc("sh", 1 * 14 + rc); s2 = pc("sh", 2 * 14 + rc)
        b.ts("dve", dst[:, :], u[:, 0:TT], s1, None, ALU.mult, None, [ur, "par"], [dres])
        b.stt(dst[:, 1:TT], u[:, 0:TT - 1], s0, dst[:, 1:TT], ALU.mult, ALU.add, [ur, "par", dres], [dres])
        b.stt(dst[:, 0:1], u[:, TT:TT + 1], s0, dst[:, 0:1], ALU.mult, ALU.add, [ur, "par", dres], [dres])
        b.stt(dst[:, 0:TT - 1], u[:, 1:TT], s2, dst[:, 0:TT - 1], ALU.mult, ALU.add, [ur, "par", dres], [dres])
        b.stt(dst[:, TT - 1:TT], u[:, TT + 1:TT + 2], s2, dst[:, TT - 1:TT], ALU.mult, ALU.add, [ur, "par", dres], [dres])

    def head_norm(pi, Pn, ones_ap, inv_n, gcol, dst_ap, dres, N=TT):
        tq, tqr = b.nx("t")
        b.cp("act", tq[0:Pn, 0:N], b.ps[pi][0:Pn, 0:N], [("ps", pi)], [tqr])
        sq, sqr = b.nx("t")
        b.act(sq[0:Pn, 0:N], b.ps[pi][0:Pn, 0:N], AF.Square, [("ps", pi)], [sqr])
        rs, rsr = fm_rstd(b, [(sq[0:Pn, 0:N], sqr)], ones_ap, Pn, N, inv_n, 1e-6, "ones")
        b.stt(dst_ap, tq[0:Pn, 0:N], gcol, rs[0:Pn, 0:N], ALU.mult, ALU.mult, [tqr, rsr, "par"], [dres])
        return tq, tqr, rs, rsr

    for tile in range(NT1):
        t0 = tile * TT
        b.load(xs[:], xT[tile], ["xs"], "xs")
        b.load(posi[:], pos[tile, :].partition_broadcast(64), ["posi"], "posi")
        wload(tile, 0)
        rmsnorm_cols(xs, 16, TH, "ng", hT, "xs", "hT", 1.0 / 2048)
        posf, posr = b.nx("th")
        b.cp("dve", posf[0:64, :], posi[:], ["posi"], [posr])
        for which, dstt, dres in ((0, ropS, "ropS"), (1, ropC, "ropC")):
            a, ar = b.nx("t")
            b.ts("dve", a[0:64, :], posf[0:64, 0:TT], pc("invf")[0:64, :], (math.pi / 2 if which else 0.0),
                 ALU.mult, ALU.add, [posr, "par"], [ar])
            y, yr = b.nx("t")
            b.ts("dve", y[0:64, :], a[0:64, :], 1.0 / (2 * math.pi), None, ALU.mult, None, [ar], [yr])
            ni = b.sb(f"ni{tile}{which}", [64, TT], I32)
            b.cp("dve", ni[:], y[0:64, :], [yr], [f"ni{tile}{which}"])
            nf, nfr = b.nx("t")
            b.cp("dve", nf[0:64, :], ni[:], [f"ni{tile}{which}"], [nfr])
            r, rr = b.nx("t")
            b.stt(r[0:64, :], nf[0:64, :], -2 * math.pi, a[0:64, :], ALU.mult, ALU.add, [nfr, ar], [rr])
            m, mr = b.nx("t")
            b.ts("dve", m[0:64, :], r[0:64, :], math.pi, -2 * math.pi, ALU.is_gt, ALU.mult, [rr], [mr])
            b.tt("dve", r[0:64, :], r[0:64, :], m[0:64, :], ALU.add, [rr, mr], [rr])
            b.ts("dve", m[0:64, :], r[0:64, :], -math.pi, 2 * math.pi, ALU.is_lt, ALU.mult, [rr], [mr])
            b.tt("dve", r[0:64, :], r[0:64, :], m[0:64, :], ALU.add, [rr, mr], [rr])
            if which == 0:
                sn, snr = b.nx("t")
                b.act(sn[0:64, :], r[0:64, :], AF.Sin, [rr], [snr])
                b.ts("dve", ropS[:], sn[0:64, :], pc("sgn")[0:64, :], None, ALU.mult, None, [snr, "par"], ["ropS"])
            else:
                b.act(ropC[:], r[0:64, :], AF.Sin, [rr], ["ropC"])

        def rope_out(t_tq, t_r, sw_tq, sw_r, rs, rsr, gi, dst_dram):
            a, ar = b.nx("t")
            b.stt(a[0:64, :], t_tq[0:64, :], pc("rpg", gi)[0:64, :], ropC[:], ALU.mult, ALU.mult, [t_r, "par", "ropC"], [ar])
            c, cr = b.nx("t")
            b.stt(c[0:64, :], sw_tq[0:64, :], pc("rpgs", gi)[0:64, :], ropS[:], ALU.mult, ALU.mult, [sw_r, "par", "ropS"], [cr])
            b.tt("dve", a[0:64, :], a[0:64, :], c[0:64, :], ALU.add, [ar, cr], [ar])
            b.tt("dve", a[0:64, :], a[0:64, :], rs[0:64, :], ALU.mult, [ar, rsr], [ar])
            b.store(dst_dram, a[0:64, :], [ar])

        rcur = {}
        for ci in range(NCH1):
            sl = (tile * NCH1 + ci) % 2
            if ci + 1 < NCH1:
                wload(tile, ci + 1)
            elif tile + 1 < NT1:
                wload(tile + 1, 0)
            b.wcast(wbf, wst, sl, "wst", "wbf")
            wr = ("wbf", sl)
            if ci >= 32:
                j = ci - 32
                for tb in range(4):
                    pi = b.nps()
                    for kc in range(16):
                        b.mm(pi, 128, 128, hT[:, kc, tb * 128:(tb + 1) * 128], wbf[:, sl, kc, :], kc == 0, kc == 15, [wr, "hT"])
                    o, orr = b.nx("t")
                    b.cp("act", o[:, 0:128], b.ps[pi][:, 0:128], [("ps", pi)], [orr])
                    b.store(o_dv[t0 + tb * 128:t0 + (tb + 1) * 128, j * 128:(j + 1) * 128], o[:, 0:128], [orr])
                continue
            if ci == 27:
                pis = []
                for hh in range(2):
                    pi = b.nps()
                    for kc in range(16):
                        b.mm(pi, 64, TT, wbf[:, sl, kc, hh * 64:(hh + 1) * 64], hT[:, kc, 0:TT], kc == 0, kc == 15, [wr, "hT"])
                    pis.append(pi)
                tq, tqr = b.nx("t")
                b.cp("act", tq[0:64, :], b.ps[pis[0]][0:64, :], [("ps", pis[0])], [tqr])
                sq, sqr = b.nx("t")
                b.act(sq[0:64, :], b.ps[pis[0]][0:64, :], AF.Square, [("ps", pis[0])], [sqr])
                sw, swr = b.nx("t")
                b.cp("act", sw[0:64, :], b.ps[pis[1]][0:64, :], [("ps", pis[1])], [swr])
                rs, rsr = fm_rstd(b, [(sq[0:64, :], sqr)], b.ones[0:64, 0:64], 64, TT, 1.0 / 64, 1e-6, "ones")
                rope_out(tq, tqr, sw, swr, rs, rsr, 1, o_mkr[:, t0:t0 + TT])
                continue
            pi = b.nps()
            for kc in range(16):
                b.mm(pi, 128, TT, wbf[:, sl, kc, :], hT[:, kc, 0:TT], kc == 0, kc == 15, [wr, "hT"])
            pih = None
            if ci < 14:
                pih = b.nps()
                for kc in range(16):
                    b.mm(pih, 128, 2, wbf[:, sl, kc, :], hT[:, kc, TT:TH], kc == 0, kc == 15, [wr, "hT"])
            if ci == 0:
                tmp, tr = b.nx("t")
                shift(pi, pih, 12, tmp, tr)
                b.act(twd[:], tmp[:], AF.Tanh, [tr], ["twd"])
            elif ci == 1:
                shift(pi, pih, 13, adl, "adl")
            elif ci < 14:
                c = (ci - 2) // 3
                kind = (ci - 2) % 3
                dst, dres = b.nx("rkv")
                shift(pi, pih, kind * 4 + c, dst, dres)
                rcur[kind] = (dst, dres)
                if kind == 0:
                    b.store(o_rw[0, c * 128:(c + 1) * 128, t0:t0 + TT], dst[:], [dres])
                if kind == 2:
                    r_t, r_r = rcur[0]
                    k_t, k_r = rcur[1]
                    v_t, v_r = rcur[2]
                    b.store(o_rw[1, c * 128:(c + 1) * 128, t0:t0 + TT], v_t[:], [v_r])
                    kr_, krr = b.nx("t")
                    b.ts("dve", kr_[:], k_t[:], pc("kk", c), None, ALU.mult, None, [k_r, "par"], [krr])
                    sq, sqr = b.nx("t")
                    b.act(sq[:], kr_[:], AF.Square, [krr], [sqr])
                    pj = b.nps()
                    b.mm(pj, 128, TT, b.blk[:], sq[:], True, True, [sqr, "blk"])
                    nr, nrr = b.nx("t")
                    b.act(nr[:], b.ps[pj][:, :], AF.Sqrt, [("ps", pj)], [nrr])
                    b.ts("dve", nr[:], nr[:], 1e-12, None, ALU.max, None, [nrr], [nrr])
                    b.recip(nr[:], nr[:], [nrr], [nrr])
                    kk_, kkr = b.nx("t")
                    b.tt("dve", kk_[:], kr_[:], nr[:], ALU.mult, [krr, nrr], [kkr])
                    b.store(o_rw[2, c * 128:(c + 1) * 128, t0:t0 + TT], kk_[:], [kkr])
                    kds = []
                    for d in range(2):
                        pw = b.nps()
                        b.mm(pw, 128, TT, wup[d * 64:(d + 1) * 64, c * 128:(c + 1) * 128], twd[d * 64:(d + 1) * 64, :],
                             True, True, ["wup", "twd"])
                        sg, sgr = b.nx("t")
                        b.act(sg[:], b.ps[pw][:, :], AF.Sigmoid, [("ps", pw), "par"], [sgr], bias=pc("w0", d * 4 + c))
                        dec, decr = b.nx("t")
                        b.act(dec[:], sg[:], AF.Copy, [sgr], [decr], scale=-math.exp(-0.5))
                        b.store(o_rw[5 + 3 * d, c * 128:(c + 1) * 128, t0:t0 + TT], dec[:], [decr])
                        pa = b.nps()
                        b.mm(pa, 128, TT, aup[d * 64:(d + 1) * 64, c * 128:(c + 1) * 128], adl[d * 64:(d + 1) * 64, :],
                             True, True, ["aup", "adl"])
                        a_, a_r = b.nx("t")
                        b.act(a_[:], b.ps[pa][:, :], AF.Sigmoid, [("ps", pa), "par"], [a_r], bias=pc("a0", d * 4 + c))
                        nb, nbr = b.nx("t")
                        b.stt(nb[:], kk_[:], -1.0, a_[:], ALU.mult, ALU.mult, [kkr, a_r], [nbr])
                        b.store(o_rw[3 + 3 * d, c * 128:(c + 1) * 128, t0:t0 + TT], nb[:], [nbr])
                        kd, kdr = b.nx("t")
                        b.ts("dve", kd[:], a_[:], pc("ka", c), pc("omka", c), ALU.mult, ALU.add, [a_r, "par"], [kdr])
                        b.tt("dve", kd[:], kd[:], k_t[:], ALU.mult, [kdr, k_r], [kdr])
                        b.store(o_rw[4 + 3 * d, c * 128:(c + 1) * 128, t0:t0 + TT], kd[:], [kdr])
                        kds.append((kd, kdr))
                    s_, s_r = b.nx("t")
                    b.tt("dve", s_[:], kds[0][0][:], kds[1][0][:], ALU.add, [kds[0][1], kds[1][1]], [s_r])
                    b.stt(s_[:], r_t[:], pc("rk", c), s_[:], ALU.mult, ALU.mult, [r_r, "par", s_r], [s_r])
                    pb_ = b.nps()
                    b.mm(pb_, 128, TT, b.blk[:], s_[:], True, True, [s_r, "blk"])
                    bo, bor = b.nx("t")
                    b.tt("dve", bo[:], v_t[:], b.ps[pb_][:, :], ALU.mult, [v_r, ("ps", pb_)], [bor])
                    b.store(o_bonus[c * 128:(c + 1) * 128, t0:t0 + TT], bo[:], [bor])
            elif ci < 22:
                isk = ci >= 18
                j = ci - (18 if isk else 14)
                o, orr = b.nx("t")
                head_norm(pi, 128, b.blk[:], 1.0 / 64, pc("dkg" if isk else "dqg"), o[:], orr)
                b.store((o_dk if isk else o_dq)[j * 128:(j + 1) * 128, t0:t0 + TT], o[:], [orr])
            elif ci < 25:
                j = ci - 22
                b.cp("act", ql[:, j, :], b.ps[pi][:, :], [("ps", pi)], ["ql"])
                if j == 2:
                    rmsnorm_cols(ql, 3, TT, "qlg", qln, "ql", "qln", 1.0 / 384)
                    for h in range(4):
                        pn = b.nps()
                        for kc in range(3):
                            b.mm(pn, 128, TT, wuq[:, kc, h * 192:h * 192 + 128], qln[:, kc, :], kc == 0, kc == 2, ["wuq", "qln"])
                        o, orr = b.nx("t")
                        head_norm(pn, 128, b.ones[:], 1.0 / 128, pc("npg", 0), o[:], orr)
                        b.store(o_mqn[h * 128:(h + 1) * 128, t0:t0 + TT], o[:], [orr])
                        pr = b.nps()
                        for kc in range(3):
                            b.mm(pr, 64, TT, wuq[:, kc, h * 192 + 128:h * 192 + 192], qln[:, kc, :], kc == 0, kc == 2, ["wuq", "qln"])
                        psw = b.nps()
                        for kc in range(3):
                            b.mm(psw, 64, TT, wuqs[:, kc, h * 64:(h + 1) * 64], qln[:, kc, :], kc == 0, kc == 2, ["wuqs", "qln"])
                        tq, tqr = b.nx("t")
                        b.cp("act", tq[0:64, :], b.ps[pr][0:64, :], [("ps", pr)], [tqr])
                        sq, sqr = b.nx("t")
                        b.act(sq[0:64, :], b.ps[pr][0:64, :], AF.Square, [("ps", pr)], [sqr])
                        sw, swr = b.nx("t")
                        b.cp("act", sw[0:64, :], b.ps[psw][0:64, :], [("ps", psw)], [swr])
                        rs, rsr = fm_rstd(b, [(sq[0:64, :], sqr)], b.ones[0:64, 0:64], 64, TT, 1.0 / 64, 1e-6, "ones")
                        rope_out(tq, tqr, sw, swr, rs, rsr, 0, o_mqr[h * 64:(h + 1) * 64, t0:t0 + TT])
            elif ci < 27:
                j = ci - 25
                b.cp("act", kvl[:, j, :], b.ps[pi][:, :], [("ps", pi)], ["kvl"])
                if j == 1:
                    rmsnorm_cols(kvl, 2, TT, "kvg", kvn, "kvl", "kvn", 1.0 / 256)
                    for h in range(4):
                        pn = b.nps()
                        for kc in range(2):
                            b.mm(pn, 128, TT, wukvk[:, kc, h * 128:(h + 1) * 128], kvn[:, kc, :], kc == 0, kc == 1, ["wukvk", "kvn"])
                        o, orr = b.nx("t")
                        head_norm(pn, 128, b.ones[:], 1.0 / 128, pc("npg", 1), o[:], orr)
                        b.store(o_mkn[h * 128:(h + 1) * 128, t0:t0 + TT], o[:], [orr])
                    for tb in range(4):
                        pv = b.nps()
                        for kc in range(2):
                            b.mm(pv, 128, 512, kvn[:, kc, tb * 128:(tb + 1) * 128], wukvv[:, kc, :], kc == 0, kc == 1, ["wukvv", "kvn"])
                        o, orr = b.nx("t")
                        b.cp("act", o[:], b.ps[pv][:, :], [("ps", pv)], [orr])
                        b.store(o_mv[t0 + tb * 128:t0 + (tb + 1) * 128, :], o[:], [orr])
            else:
                j = ci - 28
                qb, qbr = b.nx("bf")
                head_norm(pi, 128, b.ones[:], 1.0 / 128, pc("mqg", 0), qb[:], qbr)
                b.psrot = [0, 1, 2, 3]
                po, pz = attn_core(b, [(qb[:], qbr)],
                                   lambda kt, j=j: [(kmem[:, j, kt * 128:(kt + 1) * 128], "kmem")],
                                   lambda kt, j=j: (vmem[:, kt, j * 128:(j + 1) * 128], "vmem"),
                                   2, 128 ** -0.5, 128)
                b.psrot = list(range(8))
                rz, rzr = b.nx("t")
                b.recip(rz[:], b.ps[pz][:, :], [("ps", pz)], [rzr])
                o, orr = b.nx("t")
                b.tt("dve", o[:], b.ps[po][:, :], rz[:], ALU.mult, [("ps", po), rzr], [orr])
                b.store(o_yd[j * 128:(j + 1) * 128, t0:t0 + TT], o[:], [orr])
    return b.finish()


def _fm(a, nk):
    return np.ascontiguousarray(a.reshape(nk, 128, a.shape[1]).transpose(1, 0, 2))


def _col(par, name, arr, i=0):
    par[:arr.shape[0], PC[name] + i] = arr


def prep_p1(inp, l, x_cur):
    f = np.float32
    w_in = inp["w_in"][l]
    chunks = []
    for col0, n in P1COLS:
        if col0 == "krope":
            idx = [KROPE0 + j for j in range(64)] + [KROPE0 + (j + 32) % 64 for j in range(64)]
            W = w_in[:, idx]
        else:
            W = w_in[:, col0:col0 + 128]
        chunks.append(_fm(W, 16))
    w = np.stack(chunks)
    par = np.zeros((128, NPAR), f)
    par[:, PC["ng"]:PC["ng"] + 16] = inp["norm_g"][l].reshape(16, 128).T
    sh = inp["rw_shift"][l]
    for j in range(3):
        for rc in range(14):
            _col(par, "sh", sh[j, rc * 128:(rc + 1) * 128], j * 14 + rc)
    for d in range(2):
        for c in range(4):
            _col(par, "w0", inp["rw_w0"][l][d, c * 128:(c + 1) * 128], d * 4 + c)
            _col(par, "a0", inp["rw_a0"][l][d, c * 128:(c + 1) * 128], d * 4 + c)
    rk = inp["rw_r_k"][l].reshape(512)
    for c in range(4):
        _col(par, "kk", inp["rw_k_k"][l][c * 128:(c + 1) * 128], c)
        _col(par, "ka", inp["rw_k_a"][l][c * 128:(c + 1) * 128], c)
        _col(par, "rk", rk[c * 128:(c + 1) * 128], c)
    _col(par, "dqg", np.tile(inp["diff_qk_g"][l][0], 2))
    _col(par, "dkg", np.tile(inp["diff_qk_g"][l][1], 2))
    par[:, PC["qlg"]:PC["qlg"] + 3] = inp["mla_q_lat_g"][l].reshape(3, 128).T
    par[:, PC["kvg"]:PC["kvg"] + 2] = inp["mla_kv_lat_g"][l].reshape(2, 128).T
    par[:, PC["npg"]:PC["npg"] + 2] = inp["mla_nope_g"][l].T
    for gi in range(2):
        g = inp["mla_rope_g"][l][gi]
        _col(par, "rpg", g, gi)
        _col(par, "rpgs", np.concatenate([g[32:], g[:32]]), gi)
    par[:, PC["mqg"]:PC["mqg"] + 2] = inp["mem_qk_g"][l].T
    par[:, PC["mng"]:PC["mng"] + 16] = inp["mem_norm_g"][l].reshape(16, 128).T
    invf = (10000.0 ** (-np.arange(0, 64, 2, dtype=np.float32) / 64)).astype(f)
    _col(par, "invf", np.tile(invf, 2))
    _col(par, "sgn", np.concatenate([-np.ones(32, f), np.ones(32, f)]))
    wuq = inp["mla_w_uq"][l]
    sw_idx = [h * 192 + 128 + (j + 32) % 64 for h in range(4) for j in range(64)]
    wukv = inp["mla_w_ukv"][l].reshape(256, 4, 256)
    common = {
        "par": par, "w": w,
        "wup": np.ascontiguousarray(inp["rw_w_up"][l].reshape(128, 512)),
        "aup": np.ascontiguousarray(inp["rw_a_up"][l].reshape(128, 512)),
        "wuq": _fm(wuq, 3), "wuqs": _fm(wuq[:, sw_idx], 3),
        "wukvk": _fm(np.ascontiguousarray(wukv[:, :, :128]).reshape(256, 512), 2),
        "wukvv": _fm(np.ascontiguousarray(wukv[:, :, 128:]).reshape(256, 512), 2),
        "memT": np.ascontiguousarray(inp["mem"][0].reshape(256, 16, 128).transpose(2, 1, 0)),
        "wkv": np.stack([_fm(inp["mem_w_kv"][l][:, ci * 128:(ci + 1) * 128], 16) for ci in range(8)]),
    }
    S = x_cur.shape[0]
    xpad = np.concatenate([np.zeros((1, 2048), f), x_cur, np.zeros((1, 2048), f)], 0)
    posv = inp["positions"][0]
    maps = []
    for c in range(NCORES):
        xt = np.empty((NT1, 128, 16, TH), f)
        pp = np.zeros((NT1, TH), np.int32)
        for t in range(NT1):
            n0 = c * (NT1 * TT) + t * TT
            xe = np.concatenate([xpad[n0 + 1:n0 + 1 + TT], xpad[n0:n0 + 1], xpad[n0 + 1 + TT:n0 + 2 + TT]], 0)
            xt[t] = xe.reshape(TH, 16, 128).transpose(2, 1, 0)
            pp[t, :TT] = posv[n0:n0 + TT]
        m = dict(common)
        m["xT"] = xt
        m["pos"] = pp
        maps.append(m)
    return maps


_NC_CACHE = {}


def _run(name, builder, maps):
    if name not in _NC_CACHE:
        _NC_CACHE[name] = builder()
    res = run_bass_kernel_spmd(_NC_CACHE[name], maps, core_ids=list(range(NCORES)))
    return res.results


def _cat(results, key, axis):
    return np.concatenate([r[key] for r in results], axis=axis)


SEQ = 8192
SC_CH = 32
SC_NV = 5


def build_scan(T=SEQ):
    b = B()
    P = b.P
    bc = b.din("bc", [2, T, SC_NV * 64])
    vcol = b.din("vcol", [128, T])
    y = b.dout("y", [128, T])
    nch = T // SC_CH
    W = SC_NV * 64
    bcb = b.sb("bcb", [128, 2, SC_CH * W])
    vc = b.sb("vc", [128, T])
    yt = b.sb("yt", [128, T])
    S = b.sb("S", [128, 64])
    Sd = b.sb("Sd", [128, 64])
    sa = b.sb("sa", [128, 1])
    b.ring("ja", 2, [128, 64])
    b.ring("jb", 2, [128, 64])
    b.load(vc[:], vcol, ["vc"], "vc")
    b.memset("dve", S[:], 0.0, ["S"])

    def load(c):
        sl = c % 2
        for pr in range(2):
            src = bc[pr, c * SC_CH:(c + 1) * SC_CH, :].rearrange("t f -> (t f)").partition_broadcast(64)
            b.load(bcb[pr * 64:(pr + 1) * 64, sl, :], src, [("bcb", sl, pr)], ("bc", sl, pr))

    load(0)
    for c in range(nch):
        if c + 1 < nch:
            load(c + 1)
        sl = c % 2
        rd = [("bcb", sl, 0), ("bcb", sl, 1)]
        for s in range(SC_CH):
            t = c * SC_CH + s
            o = s * W
            kap = bcb[:, sl, o:o + 64]
            nb = bcb[:, sl, o + 64:o + 128]
            kd = bcb[:, sl, o + 128:o + 192]
            rt = bcb[:, sl, o + 192:o + 256]
            dec = bcb[:, sl, o + 256:o + 320]
            ja, jar = b.nx("ja")
            P.op("dve", lambda e, kap=kap, ja=ja: e.scalar_tensor_tensor(
                out=ja[:], in0=S[:], scalar=1.0, in1=kap, op0=ALU.mult, op1=ALU.mult, accum_out=sa[:]),
                reads=rd + ["S"], writes=[jar, "sa"])
            b.tt("dve", Sd[:], S[:], dec, ALU.mult, rd + ["S"], ["Sd"])
            b.stt(Sd[:], nb, sa[:, 0:1], Sd[:], ALU.mult, ALU.add, rd + ["Sd", "sa"], ["Sd"])
            b.stt(S[:], kd, vc[:, t:t + 1], Sd[:], ALU.mult, ALU.add, rd + ["Sd", "vc"], ["S"])
            jb, jbr = b.nx("jb")
            P.op("dve", lambda e, rt=rt, jb=jb, t=t: e.scalar_tensor_tensor(
                out=jb[:], in0=S[:], scalar=1.0, in1=rt, op0=ALU.mult, op1=ALU.mult, accum_out=yt[:, t:t + 1]),
                reads=rd + ["S"], writes=[jbr, "yt"])
    b.store(y, yt[:], ["yt"])
    return b.finish()


def prep_scan(o_rw, T=SEQ):
    maps = []
    for h in range(NCORES):
        sl = slice(h * 64, (h + 1) * 64)
        bc = np.empty((2, T, SC_NV * 64), np.float32)
        for d in range(2):
            parts = [o_rw[2][sl], o_rw[3 + 3 * d][sl], o_rw[4 + 3 * d][sl], o_rw[0][sl], o_rw[5 + 3 * d][sl]]
            a = np.concatenate([p.T for p in parts], axis=1)
            bc[d] = a if d == 0 else a[::-1]
        v = o_rw[1][sl]
        vcol = np.concatenate([v, v[:, ::-1]], 0)
        maps.append({"bc": bc, "vcol": np.ascontiguousarray(vcol)})
    return maps


def post_scan(results):
    yf = np.concatenate([r["y"][0:64] for r in results], 0)
    yb = np.concatenate([r["y"][64:128][:, ::-1] for r in results], 0)
    return np.ascontiguousarray(yf), np.ascontiguousarray(yb)


NQ = SEQ // 2
NKT = SEQ // 128


def _t5_breaks():
    nb, max_exact = 16, 8
    n = np.arange(0, 1024, dtype=np.int32)
    n_f = np.maximum(n, max_exact).astype(np.float32)
    large = max_exact + (np.log(n_f / np.float32(max_exact)) / np.float32(math.log(128 / max_exact))
                         * np.float32(nb - max_exact)).astype(np.int32)
    large = np.minimum(large, nb - 1)
    f = np.where(n < max_exact, n, large)
    rels = np.arange(-1023, 1024)
    bk = np.where(rels > 0, 16, 0) + f[np.abs(rels)]
    order = [int(bk[0])]
    breaks = []
    for i in range(1, len(rels)):
        if bk[i] != bk[i - 1]:
            order.append(int(bk[i]))
            breaks.append(int(rels[i]))
    return order, breaks


T5_ORDER, T5_BREAKS = _t5_breaks()
NBK = len(T5_ORDER)


def build_attn(kind):
    b = B()
    P = b.P
    diff = kind == "diff"
    b.zmod = 1 if diff else 3
    qa_d = b.din("qa", [128, NQ])
    ka_d = b.din("ka", [128, SEQ])
    v_d = b.din("v", [128, NKT, 128])
    if diff:
        tab_d = b.din("tab", [1, NBK])
        lq_d = b.din("lq", [1, 256])
        cst_d = b.din("cst", [128, 2])
    else:
        qb_d = b.din("qb", [64, NQ])
        kb_d = b.din("kb", [64, SEQ])
    o_d = b.dout("o", [128, NQ])
    setup_consts(b)
    ka = b.sb("ka", [128, SEQ], BF16)
    qa = b.sb("qa", [128, NQ], BF16)
    vv = b.sb("vv", [128, NKT, 128], BF16)
    b.ring("stg", 2, [128, 2048])
    b.ring("t", 6, [128, 512])
    b.ring("pt", 6, [128, 512], BF16)
    b.ring("zacc", 2, [128, 512])
    b.ring("zacc2", 2, [128, 512])
    b.ring("o", 2, [128, 512])

    def load_cast(dst, src, Pn, n, res):
        for i in range(0, n, 2048):
            st, sr = b.nx("stg")
            b.load(st[0:Pn, :], src[:, i:i + 2048], [sr], sr)
            b.cp("pool", dst[0:Pn, i:i + 2048], st[0:Pn, :], [sr], [res])

    load_cast(ka, ka_d, 128, SEQ, "ka")
    load_cast(qa, qa_d, 128, NQ, "qa")
    load_cast(vv[:].rearrange("p a b -> p (a b)"), v_d.rearrange("p a b -> p (a b)"), 128, NKT * 128, "vv")
    if not diff:
        kb = b.sb("kb", [64, SEQ], BF16)
        qb = b.sb("qb", [64, NQ], BF16)
        load_cast(kb, kb_d, 64, SEQ, "kb")
        load_cast(qb, qb_d, 64, NQ, "qb")
    else:
        lq = b.sb("lq", [128, 256])
        cst = b.sb("cst", [128, 2])
        tab = b.sb("tab", [128, NBK])
        b.load(lq[:], lq_d[0, :].partition_broadcast(128), ["lq"], "lq")
        b.load(cst[:], cst_d, ["cst"], "cst")
        b.load(tab[:], tab_d[0, :].partition_broadcast(128), ["tab"], "tab")
        pr = b.sb("lpr", [128, 128])
        b.tt("dve", pr[:, 0:64], lq[:, 0:64], lq[:, 64:128], ALU.mult, ["lq"], ["lpr"])
        b.tt("dve", pr[:, 64:128], lq[:, 128:192], lq[:, 192:256], ALU.mult, ["lq"], ["lpr"])
        ls = b.sb("ls", [128, 4])
        P.op("dve", lambda e: e.tensor_reduce(out=ls[:, 0:1], in_=pr[:, 0:64], axis=AX.X, op=ALU.add), reads=["lpr"], writes=["ls"])
        P.op("dve", lambda e: e.tensor_reduce(out=ls[:, 1:2], in_=pr[:, 64:128], axis=AX.X, op=ALU.add), reads=["lpr"], writes=["ls"])
        b.act(ls[:, 0:2], ls[:, 0:2], AF.Exp, ["ls"], ["ls"])
        b.tt("dve", ls[:, 2:3], ls[:, 0:1], ls[:, 1:2], ALU.subtract, ["ls"], ["ls"])
        b.tt("dve", ls[:, 2:3], ls[:, 2:3], cst[:, 0:1], ALU.add, ["ls", "cst"], ["ls"])
        b.ts("dve", ls[:, 3:4], ls[:, 2:3], -1.0, None, ALU.mult, None, ["ls"], ["ls"])
        dl = b.sb("dl", [128, NBK])
        b.tt("dve", dl[:, 1:NBK], tab[:, 1:NBK], tab[:, 0:NBK - 1], ALU.subtract, ["tab"], ["dl"])
        SW = 1152
        reli = b.sb("reli", [128, SW], I32)
        relf = b.sb("relf", [128, SW])
        P.op("pool", lambda e: e.iota(reli[:], [[-1, SW]], base=512, channel_multiplier=1), writes=["reli"])
        b.cp("dve", relf[:], reli[:], ["reli"], ["relf"])
        strip = b.sb("strip", [128, SW])
        b.ring("stmp", 2, [128, SW])
        for k in range(1, NBK):
            tmp, tr = b.nx("stmp")
            b.ts("dve", tmp[:], relf[:], float(T5_BREAKS[k - 1]), dl[:, k:k + 1], ALU.is_ge, ALU.mult, ["relf", "dl"], [tr])
            if k == 1:
                b.ts("pool", strip[:], tmp[:], tab[:, 0:1], None, ALU.add, None, [tr, "tab"], ["strip"])
            else:
                b.tt("pool", strip[:], strip[:], tmp[:], ALU.add, [tr, "strip"], ["strip"])

    scale = (64 ** -0.5) if diff else (192 ** -0.5)
    b.psrot = [0, 1, 2, 3]
    for qt in range(NQ // 512):
        qs = slice(qt * 512, (qt + 1) * 512)
        res_list = []
        for s in range(2 if diff else 1):
            if diff:
                ps_ = slice(s * 64, (s + 1) * 64)
                qlist = [(qa[ps_, qs], "qa")]
                kts = lambda kt, ps_=ps_: [(ka[ps_, kt * 128:(kt + 1) * 128], "ka")]

                def bias_fn(kt, qt=qt):
                    dk = kt - 4 * qt
                    if dk < -1:
                        return ("const", tab[:, 0:1])
                    if dk > 4:
                        return ("const", tab[:, NBK - 1:NBK])
                    return ("tile", (strip[:, (4 - dk) * 128:(4 - dk) * 128 + 512], "strip"))
            else:
                qlist = [(qa[:, qs], "qa"), (qb[:, qs], "qb")]
                kts = lambda kt: [(ka[:, kt * 128:(kt + 1) * 128], "ka"), (kb[:, kt * 128:(kt + 1) * 128], "kb")]
            vts = lambda kt: (vv[:, kt, :], "vv")
            if diff:
                res_list.append(attn_core(b, qlist, kts, vts, NKT, scale, 128, bias_fn=bias_fn))
            else:
                res_list.append(attn_core(b, qlist, kts, vts, NKT, scale, 128))
        po, pz = res_list[0]
        rz, rzr = b.nx("t")
        b.recip(rz[:], b.ps[pz][:, :], [("ps", pz)], [rzr])
        o, orr = b.nx("o")
        b.tt("dve", o[:], b.ps[po][:, :], rz[:], ALU.mult, [("ps", po), rzr], [orr])
        if diff:
            po2, pz2 = res_list[1]
            rz2, rz2r = b.nx("t")
            b.recip(rz2[:], b.ps[pz2][:, :], [("ps", pz2)], [rz2r])
            o2, o2r = b.nx("t")
            b.tt("dve", o2[:], b.ps[po2][:, :], rz2[:], ALU.mult, [("ps", po2), rz2r], [o2r])
            b.stt(o[:], o2[:], ls[:, 3:4], o[:], ALU.mult, ALU.add, [o2r, "ls", orr], [orr])
        b.store(o_d[:, qs], o[:], [orr])
    return b.finish()


def _vtiles(v):
    return np.ascontiguousarray(v.reshape(-1, 128, 128).transpose(1, 0, 2))


def prep_attn_diff(inp, l, o_dq, o_dk, o_dv):
    lam_init = 0.8 - 0.6 * math.exp(-0.3 * l)
    maps = []
    for c in range(NCORES):
        h, half = c // 2, c % 2
        rows = slice(h * 128, (h + 1) * 128)
        q, k, v = o_dq[rows], o_dk[rows], o_dv[:, rows]
        tab = inp["rel_bias"][T5_ORDER, h]
        if half == 1:
            q, k, v, tab = q[:, ::-1], k[:, ::-1], v[::-1], tab[::-1]
        cst = np.zeros((128, 2), np.float32)
        cst[:, 0] = lam_init
        maps.append({"qa": np.ascontiguousarray(q[:, :NQ]), "ka": np.ascontiguousarray(k), "v": _vtiles(np.ascontiguousarray(v)),
                     "tab": np.ascontiguousarray(tab.reshape(1, NBK)).astype(np.float32),
                     "lq": np.ascontiguousarray(inp["diff_lambda"][l].reshape(1, 256)), "cst": cst})
    return maps


def post_attn_diff(results):
    out = np.empty((512, SEQ), np.float32)
    for c in range(NCORES):
        h, half = c // 2, c % 2
        o = results[c]["o"]
        if half == 0:
            out[h * 128:(h + 1) * 128, :NQ] = o
        else:
            out[h * 128:(h + 1) * 128, NQ:] = o[:, ::-1]
    return out


def prep_attn_mla(o_mqn, o_mqr, o_mkn, o_mkr, o_mv):
    maps = []
    for c in range(NCORES):
        h, half = c // 2, c % 2
        qs = slice(half * NQ, (half + 1) * NQ)
        maps.append({"qa": np.ascontiguousarray(o_mqn[h * 128:(h + 1) * 128, qs]),
                     "qb": np.ascontiguousarray(o_mqr[h * 64:(h + 1) * 64, qs]),
                     "ka": np.ascontiguousarray(o_mkn[h * 128:(h + 1) * 128]),
                     "kb": np.ascontiguousarray(o_mkr),
                     "v": _vtiles(np.ascontiguousarray(o_mv[:, h * 128:(h + 1) * 128]))})
    return maps


def post_attn_mla(results):
    out = np.empty((512, SEQ), np.float32)
    for c in range(NCORES):
        h, half = c // 2, c % 2
        out[h * 128:(h + 1) * 128, half * NQ:(half + 1) * NQ] = results[c]["o"]
    return out


PC3 = {}
_c = 0
for _n, _w in (("ng", 16), ("gng", 4), ("gnb", 4), ("subg", 1), ("lamf", 1)):
    PC3[_n] = _c
    _c += _w
NPAR3 = _c
NCH3 = 16 + 16 * 4 + 16


def build_p3():
    b = B()
    P = b.P
    xT = b.din("xT", [NT1, 128, 16, TT])
    par_d = b.din("par", [128, NPAR3])
    wA = b.din("wA", [NCH3, 128, 16, 128])
    wB = b.din("wB", [16, 128, 16, 128])
    br_d = b.din("br", [NT1, 128, 6, 4, TT])
    o_x = b.dout("o_x", [NT1, 128, 16, TT])
    setup_consts(b)
    par = b.sb("par", [128, NPAR3])
    b.load(par[:], par_d, ["par"], "par")
    pc = lambda n, i=0: par[:, PC3[n] + i:PC3[n] + i + 1]
    xz = b.sb("xz", [128, 16, TT])
    xs = xz
    zT = xz[:].rearrange("p a t -> p (a t)").bitcast(BF16).rearrange("p (n a t) -> p n a t", n=NT1, a=16)
    hT = b.sb("hT", [128, NT1, 16, TT], BF16)
    yg = b.sb("yg", [128, NT1, 16, TT], BF16)
    b.ring("brk", 2, [128, 6, TT])
    wstA = b.sb("wstA", [128, 2, 16, 128])
    wbfA = b.sb("wbfA", [128, 2, 16, 128], BF16)
    wstB = b.sb("wstB", [128, 1, 16, 128])
    wbfB = b.sb("wbfB", [128, 2, 16, 128], BF16)
    rsx = b.sb("rsx", [128, TT])
    zacc = b.sb("zacc", [128, NT1, TT])
    b.ring("t", 10, [128, TT])
    cntA = [0]

    def loadA(ci):
        sl = cntA[0] % 2
        cntA[0] += 1
        b.load(wstA[:, sl], wA[ci], [("wstA", sl)], ("wstA", sl))
        return sl

    ysrc = [0, 3, 4, 5]
    for tile in range(NT1):
        b.load(xs[:], xT[tile], ["xz"], "xz")
        pendA = loadA(0)
        pi = b.nps()
        for kc in range(16):
            sq, sqr = b.nx("t")
            b.act(sq[:], xs[:, kc, :], AF.Square, ["xz"], [sqr])
            b.mm(pi, 128, TT, b.ones[:], sq[:], kc == 0, kc == 15, [sqr, "ones"])
        ln, lnr = b.nx("t")
        b.act(ln[:], b.ps[pi][:, :], AF.Ln, [("ps", pi)], [lnr], bias=b.epsc[1e-6][:], scale=1.0 / 2048)
        b.act(rsx[:], ln[:], AF.Exp, [lnr], ["rsx"], scale=-0.5)
        for kc in range(16):
            b.stt(hT[:, tile, kc, :], xs[:, kc, :], pc("ng", kc), rsx[:], ALU.mult, ALU.mult, ["xz", "rsx", "par"], [("hT", tile)])
        for g in range(16):
            sl = pendA
            if g + 1 < 16:
                pendA = loadA(g + 1)
            b.wcast(wbfA, wstA, sl, "wstA", "wbfA")
            pi = b.nps()
            for kc in range(16):
                b.mm(pi, 128, TT, wbfA[:, sl, kc, :], hT[:, tile, kc, :], kc == 0, kc == 15, [("wbfA", sl), ("hT", tile)])
            b.act(yg[:, tile, g, :], b.ps[pi][:, :], AF.Silu, [("ps", pi)], [("yg", tile, g)])
        for kc in range(4):
            brk, brr = b.nx("brk")
            b.load(brk[:], br_d[tile, :, :, kc, :], [brr], brr)
            ys, ysr = b.nx("t")
            b.tt("pool", ys[:], brk[:, 0, :], brk[:, 1, :], ALU.add, [brr], [ysr])
            sq, sqr = b.nx("t")
            b.act(sq[:], ys[:], AF.Square, [ysr], [sqr])
            pm = b.nps()
            b.mm(pm, 128, TT, b.blk[:], ys[:], True, True, [ysr, "blk"])
            pe2 = b.nps()
            b.mm(pe2, 128, TT, b.blk[:], sq[:], True, True, [sqr, "blk"])
            mean, mr = b.nx("t")
            b.act(mean[:], b.ps[pm][:, :], AF.Copy, [("ps", pm)], [mr], scale=1.0 / 64)
            msq, msr = b.nx("t")
            b.act(msq[:], mean[:], AF.Square, [mr], [msr])
            var, vr = b.nx("t")
            b.stt(var[:], b.ps[pe2][:, :], 1.0 / 64, msq[:], ALU.mult, ALU.subtract, [("ps", pe2), msr], [vr])
            b.act(var[:], var[:], AF.Ln, [vr], [vr], bias=b.epsc[64e-5][:])
            b.act(var[:], var[:], AF.Exp, [vr], [vr], scale=-0.5)
            b.tt("pool", ys[:], ys[:], mean[:], ALU.subtract, [ysr, mr], [ysr])
            b.stt(ys[:], ys[:], pc("gng", kc), var[:], ALU.mult, ALU.mult, [ysr, "par", vr], [ysr])
            b.stt(ys[:], ys[:], pc("gnb", kc), brk[:, 2, :], ALU.add, ALU.add, [ysr, "par", brr], [ysr])
            b.tt("dve", yg[:, tile, 0 * 4 + kc, :], yg[:, tile, 0 * 4 + kc, :], ys[:], ALU.mult, [ysr, ("yg", tile, kc)], [("yg", tile, kc)])
            sq2, sq2r = b.nx("t")
            b.act(sq2[:], brk[:, 3, :], AF.Square, [brr], [sq2r])
            rs, rsr = fm_rstd(b, [(sq2[:], sq2r)], b.ones[:], 128, TT, 1.0 / 128, 1e-6, "ones")
            yb_, ybr_ = b.nx("t")
            b.stt(yb_[:], brk[:, 3, :], pc("subg"), rs[:], ALU.mult, ALU.mult, [brr, "par", rsr], [ybr_])
            b.stt(yg[:, tile, 4 + kc, :], yb_[:], pc("lamf"), yg[:, tile, 4 + kc, :], ALU.mult, ALU.mult,
                  [ybr_, "par", ("yg", tile, 4 + kc)], [("yg", tile, 4 + kc)])
            b.tt("pool", yg[:, tile, 8 + kc, :], yg[:, tile, 8 + kc, :], brk[:, 4, :], ALU.mult, [brr, ("yg", tile, 8 + kc)], [("yg", tile, 8 + kc)])
            b.tt("pool", yg[:, tile, 12 + kc, :], yg[:, tile, 12 + kc, :], brk[:, 5, :], ALU.mult, [brr, ("yg", tile, 12 + kc)], [("yg", tile, 12 + kc)])
    ygr = lambda t: [("yg", t, g) for g in range(16)]
    nxt = 16
    pendA = loadA(nxt)
    nxt += 1
    for oc in range(16):
        slB = oc % 2
        b.load(wstB[:, 0], wB[oc], [("wstB", 0)], ("wstB", 0))
        b.wcast(wbfB, wstB, 0, "wstB", "wbfB", dsl=slB)
        for bi in range(4):
            sl = pendA
            if nxt < NCH3:
                pendA = loadA(nxt)
                nxt += 1
            b.wcast(wbfA, wstA, sl, "wstA", "wbfA")
            for tile in range(NT1):
                pm = b.nps()
                for kc in range(16):
                    b.mm(pm, 128, TT, wbfA[:, sl, kc, :], hT[:, tile, kc, :], kc == 0, kc == 15, [("wbfA", sl), ("hT", tile)])
                pb = b.nps()
                for kc in range(4):
                    b.mm(pb, 128, TT, wbfB[:, slB, bi * 4 + kc, :], yg[:, tile, bi * 4 + kc, :], kc == 0, kc == 3,
                         [("wbfB", slB), ("yg", tile, bi * 4 + kc)])
                sg, sgr = b.nx("t")
                b.act(sg[:], b.ps[pm][:, :], AF.Sigmoid, [("ps", pm)], [sgr])
                if bi == 0:
                    b.tt("dve", zacc[:, tile, :], b.ps[pb][:, :], sg[:], ALU.mult, [("ps", pb), sgr], [("zacc", tile)])
                else:
                    tmp, tr = b.nx("t")
                    b.tt("dve", tmp[:], b.ps[pb][:, :], sg[:], ALU.mult, [("ps", pb), sgr], [tr])
                    if bi < 3:
                        b.tt("pool", zacc[:, tile, :], zacc[:, tile, :], tmp[:], ALU.add, [("zacc", tile), tr], [("zacc", tile)])
                    else:
                        b.tt("pool", zT[:, tile, oc, :], zacc[:, tile, :], tmp[:], ALU.add, [("zacc", tile), tr], ["xz"])
    for oc in range(16):
        sl = pendA
        if nxt < NCH3:
            pendA = loadA(nxt)
            nxt += 1
        b.wcast(wbfA, wstA, sl, "wstA", "wbfA")
        for tile in range(NT1):
            po = b.nps()
            for kc in range(16):
                b.mm(po, 128, TT, wbfA[:, sl, kc, :], zT[:, tile, kc, :], kc == 0, kc == 15, [("wbfA", sl), "xz"])
            xr, xrr = b.nx("t")
            b.load(xr[:], xT[tile, :, oc, :], [xrr], xrr)
            xo, xor_ = b.nx("t")
            b.tt("dve", xo[:], b.ps[po][:, :], xr[:], ALU.add, [("ps", po), xrr], [xor_])
            b.store(o_x[tile, :, oc, :], xo[:], [xor_])
    return b.finish()


GM0 = 1792 + 1536 + 384 + 256 + 64 + 512


def prep_p3(inp, l, x_cur, ysf, ysb, bonus, ybr, yc, yd):
    f = np.float32
    w_in = inp["w_in"][l]
    chunks = []
    for g in range(16):
        chunks.append(_fm(w_in[:, GM0 + g * 128:GM0 + (g + 1) * 128], 16))
    M0 = GM0 + 2048
    for oc in range(16):
        for bi in range(4):
            c0 = M0 + bi * 2048 + oc * 128
            chunks.append(_fm(w_in[:, c0:c0 + 128], 16))
    for oc in range(16):
        chunks.append(_fm(inp["w_out"][l][:, oc * 128:(oc + 1) * 128], 16))
    wA = np.stack(chunks)
    wb = inp["w_branch"][l].reshape(2048, 2048)
    wB = np.stack([_fm(wb[:, oc * 128:(oc + 1) * 128], 16) for oc in range(16)])
    par = np.zeros((128, NPAR3), f)
    par[:, PC3["ng"]:PC3["ng"] + 16] = inp["norm_g"][l].reshape(16, 128).T
    par[:, PC3["gng"]:PC3["gng"] + 4] = inp["rw_gn_g"][l].reshape(4, 128).T
    par[:, PC3["gnb"]:PC3["gnb"] + 4] = inp["rw_gn_b"][l].reshape(4, 128).T
    par[:, PC3["subg"]] = inp["diff_sub_g"][l]
    par[:, PC3["lamf"]] = 1.0 - (0.8 - 0.6 * math.exp(-0.3 * l))
    maps = []
    ntok = NT1 * TT
    for c in range(NCORES):
        xt = np.empty((NT1, 128, 16, TT), f)
        brr = np.empty((NT1, 128, 6, 4, TT), f)
        for t in range(NT1):
            n0 = c * ntok + t * TT
            xt[t] = x_cur[n0:n0 + TT].reshape(TT, 16, 128).transpose(2, 1, 0)
            for i, a in enumerate((ysf, ysb, bonus, ybr, yc, yd)):
                brr[t, :, i] = a[:, n0:n0 + TT].reshape(4, 128, TT).transpose(1, 0, 2)
        maps.append({"xT": xt, "par": par, "wA": wA, "wB": wB, "br": brr})
    return maps


def post_p3(results):
    outs = []
    for r in results:
        o = r["o_x"]
        outs.append(o.transpose(0, 3, 2, 1).reshape(NT1 * TT, 2048))
    return np.ascontiguousarray(np.concatenate(outs, 0))


_P1_AXIS = {"o_dv": 0, "o_mv": 0, "o_rw": 2}


def kernel(**inputs):
    inp = {k: np.asarray(v) for k, v in inputs.items()}
    x = np.ascontiguousarray(inp["x"][0], dtype=np.float32)
    for l in range(4):
        r1 = _run("p1", build_p1, prep_p1(inp, l, x))
        o = {k: _cat(r1, k, _P1_AXIS.get(k, 1)) for k in r1[0]}
        del r1
        rs = _run("scan2", build_scan2, prep_scan2(o["o_rw"]))
        ysf, ysb = post_scan2(rs)
        del rs
        yb = post_attn_diff(_run("diff", lambda: build_attn("diff"),
                                 prep_attn_diff(inp, l, o["o_dq"], o["o_dk"], o["o_dv"])))
        yc = post_attn_mla(_run("mla", lambda: build_attn("mla"),
                                prep_attn_mla(o["o_mqn"], o["o_mqr"], o["o_mkn"], o["o_mkr"], o["o_mv"])))
        r3 = _run("p3", build_p3, prep_p3(inp, l, x, ysf, ysb, o["o_bonus"], yb, yc, o["o_yd"]))
        x = post_p3(r3)
        del r3, o
    return x[None].astype(np.float32)


SB = 512
SG = 128
SC = 64


def build_scan2(T=SEQ, f32r=False):
    b = B()
    b.f32r = f32r
    P = b.P
    fm_d = b.din("fm", [64, 2, 5, T])
    v_d = b.din("v", [64, 2, T])
    cst_d = b.din("cst", [128, 4, 128])
    m01_d = b.din("m01", [64, 2 * SB])
    y_d = b.dout("y", [64, 2, T])
    nblk = T // SB
    NI = (SB // SG) * 2
    cst = b.sb("cst", [128, 4, 128])
    m01 = b.sb("m01", [64, 2 * SB])
    b.load(cst[:], cst_d, ["cst"], "cst")
    b.load(m01[:], m01_d, ["m01"], "m01")
    Ml, Mu, MuI, I_ = (cst[:, i, :] for i in range(4))
    fmb = b.sb("fmb", [64, 2, 5, SB])
    vb = b.sb("vb", [64, 2, SB])
    sc = {n: b.sb(n, [64, 2, SB]) for n in ("KT", "NB", "KD", "RT", "NB2", "KD2")}
    b.ring("e", 4, [64, 2, SB])
    gC = b.sb("gC", [64, 2, SB // SC])
    clend = b.sb("clend", [64, 2, SB // SC])
    Hs = b.sb("Hs", [64, 2, SB // SC + 1, 64])
    yb = b.sb("yb", [64, 2, SB])
    it_buf = []
    for i in range(NI):
        d = {}
        for n, shp in (("N0", [128, 128]), ("N1", [128, 128]), ("P0", [128, 128]), ("P1", [128, 128]),
                       ("X0", [128, 128]), ("X1", [128, 128]), ("AkT", [128, 128]), ("BkT", [128, 128]),
                       ("BnbT", [128, 128]), ("NBt", [128, 64]), ("KDt", [128, 64]), ("Vt", [128, 64]),
                       ("NB2t", [128, 64]), ("KD2t", [128, 64]),
                       ("WT", [64, 128]), ("U", [128, 64]), ("G1", [64, 2, 64]), ("G2", [64, 2, 64])):
            d[n] = b.sb(f"i{i}{n}", shp)
        it_buf.append(d)
    b.memset("dve", Hs[:, :, 0, :], 0.0, ["Hs"])

    def r_(i, n):
        return (f"i{i}", n)

    def bulk(blk):
        t0 = blk * SB
        b.load(fmb[:], fm_d[:, :, :, t0:t0 + SB], ["fmb"], "fmb")
        b.load(vb[:], v_d[:, :, t0:t0 + SB], ["vb"], "vb")
        lw = fmb[:, :, 4, :]
        cl, clr = b.nx("e")
        for p in range(2):
            P.op("dve", lambda e, p=p, cl=cl: e.tensor_tensor_scan(
                out=cl[:, p, :], data0=m01[:, 0:SB], data1=fmb[:, p, 4, :], initial=0.0, op0=ALU.mult, op1=ALU.add),
                reads=["fmb", "m01"], writes=[clr])
        for p in range(2):
            b.cp("pool", clend[:, p, :], cl[:, p, SC - 1:SB:SC], [clr], ["clend"])
        e1, e1r = b.nx("e")
        b.tt("pool", e1[:], cl[:], lw, ALU.subtract, [clr, "fmb"], [e1r])
        b.act(e1[:], e1[:], AF.Exp, [e1r], [e1r])
        b.tt("dve", sc["KT"][:], fmb[:, :, 0, :], e1[:], ALU.mult, ["fmb", e1r], ["KT"])
        e2, e2r = b.nx("e")
        b.act(e2[:], cl[:], AF.Exp, [clr], [e2r], scale=-1.0)
        b.tt("pool", sc["NB"][:], fmb[:, :, 1, :], e2[:], ALU.mult, ["fmb", e2r], ["NB"])
        b.tt("dve", sc["KD"][:], fmb[:, :, 2, :], e2[:], ALU.mult, ["fmb", e2r], ["KD"])
        e3, e3r = b.nx("e")
        b.act(e3[:], cl[:], AF.Exp, [clr], [e3r])
        b.tt("pool", sc["RT"][:], fmb[:, :, 3, :], e3[:], ALU.mult, ["fmb", e3r], ["RT"])
        for p in range(2):
            b.cp("pool", gC[:, p, :], e3[:, p, SC - 1:SB:SC], [e3r], ["gC"])
        e4, e4r = b.nx("e")
        for p in range(2):
            for c in range(SB // SC):
                b.act(e4[:, p, c * SC:(c + 1) * SC], cl[:, p, c * SC:(c + 1) * SC], AF.Exp, [clr, "clend"], [e4r],
                      bias=clend[:, p, c:c + 1], scale=-1.0)
        b.tt("dve", sc["NB2"][:], fmb[:, :, 1, :], e4[:], ALU.mult, ["fmb", e4r], ["NB2"])
        b.tt("pool", sc["KD2"][:], fmb[:, :, 2, :], e4[:], ALU.mult, ["fmb", e4r], ["KD2"])

    def evac_act(dst, pi, M, N, wr, scale=None):
        if scale is None:
            b.cp("act", dst, b.ps[pi][0:M, 0:N], [("ps", pi)], wr)
        else:
            b.act(dst, b.ps[pi][0:M, 0:N], AF.Copy, [("ps", pi), "gC"], wr, scale=scale)

    def transpose(pi, in_ap, K, M, rd):
        out = b.ps[pi][0:M, 0:K]
        P.op("pe", lambda e: e.transpose(out, in_ap, I_[0:K, 0:K]), reads=rd + ["cst"], writes=[("ps", pi)])

    def stage1(blk):
        for i in range(NI):
            g, p = i // 2, i % 2
            ts = slice(g * SG, (g + 1) * SG)
            bf = it_buf[i]
            KT, NB, KD, RT = (sc[n][:, p, ts] for n in ("KT", "NB", "KD", "RT"))
            for (la, ln), (ra, rn), msk, dst in (((KT, "KT"), (NB, "NB"), Ml, "N0"), ((NB, "NB"), (KT, "KT"), Mu, "P0"),
                                                 ((KD, "KD"), (KT, "KT"), Mu, "AkT"), ((KD, "KD"), (RT, "RT"), MuI, "BkT"),
                                                 ((NB, "NB"), (RT, "RT"), MuI, "BnbT")):
                pi = b.nps()
                b.mm(pi, 128, 128, la, ra, True, True, [ln, rn])
                b.tt("dve", bf[dst][:], b.ps[pi][:, 0:128], msk, ALU.mult, [("ps", pi), "cst"], [r_(i, dst)])
            for src, sn, dst, dcols in ((sc["KT"], "KT", "X0", slice(0, 64)), (sc["NB"], "NB", "NBt", slice(0, 64)),
                                        (sc["KD"], "KD", "KDt", slice(0, 64)), (vb, "vb", "Vt", slice(0, 64)),
                                        (sc["NB2"], "NB2", "NB2t", slice(0, 64)), (sc["KD2"], "KD2", "KD2t", slice(0, 64))):
                pi = b.nps()
                transpose(pi, src[:, p, ts], 64, 128, [sn])
                evac_act(bf[dst][:, dcols], pi, 128, 64, [r_(i, dst)])
            pi = b.nps()
            b.mm(pi, 128, 64, bf["AkT"][:], bf["Vt"][:], True, True, [r_(i, "AkT"), r_(i, "Vt")])
            evac_act(bf["X0"][:, 64:128], pi, 128, 64, [r_(i, "X0")])

    def stage2(blk):
        for it in range(6):
            cur, nxt = it % 2, (it + 1) % 2
            for i in range(NI):
                bf = it_buf[i]
                Nc, Pc, Xc = bf[f"N{cur}"], bf[f"P{cur}"], bf[f"X{cur}"]
                Nn, Pn, Xn = bf[f"N{nxt}"], bf[f"P{nxt}"], bf[f"X{nxt}"]
                pi = b.nps()
                b.mm(pi, 128, 128, Pc[:], Xc[:], True, True, [r_(i, f"P{cur}"), r_(i, f"X{cur}")])
                b.tt("dve", Xn[:], Xc[:], b.ps[pi][:, 0:128], ALU.add, [("ps", pi), r_(i, f"X{cur}")], [r_(i, f"X{nxt}")])
                if it < 5:
                    pi = b.nps()
                    b.mm(pi, 128, 128, Nc[:], Pc[:], True, True, [r_(i, f"N{cur}"), r_(i, f"P{cur}")])
                    evac_act(Pn[:], pi, 128, 128, [r_(i, f"P{nxt}")])
                if it < 4:
                    pi = b.nps()
                    b.mm(pi, 128, 128, Pc[:], Nc[:], True, True, [r_(i, f"N{cur}"), r_(i, f"P{cur}")])
                    evac_act(Nn[:], pi, 128, 128, [r_(i, f"N{nxt}")])

    def stage3(blk):
        for i in range(NI):
            bf = it_buf[i]
            X = bf["X0"]
            pi = b.nps()
            transpose(pi, X[:, 0:64], 128, 64, [r_(i, "X0")])
            evac_act(bf["WT"][:], pi, 64, 128, [r_(i, "WT")])
            for c in range(2):
                cs = slice(c * SC, (c + 1) * SC)
                pi = b.nps()
                b.mm(pi, 64, 64, X[cs, 0:64], bf["NB2t"][cs, :], True, True, [r_(i, "X0"), r_(i, "NB2t")])
                pdiag, pdr = b.nx("dg")
                g, p = i // 2, i % 2
                cg = g * 2 + c
                b.ts("pool", pdiag[:], I_[0:64, 0:64], gC[:, p, cg:cg + 1], None, ALU.mult, None, ["cst", "gC"], [pdr])
                b.tt("dve", bf["G1"][:, c, :], b.ps[pi][0:64, 0:64], pdiag[:], ALU.add, [("ps", pi), pdr], [r_(i, "G1")])
                pi = b.nps()
                b.mm(pi, 64, 64, bf["NB2t"][cs, :], X[cs, 64:128], True, False, [r_(i, "X0"), r_(i, "NB2t")])
                b.mm(pi, 64, 64, bf["KD2t"][cs, :], bf["Vt"][cs, :], False, True, [r_(i, "KD2t"), r_(i, "Vt")])
                evac_act(bf["G2"][:, c, :], pi, 64, 64, [r_(i, "G2")])

    def stage4(blk):
        nchunk = SB // SC
        for cg in range(nchunk):
            for p in range(2):
                i = (cg // 2) * 2 + p
                c = cg % 2
                bf = it_buf[i]
                pi = b.nps()
                b.mm(pi, 64, 64, bf["G1"][:, c, :], Hs[:, p, cg, :], True, False, [r_(i, "G1"), ("Hs", p)])
                b.mm(pi, 64, 64, I_[0:64, 0:64], bf["G2"][:, c, :], False, True, ["cst", r_(i, "G2")])
                b.cp("act", Hs[:, p, cg + 1, :], b.ps[pi][0:64, 0:64], [("ps", pi)], [("Hs", p)])

    def stage5(blk):
        t0 = blk * SB
        for i in range(NI):
            g, p = i // 2, i % 2
            bf = it_buf[i]
            X = bf["X0"]
            for c in range(2):
                cs = slice(c * SC, (c + 1) * SC)
                cg = g * 2 + c
                pi = b.nps()
                b.mm(pi, 128, 64, bf["WT"][:], Hs[:, p, cg, :], True, True, [r_(i, "WT"), ("Hs", p)])
                b.tt("dve", bf["U"][cs, :], b.ps[pi][cs, 0:64], X[cs, 64:128], ALU.add, [("ps", pi), r_(i, "X0")], [r_(i, "U")])
            pi = b.nps()
            b.mm(pi, 64, 128, bf["Vt"][:], bf["BkT"][:], True, False, [r_(i, "Vt"), r_(i, "BkT")])
            b.mm(pi, 64, 128, bf["U"][:], bf["BnbT"][:], False, False, [r_(i, "U"), r_(i, "BnbT")])
            for c in range(2):
                cg = g * 2 + c
                out = b.ps[pi][0:64, c * SC:(c + 1) * SC]
                lhsT = Hs[:, p, cg, :]
                rhs = sc["RT"][:, p, g * SG + c * SC:g * SG + (c + 1) * SC]
                P.op("pe", lambda e, out=out, lhsT=lhsT, rhs=rhs, c=c: e.matmul(out, lhsT, rhs, start=False, stop=(c == 1)),
                     reads=[("Hs", p), "RT"], writes=[("ps", pi)])
            b.cp("act", yb[:, p, g * SG:(g + 1) * SG], b.ps[pi][0:64, 0:128], [("ps", pi)], ["yb"])
        b.store(y_d[:, :, t0:t0 + SB], yb[:], ["yb"])
        if blk + 1 < nblk:
            b.cp("pool", Hs[:, :, 0, :], Hs[:, :, SB // SC, :], [("Hs", 0), ("Hs", 1)], [("Hs", 0), ("Hs", 1)])

    b.ring("dg", 4, [64, 64])
    for blk in range(nblk):
        bulk(blk)
        stage1(blk)
        stage2(blk)
        stage3(blk)
        stage4(blk)
        stage5(blk)
    return b.finish()


def _scan2_consts():
    G, C = SG, SC
    Ml = np.zeros((G, G), np.float32)
    for t in range(G):
        for s in range(G):
            if t // C == s // C and s < t:
                Ml[t, s] = 1
    cst = np.stack([Ml, Ml.T, Ml.T + np.eye(G, dtype=np.float32), np.eye(G, dtype=np.float32)], 1)
    m01 = np.ones((64, 2 * SB), np.float32)
    m01[:, ::C] = 0
    return np.ascontiguousarray(cst), m01


def prep_scan2(o_rw, T=SEQ):
    cst, m01 = _scan2_consts()
    maps = []
    for h in range(NCORES):
        sl = slice(h * 64, (h + 1) * 64)
        fm = np.empty((64, 2, 5, T), np.float32)
        v = np.empty((64, 2, T), np.float32)
        for d in range(2):
            for k, a in enumerate((o_rw[2][sl], o_rw[3 + 3 * d][sl], o_rw[4 + 3 * d][sl], o_rw[0][sl], o_rw[5 + 3 * d][sl])):
                fm[:, d, k] = a if d == 0 else a[:, ::-1]
            v[:, d] = o_rw[1][sl] if d == 0 else o_rw[1][sl][:, ::-1]
        maps.append({"fm": fm, "v": v, "cst": cst, "m01": m01})
    return maps


def post_scan2(results):
    yf = np.concatenate([r["y"][:, 0] for r in results], 0)
    yb = np.concatenate([r["y"][:, 1][:, ::-1] for r in results], 0)
    return np.ascontiguousarray(yf), np.ascontiguousarray(yb)
```

```python
import math
from contextlib import ExitStack
import numpy as np
import concourse.bass as bass
import concourse.mybir as mybir
from concourse.bass_utils import run_bass_kernel_spmd

F32 = mybir.dt.float32
BF16 = mybir.dt.bfloat16
F32R = mybir.dt.float32r
I32 = mybir.dt.int32
ALU = mybir.AluOpType
AF = mybir.ActivationFunctionType
AX = mybir.AxisListType
ENGS = ("pe", "act", "dve", "pool", "sp")
NCORES = 8


class _Op:
    __slots__ = ("eng", "fn", "waits", "signal", "dma_key", "idx", "sigval")

    def __init__(self, eng, fn, dma_key):
        self.eng = eng
        self.fn = fn
        self.waits = []
        self.signal = False
        self.dma_key = dma_key
        self.idx = None
        self.sigval = None


class _Res:
    __slots__ = ("w", "r")

    def __init__(self):
        self.w = None
        self.r = []


class Prog:
    def __init__(self, nc):
        self.nc = nc
        self.ops = {e: [] for e in ENGS}
        self.res = {}
        self.dma_cnt = {}
        self.dma_last = {}
        self.waited = {e: {} for e in ENGS}

    def _need(self, op, tok, isd):
        if tok is None:
            return
        kind, src, val = tok
        if kind == "e" and src == op.eng and not isd and src == "pe":
            return
        w = self.waited[op.eng]
        k = (kind, src)
        if w.get(k, -1) >= val:
            return
        w[k] = val
        op.waits.append(tok)
        if kind == "e":
            self.ops[src][val].signal = True

    def op(self, eng, fn, reads=(), writes=(), dma=None):
        o = _Op(eng, fn, dma)
        o.idx = len(self.ops[eng])
        isd = dma is not None
        if isd:
            n = self.dma_cnt.get(dma, 0) + 1
            self.dma_cnt[dma] = n
            self._need(o, self.dma_last.get(dma), True)
            tok = ("d", dma, n)
            self.dma_last[dma] = tok
        else:
            tok = ("e", eng, o.idx)
        for r in reads:
            st = self.res.setdefault(r, _Res())
            self._need(o, st.w, isd)
        for r in writes:
            st = self.res.setdefault(r, _Res())
            self._need(o, st.w, isd)
            for t in st.r:
                self._need(o, t, isd)
        for r in reads:
            st = self.res[r]
            st.r.append(tok)
            if len(st.r) > 48:
                st.r = st.r[-48:]
        for r in writes:
            st = self.res[r]
            st.w = tok
            st.r = []
        self.ops[eng].append(o)
        return tok

    def wait_tokens(self, eng, toks):
        o = _Op(eng, None, None)
        o.idx = len(self.ops[eng])
        for t in toks:
            self._need(o, t, True)
        self.ops[eng].append(o)

    def emit(self):
        nc = self.nc
        esem = {e: nc.alloc_semaphore(name=f"s_{e}") for e in ENGS}
        dsem = {k: nc.alloc_semaphore(name=f"d_{i}") for i, k in enumerate(self.dma_cnt)}
        for e in ENGS:
            c = 0
            for o in self.ops[e]:
                if o.signal:
                    c += 1
                    o.sigval = c
        ops = self.ops

        def body(e):
            def f(eng):
                for o in ops[e]:
                    for kind, src, val in o.waits:
                        if kind == "e":
                            eng.wait_ge(esem[src], ops[src][val].sigval)
                        else:
                            eng.wait_ge(dsem[src], 16 * val)
                    if o.fn is None:
                        continue
                    inst = o.fn(eng)
                    if o.dma_key is not None:
                        inst.then_inc(dsem[o.dma_key], 16)
                    elif o.signal:
                        inst.then_inc(esem[e], 1)
            return f

        with nc.Block() as block:
            block.tensor(body("pe"))
            block.scalar(body("act"))
            block.vector(body("dve"))
            block.gpsimd(body("pool"))
            block.sync(body("sp"))


class B:
    def __init__(self):
        self.nc = bass.Bass("TRN2", target_bir_lowering=False)
        self.P = Prog(self.nc)
        self.es = ExitStack()
        self.ps = [self.es.enter_context(self.nc.psum_tensor(f"ps{i}", [128, 512], F32)) for i in range(8)]
        self.psi = 0
        self.psrot = list(range(8))
        self.rings = {}
        self.outtoks = []
        self.ndq = 0
        self.attn_banks = [(4, 5), (6, 7)]
        self.zmod = 3
        self.attn_par = 0

    def din(self, name, shape, dt=F32):
        return self.nc.dram_tensor(name, list(shape), dt, kind="ExternalInput").ap()

    def dout(self, name, shape, dt=F32):
        return self.nc.dram_tensor(name, list(shape), dt, kind="ExternalOutput").ap()

    def sb(self, name, shape, dt=F32):
        return self.es.enter_context(self.nc.sbuf_tensor("s_" + name, list(shape), dt))

    def nps(self):
        rot = self.psrot
        i = rot[self.psi % len(rot)]
        self.psi += 1
        return i

    def ring(self, name, n, shape, dt=F32):
        self.rings[name] = [[self.sb(f"{name}{i}", shape, dt) for i in range(n)], 0]

    def nx(self, name):
        r = self.rings[name]
        i = r[1]
        r[1] = (i + 1) % len(r[0])
        return r[0][i], (name, i)

    def mm(self, pi, M, N, lhsT, rhs, st, sp, rd, po=0):
        out = self.ps[pi][po:po + M, 0:N]
        if getattr(self, "f32r", False) and lhsT.dtype == F32 and rhs.dtype == F32:
            lhsT = lhsT.bitcast(F32R)
            rhs = rhs.bitcast(F32R)
        self.P.op("pe", lambda e: e.matmul(out, lhsT, rhs, start=st, stop=sp), reads=rd, writes=[("ps", pi)])

    def act(self, out, in_, func, rd, wr, bias=None, scale=None):
        kw = {}
        if bias is not None:
            kw["bias"] = bias
        if scale is not None:
            kw["scale"] = scale
        self.P.op("act", lambda e: e.activation(out=out, in_=in_, func=func, **kw), reads=rd, writes=wr)

    def stt(self, out, in0, scalar, in1, op0, op1, rd, wr):
        self.P.op("dve", lambda e: e.scalar_tensor_tensor(out=out, in0=in0, scalar=scalar, in1=in1,
                                                          op0=op0, op1=op1), reads=rd, writes=wr)

    def tt(self, eng, out, in0, in1, op, rd, wr):
        self.P.op(eng, lambda e: e.tensor_tensor(out=out, in0=in0, in1=in1, op=op), reads=rd, writes=wr)

    def ts(self, eng, out, in0, s1, s2, op0, op1, rd, wr):
        if op1 is None:
            self.P.op(eng, lambda e: e.tensor_scalar(out=out, in0=in0, scalar1=s1, scalar2=None, op0=op0),
                      reads=rd, writes=wr)
        else:
            self.P.op(eng, lambda e: e.tensor_scalar(out=out, in0=in0, scalar1=s1, scalar2=s2, op0=op0, op1=op1),
                      reads=rd, writes=wr)

    def cp(self, eng, out, in_, rd, wr):
        if eng == "act":
            self.P.op("act", lambda e: e.copy(out=out, in_=in_), reads=rd, writes=wr)
        else:
            self.P.op(eng, lambda e: e.tensor_copy(out=out, in_=in_), reads=rd, writes=wr)

    def wcast(self, wbf, wst, sl, rn, wn, dsl=None):
        dsl = sl if dsl is None else dsl
        self.cp("dve", wbf[:, dsl, 0:8], wst[:, sl, 0:8], [(rn, sl)], [(wn, dsl)])
        self.cp("act", wbf[:, dsl, 8:16], wst[:, sl, 8:16], [(rn, sl)], [(wn, dsl)])

    def recip(self, out, in_, rd, wr):
        self.P.op("dve", lambda e: e.reciprocal(out=out, in_=in_), reads=rd, writes=wr)

    def memset(self, eng, ap, val, wr):
        self.P.op(eng, lambda e: e.memset(ap, val), writes=wr)

    def load(self, out, in_, wr, key, q="sp"):
        return self.P.op(q, lambda e: e.dma_start(out=out, in_=in_), writes=wr, dma=key)

    def store(self, out, in_, rd, q="pool"):
        self.ndq += 1
        key = ("st", self.ndq % 6)
        t = self.P.op(q, lambda e: e.dma_start(out=out, in_=in_), reads=rd, dma=key)
        self.outtoks.append(t)

    def finish(self):
        last = {}
        for t in self.outtoks:
            last[t[1]] = t
        self.P.wait_tokens("pool", list(last.values()))
        self.P.emit()
        self.es.close()
        return self.nc


def fm_rstd(b, sq_list, ones_ap, Pn, N, inv_n, eps, consts_res):
    pi = b.nps()
    for i, (ap, res) in enumerate(sq_list):
        b.mm(pi, Pn, N, ones_ap, ap, i == 0, i == len(sq_list) - 1, [res, consts_res])
    ln, lnr = b.nx("t")
    b.act(ln[0:Pn, 0:N], b.ps[pi][0:Pn, 0:N], AF.Ln, [("ps", pi)], [lnr], bias=b.epsc[eps][0:Pn, :], scale=inv_n)
    rs, rsr = b.nx("t")
    b.act(rs[0:Pn, 0:N], ln[0:Pn, 0:N], AF.Exp, [lnr], [rsr], scale=-0.5)
    return rs, rsr


def setup_consts(b):
    b.ones = b.sb("ones", [128, 128])
    b.blk = b.sb("blk", [128, 128])
    b.memset("pool", b.ones[:], 1.0, ["ones"])
    b.memset("pool", b.blk[:], 0.0, ["blk"])
    b.memset("pool", b.blk[0:64, 0:64], 1.0, ["blk"])
    b.memset("pool", b.blk[64:128, 64:128], 1.0, ["blk"])
    b.onesb = b.sb("onesb", [128, 128], BF16)
    b.memset("pool", b.onesb[:], 1.0, ["onesb"])
    b.epsc = {}
    for i, v in enumerate((1e-6, 64e-5)):
        t = b.sb(f"epsc{i}", [128, 1])
        b.memset("pool", t[:], v, [f"epsc{i}"])
        b.epsc[v] = t
        b.P.res


def attn_core(b, qlist, kts, vts, nkt, scale, out_M, bias_fn=None, tag="a"):
    po, pz = b.attn_banks[b.attn_par]
    b.attn_par ^= 1
    acc, accr = b.nx("zacc")
    acc2, acc2r = b.nx("zacc2")
    pend = []
    zq = []
    acc_init = [False]
    LOOK = 2
    ZMOD = b.zmod if nkt > 2 else 2

    def pv(kt, pt, ptr):
        va, vr = vts(kt)
        b.mm(po, out_M, 512, va, pt[:], kt == 0, kt == nkt - 1, [vr, ptr])
        if kt % ZMOD == 0:
            b.mm(pz, out_M, 512, b.onesb[:, 0:out_M], pt[:], kt == 0, (ZMOD == 1 and kt == nkt - 1), ["onesb", ptr])

    for kt in range(nkt):
        pi = b.nps()
        ks = kts(kt)
        for i, ((qa, qr), (ka, kr)) in enumerate(zip(qlist, ks)):
            b.mm(pi, 128, 512, ka, qa, i == 0, i == len(qlist) - 1, [qr, kr])
        if len(pend) >= LOOK:
            pv(*pend.pop(0))
        pt, ptr = b.nx("pt")
        bf = bias_fn(kt) if bias_fn is not None else None
        if bf is not None:
            kind, val = bf
            if kind == "const":
                b.act(pt[:], b.ps[pi][:, :], AF.Exp, [("ps", pi), "tab"], [ptr], bias=val, scale=scale)
            else:
                tmp, tr = b.nx("t")
                bap, bres = val
                b.stt(tmp[:], b.ps[pi][:, :], scale, bap, ALU.mult, ALU.add, [("ps", pi), bres], [tr])
                b.act(pt[:], tmp[:], AF.Exp, [tr], [ptr])
        else:
            b.act(pt[:], b.ps[pi][:, :], AF.Exp, [("ps", pi)], [ptr], scale=scale)
        if kt % ZMOD == 0:
            zq.append((kt, pt, ptr))
        elif not acc_init[0]:
            b.cp("dve", acc2[:], pt[:], [ptr], [acc2r])
            acc_init[0] = True
        else:
            b.tt("dve", acc2[:], acc2[:], pt[:], ALU.add, [ptr, acc2r], [acc2r])
        pend.append((kt, pt, ptr))
    for pp in pend:
        pv(*pp)
    if ZMOD > 1:
        b.mm(pz, out_M, 512, b.ones[:, 0:out_M], acc2[:], False, True, ["ones", acc2r])
    return po, pz


TT = 512
TH = TT + 2
NT1 = 2
PC = {}
_c = 0
for _n, _w in (("ng", 16), ("sh", 42), ("w0", 8), ("a0", 8), ("kk", 4), ("ka", 4), ("rk", 4), ("dqg", 1), ("dkg", 1),
               ("qlg", 3), ("kvg", 2), ("npg", 2), ("rpg", 2), ("rpgs", 2), ("mqg", 2), ("mng", 16), ("invf", 1),
               ("sgn", 1), ("omka", 4)):
    PC[_n] = _c
    _c += _w
NPAR = _c
RW0 = 0
def _p1_cols():
    cols = []
    cols += [(RW0 + 1536, 128), (RW0 + 1664, 128)]
    for c in range(4):
        cols += [(c * 128, 128), (512 + c * 128, 128), (1024 + c * 128, 128)]
    o = 1792
    cols += [(o + i * 128, 128) for i in range(4)]
    cols += [(o + 512 + i * 128, 128) for i in range(4)]
    o2 = 1792 + 1536
    cols += [(o2 + i * 128, 128) for i in range(3)]
    cols += [(o2 + 384 + i * 128, 128) for i in range(2)]
    cols += [("krope", 128)]
    o3 = o2 + 384 + 256 + 64
    cols += [(o3 + i * 128, 128) for i in range(4)]
    cols += [(1792 + 1024 + i * 128, 128) for i in range(4)]
    return cols
P1COLS = _p1_cols()
NCH1 = len(P1COLS)
KROPE0 = 1792 + 1536 + 384 + 256


def build_p1():
    b = B()
    P = b.P
    xT = b.din("xT", [NT1, 128, 16, TH])
    pos = b.din("pos", [NT1, TH], I32)
    par_d = b.din("par", [128, NPAR])
    w = b.din("w", [NCH1, 128, 16, 128])
    wup_d = b.din("wup", [128, 512])
    aup_d = b.din("aup", [128, 512])
    wuq_d = b.din("wuq", [128, 3, 768])
    wuqs_d = b.din("wuqs", [128, 3, 256])
    wukvk_d = b.din("wukvk", [128, 2, 512])
    wukvv_d = b.din("wukvv", [128, 2, 512])
    memT_d = b.din("memT", [128, 16, 256])
    wkv_d = b.din("wkv", [8, 128, 16, 128])
    o_rw = b.dout("o_rw", [9, 512, NT1 * TT])
    o_bonus = b.dout("o_bonus", [512, NT1 * TT])
    o_dq = b.dout("o_dq", [512, NT1 * TT])
    o_dk = b.dout("o_dk", [512, NT1 * TT])
    o_dv = b.dout("o_dv", [NT1 * TT, 512])
    o_mqn = b.dout("o_mqn", [512, NT1 * TT])
    o_mqr = b.dout("o_mqr", [256, NT1 * TT])
    o_mkn = b.dout("o_mkn", [512, NT1 * TT])
    o_mkr = b.dout("o_mkr", [64, NT1 * TT])
    o_mv = b.dout("o_mv", [NT1 * TT, 512])
    o_yd = b.dout("o_yd", [512, NT1 * TT])

    setup_consts(b)
    par = b.sb("par", [128, NPAR])
    b.load(par[:], par_d, ["par"], "par")
    pc = lambda n, i=0: par[:, PC[n] + i:PC[n] + i + 1]
    b.ts("dve", par[:, PC["omka"]:PC["omka"] + 4], par[:, PC["ka"]:PC["ka"] + 4], -1.0, 1.0, ALU.mult, ALU.add,
         ["par"], ["par"])
    xs = b.sb("xs", [128, 16, TH])
    hT = b.sb("hT", [128, 16, TH], BF16)
    wst = b.sb("wst", [128, 2, 16, 128])
    wbf = b.sb("wbf", [128, 2, 16, 128], BF16)
    stg = b.sb("stg", [128, 3072])
    wup = b.sb("wup_s", [128, 512])
    aup = b.sb("aup_s", [128, 512])
    wuq = b.sb("wuq_s", [128, 3, 768], BF16)
    wuqs = b.sb("wuqs_s", [128, 3, 256], BF16)
    wukvk = b.sb("wukvk_s", [128, 2, 512], BF16)
    wukvv = b.sb("wukvv_s", [128, 2, 512], BF16)
    memn = b.sb("memn", [128, 16, 256], BF16)
    kmem = b.sb("kmem", [128, 4, 256], BF16)
    vmem = b.sb("vmem", [128, 2, 512], BF16)
    b.ring("t", 14, [128, TT])
    b.ring("th", 3, [128, TH])
    b.ring("rkv", 4, [128, TT])
    b.ring("bf", 4, [128, TT], BF16)
    b.ring("pt", 3, [128, TT], BF16)
    b.ring("zacc", 1, [128, TT])
    b.ring("zacc2", 1, [128, TT])
    twd = b.sb("twd", [128, TT])
    adl = b.sb("adl", [128, TT])
    ropC = b.sb("ropC", [64, TT])
    ropS = b.sb("ropS", [64, TT])
    rsx = b.sb("rsx", [128, TH])
    ql = b.sb("ql", [128, 3, TT])
    qln = b.sb("qln", [128, 3, TT], BF16)
    kvl = b.sb("kvl", [128, 2, TT])
    kvn = b.sb("kvn", [128, 2, TT], BF16)
    posi = b.sb("posi", [64, TH], I32)

    b.load(wup[:], wup_d, ["wup"], "wl0")
    b.load(aup[:], aup_d, ["aup"], "wl1")
    for dst, src, n, nm in ((wuq, wuq_d, 3 * 768, "wuq"), (wuqs, wuqs_d, 3 * 256, "wuqs"),
                            (wukvk, wukvk_d, 1024, "wukvk"), (wukvv, wukvv_d, 1024, "wukvv")):
        b.load(stg[:, 0:n], src.rearrange("p a b -> p (a b)"), ["stg"], "stg")
        b.cp("dve", dst[:].rearrange("p a b -> p (a b)"), stg[:, 0:n], ["stg"], [nm])

    def rmsnorm_cols(src_tile, nkc, N, gname, dst_tile, res_src, res_dst, inv_n):
        pi = b.nps()
        pih = b.nps() if N > 512 else None
        for kc in range(nkc):
            sq, sqr = b.nx("th")
            b.act(sq[:, 0:N], src_tile[:, kc, 0:N], AF.Square, [res_src], [sqr])
            b.mm(pi, 128, min(N, 512), b.ones[:], sq[:, 0:min(N, 512)], kc == 0, kc == nkc - 1, [sqr, "ones"])
            if pih is not None:
                b.mm(pih, 128, N - 512, b.ones[:], sq[:, 512:N], kc == 0, kc == nkc - 1, [sqr, "ones"])
        ln, lnr = b.nx("th")
        b.act(ln[:, 0:min(N, 512)], b.ps[pi][:, 0:min(N, 512)], AF.Ln, [("ps", pi)], [lnr],
              bias=b.epsc[1e-6][:], scale=inv_n)
        if pih is not None:
            b.act(ln[:, 512:N], b.ps[pih][:, 0:N - 512], AF.Ln, [("ps", pih)], [lnr], bias=b.epsc[1e-6][:], scale=inv_n)
        b.act(rsx[:, 0:N], ln[:, 0:N], AF.Exp, [lnr], ["rsx"], scale=-0.5)
        for kc in range(nkc):
            b.stt(dst_tile[:, kc, 0:N], src_tile[:, kc, 0:N], pc(gname, kc), rsx[:, 0:N], ALU.mult, ALU.mult,
                  [res_src, "rsx", "par"], [res_dst])

    b.load(xs[:, :, 0:256], memT_d, ["xs"], "xs")
    rmsnorm_cols(xs, 16, 256, "mng", memn, "xs", "memn", 1.0 / 2048)
    for ci in range(8):
        sl = ci % 2
        b.load(wst[:, sl], wkv_d[ci], [("wst", sl)], ("wst", sl))
        b.wcast(wbf, wst, sl, "wst", "wbf")
        if ci < 4:
            pi = b.nps()
            for kc in range(16):
                b.mm(pi, 128, 256, wbf[:, sl, kc, :], memn[:, kc, :], kc == 0, kc == 15, [("wbf", sl), "memn"])
            tq, tqr = b.nx("t")
            b.cp("act", tq[:, 0:256], b.ps[pi][:, 0:256], [("ps", pi)], [tqr])
            sq, sqr = b.nx("t")
            b.act(sq[:, 0:256], b.ps[pi][:, 0:256], AF.Square, [("ps", pi)], [sqr])
            rs, rsr = fm_rstd(b, [(sq[:, 0:256], sqr)], b.ones[:], 128, 256, 1.0 / 128, 1e-6, "ones")
            b.stt(kmem[:, ci, :], tq[:, 0:256], pc("mqg", 1), rs[:, 0:256], ALU.mult, ALU.mult, [tqr, rsr, "par"], ["kmem"])
        else:
            for tb in range(2):
                pi = b.nps()
                for kc in range(16):
                    b.mm(pi, 128, 128, memn[:, kc, tb * 128:(tb + 1) * 128], wbf[:, sl, kc, :], kc == 0, kc == 15,
                         [("wbf", sl), "memn"])
                b.cp("act", vmem[:, tb, (ci - 4) * 128:(ci - 3) * 128], b.ps[pi][:, 0:128], [("ps", pi)], ["vmem"])

    def wload(tile, ci):
        sl = (tile * NCH1 + ci) % 2
        b.load(wst[:, sl], w[ci], [("wst", sl)], ("wst", sl))

    def shift(pi, pih, rc, dst, dres):
        u, ur = b.nx("th")
        b.cp("act", u[:, 0:TT], b.ps[pi][:, :], [("ps", pi)], [ur])
        b.cp("act", u[:, TT:TH], b.ps[pih][:, 0:2], [("ps", pih)], [ur])
        s0 = pc("sh", 0 * 14 + rc); s1 = pc("sh", 1 * 14 + rc); s2 = pc("sh", 2 * 14 + rc)
        b.ts("dve", dst[:, :], u[:, 0:TT], s1, None, ALU.mult, None, [ur, "par"], [dres])
        b.stt(dst[:, 1:TT], u[:, 0:TT - 1], s0, dst[:, 1:TT], ALU.mult, ALU.add, [ur, "par", dres], [dres])
        b.stt(dst[:, 0:1], u[:, TT:TT + 1], s0, dst[:, 0:1], ALU.mult, ALU.add, [ur, "par", dres], [dres])
        b.stt(dst[:, 0:TT - 1], u[:, 1:TT], s2, dst[:, 0:TT - 1], ALU.mult, ALU.add, [ur, "par", dres], [dres])
        b.stt(dst[:, TT - 1:TT], u[:, TT + 1:TT + 2], s2, dst[:, TT - 1:TT], ALU.mult, ALU.add, [ur, "par", dres], [dres])

    def head_norm(pi, Pn, ones_ap, inv_n, gcol, dst_ap, dres, N=TT):
        tq, tqr = b.nx("t")
        b.cp("act", tq[0:Pn, 0:N], b.ps[pi][0:Pn, 0:N], [("ps", pi)], [tqr])
        sq, sqr = b.nx("t")
        b.act(sq[0:Pn, 0:N], b.ps[pi][0:Pn, 0:N], AF.Square, [("ps", pi)], [sqr])
        rs, rsr = fm_rstd(b, [(sq[0:Pn, 0:N], sqr)], ones_ap, Pn, N, inv_n, 1e-6, "ones")
        b.stt(dst_ap, tq[0:Pn, 0:N], gcol, rs[0:Pn, 0:N], ALU.mult, ALU.mult, [tqr, rsr, "par"], [dres])
        return tq, tqr, rs, rsr

    for tile in range(NT1):
        t0 = tile * TT
        b.load(xs[:], xT[tile], ["xs"], "xs")
        b.load(posi[:], pos[tile, :].partition_broadcast(64), ["posi"], "posi")
        wload(tile, 0)
        rmsnorm_cols(xs, 16, TH, "ng", hT, "xs", "hT", 1.0 / 2048)
        posf, posr = b.nx("th")
        b.cp("dve", posf[0:64, :], posi[:], ["posi"], [posr])
        for which, dstt, dres in ((0, ropS, "ropS"), (1, ropC, "ropC")):
            a, ar = b.nx("t")
            b.ts("dve", a[0:64, :], posf[0:64, 0:TT], pc("invf")[0:64, :], (math.pi / 2 if which else 0.0),
                 ALU.mult, ALU.add, [posr, "par"], [ar])
            y, yr = b.nx("t")
            b.ts("dve", y[0:64, :], a[0:64, :], 1.0 / (2 * math.pi), None, ALU.mult, None, [ar], [yr])
            ni = b.sb(f"ni{tile}{which}", [64, TT], I32)
            b.cp("dve", ni[:], y[0:64, :], [yr], [f"ni{tile}{which}"])
            nf, nfr = b.nx("t")
            b.cp("dve", nf[0:64, :], ni[:], [f"ni{tile}{which}"], [nfr])
            r, rr = b.nx("t")
            b.stt(r[0:64, :], nf[0:64, :], -2 * math.pi, a[0:64, :], ALU.mult, ALU.add, [nfr, ar], [rr])
            m, mr = b.nx("t")
            b.ts("dve", m[0:64, :], r[0:64, :], math.pi, -2 * math.pi, ALU.is_gt, ALU.mult, [rr], [mr])
            b.tt("dve", r[0:64, :], r[0:64, :], m[0:64, :], ALU.add, [rr, mr], [rr])
            b.ts("dve", m[0:64, :], r[0:64, :], -math.pi, 2 * math.pi, ALU.is_lt, ALU.mult, [rr], [mr])
            b.tt("dve", r[0:64, :], r[0:64, :], m[0:64, :], ALU.add, [rr, mr], [rr])
            if which == 0:
                sn, snr = b.nx("t")
                b.act(sn[0:64, :], r[0:64, :], AF.Sin, [rr], [snr])
                b.ts("dve", ropS[:], sn[0:64, :], pc("sgn")[0:64, :], None, ALU.mult, None, [snr, "par"], ["ropS"])
            else:
                b.act(ropC[:], r[0:64, :], AF.Sin, [rr], ["ropC"])

        def rope_out(t_tq, t_r, sw_tq, sw_r, rs, rsr, gi, dst_dram):
            a, ar = b.nx("t")
            b.stt(a[0:64, :], t_tq[0:64, :], pc("rpg", gi)[0:64, :], ropC[:], ALU.mult, ALU.mult, [t_r, "par", "ropC"], [ar])
            c, cr = b.nx("t")
            b.stt(c[0:64, :], sw_tq[0:64, :], pc("rpgs", gi)[0:64, :], ropS[:], ALU.mult, ALU.mult, [sw_r, "par", "ropS"], [cr])
            b.tt("dve", a[0:64, :], a[0:64, :], c[0:64, :], ALU.add, [ar, cr], [ar])
            b.tt("dve", a[0:64, :], a[0:64, :], rs[0:64, :], ALU.mult, [ar, rsr], [ar])
            b.store(dst_dram, a[0:64, :], [ar])

        rcur = {}
        for ci in range(NCH1):
            sl = (tile * NCH1 + ci) % 2
            if ci + 1 < NCH1:
                wload(tile, ci + 1)
            elif tile + 1 < NT1:
                wload(tile + 1, 0)
            b.wcast(wbf, wst, sl, "wst", "wbf")
            wr = ("wbf", sl)
            if ci >= 32:
                j = ci - 32
                for tb in range(4):
                    pi = b.nps()
                    for kc in range(16):
                        b.mm(pi, 128, 128, hT[:, kc, tb * 128:(tb + 1) * 128], wbf[:, sl, kc, :], kc == 0, kc == 15, [wr, "hT"])
                    o, orr = b.nx("t")
                    b.cp("act", o[:, 0:128], b.ps[pi][:, 0:128], [("ps", pi)], [orr])
                    b.store(o_dv[t0 + tb * 128:t0 + (tb + 1) * 128, j * 128:(j + 1) * 128], o[:, 0:128], [orr])
                continue
            if ci == 27:
                pis = []
                for hh in range(2):
                    pi = b.nps()
                    for kc in range(16):
                        b.mm(pi, 64, TT, wbf[:, sl, kc, hh * 64:(hh + 1) * 64], hT[:, kc, 0:TT], kc == 0, kc == 15, [wr, "hT"])
                    pis.append(pi)
                tq, tqr = b.nx("t")
                b.cp("act", tq[0:64, :], b.ps[pis[0]][0:64, :], [("ps", pis[0])], [tqr])
                sq, sqr = b.nx("t")
                b.act(sq[0:64, :], b.ps[pis[0]][0:64, :], AF.Square, [("ps", pis[0])], [sqr])
                sw, swr = b.nx("t")
                b.cp("act", sw[0:64, :], b.ps[pis[1]][0:64, :], [("ps", pis[1])], [swr])
                rs, rsr = fm_rstd(b, [(sq[0:64, :], sqr)], b.ones[0:64, 0:64], 64, TT, 1.0 / 64, 1e-6, "ones")
                rope_out(tq, tqr, sw, swr, rs, rsr, 1, o_mkr[:, t0:t0 + TT])
                continue
            pi = b.nps()
            for kc in range(16):
                b.mm(pi, 128, TT, wbf[:, sl, kc, :], hT[:, kc, 0:TT], kc == 0, kc == 15, [wr, "hT"])
            pih = None
            if ci < 14:
                pih = b.nps()
                for kc in range(16):
                    b.mm(pih, 128, 2, wbf[:, sl, kc, :], hT[:, kc, TT:TH], kc == 0, kc == 15, [wr, "hT"])
            if ci == 0:
                tmp, tr = b.nx("t")
                shift(pi, pih, 12, tmp, tr)
                b.act(twd[:], tmp[:], AF.Tanh, [tr], ["twd"])
            elif ci == 1:
                shift(pi, pih, 13, adl, "adl")
            elif ci < 14:
                c = (ci - 2) // 3
                kind = (ci - 2) % 3
                dst, dres = b.nx("rkv")
                shift(pi, pih, kind * 4 + c, dst, dres)
                rcur[kind] = (dst, dres)
                if kind == 0:
                    b.store(o_rw[0, c * 128:(c + 1) * 128, t0:t0 + TT], dst[:], [dres])
                if kind == 2:
                    r_t, r_r = rcur[0]
                    k_t, k_r = rcur[1]
                    v_t, v_r = rcur[2]
                    b.store(o_rw[1, c * 128:(c + 1) * 128, t0:t0 + TT], v_t[:], [v_r])
                    kr_, krr = b.nx("t")
                    b.ts("dve", kr_[:], k_t[:], pc("kk", c), None, ALU.mult, None, [k_r, "par"], [krr])
                    sq, sqr = b.nx("t")
                    b.act(sq[:], kr_[:], AF.Square, [krr], [sqr])
                    pj = b.nps()
                    b.mm(pj, 128, TT, b.blk[:], sq[:], True, True, [sqr, "blk"])
                    nr, nrr = b.nx("t")
                    b.act(nr[:], b.ps[pj][:, :], AF.Sqrt, [("ps", pj)], [nrr])
                    b.ts("dve", nr[:], nr[:], 1e-12, None, ALU.max, None, [nrr], [nrr])
                    b.recip(nr[:], nr[:], [nrr], [nrr])
                    kk_, kkr = b.nx("t")
                    b.tt("dve", kk_[:], kr_[:], nr[:], ALU.mult, [krr, nrr], [kkr])
                    b.store(o_rw[2, c * 128:(c + 1) * 128, t0:t0 + TT], kk_[:], [kkr])
                    kds = []
                    for d in range(2):
                        pw = b.nps()
                        b.mm(pw, 128, TT, wup[d * 64:(d + 1) * 64, c * 128:(c + 1) * 128], twd[d * 64:(d + 1) * 64, :],
                             True, True, ["wup", "twd"])
                        sg, sgr = b.nx("t")
                        b.act(sg[:], b.ps[pw][:, :], AF.Sigmoid, [("ps", pw), "par"], [sgr], bias=pc("w0", d * 4 + c))
                        dec, decr = b.nx("t")
                        b.act(dec[:], sg[:], AF.Copy, [sgr], [decr], scale=-math.exp(-0.5))
                        b.store(o_rw[5 + 3 * d, c * 128:(c + 1) * 128, t0:t0 + TT], dec[:], [decr])
                        pa = b.nps()
                        b.mm(pa, 128, TT, aup[d * 64:(d + 1) * 64, c * 128:(c + 1) * 128], adl[d * 64:(d + 1) * 64, :],
                             True, True, ["aup", "adl"])
                        a_, a_r = b.nx("t")
                        b.act(a_[:], b.ps[pa][:, :], AF.Sigmoid, [("ps", pa), "par"], [a_r], bias=pc("a0", d * 4 + c))
                        nb, nbr = b.nx("t")
                        b.stt(nb[:], kk_[:], -1.0, a_[:], ALU.mult, ALU.mult, [kkr, a_r], [nbr])
                        b.store(o_rw[3 + 3 * d, c * 128:(c + 1) * 128, t0:t0 + TT], nb[:], [nbr])
                        kd, kdr = b.nx("t")
                        b.ts("dve", kd[:], a_[:], pc("ka", c), pc("omka", c), ALU.mult, ALU.add, [a_r, "par"], [kdr])
                        b.tt("dve", kd[:], kd[:], k_t[:], ALU.mult, [kdr, k_r], [kdr])
                        b.store(o_rw[4 + 3 * d, c * 128:(c + 1) * 128, t0:t0 + TT], kd[:], [kdr])
                        kds.append((kd, kdr))
                    s_, s_r = b.nx("t")
                    b.tt("dve", s_[:], kds[0][0][:], kds[1][0][:], ALU.add, [kds[0][1], kds[1][1]], [s_r])
                    b.stt(s_[:], r_t[:], pc("rk", c), s_[:], ALU.mult, ALU.mult, [r_r, "par", s_r], [s_r])
                    pb_ = b.nps()
                    b.mm(pb_, 128, TT, b.blk[:], s_[:], True, True, [s_r, "blk"])
                    bo, bor = b.nx("t")
                    b.tt("dve", bo[:], v_t[:], b.ps[pb_][:, :], ALU.mult, [v_r, ("ps", pb_)], [bor])
                    b.store(o_bonus[c * 128:(c + 1) * 128, t0:t0 + TT], bo[:], [bor])
            elif ci < 22:
                isk = ci >= 18
                j = ci - (18 if isk else 14)
                o, orr = b.nx("t")
                head_norm(pi, 128, b.blk[:], 1.0 / 64, pc("dkg" if isk else "dqg"), o[:], orr)
                b.store((o_dk if isk else o_dq)[j * 128:(j + 1) * 128, t0:t0 + TT], o[:], [orr])
            elif ci < 25:
                j = ci - 22
                b.cp("act", ql[:, j, :], b.ps[pi][:, :], [("ps", pi)], ["ql"])
                if j == 2:
                    rmsnorm_cols(ql, 3, TT, "qlg", qln, "ql", "qln", 1.0 / 384)
                    for h in range(4):
                        pn = b.nps()
                        for kc in range(3):
                            b.mm(pn, 128, TT, wuq[:, kc, h * 192:h * 192 + 128], qln[:, kc, :], kc == 0, kc == 2, ["wuq", "qln"])
                        o, orr = b.nx("t")
                        head_norm(pn, 128, b.ones[:], 1.0 / 128, pc("npg", 0), o[:], orr)
                        b.store(o_mqn[h * 128:(h + 1) * 128, t0:t0 + TT], o[:], [orr])
                        pr = b.nps()
                        for kc in range(3):
                            b.mm(pr, 64, TT, wuq[:, kc, h * 192 + 128:h * 192 + 192], qln[:, kc, :], kc == 0, kc == 2, ["wuq", "qln"])
                        psw = b.nps()
                        for kc in range(3):
                            b.mm(psw, 64, TT, wuqs[:, kc, h * 64:(h + 1) * 64], qln[:, kc, :], kc == 0, kc == 2, ["wuqs", "qln"])
                        tq, tqr = b.nx("t")
                        b.cp("act", tq[0:64, :], b.ps[pr][0:64, :], [("ps", pr)], [tqr])
                        sq, sqr = b.nx("t")
                        b.act(sq[0:64, :], b.ps[pr][0:64, :], AF.Square, [("ps", pr)], [sqr])
                        sw, swr = b.nx("t")
                        b.cp("act", sw[0:64, :], b.ps[psw][0:64, :], [("ps", psw)], [swr])
                        rs, rsr = fm_rstd(b, [(sq[0:64, :], sqr)], b.ones[0:64, 0:64], 64, TT, 1.0 / 64, 1e-6, "ones")
                        rope_out(tq, tqr, sw, swr, rs, rsr, 0, o_mqr[h * 64:(h + 1) * 64, t0:t0 + TT])
            elif ci < 27:
                j = ci - 25
                b.cp("act", kvl[:, j, :], b.ps[pi][:, :], [("ps", pi)], ["kvl"])
                if j == 1:
                    rmsnorm_cols(kvl, 2, TT, "kvg", kvn, "kvl", "kvn", 1.0 / 256)
                    for h in range(4):
                        pn = b.nps()
                        for kc in range(2):
                            b.mm(pn, 128, TT, wukvk[:, kc, h * 128:(h + 1) * 128], kvn[:, kc, :], kc == 0, kc == 1, ["wukvk", "kvn"])
                        o, orr = b.nx("t")
                        head_norm(pn, 128, b.ones[:], 1.0 / 128, pc("npg", 1), o[:], orr)
                        b.store(o_mkn[h * 128:(h + 1) * 128, t0:t0 + TT], o[:], [orr])
                    for tb in range(4):
                        pv = b.nps()
                        for kc in range(2):
                            b.mm(pv, 128, 512, kvn[:, kc, tb * 128:(tb + 1) * 128], wukvv[:, kc, :], kc == 0, kc == 1, ["wukvv", "kvn"])
                        o, orr = b.nx("t")
                        b.cp("act", o[:], b.ps[pv][:, :], [("ps", pv)], [orr])
                        b.store(o_mv[t0 + tb * 128:t0 + (tb + 1) * 128, :], o[:], [orr])
            else:
                j = ci - 28
                qb, qbr = b.nx("bf")
                head_norm(pi, 128, b.ones[:], 1.0 / 128, pc("mqg", 0), qb[:], qbr)
                b.psrot = [0, 1, 2, 3]
                po, pz = attn_core(b, [(qb[:], qbr)],
                                   lambda kt, j=j: [(kmem[:, j, kt * 128:(kt + 1) * 128], "kmem")],
                                   lambda kt, j=j: (vmem[:, kt, j * 128:(j + 1) * 128], "vmem"),
                                   2, 128 ** -0.5, 128)
                b.psrot = list(range(8))
                rz, rzr = b.nx("t")
                b.recip(rz[:], b.ps[pz][:, :], [("ps", pz)], [rzr])
                o, orr = b.nx("t")
                b.tt("dve", o[:], b.ps[po][:, :], rz[:], ALU.mult, [("ps", po), rzr], [orr])
                b.store(o_yd[j * 128:(j + 1) * 128, t0:t0 + TT], o[:], [orr])
    return b.finish()


def _fm(a, nk):
    return np.ascontiguousarray(a.reshape(nk, 128, a.shape[1]).transpose(1, 0, 2))


def _col(par, name, arr, i=0):
    par[:arr.shape[0], PC[name] + i] = arr


def prep_p1(inp, l, x_cur):
    f = np.float32
    w_in = inp["w_in"][l]
    chunks = []
    for col0, n in P1COLS:
        if col0 == "krope":
            idx = [KROPE0 + j for j in range(64)] + [KROPE0 + (j + 32) % 64 for j in range(64)]
            W = w_in[:, idx]
        else:
            W = w_in[:, col0:col0 + 128]
        chunks.append(_fm(W, 16))
    w = np.stack(chunks)
    par = np.zeros((128, NPAR), f)
    par[:, PC["ng"]:PC["ng"] + 16] = inp["norm_g"][l].reshape(16, 128).T
    sh = inp["rw_shift"][l]
    for j in range(3):
        for rc in range(14):
            _col(par, "sh", sh[j, rc * 128:(rc + 1) * 128], j * 14 + rc)
    for d in range(2):
        for c in range(4):
            _col(par, "w0", inp["rw_w0"][l][d, c * 128:(c + 1) * 128], d * 4 + c)
            _col(par, "a0", inp["rw_a0"][l][d, c * 128:(c + 1) * 128], d * 4 + c)
    rk = inp["rw_r_k"][l].reshape(512)
    for c in range(4):
        _col(par, "kk", inp["rw_k_k"][l][c * 128:(c + 1) * 128], c)
        _col(par, "ka", inp["rw_k_a"][l][c * 128:(c + 1) * 128], c)
        _col(par, "rk", rk[c * 128:(c + 1) * 128], c)
    _col(par, "dqg", np.tile(inp["diff_qk_g"][l][0], 2))
    _col(par, "dkg", np.tile(inp["diff_qk_g"][l][1], 2))
    par[:, PC["qlg"]:PC["qlg"] + 3] = inp["mla_q_lat_g"][l].reshape(3, 128).T
    par[:, PC["kvg"]:PC["kvg"] + 2] = inp["mla_kv_lat_g"][l].reshape(2, 128).T
    par[:, PC["npg"]:PC["npg"] + 2] = inp["mla_nope_g"][l].T
    for gi in range(2):
        g = inp["mla_rope_g"][l][gi]
        _col(par, "rpg", g, gi)
        _col(par, "rpgs", np.concatenate([g[32:], g[:32]]), gi)
    par[:, PC["mqg"]:PC["mqg"] + 2] = inp["mem_qk_g"][l].T
    par[:, PC["mng"]:PC["mng"] + 16] = inp["mem_norm_g"][l].reshape(16, 128).T
    invf = (10000.0 ** (-np.arange(0, 64, 2, dtype=np.float32) / 64)).astype(f)
    _col(par, "invf", np.tile(invf, 2))
    _col(par, "sgn", np.concatenate([-np.ones(32, f), np.ones(32, f)]))
    wuq = inp["mla_w_uq"][l]
    sw_idx = [h * 192 + 128 + (j + 32) % 64 for h in range(4) for j in range(64)]
    wukv = inp["mla_w_ukv"][l].reshape(256, 4, 256)
    common = {
        "par": par, "w": w,
        "wup": np.ascontiguousarray(inp["rw_w_up"][l].reshape(128, 512)),
        "aup": np.ascontiguousarray(inp["rw_a_up"][l].reshape(128, 512)),
        "wuq": _fm(wuq, 3), "wuqs": _fm(wuq[:, sw_idx], 3),
        "wukvk": _fm(np.ascontiguousarray(wukv[:, :, :128]).reshape(256, 512), 2),
        "wukvv": _fm(np.ascontiguousarray(wukv[:, :, 128:]).reshape(256, 512), 2),
        "memT": np.ascontiguousarray(inp["mem"][0].reshape(256, 16, 128).transpose(2, 1, 0)),
        "wkv": np.stack([_fm(inp["mem_w_kv"][l][:, ci * 128:(ci + 1) * 128], 16) for ci in range(8)]),
    }
    S = x_cur.shape[0]
    xpad = np.concatenate([np.zeros((1, 2048), f), x_cur, np.zeros((1, 2048), f)], 0)
    posv = inp["positions"][0]
    maps = []
    for c in range(NCORES):
        xt = np.empty((NT1, 128, 16, TH), f)
        pp = np.zeros((NT1, TH), np.int32)
        for t in range(NT1):
            n0 = c * (NT1 * TT) + t * TT
            xe = np.concatenate([xpad[n0 + 1:n0 + 1 + TT], xpad[n0:n0 + 1], xpad[n0 + 1 + TT:n0 + 2 + TT]], 0)
            xt[t] = xe.reshape(TH, 16, 128).transpose(2, 1, 0)
            pp[t, :TT] = posv[n0:n0 + TT]
        m = dict(common)
        m["xT"] = xt
        m["pos"] = pp
        maps.append(m)
    return maps


_NC_CACHE = {}


def _run(name, builder, maps):
    if name not in _NC_CACHE:
        _NC_CACHE[name] = builder()
    res = run_bass_kernel_spmd(_NC_CACHE[name], maps, core_ids=list(range(NCORES)))
    return res.results


def _cat(results, key, axis):
    return np.concatenate([r[key] for r in results], axis=axis)


SEQ = 8192
SC_CH = 32
SC_NV = 5


def build_scan(T=SEQ):
    b = B()
    P = b.P
    bc = b.din("bc", [2, T, SC_NV * 64])
    vcol = b.din("vcol", [128, T])
    y = b.dout("y", [128, T])
    nch = T // SC_CH
    W = SC_NV * 64
    bcb = b.sb("bcb", [128, 2, SC_CH * W])
    vc = b.sb("vc", [128, T])
    yt = b.sb("yt", [128, T])
    S = b.sb("S", [128, 64])
    Sd = b.sb("Sd", [128, 64])
    sa = b.sb("sa", [128, 1])
    b.ring("ja", 2, [128, 64])
    b.ring("jb", 2, [128, 64])
    b.load(vc[:], vcol, ["vc"], "vc")
    b.memset("dve", S[:], 0.0, ["S"])

    def load(c):
        sl = c % 2
        for pr in range(2):
            src = bc[pr, c * SC_CH:(c + 1) * SC_CH, :].rearrange("t f -> (t f)").partition_broadcast(64)
            b.load(bcb[pr * 64:(pr + 1) * 64, sl, :], src, [("bcb", sl, pr)], ("bc", sl, pr))

    load(0)
    for c in range(nch):
        if c + 1 < nch:
            load(c + 1)
        sl = c % 2
        rd = [("bcb", sl, 0), ("bcb", sl, 1)]
        for s in range(SC_CH):
            t = c * SC_CH + s
            o = s * W
            kap = bcb[:, sl, o:o + 64]
            nb = bcb[:, sl, o + 64:o + 128]
            kd = bcb[:, sl, o + 128:o + 192]
            rt = bcb[:, sl, o + 192:o + 256]
            dec = bcb[:, sl, o + 256:o + 320]
            ja, jar = b.nx("ja")
            P.op("dve", lambda e, kap=kap, ja=ja: e.scalar_tensor_tensor(
                out=ja[:], in0=S[:], scalar=1.0, in1=kap, op0=ALU.mult, op1=ALU.mult, accum_out=sa[:]),
                reads=rd + ["S"], writes=[jar, "sa"])
            b.tt("dve", Sd[:], S[:], dec, ALU.mult, rd + ["S"], ["Sd"])
            b.stt(Sd[:], nb, sa[:, 0:1], Sd[:], ALU.mult, ALU.add, rd + ["Sd", "sa"], ["Sd"])
            b.stt(S[:], kd, vc[:, t:t + 1], Sd[:], ALU.mult, ALU.add, rd + ["Sd", "vc"], ["S"])
            jb, jbr = b.nx("jb")
            P.op("dve", lambda e, rt=rt, jb=jb, t=t: e.scalar_tensor_tensor(
                out=jb[:], in0=S[:], scalar=1.0, in1=rt, op0=ALU.mult, op1=ALU.mult, accum_out=yt[:, t:t + 1]),
                reads=rd + ["S"], writes=[jbr, "yt"])
    b.store(y, yt[:], ["yt"])
    return b.finish()


def prep_scan(o_rw, T=SEQ):
    maps = []
    for h in range(NCORES):
        sl = slice(h * 64, (h + 1) * 64)
        bc = np.empty((2, T, SC_NV * 64), np.float32)
        for d in range(2):
            parts = [o_rw[2][sl], o_rw[3 + 3 * d][sl], o_rw[4 + 3 * d][sl], o_rw[0][sl], o_rw[5 + 3 * d][sl]]
            a = np.concatenate([p.T for p in parts], axis=1)
            bc[d] = a if d == 0 else a[::-1]
        v = o_rw[1][sl]
        vcol = np.concatenate([v, v[:, ::-1]], 0)
        maps.append({"bc": bc, "vcol": np.ascontiguousarray(vcol)})
    return maps


def post_scan(results):
    yf = np.concatenate([r["y"][0:64] for r in results], 0)
    yb = np.concatenate([r["y"][64:128][:, ::-1] for r in results], 0)
    return np.ascontiguousarray(yf), np.ascontiguousarray(yb)


NQ = SEQ // 2
NKT = SEQ // 128


def _t5_breaks():
    nb, max_exact = 16, 8
    n = np.arange(0, 1024, dtype=np.int32)
    n_f = np.maximum(n, max_exact).astype(np.float32)
    large = max_exact + (np.log(n_f / np.float32(max_exact)) / np.float32(math.log(128 / max_exact))
                         * np.float32(nb - max_exact)).astype(np.int32)
    large = np.minimum(large, nb - 1)
    f = np.where(n < max_exact, n, large)
    rels = np.arange(-1023, 1024)
    bk = np.where(rels > 0, 16, 0) + f[np.abs(rels)]
    order = [int(bk[0])]
    breaks = []
    for i in range(1, len(rels)):
        if bk[i] != bk[i - 1]:
            order.append(int(bk[i]))
            breaks.append(int(rels[i]))
    return order, breaks


T5_ORDER, T5_BREAKS = _t5_breaks()
NBK = len(T5_ORDER)


def build_attn(kind, b=None, pre=""):
    own = b is None
    if own:
        b = B()
        setup_consts(b)
        b.shared = {}
    P = b.P
    diff = kind == "diff"
    b.zmod = 1 if diff else 3
    qa_d = b.din(pre + "qa", [128, NQ])
    ka_d = b.din(pre + "ka", [128, SEQ])
    v_d = b.din(pre + "v", [128, NKT, 128])
    if diff:
        tab_d = b.din(pre + "tab", [1, NBK])
        lq_d = b.din(pre + "lq", [1, 256])
        cst_d = b.din(pre + "cst", [128, 2])
    else:
        qb_d = b.din(pre + "qb", [64, NQ])
        kb_d = b.din(pre + "kb", [64, SEQ])
    o_d = b.dout(pre + "o", [128, NQ])
    rka, rqa, rvv = "ka", "qa", "vv"
    if "rings" not in b.shared:
        b.shared[rka] = b.sb(rka, [128, SEQ], BF16)
        b.shared[rqa] = b.sb(rqa, [128, NQ], BF16)
        b.shared[rvv] = b.sb(rvv, [128, NKT, 128], BF16)
        b.shared["rings"] = True
        b.ring("stg", 2, [128, 2048])
        b.ring("t", 6, [128, 512])
        b.ring("pt", 6, [128, 512], BF16)
        b.ring("zacc", 2, [128, 512])
        b.ring("zacc2", 2, [128, 512])
        b.ring("o", 2, [128, 512])
    ka, qa, vv = b.shared[rka], b.shared[rqa], b.shared[rvv]

    def load_cast(dst, src, Pn, n, res):
        for i in range(0, n, 2048):
            st, sr = b.nx("stg")
            b.load(st[0:Pn, :], src[:, i:i + 2048], [sr], sr)
            b.cp("pool", dst[0:Pn, i:i + 2048], st[0:Pn, :], [sr], [res])

    load_cast(ka, ka_d, 128, SEQ, rka)
    load_cast(qa, qa_d, 128, NQ, rqa)
    load_cast(vv[:].rearrange("p a b -> p (a b)"), v_d.rearrange("p a b -> p (a b)"), 128, NKT * 128, rvv)
    if not diff:
        kb = b.sb("kb", [64, SEQ], BF16)
        qb = b.sb("qb", [64, NQ], BF16)
        load_cast(kb, kb_d, 64, SEQ, "kb")
        load_cast(qb, qb_d, 64, NQ, "qb")
    else:
        lq = b.sb("lq", [128, 256])
        cst = b.sb("cst", [128, 2])
        tab = b.sb("tab", [128, NBK])
        b.load(lq[:], lq_d[0, :].partition_broadcast(128), ["lq"], "lq")
        b.load(cst[:], cst_d, ["cst"], "cst")
        b.load(tab[:], tab_d[0, :].partition_broadcast(128), ["tab"], "tab")
        pr = b.sb("lpr", [128, 128])
        b.tt("dve", pr[:, 0:64], lq[:, 0:64], lq[:, 64:128], ALU.mult, ["lq"], ["lpr"])
        b.tt("dve", pr[:, 64:128], lq[:, 128:192], lq[:, 192:256], ALU.mult, ["lq"], ["lpr"])
        ls = b.sb("ls", [128, 4])
        P.op("dve", lambda e: e.tensor_reduce(out=ls[:, 0:1], in_=pr[:, 0:64], axis=AX.X, op=ALU.add), reads=["lpr"], writes=["ls"])
        P.op("dve", lambda e: e.tensor_reduce(out=ls[:, 1:2], in_=pr[:, 64:128], axis=AX.X, op=ALU.add), reads=["lpr"], writes=["ls"])
        b.act(ls[:, 0:2], ls[:, 0:2], AF.Exp, ["ls"], ["ls"])
        b.tt("dve", ls[:, 2:3], ls[:, 0:1], ls[:, 1:2], ALU.subtract, ["ls"], ["ls"])
        b.tt("dve", ls[:, 2:3], ls[:, 2:3], cst[:, 0:1], ALU.add, ["ls", "cst"], ["ls"])
        b.ts("dve", ls[:, 3:4], ls[:, 2:3], -1.0, None, ALU.mult, None, ["ls"], ["ls"])
        dl = b.sb("dl", [128, NBK])
        b.tt("dve", dl[:, 1:NBK], tab[:, 1:NBK], tab[:, 0:NBK - 1], ALU.subtract, ["tab"], ["dl"])
        SW = 1152
        reli = b.sb("reli", [128, SW], I32)
        relf = b.sb("relf", [128, SW])
        P.op("pool", lambda e: e.iota(reli[:], [[-1, SW]], base=512, channel_multiplier=1), writes=["reli"])
        b.cp("dve", relf[:], reli[:], ["reli"], ["relf"])
        strip = b.sb("strip", [128, SW])
        b.ring("stmp", 2, [128, SW])
        for k in range(1, NBK):
            tmp, tr = b.nx("stmp")
            b.ts("dve", tmp[:], relf[:], float(T5_BREAKS[k - 1]), dl[:, k:k + 1], ALU.is_ge, ALU.mult, ["relf", "dl"], [tr])
            if k == 1:
                b.ts("pool", strip[:], tmp[:], tab[:, 0:1], None, ALU.add, None, [tr, "tab"], ["strip"])
            else:
                b.tt("pool", strip[:], strip[:], tmp[:], ALU.add, [tr, "strip"], ["strip"])

    scale = (64 ** -0.5) if diff else (192 ** -0.5)
    b.psrot = [0, 1, 2, 3]
    for qt in range(NQ // 512):
        qs = slice(qt * 512, (qt + 1) * 512)
        res_list = []
        for s in range(2 if diff else 1):
            if diff:
                ps_ = slice(s * 64, (s + 1) * 64)
                qlist = [(qa[ps_, qs], rqa)]
                kts = lambda kt, ps_=ps_: [(ka[ps_, kt * 128:(kt + 1) * 128], rka)]

                def bias_fn(kt, qt=qt):
                    dk = kt - 4 * qt
                    if dk < -1:
                        return ("const", tab[:, 0:1])
                    if dk > 4:
                        return ("const", tab[:, NBK - 1:NBK])
                    return ("tile", (strip[:, (4 - dk) * 128:(4 - dk) * 128 + 512], "strip"))
            else:
                qlist = [(qa[:, qs], rqa), (qb[:, qs], "qb")]
                kts = lambda kt: [(ka[:, kt * 128:(kt + 1) * 128], rka), (kb[:, kt * 128:(kt + 1) * 128], "kb")]
            vts = lambda kt: (vv[:, kt, :], rvv)
            if diff:
                res_list.append(attn_core(b, qlist, kts, vts, NKT, scale, 128, bias_fn=bias_fn))
            else:
                res_list.append(attn_core(b, qlist, kts, vts, NKT, scale, 128))
        po, pz = res_list[0]
        rz, rzr = b.nx("t")
        b.recip(rz[:], b.ps[pz][:, :], [("ps", pz)], [rzr])
        o, orr = b.nx("o")
        b.tt("dve", o[:], b.ps[po][:, :], rz[:], ALU.mult, [("ps", po), rzr], [orr])
        if diff:
            po2, pz2 = res_list[1]
            rz2, rz2r = b.nx("t")
            b.recip(rz2[:], b.ps[pz2][:, :], [("ps", pz2)], [rz2r])
            o2, o2r = b.nx("t")
            b.tt("dve", o2[:], b.ps[po2][:, :], rz2[:], ALU.mult, [("ps", po2), rz2r], [o2r])
            b.stt(o[:], o2[:], ls[:, 3:4], o[:], ALU.mult, ALU.add, [o2r, "ls", orr], [orr])
        b.store(o_d[:, qs], o[:], [orr])
    if own:
        return b.finish()
    return None


def build_attn2():
    b = B()
    setup_consts(b)
    b.shared = {}
    build_attn("diff", b, "d_")
    build_attn("mla", b, "m_")
    return b.finish()


def _vtiles(v):
    return np.ascontiguousarray(v.reshape(-1, 128, 128).transpose(1, 0, 2))


def prep_attn_diff(inp, l, o_dq, o_dk, o_dv):
    lam_init = 0.8 - 0.6 * math.exp(-0.3 * l)
    maps = []
    for c in range(NCORES):
        h, half = c // 2, c % 2
        rows = slice(h * 128, (h + 1) * 128)
        q, k, v = o_dq[rows], o_dk[rows], o_dv[:, rows]
        tab = inp["rel_bias"][T5_ORDER, h]
        if half == 1:
            q, k, v, tab = q[:, ::-1], k[:, ::-1], v[::-1], tab[::-1]
        cst = np.zeros((128, 2), np.float32)
        cst[:, 0] = lam_init
        maps.append({"qa": np.ascontiguousarray(q[:, :NQ]), "ka": np.ascontiguousarray(k), "v": _vtiles(np.ascontiguousarray(v)),
                     "tab": np.ascontiguousarray(tab.reshape(1, NBK)).astype(np.float32),
                     "lq": np.ascontiguousarray(inp["diff_lambda"][l].reshape(1, 256)), "cst": cst})
    return maps


def post_attn_diff(results):
    out = np.empty((512, SEQ), np.float32)
    for c in range(NCORES):
        h, half = c // 2, c % 2
        o = results[c]["o"]
        if half == 0:
            out[h * 128:(h + 1) * 128, :NQ] = o
        else:
            out[h * 128:(h + 1) * 128, NQ:] = o[:, ::-1]
    return out


def prep_attn_mla(o_mqn, o_mqr, o_mkn, o_mkr, o_mv):
    maps = []
    for c in range(NCORES):
        h, half = c // 2, c % 2
        qs = slice(half * NQ, (half + 1) * NQ)
        maps.append({"qa": np.ascontiguousarray(o_mqn[h * 128:(h + 1) * 128, qs]),
                     "qb": np.ascontiguousarray(o_mqr[h * 64:(h + 1) * 64, qs]),
                     "ka": np.ascontiguousarray(o_mkn[h * 128:(h + 1) * 128]),
                     "kb": np.ascontiguousarray(o_mkr),
                     "v": _vtiles(np.ascontiguousarray(o_mv[:, h * 128:(h + 1) * 128]))})
    return maps


def post_attn_mla(results):
    out = np.empty((512, SEQ), np.float32)
    for c in range(NCORES):
        h, half = c // 2, c % 2
        out[h * 128:(h + 1) * 128, half * NQ:(half + 1) * NQ] = results[c]["o"]
    return out


PC3 = {}
_c = 0
for _n, _w in (("ng", 16), ("gng", 4), ("gnb", 4), ("subg", 1), ("lamf", 1)):
    PC3[_n] = _c
    _c += _w
NPAR3 = _c
NCH3 = 16 + 16 * 4 + 16


def build_p3():
    b = B()
    P = b.P
    xT = b.din("xT", [NT1, 128, 16, TT])
    par_d = b.din("par", [128, NPAR3])
    wA = b.din("wA", [NCH3, 128, 16, 128])
    wB = b.din("wB", [16, 128, 16, 128])
    br_d = b.din("br", [NT1, 128, 6, 4, TT])
    o_x = b.dout("o_x", [NT1, 128, 16, TT])
    setup_consts(b)
    par = b.sb("par", [128, NPAR3])
    b.load(par[:], par_d, ["par"], "par")
    pc = lambda n, i=0: par[:, PC3[n] + i:PC3[n] + i + 1]
    xz = b.sb("xz", [128, 16, TT])
    xs = xz
    zT = xz[:].rearrange("p a t -> p (a t)").bitcast(BF16).rearrange("p (n a t) -> p n a t", n=NT1, a=16)
    hT = b.sb("hT", [128, NT1, 16, TT], BF16)
    yg = b.sb("yg", [128, NT1, 16, TT], BF16)
    b.ring("brk", 2, [128, 6, TT])
    wstA = b.sb("wstA", [128, 2, 16, 128])
    wbfA = b.sb("wbfA", [128, 2, 16, 128], BF16)
    wstB = b.sb("wstB", [128, 1, 16, 128])
    wbfB = b.sb("wbfB", [128, 2, 16, 128], BF16)
    rsx = b.sb("rsx", [128, TT])
    zacc = b.sb("zacc", [128, NT1, TT])
    b.ring("t", 10, [128, TT])
    cntA = [0]

    def loadA(ci):
        sl = cntA[0] % 2
        cntA[0] += 1
        b.load(wstA[:, sl], wA[ci], [("wstA", sl)], ("wstA", sl))
        return sl

    ysrc = [0, 3, 4, 5]
    for tile in range(NT1):
        b.load(xs[:], xT[tile], ["xz"], "xz")
        pendA = loadA(0)
        pi = b.nps()
        for kc in range(16):
            sq, sqr = b.nx("t")
            b.act(sq[:], xs[:, kc, :], AF.Square, ["xz"], [sqr])
            b.mm(pi, 128, TT, b.ones[:], sq[:], kc == 0, kc == 15, [sqr, "ones"])
        ln, lnr = b.nx("t")
        b.act(ln[:], b.ps[pi][:, :], AF.Ln, [("ps", pi)], [lnr], bias=b.epsc[1e-6][:], scale=1.0 / 2048)
        b.act(rsx[:], ln[:], AF.Exp, [lnr], ["rsx"], scale=-0.5)
        for kc in range(16):
            b.stt(hT[:, tile, kc, :], xs[:, kc, :], pc("ng", kc), rsx[:], ALU.mult, ALU.mult, ["xz", "rsx", "par"], [("hT", tile)])
        for g in range(16):
            sl = pendA
            if g + 1 < 16:
                pendA = loadA(g + 1)
            b.wcast(wbfA, wstA, sl, "wstA", "wbfA")
            pi = b.nps()
            for kc in range(16):
                b.mm(pi, 128, TT, wbfA[:, sl, kc, :], hT[:, tile, kc, :], kc == 0, kc == 15, [("wbfA", sl), ("hT", tile)])
            b.act(yg[:, tile, g, :], b.ps[pi][:, :], AF.Silu, [("ps", pi)], [("yg", tile, g)])
        for kc in range(4):
            brk, brr = b.nx("brk")
            b.load(brk[:], br_d[tile, :, :, kc, :], [brr], brr)
            ys, ysr = b.nx("t")
            b.tt("pool", ys[:], brk[:, 0, :], brk[:, 1, :], ALU.add, [brr], [ysr])
            sq, sqr = b.nx("t")
            b.act(sq[:], ys[:], AF.Square, [ysr], [sqr])
            pm = b.nps()
            b.mm(pm, 128, TT, b.blk[:], ys[:], True, True, [ysr, "blk"])
            pe2 = b.nps()
            b.mm(pe2, 128, TT, b.blk[:], sq[:], True, True, [sqr, "blk"])
            mean, mr = b.nx("t")
            b.act(mean[:], b.ps[pm][:, :], AF.Copy, [("ps", pm)], [mr], scale=1.0 / 64)
            msq, msr = b.nx("t")
            b.act(msq[:], mean[:], AF.Square, [mr], [msr])
            var, vr = b.nx("t")
            b.stt(var[:], b.ps[pe2][:, :], 1.0 / 64, msq[:], ALU.mult, ALU.subtract, [("ps", pe2), msr], [vr])
            b.act(var[:], var[:], AF.Ln, [vr], [vr], bias=b.epsc[64e-5][:])
            b.act(var[:], var[:], AF.Exp, [vr], [vr], scale=-0.5)
            b.tt("pool", ys[:], ys[:], mean[:], ALU.subtract, [ysr, mr], [ysr])
            b.stt(ys[:], ys[:], pc("gng", kc), var[:], ALU.mult, ALU.mult, [ysr, "par", vr], [ysr])
            b.stt(ys[:], ys[:], pc("gnb", kc), brk[:, 2, :], ALU.add, ALU.add, [ysr, "par", brr], [ysr])
            b.tt("dve", yg[:, tile, 0 * 4 + kc, :], yg[:, tile, 0 * 4 + kc, :], ys[:], ALU.mult, [ysr, ("yg", tile, kc)], [("yg", tile, kc)])
            sq2, sq2r = b.nx("t")
            b.act(sq2[:], brk[:, 3, :], AF.Square, [brr], [sq2r])
            rs, rsr = fm_rstd(b, [(sq2[:], sq2r)], b.ones[:], 128, TT, 1.0 / 128, 1e-6, "ones")
            yb_, ybr_ = b.nx("t")
            b.stt(yb_[:], brk[:, 3, :], pc("subg"), rs[:], ALU.mult, ALU.mult, [brr, "par", rsr], [ybr_])
            b.stt(yg[:, tile, 4 + kc, :], yb_[:], pc("lamf"), yg[:, tile, 4 + kc, :], ALU.mult, ALU.mult,
                  [ybr_, "par", ("yg", tile, 4 + kc)], [("yg", tile, 4 + kc)])
            b.tt("pool", yg[:, tile, 8 + kc, :], yg[:, tile, 8 + kc, :], brk[:, 4, :], ALU.mult, [brr, ("yg", tile, 8 + kc)], [("yg", tile, 8 + kc)])
            b.tt("pool", yg[:, tile, 12 + kc, :], yg[:, tile, 12 + kc, :], brk[:, 5, :], ALU.mult, [brr, ("yg", tile, 12 + kc)], [("yg", tile, 12 + kc)])
    ygr = lambda t: [("yg", t, g) for g in range(16)]
    nxt = 16
    pendA = loadA(nxt)
    nxt += 1
    for oc in range(16):
        slB = oc % 2
        b.load(wstB[:, 0], wB[oc], [("wstB", 0)], ("wstB", 0))
        b.wcast(wbfB, wstB, 0, "wstB", "wbfB", dsl=slB)
        for bi in range(4):
            sl = pendA
            if nxt < NCH3:
                pendA = loadA(nxt)
                nxt += 1
            b.wcast(wbfA, wstA, sl, "wstA", "wbfA")
            for tile in range(NT1):
                pm = b.nps()
                for kc in range(16):
                    b.mm(pm, 128, TT, wbfA[:, sl, kc, :], hT[:, tile, kc, :], kc == 0, kc == 15, [("wbfA", sl), ("hT", tile)])
                pb = b.nps()
                for kc in range(4):
                    b.mm(pb, 128, TT, wbfB[:, slB, bi * 4 + kc, :], yg[:, tile, bi * 4 + kc, :], kc == 0, kc == 3,
                         [("wbfB", slB), ("yg", tile, bi * 4 + kc)])
                sg, sgr = b.nx("t")
                b.act(sg[:], b.ps[pm][:, :], AF.Sigmoid, [("ps", pm)], [sgr])
                if bi == 0:
                    b.tt("dve", zacc[:, tile, :], b.ps[pb][:, :], sg[:], ALU.mult, [("ps", pb), sgr], [("zacc", tile)])
                else:
                    tmp, tr = b.nx("t")
                    b.tt("dve", tmp[:], b.ps[pb][:, :], sg[:], ALU.mult, [("ps", pb), sgr], [tr])
                    if bi < 3:
                        b.tt("pool", zacc[:, tile, :], zacc[:, tile, :], tmp[:], ALU.add, [("zacc", tile), tr], [("zacc", tile)])
                    else:
                        b.tt("pool", zT[:, tile, oc, :], zacc[:, tile, :], tmp[:], ALU.add, [("zacc", tile), tr], ["xz"])
    for oc in range(16):
        sl = pendA
        if nxt < NCH3:
            pendA = loadA(nxt)
            nxt += 1
        b.wcast(wbfA, wstA, sl, "wstA", "wbfA")
        for tile in range(NT1):
            po = b.nps()
            for kc in range(16):
                b.mm(po, 128, TT, wbfA[:, sl, kc, :], zT[:, tile, kc, :], kc == 0, kc == 15, [("wbfA", sl), "xz"])
            xr, xrr = b.nx("t")
            b.load(xr[:], xT[tile, :, oc, :], [xrr], xrr)
            xo, xor_ = b.nx("t")
            b.tt("dve", xo[:], b.ps[po][:, :], xr[:], ALU.add, [("ps", po), xrr], [xor_])
            b.store(o_x[tile, :, oc, :], xo[:], [xor_])
    return b.finish()


GM0 = 1792 + 1536 + 384 + 256 + 64 + 512


def prep_p3(inp, l, x_cur, ysf, ysb, bonus, ybr, yc, yd):
    f = np.float32
    w_in = inp["w_in"][l]
    chunks = []
    for g in range(16):
        chunks.append(_fm(w_in[:, GM0 + g * 128:GM0 + (g + 1) * 128], 16))
    M0 = GM0 + 2048
    for oc in range(16):
        for bi in range(4):
            c0 = M0 + bi * 2048 + oc * 128
            chunks.append(_fm(w_in[:, c0:c0 + 128], 16))
    for oc in range(16):
        chunks.append(_fm(inp["w_out"][l][:, oc * 128:(oc + 1) * 128], 16))
    wA = np.stack(chunks)
    wb = inp["w_branch"][l].reshape(2048, 2048)
    wB = np.stack([_fm(wb[:, oc * 128:(oc + 1) * 128], 16) for oc in range(16)])
    par = np.zeros((128, NPAR3), f)
    par[:, PC3["ng"]:PC3["ng"] + 16] = inp["norm_g"][l].reshape(16, 128).T
    par[:, PC3["gng"]:PC3["gng"] + 4] = inp["rw_gn_g"][l].reshape(4, 128).T
    par[:, PC3["gnb"]:PC3["gnb"] + 4] = inp["rw_gn_b"][l].reshape(4, 128).T
    par[:, PC3["subg"]] = inp["diff_sub_g"][l]
    par[:, PC3["lamf"]] = 1.0 - (0.8 - 0.6 * math.exp(-0.3 * l))
    maps = []
    ntok = NT1 * TT
    for c in range(NCORES):
        xt = np.empty((NT1, 128, 16, TT), f)
        brr = np.empty((NT1, 128, 6, 4, TT), f)
        for t in range(NT1):
            n0 = c * ntok + t * TT
            xt[t] = x_cur[n0:n0 + TT].reshape(TT, 16, 128).transpose(2, 1, 0)
            for i, a in enumerate((ysf, ysb, bonus, ybr, yc, yd)):
                brr[t, :, i] = a[:, n0:n0 + TT].reshape(4, 128, TT).transpose(1, 0, 2)
        maps.append({"xT": xt, "par": par, "wA": wA, "wB": wB, "br": brr})
    return maps


def post_p3(results):
    outs = []
    for r in results:
        o = r["o_x"]
        outs.append(o.transpose(0, 3, 2, 1).reshape(NT1 * TT, 2048))
    return np.ascontiguousarray(np.concatenate(outs, 0))


_P1_AXIS = {"o_dv": 0, "o_mv": 0, "o_rw": 2}


def kernel(**inputs):
    inp = {k: np.asarray(v) for k, v in inputs.items()}
    x = np.ascontiguousarray(inp["x"][0], dtype=np.float32)
    for l in range(4):
        r1 = _run("p1", build_p1, prep_p1(inp, l, x))
        o = {k: _cat(r1, k, _P1_AXIS.get(k, 1)) for k in r1[0]}
        del r1
        rs = _run("scan2", build_scan2, prep_scan2(o["o_rw"]))
        ysf, ysb = post_scan2(rs)
        del rs
        md = prep_attn_diff(inp, l, o["o_dq"], o["o_dk"], o["o_dv"])
        mm_ = prep_attn_mla(o["o_mqn"], o["o_mqr"], o["o_mkn"], o["o_mkr"], o["o_mv"])
        maps2 = [{**{"d_" + k: v for k, v in a.items()}, **{"m_" + k: v for k, v in c_.items()}}
                 for a, c_ in zip(md, mm_)]
        del md, mm_
        ra = _run("attn2", build_attn2, maps2)
        del maps2
        yb = post_attn_diff([{"o": r["d_o"]} for r in ra])
        yc = post_attn_mla([{"o": r["m_o"]} for r in ra])
        del ra
        r3 = _run("p3", build_p3, prep_p3(inp, l, x, ysf, ysb, o["o_bonus"], yb, yc, o["o_yd"]))
        x = post_p3(r3)
        del r3, o
    return x[None].astype(np.float32)


SB = 512
SG = 128
SC = 64


def build_scan2(T=SEQ, f32r=False):
    b = B()
    b.f32r = f32r
    P = b.P
    fm_d = b.din("fm", [64, 2, 5, T])
    v_d = b.din("v", [64, 2, T])
    cst_d = b.din("cst", [128, 4, 128])
    m01_d = b.din("m01", [64, 2 * SB])
    y_d = b.dout("y", [64, 2, T])
    nblk = T // SB
    NI = (SB // SG) * 2
    cst = b.sb("cst", [128, 4, 128])
    m01 = b.sb("m01", [64, 2 * SB])
    b.load(cst[:], cst_d, ["cst"], "cst")
    b.load(m01[:], m01_d, ["m01"], "m01")
    Ml, Mu, MuI, I_ = (cst[:, i, :] for i in range(4))
    fmb = b.sb("fmb", [64, 2, 5, SB])
    vb = b.sb("vb", [64, 2, SB])
    sc = {n: b.sb(n, [64, 2, SB]) for n in ("KT", "NB", "KD", "RT", "NB2", "KD2")}
    b.ring("e", 4, [64, 2, SB])
    gC = b.sb("gC", [64, 2, SB // SC])
    clend = b.sb("clend", [64, 2, SB // SC])
    Hs = b.sb("Hs", [64, 2, SB // SC + 1, 64])
    yb = b.sb("yb", [64, 2, SB])
    it_buf = []
    for i in range(NI):
        d = {}
        for n, shp in (("N0", [128, 128]), ("N1", [128, 128]), ("P0", [128, 128]), ("P1", [128, 128]),
                       ("X0", [128, 128]), ("X1", [128, 128]), ("AkT", [128, 128]), ("BkT", [128, 128]),
                       ("BnbT", [128, 128]), ("NBt", [128, 64]), ("KDt", [128, 64]), ("Vt", [128, 64]),
                       ("NB2t", [128, 64]), ("KD2t", [128, 64]),
                       ("WT", [64, 128]), ("U", [128, 64]), ("G1", [64, 2, 64]), ("G2", [64, 2, 64])):
            d[n] = b.sb(f"i{i}{n}", shp)
        it_buf.append(d)
    b.memset("dve", Hs[:, :, 0, :], 0.0, ["Hs"])

    def r_(i, n):
        return (f"i{i}", n)

    def bulk(blk):
        t0 = blk * SB
        b.load(fmb[:], fm_d[:, :, :, t0:t0 + SB], ["fmb"], "fmb")
        b.load(vb[:], v_d[:, :, t0:t0 + SB], ["vb"], "vb")
        lw = fmb[:, :, 4, :]
        cl, clr = b.nx("e")
        for p in range(2):
            P.op("dve", lambda e, p=p, cl=cl: e.tensor_tensor_scan(
                out=cl[:, p, :], data0=m01[:, 0:SB], data1=fmb[:, p, 4, :], initial=0.0, op0=ALU.mult, op1=ALU.add),
                reads=["fmb", "m01"], writes=[clr])
        for p in range(2):
            b.cp("pool", clend[:, p, :], cl[:, p, SC - 1:SB:SC], [clr], ["clend"])
        e1, e1r = b.nx("e")
        b.tt("pool", e1[:], cl[:], lw, ALU.subtract, [clr, "fmb"], [e1r])
        b.act(e1[:], e1[:], AF.Exp, [e1r], [e1r])
        b.tt("dve", sc["KT"][:], fmb[:, :, 0, :], e1[:], ALU.mult, ["fmb", e1r], ["KT"])
        e2, e2r = b.nx("e")
        b.act(e2[:], cl[:], AF.Exp, [clr], [e2r], scale=-1.0)
        b.tt("pool", sc["NB"][:], fmb[:, :, 1, :], e2[:], ALU.mult, ["fmb", e2r], ["NB"])
        b.tt("dve", sc["KD"][:], fmb[:, :, 2, :], e2[:], ALU.mult, ["fmb", e2r], ["KD"])
        e3, e3r = b.nx("e")
        b.act(e3[:], cl[:], AF.Exp, [clr], [e3r])
        b.tt("pool", sc["RT"][:], fmb[:, :, 3, :], e3[:], ALU.mult, ["fmb", e3r], ["RT"])
        for p in range(2):
            b.cp("pool", gC[:, p, :], e3[:, p, SC - 1:SB:SC], [e3r], ["gC"])
        e4, e4r = b.nx("e")
        for p in range(2):
            for c in range(SB // SC):
                b.act(e4[:, p, c * SC:(c + 1) * SC], cl[:, p, c * SC:(c + 1) * SC], AF.Exp, [clr, "clend"], [e4r],
                      bias=clend[:, p, c:c + 1], scale=-1.0)
        b.tt("dve", sc["NB2"][:], fmb[:, :, 1, :], e4[:], ALU.mult, ["fmb", e4r], ["NB2"])
        b.tt("pool", sc["KD2"][:], fmb[:, :, 2, :], e4[:], ALU.mult, ["fmb", e4r], ["KD2"])

    def evac_act(dst, pi, M, N, wr, scale=None):
        if scale is None:
            b.cp("act", dst, b.ps[pi][0:M, 0:N], [("ps", pi)], wr)
        else:
            b.act(dst, b.ps[pi][0:M, 0:N], AF.Copy, [("ps", pi), "gC"], wr, scale=scale)

    def transpose(pi, in_ap, K, M, rd):
        out = b.ps[pi][0:M, 0:K]
        P.op("pe", lambda e: e.transpose(out, in_ap, I_[0:K, 0:K]), reads=rd + ["cst"], writes=[("ps", pi)])

    def stage1(blk):
        for i in range(NI):
            g, p = i // 2, i % 2
            ts = slice(g * SG, (g + 1) * SG)
            bf = it_buf[i]
            KT, NB, KD, RT = (sc[n][:, p, ts] for n in ("KT", "NB", "KD", "RT"))
            for (la, ln), (ra, rn), msk, dst in (((KT, "KT"), (NB, "NB"), Ml, "N0"), ((NB, "NB"), (KT, "KT"), Mu, "P0"),
                                                 ((KD, "KD"), (KT, "KT"), Mu, "AkT"), ((KD, "KD"), (RT, "RT"), MuI, "BkT"),
                                                 ((NB, "NB"), (RT, "RT"), MuI, "BnbT")):
                pi = b.nps()
                b.mm(pi, 128, 128, la, ra, True, True, [ln, rn])
                b.tt("dve", bf[dst][:], b.ps[pi][:, 0:128], msk, ALU.mult, [("ps", pi), "cst"], [r_(i, dst)])
            for src, sn, dst, dcols in ((sc["KT"], "KT", "X0", slice(0, 64)), (sc["NB"], "NB", "NBt", slice(0, 64)),
                                        (sc["KD"], "KD", "KDt", slice(0, 64)), (vb, "vb", "Vt", slice(0, 64)),
                                        (sc["NB2"], "NB2", "NB2t", slice(0, 64)), (sc["KD2"], "KD2", "KD2t", slice(0, 64))):
                pi = b.nps()
                transpose(pi, src[:, p, ts], 64, 128, [sn])
                evac_act(bf[dst][:, dcols], pi, 128, 64, [r_(i, dst)])
            pi = b.nps()
            b.mm(pi, 128, 64, bf["AkT"][:], bf["Vt"][:], True, True, [r_(i, "AkT"), r_(i, "Vt")])
            evac_act(bf["X0"][:, 64:128], pi, 128, 64, [r_(i, "X0")])

    def stage2(blk):
        for it in range(6):
            cur, nxt = it % 2, (it + 1) % 2
            for i in range(NI):
                bf = it_buf[i]
                Nc, Pc, Xc = bf[f"N{cur}"], bf[f"P{cur}"], bf[f"X{cur}"]
                Nn, Pn, Xn = bf[f"N{nxt}"], bf[f"P{nxt}"], bf[f"X{nxt}"]
                pi = b.nps()
                b.mm(pi, 128, 128, Pc[:], Xc[:], True, True, [r_(i, f"P{cur}"), r_(i, f"X{cur}")])
                b.tt("dve", Xn[:], Xc[:], b.ps[pi][:, 0:128], ALU.add, [("ps", pi), r_(i, f"X{cur}")], [r_(i, f"X{nxt}")])
                if it < 5:
                    pi = b.nps()
                    b.mm(pi, 128, 128, Nc[:], Pc[:], True, True, [r_(i, f"N{cur}"), r_(i, f"P{cur}")])
                    evac_act(Pn[:], pi, 128, 128, [r_(i, f"P{nxt}")])
                if it < 4:
                    pi = b.nps()
                    b.mm(pi, 128, 128, Pc[:], Nc[:], True, True, [r_(i, f"N{cur}"), r_(i, f"P{cur}")])
                    evac_act(Nn[:], pi, 128, 128, [r_(i, f"N{nxt}")])

    def stage3(blk):
        for i in range(NI):
            bf = it_buf[i]
            X = bf["X0"]
            pi = b.nps()
            transpose(pi, X[:, 0:64], 128, 64, [r_(i, "X0")])
            evac_act(bf["WT"][:], pi, 64, 128, [r_(i, "WT")])
            for c in range(2):
                cs = slice(c * SC, (c + 1) * SC)
                pi = b.nps()
                b.mm(pi, 64, 64, X[cs, 0:64], bf["NB2t"][cs, :], True, True, [r_(i, "X0"), r_(i, "NB2t")])
                pdiag, pdr = b.nx("dg")
                g, p = i // 2, i % 2
                cg = g * 2 + c
                b.ts("pool", pdiag[:], I_[0:64, 0:64], gC[:, p, cg:cg + 1], None, ALU.mult, None, ["cst", "gC"], [pdr])
                b.tt("dve", bf["G1"][:, c, :], b.ps[pi][0:64, 0:64], pdiag[:], ALU.add, [("ps", pi), pdr], [r_(i, "G1")])
                pi = b.nps()
                b.mm(pi, 64, 64, bf["NB2t"][cs, :], X[cs, 64:128], True, False, [r_(i, "X0"), r_(i, "NB2t")])
                b.mm(pi, 64, 64, bf["KD2t"][cs, :], bf["Vt"][cs, :], False, True, [r_(i, "KD2t"), r_(i, "Vt")])
                evac_act(bf["G2"][:, c, :], pi, 64, 64, [r_(i, "G2")])

    def stage4(blk):
        nchunk = SB // SC
        for cg in range(nchunk):
            for p in range(2):
                i = (cg // 2) * 2 + p
                c = cg % 2
                bf = it_buf[i]
                pi = b.nps()
                b.mm(pi, 64, 64, bf["G1"][:, c, :], Hs[:, p, cg, :], True, False, [r_(i, "G1"), ("Hs", p)])
                b.mm(pi, 64, 64, I_[0:64, 0:64], bf["G2"][:, c, :], False, True, ["cst", r_(i, "G2")])
                b.cp("act", Hs[:, p, cg + 1, :], b.ps[pi][0:64, 0:64], [("ps", pi)], [("Hs", p)])

    def stage5(blk):
        t0 = blk * SB
        for i in range(NI):
            g, p = i // 2, i % 2
            bf = it_buf[i]
            X = bf["X0"]
            for c in range(2):
                cs = slice(c * SC, (c + 1) * SC)
                cg = g * 2 + c
                pi = b.nps()
                b.mm(pi, 128, 64, bf["WT"][:], Hs[:, p, cg, :], True, True, [r_(i, "WT"), ("Hs", p)])
                b.tt("dve", bf["U"][cs, :], b.ps[pi][cs, 0:64], X[cs, 64:128], ALU.add, [("ps", pi), r_(i, "X0")], [r_(i, "U")])
            pi = b.nps()
            b.mm(pi, 64, 128, bf["Vt"][:], bf["BkT"][:], True, False, [r_(i, "Vt"), r_(i, "BkT")])
            b.mm(pi, 64, 128, bf["U"][:], bf["BnbT"][:], False, False, [r_(i, "U"), r_(i, "BnbT")])
            for c in range(2):
                cg = g * 2 + c
                out = b.ps[pi][0:64, c * SC:(c + 1) * SC]
                lhsT = Hs[:, p, cg, :]
                rhs = sc["RT"][:, p, g * SG + c * SC:g * SG + (c + 1) * SC]
                P.op("pe", lambda e, out=out, lhsT=lhsT, rhs=rhs, c=c: e.matmul(out, lhsT, rhs, start=False, stop=(c == 1)),
                     reads=[("Hs", p), "RT"], writes=[("ps", pi)])
            b.cp("act", yb[:, p, g * SG:(g + 1) * SG], b.ps[pi][0:64, 0:128], [("ps", pi)], ["yb"])
        b.store(y_d[:, :, t0:t0 + SB], yb[:], ["yb"])
        if blk + 1 < nblk:
            b.cp("pool", Hs[:, :, 0, :], Hs[:, :, SB // SC, :], [("Hs", 0), ("Hs", 1)], [("Hs", 0), ("Hs", 1)])

    b.ring("dg", 4, [64, 64])
    for blk in range(nblk):
        bulk(blk)
        stage1(blk)
        stage2(blk)
        stage3(blk)
        stage4(blk)
        stage5(blk)
    return b.finish()


def _scan2_consts():
    G, C = SG, SC
    Ml = np.zeros((G, G), np.float32)
    for t in range(G):
        for s in range(G):
            if t // C == s // C and s < t:
                Ml[t, s] = 1
    cst = np.stack([Ml, Ml.T, Ml.T + np.eye(G, dtype=np.float32), np.eye(G, dtype=np.float32)], 1)
    m01 = np.ones((64, 2 * SB), np.float32)
    m01[:, ::C] = 0
    return np.ascontiguousarray(cst), m01


def prep_scan2(o_rw, T=SEQ):
    cst, m01 = _scan2_consts()
    maps = []
    for h in range(NCORES):
        sl = slice(h * 64, (h + 1) * 64)
        fm = np.empty((64, 2, 5, T), np.float32)
        v = np.empty((64, 2, T), np.float32)
        for d in range(2):
            for k, a in enumerate((o_rw[2][sl], o_rw[3 + 3 * d][sl], o_rw[4 + 3 * d][sl], o_rw[0][sl], o_rw[5 + 3 * d][sl])):
                fm[:, d, k] = a if d == 0 else a[:, ::-1]
            v[:, d] = o_rw[1][sl] if d == 0 else o_rw[1][sl][:, ::-1]
        maps.append({"fm": fm, "v": v, "cst": cst, "m01": m01})
    return maps


def post_scan2(results):
    yf = np.concatenate([r["y"][:, 0] for r in results], 0)
    yb = np.concatenate([r["y"][:, 1][:, ::-1] for r in results], 0)
    return np.ascontiguousarray(yf), np.ascontiguousarray(yb)
```

```python
import math
from contextlib import ExitStack
import numpy as np
import concourse.bass as bass
import concourse.mybir as mybir
from concourse.bass_utils import run_bass_kernel_spmd

F32 = mybir.dt.float32
BF16 = mybir.dt.bfloat16
F32R = mybir.dt.float32r
I32 = mybir.dt.int32
ALU = mybir.AluOpType
AF = mybir.ActivationFunctionType
AX = mybir.AxisListType
ENGS = ("pe", "act", "dve", "pool", "sp")
NCORES = 8


class _Op:
    __slots__ = ("eng", "fn", "waits", "signal", "dma_key", "idx", "sigval")

    def __init__(self, eng, fn, dma_key):
        self.eng = eng
        self.fn = fn
        self.waits = []
        self.signal = False
        self.dma_key = dma_key
        self.idx = None
        self.sigval = None


class _Res:
    __slots__ = ("w", "r")

    def __init__(self):
        self.w = None
        self.r = []


class Prog:
    def __init__(self, nc):
        self.nc = nc
        self.ops = {e: [] for e in ENGS}
        self.res = {}
        self.dma_cnt = {}
        self.dma_last = {}
        self.waited = {e: {} for e in ENGS}

    def _need(self, op, tok, isd):
        if tok is None:
            return
        kind, src, val = tok
        if kind == "e" and src == op.eng and not isd and src == "pe":
            return
        w = self.waited[op.eng]
        k = (kind, src)
        if w.get(k, -1) >= val:
            return
        w[k] = val
        op.waits.append(tok)
        if kind == "e":
            self.ops[src][val].signal = True

    def op(self, eng, fn, reads=(), writes=(), dma=None):
        o = _Op(eng, fn, dma)
        o.idx = len(self.ops[eng])
        isd = dma is not None
        if isd:
            n = self.dma_cnt.get(dma, 0) + 1
            self.dma_cnt[dma] = n
            self._need(o, self.dma_last.get(dma), True)
            tok = ("d", dma, n)
            self.dma_last[dma] = tok
        else:
            tok = ("e", eng, o.idx)
        for r in reads:
            st = self.res.setdefault(r, _Res())
            self._need(o, st.w, isd)
        for r in writes:
            st = self.res.setdefault(r, _Res())
            self._need(o, st.w, isd)
            for t in st.r:
                self._need(o, t, isd)
        for r in reads:
            st = self.res[r]
            st.r.append(tok)
            if len(st.r) > 48:
                st.r = st.r[-48:]
        for r in writes:
            st = self.res[r]
            st.w = tok
            st.r = []
        self.ops[eng].append(o)
        return tok

    def wait_tokens(self, eng, toks):
        o = _Op(eng, None, None)
        o.idx = len(self.ops[eng])
        for t in toks:
            self._need(o, t, True)
        self.ops[eng].append(o)

    def emit(self):
        nc = self.nc
        esem = {e: nc.alloc_semaphore(name=f"s_{e}") for e in ENGS}
        dsem = {k: nc.alloc_semaphore(name=f"d_{i}") for i, k in enumerate(self.dma_cnt)}
        for e in ENGS:
            c = 0
            for o in self.ops[e]:
                if o.signal:
                    c += 1
                    o.sigval = c
        ops = self.ops

        def body(e):
            def f(eng):
                for o in ops[e]:
                    for kind, src, val in o.waits:
                        if kind == "e":
                            eng.wait_ge(esem[src], ops[src][val].sigval)
                        else:
                            eng.wait_ge(dsem[src], 16 * val)
                    if o.fn is None:
                        continue
                    inst = o.fn(eng)
                    if o.dma_key is not None:
                        inst.then_inc(dsem[o.dma_key], 16)
                    elif o.signal:
                        inst.then_inc(esem[e], 1)
            return f

        with nc.Block() as block:
            block.tensor(body("pe"))
            block.scalar(body("act"))
            block.vector(body("dve"))
            block.gpsimd(body("pool"))
            block.sync(body("sp"))


class B:
    def __init__(self):
        self.nc = bass.Bass("TRN2", target_bir_lowering=False)
        self.P = Prog(self.nc)
        self.es = ExitStack()
        self.ps = [self.es.enter_context(self.nc.psum_tensor(f"ps{i}", [128, 512], F32)) for i in range(8)]
        self.psi = 0
        self.psrot = list(range(8))
        self.rings = {}
        self.outtoks = []
        self.ndq = 0
        self.attn_banks = [(4, 5), (6, 7)]
        self.zmod = 3
        self.attn_par = 0

    def din(self, name, shape, dt=F32):
        return self.nc.dram_tensor(name, list(shape), dt, kind="ExternalInput").ap()

    def dout(self, name, shape, dt=F32):
        return self.nc.dram_tensor(name, list(shape), dt, kind="ExternalOutput").ap()

    def sb(self, name, shape, dt=F32):
        return self.es.enter_context(self.nc.sbuf_tensor("s_" + name, list(shape), dt))

    def nps(self):
        rot = self.psrot
        i = rot[self.psi % len(rot)]
        self.psi += 1
        return i

    def ring(self, name, n, shape, dt=F32):
        self.rings[name] = [[self.sb(f"{name}{i}", shape, dt) for i in range(n)], 0]

    def nx(self, name):
        r = self.rings[name]
        i = r[1]
        r[1] = (i + 1) % len(r[0])
        return r[0][i], (name, i)

    def mm(self, pi, M, N, lhsT, rhs, st, sp, rd, po=0):
        out = self.ps[pi][po:po + M, 0:N]
        if getattr(self, "f32r", False) and lhsT.dtype == F32 and rhs.dtype == F32:
            lhsT = lhsT.bitcast(F32R)
            rhs = rhs.bitcast(F32R)
        self.P.op("pe", lambda e: e.matmul(out, lhsT, rhs, start=st, stop=sp), reads=rd, writes=[("ps", pi)])

    def act(self, out, in_, func, rd, wr, bias=None, scale=None):
        kw = {}
        if bias is not None:
            kw["bias"] = bias
        if scale is not None:
            kw["scale"] = scale
        self.P.op("act", lambda e: e.activation(out=out, in_=in_, func=func, **kw), reads=rd, writes=wr)

    def stt(self, out, in0, scalar, in1, op0, op1, rd, wr):
        self.P.op("dve", lambda e: e.scalar_tensor_tensor(out=out, in0=in0, scalar=scalar, in1=in1,
                                                          op0=op0, op1=op1), reads=rd, writes=wr)

    def tt(self, eng, out, in0, in1, op, rd, wr):
        self.P.op(eng, lambda e: e.tensor_tensor(out=out, in0=in0, in1=in1, op=op), reads=rd, writes=wr)

    def ts(self, eng, out, in0, s1, s2, op0, op1, rd, wr):
        if op1 is None:
            self.P.op(eng, lambda e: e.tensor_scalar(out=out, in0=in0, scalar1=s1, scalar2=None, op0=op0),
                      reads=rd, writes=wr)
        else:
            self.P.op(eng, lambda e: e.tensor_scalar(out=out, in0=in0, scalar1=s1, scalar2=s2, op0=op0, op1=op1),
                      reads=rd, writes=wr)

    def cp(self, eng, out, in_, rd, wr):
        if eng == "act":
            self.P.op("act", lambda e: e.copy(out=out, in_=in_), reads=rd, writes=wr)
        else:
            self.P.op(eng, lambda e: e.tensor_copy(out=out, in_=in_), reads=rd, writes=wr)

    def wcast(self, wbf, wst, sl, rn, wn, dsl=None):
        dsl = sl if dsl is None else dsl
        self.cp("dve", wbf[:, dsl, 0:8], wst[:, sl, 0:8], [(rn, sl)], [(wn, dsl)])
        self.cp("act", wbf[:, dsl, 8:16], wst[:, sl, 8:16], [(rn, sl)], [(wn, dsl)])

    def recip(self, out, in_, rd, wr):
        self.P.op("dve", lambda e: e.reciprocal(out=out, in_=in_), reads=rd, writes=wr)

    def memset(self, eng, ap, val, wr):
        self.P.op(eng, lambda e: e.memset(ap, val), writes=wr)

    def load(self, out, in_, wr, key, q="sp"):
        return self.P.op(q, lambda e: e.dma_start(out=out, in_=in_), writes=wr, dma=key)

    def store(self, out, in_, rd, q="pool"):
        self.ndq += 1
        key = ("st", self.ndq % 6)
        t = self.P.op(q, lambda e: e.dma_start(out=out, in_=in_), reads=rd, dma=key)
        self.outtoks.append(t)

    def finish(self):
        last = {}
        for t in self.outtoks:
            last[t[1]] = t
        self.P.wait_tokens("pool", list(last.values()))
        self.P.emit()
        self.es.close()
        return self.nc


def fm_rstd(b, sq_list, ones_ap, Pn, N, inv_n, eps, consts_res):
    pi = b.nps()
    for i, (ap, res) in enumerate(sq_list):
        b.mm(pi, Pn, N, ones_ap, ap, i == 0, i == len(sq_list) - 1, [res, consts_res])
    ln, lnr = b.nx("t")
    b.act(ln[0:Pn, 0:N], b.ps[pi][0:Pn, 0:N], AF.Ln, [("ps", pi), b.epsr[eps]], [lnr], bias=b.epsc[eps][0:Pn, :], scale=inv_n)
    rs, rsr = b.nx("t")
    b.act(rs[0:Pn, 0:N], ln[0:Pn, 0:N], AF.Exp, [lnr], [rsr], scale=-0.5)
    return rs, rsr


def setup_consts(b):
    b.ones = b.sb("ones", [128, 128])
    b.blk = b.sb("blk", [128, 128])
    b.memset("pool", b.ones[:], 1.0, ["ones"])
    b.memset("pool", b.blk[:], 0.0, ["blk"])
    b.memset("pool", b.blk[0:64, 0:64], 1.0, ["blk"])
    b.memset("pool", b.blk[64:128, 64:128], 1.0, ["blk"])
    b.onesb = b.sb("onesb", [128, 128], BF16)
    b.memset("pool", b.onesb[:], 1.0, ["onesb"])
    b.epsc = {}
    b.epsr = {}
    for i, v in enumerate((1e-6, 64e-5)):
        t = b.sb(f"epsc{i}", [128, 1])
        b.memset("pool", t[:], v, [f"epsc{i}"])
        b.epsc[v] = t
        b.epsr[v] = f"epsc{i}"
        b.P.res


def attn_core(b, qlist, kts, vts, nkt, scale, out_M, bias_fn=None, tag="a"):
    po, pz = b.attn_banks[b.attn_par]
    b.attn_par ^= 1
    acc, accr = b.nx("zacc")
    acc2, acc2r = b.nx("zacc2")
    pend = []
    zq = []
    acc_init = [False]
    LOOK = 2
    ZMOD = b.zmod if nkt > 2 else 2

    def pv(kt, pt, ptr):
        va, vr = vts(kt)
        b.mm(po, out_M, 512, va, pt[:], kt == 0, kt == nkt - 1, [vr, ptr])
        if kt % ZMOD == 0:
            b.mm(pz, out_M, 512, b.onesb[:, 0:out_M], pt[:], kt == 0, (ZMOD == 1 and kt == nkt - 1), ["onesb", ptr])

    for kt in range(nkt):
        pi = b.nps()
        ks = kts(kt)
        for i, ((qa, qr), (ka, kr)) in enumerate(zip(qlist, ks)):
            b.mm(pi, 128, 512, ka, qa, i == 0, i == len(qlist) - 1, [qr, kr])
        if len(pend) >= LOOK:
            pv(*pend.pop(0))
        pt, ptr = b.nx("pt")
        bf = bias_fn(kt) if bias_fn is not None else None
        if bf is not None:
            kind, val = bf
            if kind == "const":
                b.act(pt[:], b.ps[pi][:, :], AF.Exp, [("ps", pi), "tab"], [ptr], bias=val, scale=scale)
            else:
                tmp, tr = b.nx("t")
                bap, bres = val
                b.stt(tmp[:], b.ps[pi][:, :], scale, bap, ALU.mult, ALU.add, [("ps", pi), bres], [tr])
                b.act(pt[:], tmp[:], AF.Exp, [tr], [ptr])
        else:
            b.act(pt[:], b.ps[pi][:, :], AF.Exp, [("ps", pi)], [ptr], scale=scale)
        if kt % ZMOD == 0:
            zq.append((kt, pt, ptr))
        elif not acc_init[0]:
            b.cp("dve", acc2[:], pt[:], [ptr], [acc2r])
            acc_init[0] = True
        else:
            b.tt("dve", acc2[:], acc2[:], pt[:], ALU.add, [ptr, acc2r], [acc2r])
        pend.append((kt, pt, ptr))
    for pp in pend:
        pv(*pp)
    if ZMOD > 1:
        b.mm(pz, out_M, 512, b.ones[:, 0:out_M], acc2[:], False, True, ["ones", acc2r])
    return po, pz


TT = 512
TH = TT + 2
NT1 = 2
PC = {}
_c = 0
for _n, _w in (("ng", 16), ("sh", 42), ("w0", 8), ("a0", 8), ("kk", 4), ("ka", 4), ("rk", 4), ("dqg", 1), ("dkg", 1),
               ("qlg", 3), ("kvg", 2), ("npg", 2), ("rpg", 2), ("rpgs", 2), ("mqg", 2), ("mng", 16), ("invf", 1),
               ("sgn", 1), ("omka", 4)):
    PC[_n] = _c
    _c += _w
NPAR = _c
RW0 = 0
def _p1_cols():
    cols = []
    cols += [(RW0 + 1536, 128), (RW0 + 1664, 128)]
    for c in range(4):
        cols += [(c * 128, 128), (512 + c * 128, 128), (1024 + c * 128, 128)]
    o = 1792
    cols += [(o + i * 128, 128) for i in range(4)]
    cols += [(o + 512 + i * 128, 128) for i in range(4)]
    o2 = 1792 + 1536
    cols += [(o2 + i * 128, 128) for i in range(3)]
    cols += [(o2 + 384 + i * 128, 128) for i in range(2)]
    cols += [("krope", 128)]
    o3 = o2 + 384 + 256 + 64
    cols += [(o3 + i * 128, 128) for i in range(4)]
    cols += [(1792 + 1024 + i * 128, 128) for i in range(4)]
    return cols
P1COLS = _p1_cols()
NCH1 = len(P1COLS)
KROPE0 = 1792 + 1536 + 384 + 256


def build_p1():
    b = B()
    P = b.P
    xT = b.din("xT", [NT1, 128, 16, TH])
    pos = b.din("pos", [NT1, TH], I32)
    par_d = b.din("par", [128, NPAR])
    w = b.din("w", [NCH1, 128, 16, 128])
    wup_d = b.din("wup", [128, 512])
    aup_d = b.din("aup", [128, 512])
    wuq_d = b.din("wuq", [128, 3, 768])
    wuqs_d = b.din("wuqs", [128, 3, 256])
    wukvk_d = b.din("wukvk", [128, 2, 512])
    wukvv_d = b.din("wukvv", [128, 2, 512])
    memT_d = b.din("memT", [128, 16, 256])
    wkv_d = b.din("wkv", [8, 128, 16, 128])
    o_rw = b.dout("o_rw", [9, 512, NT1 * TT])
    o_bonus = b.dout("o_bonus", [512, NT1 * TT])
    o_dq = b.dout("o_dq", [512, NT1 * TT])
    o_dk = b.dout("o_dk", [512, NT1 * TT])
    o_dv = b.dout("o_dv", [NT1 * TT, 512])
    o_mqn = b.dout("o_mqn", [512, NT1 * TT])
    o_mqr = b.dout("o_mqr", [256, NT1 * TT])
    o_mkn = b.dout("o_mkn", [512, NT1 * TT])
    o_mkr = b.dout("o_mkr", [64, NT1 * TT])
    o_mv = b.dout("o_mv", [NT1 * TT, 512])
    o_yd = b.dout("o_yd", [512, NT1 * TT])

    setup_consts(b)
    par = b.sb("par", [128, NPAR])
    b.load(par[:], par_d, ["par"], "par")
    pc = lambda n, i=0: par[:, PC[n] + i:PC[n] + i + 1]
    b.ts("dve", par[:, PC["omka"]:PC["omka"] + 4], par[:, PC["ka"]:PC["ka"] + 4], -1.0, 1.0, ALU.mult, ALU.add,
         ["par"], ["par"])
    xs = b.sb("xs", [128, 16, TH])
    hT = b.sb("hT", [128, 16, TH], BF16)
    wst = b.sb("wst", [128, 2, 16, 128])
    wbf = b.sb("wbf", [128, 2, 16, 128], BF16)
    stg = b.sb("stg", [128, 3072])
    wup = b.sb("wup_s", [128, 512])
    aup = b.sb("aup_s", [128, 512])
    wuq = b.sb("wuq_s", [128, 3, 768], BF16)
    wuqs = b.sb("wuqs_s", [128, 3, 256], BF16)
    wukvk = b.sb("wukvk_s", [128, 2, 512], BF16)
    wukvv = b.sb("wukvv_s", [128, 2, 512], BF16)
    memn = b.sb("memn", [128, 16, 256], BF16)
    kmem = b.sb("kmem", [128, 4, 256], BF16)
    vmem = b.sb("vmem", [128, 2, 512], BF16)
    b.ring("t", 14, [128, TT])
    b.ring("th", 3, [128, TH])
    b.ring("rkv", 4, [128, TT])
    b.ring("bf", 4, [128, TT], BF16)
    b.ring("pt", 3, [128, TT], BF16)
    b.ring("zacc", 1, [128, TT])
    b.ring("zacc2", 1, [128, TT])
    twd = b.sb("twd", [128, TT])
    adl = b.sb("adl", [128, TT])
    ropC = b.sb("ropC", [64, TT])
    ropS = b.sb("ropS", [64, TT])
    rsx = b.sb("rsx", [128, TH])
    ql = b.sb("ql", [128, 3, TT])
    qln = b.sb("qln", [128, 3, TT], BF16)
    kvl = b.sb("kvl", [128, 2, TT])
    kvn = b.sb("kvn", [128, 2, TT], BF16)
    posi = b.sb("posi", [64, TH], I32)

    b.load(wup[:], wup_d, ["wup"], "wl0")
    b.load(aup[:], aup_d, ["aup"], "wl1")
    for dst, src, n, nm in ((wuq, wuq_d, 3 * 768, "wuq"), (wuqs, wuqs_d, 3 * 256, "wuqs"),
                            (wukvk, wukvk_d, 1024, "wukvk"), (wukvv, wukvv_d, 1024, "wukvv")):
        b.load(stg[:, 0:n], src.rearrange("p a b -> p (a b)"), ["stg"], "stg")
        b.cp("dve", dst[:].rearrange("p a b -> p (a b)"), stg[:, 0:n], ["stg"], [nm])

    def rmsnorm_cols(src_tile, nkc, N, gname, dst_tile, res_src, res_dst, inv_n):
        pi = b.nps()
        pih = b.nps() if N > 512 else None
        for kc in range(nkc):
            sq, sqr = b.nx("th")
            b.act(sq[:, 0:N], src_tile[:, kc, 0:N], AF.Square, [res_src], [sqr])
            b.mm(pi, 128, min(N, 512), b.ones[:], sq[:, 0:min(N, 512)], kc == 0, kc == nkc - 1, [sqr, "ones"])
            if pih is not None:
                b.mm(pih, 128, N - 512, b.ones[:], sq[:, 512:N], kc == 0, kc == nkc - 1, [sqr, "ones"])
        ln, lnr = b.nx("th")
        b.act(ln[:, 0:min(N, 512)], b.ps[pi][:, 0:min(N, 512)], AF.Ln, [("ps", pi), "epsc0"], [lnr],
              bias=b.epsc[1e-6][:], scale=inv_n)
        if pih is not None:
            b.act(ln[:, 512:N], b.ps[pih][:, 0:N - 512], AF.Ln, [("ps", pih), "epsc0"], [lnr], bias=b.epsc[1e-6][:], scale=inv_n)
        b.act(rsx[:, 0:N], ln[:, 0:N], AF.Exp, [lnr], ["rsx"], scale=-0.5)
        for kc in range(nkc):
            b.stt(dst_tile[:, kc, 0:N], src_tile[:, kc, 0:N], pc(gname, kc), rsx[:, 0:N], ALU.mult, ALU.mult,
                  [res_src, "rsx", "par"], [res_dst])

    b.load(xs[:, :, 0:256], memT_d, ["xs"], "xs")
    rmsnorm_cols(xs, 16, 256, "mng", memn, "xs", "memn", 1.0 / 2048)
    for ci in range(8):
        sl = ci % 2
        b.load(wst[:, sl], wkv_d[ci], [("wst", sl)], ("wst", sl))
        b.wcast(wbf, wst, sl, "wst", "wbf")
        if ci < 4:
            pi = b.nps()
            for kc in range(16):
                b.mm(pi, 128, 256, wbf[:, sl, kc, :], memn[:, kc, :], kc == 0, kc == 15, [("wbf", sl), "memn"])
            tq, tqr = b.nx("t")
            b.cp("act", tq[:, 0:256], b.ps[pi][:, 0:256], [("ps", pi)], [tqr])
            sq, sqr = b.nx("t")
            b.act(sq[:, 0:256], b.ps[pi][:, 0:256], AF.Square, [("ps", pi)], [sqr])
            rs, rsr = fm_rstd(b, [(sq[:, 0:256], sqr)], b.ones[:], 128, 256, 1.0 / 128, 1e-6, "ones")
            b.stt(kmem[:, ci, :], tq[:, 0:256], pc("mqg", 1), rs[:, 0:256], ALU.mult, ALU.mult, [tqr, rsr, "par"], ["kmem"])
        else:
            for tb in range(2):
                pi = b.nps()
                for kc in range(16):
                    b.mm(pi, 128, 128, memn[:, kc, tb * 128:(tb + 1) * 128], wbf[:, sl, kc, :], kc == 0, kc == 15,
                         [("wbf", sl), "memn"])
                b.cp("act", vmem[:, tb, (ci - 4) * 128:(ci - 3) * 128], b.ps[pi][:, 0:128], [("ps", pi)], ["vmem"])

    def wload(tile, ci):
        sl = (tile * NCH1 + ci) % 2
        b.load(wst[:, sl], w[ci], [("wst", sl)], ("wst", sl))

    def shift(pi, pih, rc, dst, dres):
        u, ur = b.nx("th")
        b.cp("act", u[:, 0:TT], b.ps[pi][:, :], [("ps", pi)], [ur])
        b.cp("act", u[:, TT:TH], b.ps[pih][:, 0:2], [("ps", pih)], [ur])
        s0 = pc("sh", 0 * 14 + rc); s1 = pc("sh", 1 * 14 + rc); s2 = pc("sh", 2 * 14 + rc)
        b.ts("dve", dst[:, :], u[:, 0:TT], s1, None, ALU.mult, None, [ur, "par"], [dres])
        b.stt(dst[:, 1:TT], u[:, 0:TT - 1], s0, dst[:, 1:TT], ALU.mult, ALU.add, [ur, "par", dres], [dres])
        b.stt(dst[:, 0:1], u[:, TT:TT + 1], s0, dst[:, 0:1], ALU.mult, ALU.add, [ur, "par", dres], [dres])
        b.stt(dst[:, 0:TT - 1], u[:, 1:TT], s2, dst[:, 0:TT - 1], ALU.mult, ALU.add, [ur, "par", dres], [dres])
        b.stt(dst[:, TT - 1:TT], u[:, TT + 1:TT + 2], s2, dst[:, TT - 1:TT], ALU.mult, ALU.add, [ur, "par", dres], [dres])

    def head_norm(pi, Pn, ones_ap, inv_n, gcol, dst_ap, dres, N=TT):
        tq, tqr = b.nx("t")
        b.cp("act", tq[0:Pn, 0:N], b.ps[pi][0:Pn, 0:N], [("ps", pi)], [tqr])
        sq, sqr = b.nx("t")
        b.act(sq[0:Pn, 0:N], b.ps[pi][0:Pn, 0:N], AF.Square, [("ps", pi)], [sqr])
        rs, rsr = fm_rstd(b, [(sq[0:Pn, 0:N], sqr)], ones_ap, Pn, N, inv_n, 1e-6, "ones")
        b.stt(dst_ap, tq[0:Pn, 0:N], gcol, rs[0:Pn, 0:N], ALU.mult, ALU.mult, [tqr, rsr, "par"], [dres])
        return tq, tqr, rs, rsr

    for tile in range(NT1):
        t0 = tile * TT
        b.load(xs[:], xT[tile], ["xs"], "xs")
        b.load(posi[:], pos[tile, :].partition_broadcast(64), ["posi"], "posi")
        wload(tile, 0)
        rmsnorm_cols(xs, 16, TH, "ng", hT, "xs", "hT", 1.0 / 2048)
        posf, posr = b.nx("th")
        b.cp("dve", posf[0:64, :], posi[:], ["posi"], [posr])
        for which, dstt, dres in ((0, ropS, "ropS"), (1, ropC, "ropC")):
            a, ar = b.nx("t")
            b.ts("dve", a[0:64, :], posf[0:64, 0:TT], pc("invf")[0:64, :], (math.pi / 2 if which else 0.0),
                 ALU.mult, ALU.add, [posr, "par"], [ar])
            y, yr = b.nx("t")
            b.ts("dve", y[0:64, :], a[0:64, :], 1.0 / (2 * math.pi), None, ALU.mult, None, [ar], [yr])
            ni = b.sb(f"ni{tile}{which}", [64, TT], I32)
            b.cp("dve", ni[:], y[0:64, :], [yr], [f"ni{tile}{which}"])
            nf, nfr = b.nx("t")
            b.cp("dve", nf[0:64, :], ni[:], [f"ni{tile}{which}"], [nfr])
            r, rr = b.nx("t")
            b.stt(r[0:64, :], nf[0:64, :], -2 * math.pi, a[0:64, :], ALU.mult, ALU.add, [nfr, ar], [rr])
            m, mr = b.nx("t")
            b.ts("dve", m[0:64, :], r[0:64, :], math.pi, -2 * math.pi, ALU.is_gt, ALU.mult, [rr], [mr])
            b.tt("dve", r[0:64, :], r[0:64, :], m[0:64, :], ALU.add, [rr, mr], [rr])
            b.ts("dve", m[0:64, :], r[0:64, :], -math.pi, 2 * math.pi, ALU.is_lt, ALU.mult, [rr], [mr])
            b.tt("dve", r[0:64, :], r[0:64, :], m[0:64, :], ALU.add, [rr, mr], [rr])
            if which == 0:
                sn, snr = b.nx("t")
                b.act(sn[0:64, :], r[0:64, :], AF.Sin, [rr], [snr])
                b.ts("dve", ropS[:], sn[0:64, :], pc("sgn")[0:64, :], None, ALU.mult, None, [snr, "par"], ["ropS"])
            else:
                b.act(ropC[:], r[0:64, :], AF.Sin, [rr], ["ropC"])

        def rope_out(t_tq, t_r, sw_tq, sw_r, rs, rsr, gi, dst_dram):
            a, ar = b.nx("t")
            b.stt(a[0:64, :], t_tq[0:64, :], pc("rpg", gi)[0:64, :], ropC[:], ALU.mult, ALU.mult, [t_r, "par", "ropC"], [ar])
            c, cr = b.nx("t")
            b.stt(c[0:64, :], sw_tq[0:64, :], pc("rpgs", gi)[0:64, :], ropS[:], ALU.mult, ALU.mult, [sw_r, "par", "ropS"], [cr])
            b.tt("dve", a[0:64, :], a[0:64, :], c[0:64, :], ALU.add, [ar, cr], [ar])
            b.tt("dve", a[0:64, :], a[0:64, :], rs[0:64, :], ALU.mult, [ar, rsr], [ar])
            b.store(dst_dram, a[0:64, :], [ar])

        rcur = {}
        for ci in range(NCH1):
            sl = (tile * NCH1 + ci) % 2
            if ci + 1 < NCH1:
                wload(tile, ci + 1)
            elif tile + 1 < NT1:
                wload(tile + 1, 0)
            b.wcast(wbf, wst, sl, "wst", "wbf")
            wr = ("wbf", sl)
            if ci >= 32:
                j = ci - 32
                for tb in range(4):
                    pi = b.nps()
                    for kc in range(16):
                        b.mm(pi, 128, 128, hT[:, kc, tb * 128:(tb + 1) * 128], wbf[:, sl, kc, :], kc == 0, kc == 15, [wr, "hT"])
                    o, orr = b.nx("t")
                    b.cp("act", o[:, 0:128], b.ps[pi][:, 0:128], [("ps", pi)], [orr])
                    b.store(o_dv[t0 + tb * 128:t0 + (tb + 1) * 128, j * 128:(j + 1) * 128], o[:, 0:128], [orr])
                continue
            if ci == 27:
                pis = []
                for hh in range(2):
                    pi = b.nps()
                    for kc in range(16):
                        b.mm(pi, 64, TT, wbf[:, sl, kc, hh * 64:(hh + 1) * 64], hT[:, kc, 0:TT], kc == 0, kc == 15, [wr, "hT"])
                    pis.append(pi)
                tq, tqr = b.nx("t")
                b.cp("act", tq[0:64, :], b.ps[pis[0]][0:64, :], [("ps", pis[0])], [tqr])
                sq, sqr = b.nx("t")
                b.act(sq[0:64, :], b.ps[pis[0]][0:64, :], AF.Square, [("ps", pis[0])], [sqr])
                sw, swr = b.nx("t")
                b.cp("act", sw[0:64, :], b.ps[pis[1]][0:64, :], [("ps", pis[1])], [swr])
                rs, rsr = fm_rstd(b, [(sq[0:64, :], sqr)], b.ones[0:64, 0:64], 64, TT, 1.0 / 64, 1e-6, "ones")
                rope_out(tq, tqr, sw, swr, rs, rsr, 1, o_mkr[:, t0:t0 + TT])
                continue
            pi = b.nps()
            for kc in range(16):
                b.mm(pi, 128, TT, wbf[:, sl, kc, :], hT[:, kc, 0:TT], kc == 0, kc == 15, [wr, "hT"])
            pih = None
            if ci < 14:
                pih = b.nps()
                for kc in range(16):
                    b.mm(pih, 128, 2, wbf[:, sl, kc, :], hT[:, kc, TT:TH], kc == 0, kc == 15, [wr, "hT"])
            if ci == 0:
                tmp, tr = b.nx("t")
                shift(pi, pih, 12, tmp, tr)
                b.act(twd[:], tmp[:], AF.Tanh, [tr], ["twd"])
            elif ci == 1:
                shift(pi, pih, 13, adl, "adl")
            elif ci < 14:
                c = (ci - 2) // 3
                kind = (ci - 2) % 3
                dst, dres = b.nx("rkv")
                shift(pi, pih, kind * 4 + c, dst, dres)
                rcur[kind] = (dst, dres)
                if kind == 0:
                    b.store(o_rw[0, c * 128:(c + 1) * 128, t0:t0 + TT], dst[:], [dres])
                if kind == 2:
                    r_t, r_r = rcur[0]
                    k_t, k_r = rcur[1]
                    v_t, v_r = rcur[2]
                    b.store(o_rw[1, c * 128:(c + 1) * 128, t0:t0 + TT], v_t[:], [v_r])
                    kr_, krr = b.nx("t")
                    b.ts("dve", kr_[:], k_t[:], pc("kk", c), None, ALU.mult, None, [k_r, "par"], [krr])
                    sq, sqr = b.nx("t")
                    b.act(sq[:], kr_[:], AF.Square, [krr], [sqr])
                    pj = b.nps()
                    b.mm(pj, 128, TT, b.blk[:], sq[:], True, True, [sqr, "blk"])
                    nr, nrr = b.nx("t")
                    b.act(nr[:], b.ps[pj][:, :], AF.Sqrt, [("ps", pj)], [nrr])
                    b.ts("dve", nr[:], nr[:], 1e-12, None, ALU.max, None, [nrr], [nrr])
                    b.recip(nr[:], nr[:], [nrr], [nrr])
                    kk_, kkr = b.nx("t")
                    b.tt("dve", kk_[:], kr_[:], nr[:], ALU.mult, [krr, nrr], [kkr])
                    b.store(o_rw[2, c * 128:(c + 1) * 128, t0:t0 + TT], kk_[:], [kkr])
                    kds = []
                    for d in range(2):
                        pw = b.nps()
                        b.mm(pw, 128, TT, wup[d * 64:(d + 1) * 64, c * 128:(c + 1) * 128], twd[d * 64:(d + 1) * 64, :],
                             True, True, ["wup", "twd"])
                        sg, sgr = b.nx("t")
                        b.act(sg[:], b.ps[pw][:, :], AF.Sigmoid, [("ps", pw), "par"], [sgr], bias=pc("w0", d * 4 + c))
                        dec, decr = b.nx("t")
                        b.act(dec[:], sg[:], AF.Copy, [sgr], [decr], scale=-math.exp(-0.5))
                        b.store(o_rw[5 + 3 * d, c * 128:(c + 1) * 128, t0:t0 + TT], dec[:], [decr])
                        pa = b.nps()
                        b.mm(pa, 128, TT, aup[d * 64:(d + 1) * 64, c * 128:(c + 1) * 128], adl[d * 64:(d + 1) * 64, :],
                             True, True, ["aup", "adl"])
                        a_, a_r = b.nx("t")
                        b.act(a_[:], b.ps[pa][:, :], AF.Sigmoid, [("ps", pa), "par"], [a_r], bias=pc("a0", d * 4 + c))
                        nb, nbr = b.nx("t")
                        b.stt(nb[:], kk_[:], -1.0, a_[:], ALU.mult, ALU.mult, [kkr, a_r], [nbr])
                        b.store(o_rw[3 + 3 * d, c * 128:(c + 1) * 128, t0:t0 + TT], nb[:], [nbr])
                        kd, kdr = b.nx("t")
                        b.ts("dve", kd[:], a_[:], pc("ka", c), pc("omka", c), ALU.mult, ALU.add, [a_r, "par"], [kdr])
                        b.tt("dve", kd[:], kd[:], k_t[:], ALU.mult, [kdr, k_r], [kdr])
                        b.store(o_rw[4 + 3 * d, c * 128:(c + 1) * 128, t0:t0 + TT], kd[:], [kdr])
                        kds.append((kd, kdr))
                    s_, s_r = b.nx("t")
                    b.tt("dve", s_[:], kds[0][0][:], kds[1][0][:], ALU.add, [kds[0][1], kds[1][1]], [s_r])
                    b.stt(s_[:], r_t[:], pc("rk", c), s_[:], ALU.mult, ALU.mult, [r_r, "par", s_r], [s_r])
                    pb_ = b.nps()
                    b.mm(pb_, 128, TT, b.blk[:], s_[:], True, True, [s_r, "blk"])
                    bo, bor = b.nx("t")
                    b.tt("dve", bo[:], v_t[:], b.ps[pb_][:, :], ALU.mult, [v_r, ("ps", pb_)], [bor])
                    b.store(o_bonus[c * 128:(c + 1) * 128, t0:t0 + TT], bo[:], [bor])
            elif ci < 22:
                isk = ci >= 18
                j = ci - (18 if isk else 14)
                o, orr = b.nx("t")
                head_norm(pi, 128, b.blk[:], 1.0 / 64, pc("dkg" if isk else "dqg"), o[:], orr)
                b.store((o_dk if isk else o_dq)[j * 128:(j + 1) * 128, t0:t0 + TT], o[:], [orr])
            elif ci < 25:
                j = ci - 22
                b.cp("act", ql[:, j, :], b.ps[pi][:, :], [("ps", pi)], ["ql"])
                if j == 2:
                    rmsnorm_cols(ql, 3, TT, "qlg", qln, "ql", "qln", 1.0 / 384)
                    for h in range(4):
                        pn = b.nps()
                        for kc in range(3):
                            b.mm(pn, 128, TT, wuq[:, kc, h * 192:h * 192 + 128], qln[:, kc, :], kc == 0, kc == 2, ["wuq", "qln"])
                        o, orr = b.nx("t")
                        head_norm(pn, 128, b.ones[:], 1.0 / 128, pc("npg", 0), o[:], orr)
                        b.store(o_mqn[h * 128:(h + 1) * 128, t0:t0 + TT], o[:], [orr])
                        pr = b.nps()
                        for kc in range(3):
                            b.mm(pr, 64, TT, wuq[:, kc, h * 192 + 128:h * 192 + 192], qln[:, kc, :], kc == 0, kc == 2, ["wuq", "qln"])
                        psw = b.nps()
                        for kc in range(3):
                            b.mm(psw, 64, TT, wuqs[:, kc, h * 64:(h + 1) * 64], qln[:, kc, :], kc == 0, kc == 2, ["wuqs", "qln"])
                        tq, tqr = b.nx("t")
                        b.cp("act", tq[0:64, :], b.ps[pr][0:64, :], [("ps", pr)], [tqr])
                        sq, sqr = b.nx("t")
                        b.act(sq[0:64, :], b.ps[pr][0:64, :], AF.Square, [("ps", pr)], [sqr])
                        sw, swr = b.nx("t")
                        b.cp("act", sw[0:64, :], b.ps[psw][0:64, :], [("ps", psw)], [swr])
                        rs, rsr = fm_rstd(b, [(sq[0:64, :], sqr)], b.ones[0:64, 0:64], 64, TT, 1.0 / 64, 1e-6, "ones")
                        rope_out(tq, tqr, sw, swr, rs, rsr, 0, o_mqr[h * 64:(h + 1) * 64, t0:t0 + TT])
            elif ci < 27:
                j = ci - 25
                b.cp("act", kvl[:, j, :], b.ps[pi][:, :], [("ps", pi)], ["kvl"])
                if j == 1:
                    rmsnorm_cols(kvl, 2, TT, "kvg", kvn, "kvl", "kvn", 1.0 / 256)
                    for h in range(4):
                        pn = b.nps()
                        for kc in range(2):
                            b.mm(pn, 128, TT, wukvk[:, kc, h * 128:(h + 1) * 128], kvn[:, kc, :], kc == 0, kc == 1, ["wukvk", "kvn"])
                        o, orr = b.nx("t")
                        head_norm(pn, 128, b.ones[:], 1.0 / 128, pc("npg", 1), o[:], orr)
                        b.store(o_mkn[h * 128:(h + 1) * 128, t0:t0 + TT], o[:], [orr])
                    for tb in range(4):
                        pv = b.nps()
                        for kc in range(2):
                            b.mm(pv, 128, 512, kvn[:, kc, tb * 128:(tb + 1) * 128], wukvv[:, kc, :], kc == 0, kc == 1, ["wukvv", "kvn"])
                        o, orr = b.nx("t")
                        b.cp("act", o[:], b.ps[pv][:, :], [("ps", pv)], [orr])
                        b.store(o_mv[t0 + tb * 128:t0 + (tb + 1) * 128, :], o[:], [orr])
            else:
                j = ci - 28
                qb, qbr = b.nx("bf")
                head_norm(pi, 128, b.ones[:], 1.0 / 128, pc("mqg", 0), qb[:], qbr)
                b.psrot = [0, 1, 2, 3]
                po, pz = attn_core(b, [(qb[:], qbr)],
                                   lambda kt, j=j: [(kmem[:, j, kt * 128:(kt + 1) * 128], "kmem")],
                                   lambda kt, j=j: (vmem[:, kt, j * 128:(j + 1) * 128], "vmem"),
                                   2, 128 ** -0.5, 128)
                b.psrot = list(range(8))
                rz, rzr = b.nx("t")
                b.recip(rz[:], b.ps[pz][:, :], [("ps", pz)], [rzr])
                o, orr = b.nx("t")
                b.tt("dve", o[:], b.ps[po][:, :], rz[:], ALU.mult, [("ps", po), rzr], [orr])
                b.store(o_yd[j * 128:(j + 1) * 128, t0:t0 + TT], o[:], [orr])
    return b.finish()


def _fm(a, nk):
    return np.ascontiguousarray(a.reshape(nk, 128, a.shape[1]).transpose(1, 0, 2))


def _col(par, name, arr, i=0):
    par[:arr.shape[0], PC[name] + i] = arr


def prep_p1(inp, l, x_cur):
    f = np.float32
    w_in = inp["w_in"][l]
    chunks = []
    for col0, n in P1COLS:
        if col0 == "krope":
            idx = [KROPE0 + j for j in range(64)] + [KROPE0 + (j + 32) % 64 for j in range(64)]
            W = w_in[:, idx]
        else:
            W = w_in[:, col0:col0 + 128]
        chunks.append(_fm(W, 16))
    w = np.stack(chunks)
    par = np.zeros((128, NPAR), f)
    par[:, PC["ng"]:PC["ng"] + 16] = inp["norm_g"][l].reshape(16, 128).T
    sh = inp["rw_shift"][l]
    for j in range(3):
        for rc in range(14):
            _col(par, "sh", sh[j, rc * 128:(rc + 1) * 128], j * 14 + rc)
    for d in range(2):
        for c in range(4):
            _col(par, "w0", inp["rw_w0"][l][d, c * 128:(c + 1) * 128], d * 4 + c)
            _col(par, "a0", inp["rw_a0"][l][d, c * 128:(c + 1) * 128], d * 4 + c)
    rk = inp["rw_r_k"][l].reshape(512)
    for c in range(4):
        _col(par, "kk", inp["rw_k_k"][l][c * 128:(c + 1) * 128], c)
        _col(par, "ka", inp["rw_k_a"][l][c * 128:(c + 1) * 128], c)
        _col(par, "rk", rk[c * 128:(c + 1) * 128], c)
    _col(par, "dqg", np.tile(inp["diff_qk_g"][l][0], 2))
    _col(par, "dkg", np.tile(inp["diff_qk_g"][l][1], 2))
    par[:, PC["qlg"]:PC["qlg"] + 3] = inp["mla_q_lat_g"][l].reshape(3, 128).T
    par[:, PC["kvg"]:PC["kvg"] + 2] = inp["mla_kv_lat_g"][l].reshape(2, 128).T
    par[:, PC["npg"]:PC["npg"] + 2] = inp["mla_nope_g"][l].T
    for gi in range(2):
        g = inp["mla_rope_g"][l][gi]
        _col(par, "rpg", g, gi)
        _col(par, "rpgs", np.concatenate([g[32:], g[:32]]), gi)
    par[:, PC["mqg"]:PC["mqg"] + 2] = inp["mem_qk_g"][l].T
    par[:, PC["mng"]:PC["mng"] + 16] = inp["mem_norm_g"][l].reshape(16, 128).T
    invf = (10000.0 ** (-np.arange(0, 64, 2, dtype=np.float32) / 64)).astype(f)
    _col(par, "invf", np.tile(invf, 2))
    _col(par, "sgn", np.concatenate([-np.ones(32, f), np.ones(32, f)]))
    wuq = inp["mla_w_uq"][l]
    sw_idx = [h * 192 + 128 + (j + 32) % 64 for h in range(4) for j in range(64)]
    wukv = inp["mla_w_ukv"][l].reshape(256, 4, 256)
    common = {
        "par": par, "w": w,
        "wup": np.ascontiguousarray(inp["rw_w_up"][l].reshape(128, 512)),
        "aup": np.ascontiguousarray(inp["rw_a_up"][l].reshape(128, 512)),
        "wuq": _fm(wuq, 3), "wuqs": _fm(wuq[:, sw_idx], 3),
        "wukvk": _fm(np.ascontiguousarray(wukv[:, :, :128]).reshape(256, 512), 2),
        "wukvv": _fm(np.ascontiguousarray(wukv[:, :, 128:]).reshape(256, 512), 2),
        "memT": np.ascontiguousarray(inp["mem"][0].reshape(256, 16, 128).transpose(2, 1, 0)),
        "wkv": np.stack([_fm(inp["mem_w_kv"][l][:, ci * 128:(ci + 1) * 128], 16) for ci in range(8)]),
    }
    S = x_cur.shape[0]
    xpad = np.concatenate([np.zeros((1, 2048), f), x_cur, np.zeros((1, 2048), f)], 0)
    posv = inp["positions"][0]
    maps = []
    for c in range(NCORES):
        xt = np.empty((NT1, 128, 16, TH), f)
        pp = np.zeros((NT1, TH), np.int32)
        for t in range(NT1):
            n0 = c * (NT1 * TT) + t * TT
            xe = np.concatenate([xpad[n0 + 1:n0 + 1 + TT], xpad[n0:n0 + 1], xpad[n0 + 1 + TT:n0 + 2 + TT]], 0)
            xt[t] = xe.reshape(TH, 16, 128).transpose(2, 1, 0)
            pp[t, :TT] = posv[n0:n0 + TT]
        m = dict(common)
        m["xT"] = xt
        m["pos"] = pp
        maps.append(m)
    return maps


_NC_CACHE = {}


def _run(name, builder, maps):
    if name not in _NC_CACHE:
        _NC_CACHE[name] = builder()
    res = run_bass_kernel_spmd(_NC_CACHE[name], maps, core_ids=list(range(NCORES)))
    return res.results


def _cat(results, key, axis):
    return np.concatenate([r[key] for r in results], axis=axis)


SEQ = 8192
SC_CH = 32
SC_NV = 5


def build_scan(T=SEQ):
    b = B()
    P = b.P
    bc = b.din("bc", [2, T, SC_NV * 64])
    vcol = b.din("vcol", [128, T])
    y = b.dout("y", [128, T])
    nch = T // SC_CH
    W = SC_NV * 64
    bcb = b.sb("bcb", [128, 2, SC_CH * W])
    vc = b.sb("vc", [128, T])
    yt = b.sb("yt", [128, T])
    S = b.sb("S", [128, 64])
    Sd = b.sb("Sd", [128, 64])
    sa = b.sb("sa", [128, 1])
    b.ring("ja", 2, [128, 64])
    b.ring("jb", 2, [128, 64])
    b.load(vc[:], vcol, ["vc"], "vc")
    b.memset("dve", S[:], 0.0, ["S"])

    def load(c):
        sl = c % 2
        for pr in range(2):
            src = bc[pr, c * SC_CH:(c + 1) * SC_CH, :].rearrange("t f -> (t f)").partition_broadcast(64)
            b.load(bcb[pr * 64:(pr + 1) * 64, sl, :], src, [("bcb", sl, pr)], ("bc", sl, pr))

    load(0)
    for c in range(nch):
        if c + 1 < nch:
            load(c + 1)
        sl = c % 2
        rd = [("bcb", sl, 0), ("bcb", sl, 1)]
        for s in range(SC_CH):
            t = c * SC_CH + s
            o = s * W
            kap = bcb[:, sl, o:o + 64]
            nb = bcb[:, sl, o + 64:o + 128]
            kd = bcb[:, sl, o + 128:o + 192]
            rt = bcb[:, sl, o + 192:o + 256]
            dec = bcb[:, sl, o + 256:o + 320]
            ja, jar = b.nx("ja")
            P.op("dve", lambda e, kap=kap, ja=ja: e.scalar_tensor_tensor(
                out=ja[:], in0=S[:], scalar=1.0, in1=kap, op0=ALU.mult, op1=ALU.mult, accum_out=sa[:]),
                reads=rd + ["S"], writes=[jar, "sa"])
            b.tt("dve", Sd[:], S[:], dec, ALU.mult, rd + ["S"], ["Sd"])
            b.stt(Sd[:], nb, sa[:, 0:1], Sd[:], ALU.mult, ALU.add, rd + ["Sd", "sa"], ["Sd"])
            b.stt(S[:], kd, vc[:, t:t + 1], Sd[:], ALU.mult, ALU.add, rd + ["Sd", "vc"], ["S"])
            jb, jbr = b.nx("jb")
            P.op("dve", lambda e, rt=rt, jb=jb, t=t: e.scalar_tensor_tensor(
                out=jb[:], in0=S[:], scalar=1.0, in1=rt, op0=ALU.mult, op1=ALU.mult, accum_out=yt[:, t:t + 1]),
                reads=rd + ["S"], writes=[jbr, "yt"])
    b.store(y, yt[:], ["yt"])
    return b.finish()


def prep_scan(o_rw, T=SEQ):
    maps = []
    for h in range(NCORES):
        sl = slice(h * 64, (h + 1) * 64)
        bc = np.empty((2, T, SC_NV * 64), np.float32)
        for d in range(2):
            parts = [o_rw[2][sl], o_rw[3 + 3 * d][sl], o_rw[4 + 3 * d][sl], o_rw[0][sl], o_rw[5 + 3 * d][sl]]
            a = np.concatenate([p.T for p in parts], axis=1)
            bc[d] = a if d == 0 else a[::-1]
        v = o_rw[1][sl]
        vcol = np.concatenate([v, v[:, ::-1]], 0)
        maps.append({"bc": bc, "vcol": np.ascontiguousarray(vcol)})
    return maps


def post_scan(results):
    yf = np.concatenate([r["y"][0:64] for r in results], 0)
    yb = np.concatenate([r["y"][64:128][:, ::-1] for r in results], 0)
    return np.ascontiguousarray(yf), np.ascontiguousarray(yb)


NQ = SEQ // 2
NKT = SEQ // 128


def _t5_breaks():
    nb, max_exact = 16, 8
    n = np.arange(0, 1024, dtype=np.int32)
    n_f = np.maximum(n, max_exact).astype(np.float32)
    large = max_exact + (np.log(n_f / np.float32(max_exact)) / np.float32(math.log(128 / max_exact))
                         * np.float32(nb - max_exact)).astype(np.int32)
    large = np.minimum(large, nb - 1)
    f = np.where(n < max_exact, n, large)
    rels = np.arange(-1023, 1024)
    bk = np.where(rels > 0, 16, 0) + f[np.abs(rels)]
    order = [int(bk[0])]
    breaks = []
    for i in range(1, len(rels)):
        if bk[i] != bk[i - 1]:
            order.append(int(bk[i]))
            breaks.append(int(rels[i]))
    return order, breaks


T5_ORDER, T5_BREAKS = _t5_breaks()
NBK = len(T5_ORDER)


def build_attn(kind, b=None, pre=""):
    own = b is None
    if own:
        b = B()
        setup_consts(b)
        b.shared = {}
    P = b.P
    diff = kind == "diff"
    b.zmod = 1 if diff else 3
    qa_d = b.din(pre + "qa", [128, NQ])
    ka_d = b.din(pre + "ka", [128, SEQ])
    v_d = b.din(pre + "v", [128, NKT, 128])
    if diff:
        tab_d = b.din(pre + "tab", [1, NBK])
        lq_d = b.din(pre + "lq", [1, 256])
        cst_d = b.din(pre + "cst", [128, 2])
    else:
        qb_d = b.din(pre + "qb", [64, NQ])
        kb_d = b.din(pre + "kb", [64, SEQ])
    o_d = b.dout(pre + "o", [128, NQ])
    rka, rqa, rvv = "ka", "qa", "vv"
    if "rings" not in b.shared:
        b.shared[rka] = b.sb(rka, [128, SEQ], BF16)
        b.shared[rqa] = b.sb(rqa, [128, NQ], BF16)
        b.shared[rvv] = b.sb(rvv, [128, NKT, 128], BF16)
        b.shared["rings"] = True
        b.ring("stg", 2, [128, 2048])
        b.ring("t", 6, [128, 512])
        b.ring("pt", 6, [128, 512], BF16)
        b.ring("zacc", 2, [128, 512])
        b.ring("zacc2", 2, [128, 512])
        b.ring("o", 2, [128, 512])
    ka, qa, vv = b.shared[rka], b.shared[rqa], b.shared[rvv]

    def load_cast(dst, src, Pn, n, res):
        for i in range(0, n, 2048):
            st, sr = b.nx("stg")
            b.load(st[0:Pn, :], src[:, i:i + 2048], [sr], sr)
            b.cp("pool", dst[0:Pn, i:i + 2048], st[0:Pn, :], [sr], [res])

    load_cast(ka, ka_d, 128, SEQ, rka)
    load_cast(qa, qa_d, 128, NQ, rqa)
    load_cast(vv[:].rearrange("p a b -> p (a b)"), v_d.rearrange("p a b -> p (a b)"), 128, NKT * 128, rvv)
    if not diff:
        kb = b.sb("kb", [64, SEQ], BF16)
        qb = b.sb("qb", [64, NQ], BF16)
        load_cast(kb, kb_d, 64, SEQ, "kb")
        load_cast(qb, qb_d, 64, NQ, "qb")
    else:
        lq = b.sb("lq", [128, 256])
        cst = b.sb("cst", [128, 2])
        tab = b.sb("tab", [128, NBK])
        b.load(lq[:], lq_d[0, :].partition_broadcast(128), ["lq"], "lq")
        b.load(cst[:], cst_d, ["cst"], "cst")
        b.load(tab[:], tab_d[0, :].partition_broadcast(128), ["tab"], "tab")
        pr = b.sb("lpr", [128, 128])
        b.tt("dve", pr[:, 0:64], lq[:, 0:64], lq[:, 64:128], ALU.mult, ["lq"], ["lpr"])
        b.tt("dve", pr[:, 64:128], lq[:, 128:192], lq[:, 192:256], ALU.mult, ["lq"], ["lpr"])
        ls = b.sb("ls", [128, 4])
        P.op("dve", lambda e: e.tensor_reduce(out=ls[:, 0:1], in_=pr[:, 0:64], axis=AX.X, op=ALU.add), reads=["lpr"], writes=["ls"])
        P.op("dve", lambda e: e.tensor_reduce(out=ls[:, 1:2], in_=pr[:, 64:128], axis=AX.X, op=ALU.add), reads=["lpr"], writes=["ls"])
        b.act(ls[:, 0:2], ls[:, 0:2], AF.Exp, ["ls"], ["ls"])
        b.tt("dve", ls[:, 2:3], ls[:, 0:1], ls[:, 1:2], ALU.subtract, ["ls"], ["ls"])
        b.tt("dve", ls[:, 2:3], ls[:, 2:3], cst[:, 0:1], ALU.add, ["ls", "cst"], ["ls"])
        b.ts("dve", ls[:, 3:4], ls[:, 2:3], -1.0, None, ALU.mult, None, ["ls"], ["ls"])
        dl = b.sb("dl", [128, NBK])
        b.tt("dve", dl[:, 1:NBK], tab[:, 1:NBK], tab[:, 0:NBK - 1], ALU.subtract, ["tab"], ["dl"])
        SW = 1152
        reli = b.sb("reli", [128, SW], I32)
        relf = b.sb("relf", [128, SW])
        P.op("pool", lambda e: e.iota(reli[:], [[-1, SW]], base=512, channel_multiplier=1), writes=["reli"])
        b.cp("dve", relf[:], reli[:], ["reli"], ["relf"])
        strip = b.sb("strip", [128, SW])
        b.ring("stmp", 2, [128, SW])
        for k in range(1, NBK):
            tmp, tr = b.nx("stmp")
            b.ts("dve", tmp[:], relf[:], float(T5_BREAKS[k - 1]), dl[:, k:k + 1], ALU.is_ge, ALU.mult, ["relf", "dl"], [tr])
            if k == 1:
                b.ts("pool", strip[:], tmp[:], tab[:, 0:1], None, ALU.add, None, [tr, "tab"], ["strip"])
            else:
                b.tt("pool", strip[:], strip[:], tmp[:], ALU.add, [tr, "strip"], ["strip"])

    scale = (64 ** -0.5) if diff else (192 ** -0.5)
    b.psrot = [0, 1, 2, 3]
    for qt in range(NQ // 512):
        qs = slice(qt * 512, (qt + 1) * 512)
        res_list = []
        for s in range(2 if diff else 1):
            if diff:
                ps_ = slice(s * 64, (s + 1) * 64)
                qlist = [(qa[ps_, qs], rqa)]
                kts = lambda kt, ps_=ps_: [(ka[ps_, kt * 128:(kt + 1) * 128], rka)]

                def bias_fn(kt, qt=qt):
                    dk = kt - 4 * qt
                    if dk < -1:
                        return ("const", tab[:, 0:1])
                    if dk > 4:
                        return ("const", tab[:, NBK - 1:NBK])
                    return ("tile", (strip[:, (4 - dk) * 128:(4 - dk) * 128 + 512], "strip"))
            else:
                qlist = [(qa[:, qs], rqa), (qb[:, qs], "qb")]
                kts = lambda kt: [(ka[:, kt * 128:(kt + 1) * 128], rka), (kb[:, kt * 128:(kt + 1) * 128], "kb")]
            vts = lambda kt: (vv[:, kt, :], rvv)
            if diff:
                res_list.append(attn_core(b, qlist, kts, vts, NKT, scale, 128, bias_fn=bias_fn))
            else:
                res_list.append(attn_core(b, qlist, kts, vts, NKT, scale, 128))
        po, pz = res_list[0]
        rz, rzr = b.nx("t")
        b.recip(rz[:], b.ps[pz][:, :], [("ps", pz)], [rzr])
        o, orr = b.nx("o")
        b.tt("dve", o[:], b.ps[po][:, :], rz[:], ALU.mult, [("ps", po), rzr], [orr])
        if diff:
            po2, pz2 = res_list[1]
            rz2, rz2r = b.nx("t")
            b.recip(rz2[:], b.ps[pz2][:, :], [("ps", pz2)], [rz2r])
            o2, o2r = b.nx("t")
            b.tt("dve", o2[:], b.ps[po2][:, :], rz2[:], ALU.mult, [("ps", po2), rz2r], [o2r])
            b.stt(o[:], o2[:], ls[:, 3:4], o[:], ALU.mult, ALU.add, [o2r, "ls", orr], [orr])
        b.store(o_d[:, qs], o[:], [orr])
    if own:
        return b.finish()
    return None


def build_attn2():
    b = B()
    setup_consts(b)
    b.shared = {}
    build_attn("diff", b, "d_")
    build_attn("mla", b, "m_")
    return b.finish()


def _vtiles(v):
    return np.ascontiguousarray(v.reshape(-1, 128, 128).transpose(1, 0, 2))


def prep_attn_diff(inp, l, o_dq, o_dk, o_dv):
    lam_init = 0.8 - 0.6 * math.exp(-0.3 * l)
    maps = []
    for c in range(NCORES):
        h, half = c // 2, c % 2
        rows = slice(h * 128, (h + 1) * 128)
        q, k, v = o_dq[rows], o_dk[rows], o_dv[:, rows]
        tab = inp["rel_bias"][T5_ORDER, h]
        if half == 1:
            q, k, v, tab = q[:, ::-1], k[:, ::-1], v[::-1], tab[::-1]
        cst = np.zeros((128, 2), np.float32)
        cst[:, 0] = lam_init
        maps.append({"qa": np.ascontiguousarray(q[:, :NQ]), "ka": np.ascontiguousarray(k), "v": _vtiles(np.ascontiguousarray(v)),
                     "tab": np.ascontiguousarray(tab.reshape(1, NBK)).astype(np.float32),
                     "lq": np.ascontiguousarray(inp["diff_lambda"][l].reshape(1, 256)), "cst": cst})
    return maps


def post_attn_diff(results):
    out = np.empty((512, SEQ), np.float32)
    for c in range(NCORES):
        h, half = c // 2, c % 2
        o = results[c]["o"]
        if half == 0:
            out[h * 128:(h + 1) * 128, :NQ] = o
        else:
            out[h * 128:(h + 1) * 128, NQ:] = o[:, ::-1]
    return out


def prep_attn_mla(o_mqn, o_mqr, o_mkn, o_mkr, o_mv):
    maps = []
    for c in range(NCORES):
        h, half = c // 2, c % 2
        qs = slice(half * NQ, (half + 1) * NQ)
        maps.append({"qa": np.ascontiguousarray(o_mqn[h * 128:(h + 1) * 128, qs]),
                     "qb": np.ascontiguousarray(o_mqr[h * 64:(h + 1) * 64, qs]),
                     "ka": np.ascontiguousarray(o_mkn[h * 128:(h + 1) * 128]),
                     "kb": np.ascontiguousarray(o_mkr),
                     "v": _vtiles(np.ascontiguousarray(o_mv[:, h * 128:(h + 1) * 128]))})
    return maps


def post_attn_mla(results):
    out = np.empty((512, SEQ), np.float32)
    for c in range(NCORES):
        h, half = c // 2, c % 2
        out[h * 128:(h + 1) * 128, half * NQ:(half + 1) * NQ] = results[c]["o"]
    return out


PC3 = {}
_c = 0
for _n, _w in (("ng", 16), ("gng", 4), ("gnb", 4), ("subg", 1), ("lamf", 1)):
    PC3[_n] = _c
    _c += _w
NPAR3 = _c
NCH3 = 16 + 16 * 4 + 16


def build_p3():
    b = B()
    P = b.P
    xT = b.din("xT", [NT1, 128, 16, TT])
    par_d = b.din("par", [128, NPAR3])
    wA = b.din("wA", [NCH3, 128, 16, 128])
    wB = b.din("wB", [16, 128, 16, 128])
    br_d = b.din("br", [NT1, 128, 6, 4, TT])
    o_x = b.dout("o_x", [NT1, 128, 16, TT])
    setup_consts(b)
    par = b.sb("par", [128, NPAR3])
    b.load(par[:], par_d, ["par"], "par")
    pc = lambda n, i=0: par[:, PC3[n] + i:PC3[n] + i + 1]
    xz = b.sb("xz", [128, 16, TT])
    xs = xz
    zT = xz[:].rearrange("p a t -> p (a t)").bitcast(BF16).rearrange("p (n a t) -> p n a t", n=NT1, a=16)
    hT = b.sb("hT", [128, NT1, 16, TT], BF16)
    yg = b.sb("yg", [128, NT1, 16, TT], BF16)
    b.ring("brk", 2, [128, 6, TT])
    wstA = b.sb("wstA", [128, 2, 16, 128])
    wbfA = b.sb("wbfA", [128, 2, 16, 128], BF16)
    wstB = b.sb("wstB", [128, 1, 16, 128])
    wbfB = b.sb("wbfB", [128, 2, 16, 128], BF16)
    rsx = b.sb("rsx", [128, TT])
    zacc = b.sb("zacc", [128, NT1, TT])
    b.ring("t", 10, [128, TT])
    cntA = [0]

    def loadA(ci):
        sl = cntA[0] % 2
        cntA[0] += 1
        b.load(wstA[:, sl], wA[ci], [("wstA", sl)], ("wstA", sl))
        return sl

    ysrc = [0, 3, 4, 5]
    for tile in range(NT1):
        b.load(xs[:], xT[tile], ["xz"], "xz")
        pendA = loadA(0)
        pi = b.nps()
        for kc in range(16):
            sq, sqr = b.nx("t")
            b.act(sq[:], xs[:, kc, :], AF.Square, ["xz"], [sqr])
            b.mm(pi, 128, TT, b.ones[:], sq[:], kc == 0, kc == 15, [sqr, "ones"])
        ln, lnr = b.nx("t")
        b.act(ln[:], b.ps[pi][:, :], AF.Ln, [("ps", pi), "epsc0"], [lnr], bias=b.epsc[1e-6][:], scale=1.0 / 2048)
        b.act(rsx[:], ln[:], AF.Exp, [lnr], ["rsx"], scale=-0.5)
        for kc in range(16):
            b.stt(hT[:, tile, kc, :], xs[:, kc, :], pc("ng", kc), rsx[:], ALU.mult, ALU.mult, ["xz", "rsx", "par"], [("hT", tile)])
        for g in range(16):
            sl = pendA
            if g + 1 < 16:
                pendA = loadA(g + 1)
            b.wcast(wbfA, wstA, sl, "wstA", "wbfA")
            pi = b.nps()
            for kc in range(16):
                b.mm(pi, 128, TT, wbfA[:, sl, kc, :], hT[:, tile, kc, :], kc == 0, kc == 15, [("wbfA", sl), ("hT", tile)])
            b.act(yg[:, tile, g, :], b.ps[pi][:, :], AF.Silu, [("ps", pi)], [("yg", tile, g)])
        for kc in range(4):
            brk, brr = b.nx("brk")
            b.load(brk[:], br_d[tile, :, :, kc, :], [brr], brr)
            ys, ysr = b.nx("t")
            b.tt("pool", ys[:], brk[:, 0, :], brk[:, 1, :], ALU.add, [brr], [ysr])
            sq, sqr = b.nx("t")
            b.act(sq[:], ys[:], AF.Square, [ysr], [sqr])
            pm = b.nps()
            b.mm(pm, 128, TT, b.blk[:], ys[:], True, True, [ysr, "blk"])
            pe2 = b.nps()
            b.mm(pe2, 128, TT, b.blk[:], sq[:], True, True, [sqr, "blk"])
            mean, mr = b.nx("t")
            b.act(mean[:], b.ps[pm][:, :], AF.Copy, [("ps", pm)], [mr], scale=1.0 / 64)
            msq, msr = b.nx("t")
            b.act(msq[:], mean[:], AF.Square, [mr], [msr])
            var, vr = b.nx("t")
            b.stt(var[:], b.ps[pe2][:, :], 1.0 / 64, msq[:], ALU.mult, ALU.subtract, [("ps", pe2), msr], [vr])
            b.act(var[:], var[:], AF.Ln, [vr, "epsc1"], [vr], bias=b.epsc[64e-5][:])
            b.act(var[:], var[:], AF.Exp, [vr], [vr], scale=-0.5)
            b.tt("pool", ys[:], ys[:], mean[:], ALU.subtract, [ysr, mr], [ysr])
            b.stt(ys[:], ys[:], pc("gng", kc), var[:], ALU.mult, ALU.mult, [ysr, "par", vr], [ysr])
            b.stt(ys[:], ys[:], pc("gnb", kc), brk[:, 2, :], ALU.add, ALU.add, [ysr, "par", brr], [ysr])
            b.tt("dve", yg[:, tile, 0 * 4 + kc, :], yg[:, tile, 0 * 4 + kc, :], ys[:], ALU.mult, [ysr, ("yg", tile, kc)], [("yg", tile, kc)])
            sq2, sq2r = b.nx("t")
            b.act(sq2[:], brk[:, 3, :], AF.Square, [brr], [sq2r])
            rs, rsr = fm_rstd(b, [(sq2[:], sq2r)], b.ones[:], 128, TT, 1.0 / 128, 1e-6, "ones")
            yb_, ybr_ = b.nx("t")
            b.stt(yb_[:], brk[:, 3, :], pc("subg"), rs[:], ALU.mult, ALU.mult, [brr, "par", rsr], [ybr_])
            b.stt(yg[:, tile, 4 + kc, :], yb_[:], pc("lamf"), yg[:, tile, 4 + kc, :], ALU.mult, ALU.mult,
                  [ybr_, "par", ("yg", tile, 4 + kc)], [("yg", tile, 4 + kc)])
            b.tt("pool", yg[:, tile, 8 + kc, :], yg[:, tile, 8 + kc, :], brk[:, 4, :], ALU.mult, [brr, ("yg", tile, 8 + kc)], [("yg", tile, 8 + kc)])
            b.tt("pool", yg[:, tile, 12 + kc, :], yg[:, tile, 12 + kc, :], brk[:, 5, :], ALU.mult, [brr, ("yg", tile, 12 + kc)], [("yg", tile, 12 + kc)])
    ygr = lambda t: [("yg", t, g) for g in range(16)]
    nxt = 16
    pendA = loadA(nxt)
    nxt += 1
    for oc in range(16):
        slB = oc % 2
        b.load(wstB[:, 0], wB[oc], [("wstB", 0)], ("wstB", 0))
        b.wcast(wbfB, wstB, 0, "wstB", "wbfB", dsl=slB)
        for bi in range(4):
            sl = pendA
            if nxt < NCH3:
                pendA = loadA(nxt)
                nxt += 1
            b.wcast(wbfA, wstA, sl, "wstA", "wbfA")
            for tile in range(NT1):
                pm = b.nps()
                for kc in range(16):
                    b.mm(pm, 128, TT, wbfA[:, sl, kc, :], hT[:, tile, kc, :], kc == 0, kc == 15, [("wbfA", sl), ("hT", tile)])
                pb = b.nps()
                for kc in range(4):
                    b.mm(pb, 128, TT, wbfB[:, slB, bi * 4 + kc, :], yg[:, tile, bi * 4 + kc, :], kc == 0, kc == 3,
                         [("wbfB", slB), ("yg", tile, bi * 4 + kc)])
                sg, sgr = b.nx("t")
                b.act(sg[:], b.ps[pm][:, :], AF.Sigmoid, [("ps", pm)], [sgr])
                if bi == 0:
                    b.tt("dve", zacc[:, tile, :], b.ps[pb][:, :], sg[:], ALU.mult, [("ps", pb), sgr], [("zacc", tile)])
                else:
                    tmp, tr = b.nx("t")
                    b.tt("dve", tmp[:], b.ps[pb][:, :], sg[:], ALU.mult, [("ps", pb), sgr], [tr])
                    if bi < 3:
                        b.tt("pool", zacc[:, tile, :], zacc[:, tile, :], tmp[:], ALU.add, [("zacc", tile), tr], [("zacc", tile)])
                    else:
                        b.tt("pool", zT[:, tile, oc, :], zacc[:, tile, :], tmp[:], ALU.add, [("zacc", tile), tr], ["xz"])
    for oc in range(16):
        sl = pendA
        if nxt < NCH3:
            pendA = loadA(nxt)
            nxt += 1
        b.wcast(wbfA, wstA, sl, "wstA", "wbfA")
        for tile in range(NT1):
            po = b.nps()
            for kc in range(16):
                b.mm(po, 128, TT, wbfA[:, sl, kc, :], zT[:, tile, kc, :], kc == 0, kc == 15, [("wbfA", sl), "xz"])
            xr, xrr = b.nx("t")
            b.load(xr[:], xT[tile, :, oc, :], [xrr], xrr)
            xo, xor_ = b.nx("t")
            b.tt("dve", xo[:], b.ps[po][:, :], xr[:], ALU.add, [("ps", po), xrr], [xor_])
            b.store(o_x[tile, :, oc, :], xo[:], [xor_])
    return b.finish()


GM0 = 1792 + 1536 + 384 + 256 + 64 + 512


def prep_p3(inp, l, x_cur, ysf, ysb, bonus, ybr, yc, yd):
    f = np.float32
    w_in = inp["w_in"][l]
    chunks = []
    for g in range(16):
        chunks.append(_fm(w_in[:, GM0 + g * 128:GM0 + (g + 1) * 128], 16))
    M0 = GM0 + 2048
    for oc in range(16):
        for bi in range(4):
            c0 = M0 + bi * 2048 + oc * 128
            chunks.append(_fm(w_in[:, c0:c0 + 128], 16))
    for oc in range(16):
        chunks.append(_fm(inp["w_out"][l][:, oc * 128:(oc + 1) * 128], 16))
    wA = np.stack(chunks)
    wb = inp["w_branch"][l].reshape(2048, 2048)
    wB = np.stack([_fm(wb[:, oc * 128:(oc + 1) * 128], 16) for oc in range(16)])
    par = np.zeros((128, NPAR3), f)
    par[:, PC3["ng"]:PC3["ng"] + 16] = inp["norm_g"][l].reshape(16, 128).T
    par[:, PC3["gng"]:PC3["gng"] + 4] = inp["rw_gn_g"][l].reshape(4, 128).T
    par[:, PC3["gnb"]:PC3["gnb"] + 4] = inp["rw_gn_b"][l].reshape(4, 128).T
    par[:, PC3["subg"]] = inp["diff_sub_g"][l]
    par[:, PC3["lamf"]] = 1.0 - (0.8 - 0.6 * math.exp(-0.3 * l))
    maps = []
    ntok = NT1 * TT
    for c in range(NCORES):
        xt = np.empty((NT1, 128, 16, TT), f)
        brr = np.empty((NT1, 128, 6, 4, TT), f)
        for t in range(NT1):
            n0 = c * ntok + t * TT
            xt[t] = x_cur[n0:n0 + TT].reshape(TT, 16, 128).transpose(2, 1, 0)
            for i, a in enumerate((ysf, ysb, bonus, ybr, yc, yd)):
                brr[t, :, i] = a[:, n0:n0 + TT].reshape(4, 128, TT).transpose(1, 0, 2)
        maps.append({"xT": xt, "par": par, "wA": wA, "wB": wB, "br": brr})
    return maps


def post_p3(results):
    outs = []
    for r in results:
        o = r["o_x"]
        outs.append(o.transpose(0, 3, 2, 1).reshape(NT1 * TT, 2048))
    return np.ascontiguousarray(np.concatenate(outs, 0))


_P1_AXIS = {"o_dv": 0, "o_mv": 0, "o_rw": 2}


def kernel(**inputs):
    inp = {k: np.asarray(v) for k, v in inputs.items()}
    x = np.ascontiguousarray(inp["x"][0], dtype=np.float32)
    for l in range(4):
        r1 = _run("p1", build_p1, prep_p1(inp, l, x))
        o = {k: _cat(r1, k, _P1_AXIS.get(k, 1)) for k in r1[0]}
        del r1
        rs = _run("scan2", build_scan2, prep_scan2(o["o_rw"]))
        ysf, ysb = post_scan2(rs)
        del rs
        md = prep_attn_diff(inp, l, o["o_dq"], o["o_dk"], o["o_dv"])
        mm_ = prep_attn_mla(o["o_mqn"], o["o_mqr"], o["o_mkn"], o["o_mkr"], o["o_mv"])
        maps2 = [{**{"d_" + k: v for k, v in a.items()}, **{"m_" + k: v for k, v in c_.items()}}
                 for a, c_ in zip(md, mm_)]
        del md, mm_
        ra = _run("attn2", build_attn2, maps2)
        del maps2
        yb = post_attn_diff([{"o": r["d_o"]} for r in ra])
        yc = post_attn_mla([{"o": r["m_o"]} for r in ra])
        del ra
        r3 = _run("p3", build_p3, prep_p3(inp, l, x, ysf, ysb, o["o_bonus"], yb, yc, o["o_yd"]))
        x = post_p3(r3)
        del r3, o
    return x[None].astype(np.float32)


SB = 512
SG = 128
SC = 64


def build_scan2(T=SEQ, f32r=False):
    b = B()
    b.f32r = f32r
    P = b.P
    fm_d = b.din("fm", [64, 2, 5, T])
    v_d = b.din("v", [64, 2, T])
    cst_d = b.din("cst", [128, 4, 128])
    m01_d = b.din("m01", [64, 2 * SB])
    y_d = b.dout("y", [64, 2, T])
    nblk = T // SB
    NI = (SB // SG) * 2
    cst = b.sb("cst", [128, 4, 128])
    m01 = b.sb("m01", [64, 2 * SB])
    b.load(cst[:], cst_d, ["cst"], "cst")
    b.load(m01[:], m01_d, ["m01"], "m01")
    Ml, Mu, MuI, I_ = (cst[:, i, :] for i in range(4))
    fmb = b.sb("fmb", [64, 2, 5, SB])
    vb = b.sb("vb", [64, 2, SB])
    sc = {n: b.sb(n, [64, 2, SB]) for n in ("KT", "NB", "KD", "RT", "NB2", "KD2")}
    b.ring("e", 4, [64, 2, SB])
    gC = b.sb("gC", [64, 2, SB // SC])
    clend = b.sb("clend", [64, 2, SB // SC])
    Hs = b.sb("Hs", [64, 2, SB // SC + 1, 64])
    yb = b.sb("yb", [64, 2, SB])
    it_buf = []
    for i in range(NI):
        d = {}
        for n, shp in (("N0", [128, 128]), ("N1", [128, 128]), ("P0", [128, 128]), ("P1", [128, 128]),
                       ("X0", [128, 128]), ("X1", [128, 128]), ("AkT", [128, 128]), ("BkT", [128, 128]),
                       ("BnbT", [128, 128]), ("NBt", [128, 64]), ("KDt", [128, 64]), ("Vt", [128, 64]),
                       ("NB2t", [128, 64]), ("KD2t", [128, 64]),
                       ("WT", [64, 128]), ("U", [128, 64]), ("G1", [64, 2, 64]), ("G2", [64, 2, 64])):
            d[n] = b.sb(f"i{i}{n}", shp)
        it_buf.append(d)
    b.memset("dve", Hs[:, :, 0, :], 0.0, ["Hs"])

    def r_(i, n):
        return (f"i{i}", n)

    def bulk(blk):
        t0 = blk * SB
        b.load(fmb[:], fm_d[:, :, :, t0:t0 + SB], ["fmb"], "fmb")
        b.load(vb[:], v_d[:, :, t0:t0 + SB], ["vb"], "vb")
        lw = fmb[:, :, 4, :]
        cl, clr = b.nx("e")
        for p in range(2):
            P.op("dve", lambda e, p=p, cl=cl: e.tensor_tensor_scan(
                out=cl[:, p, :], data0=m01[:, 0:SB], data1=fmb[:, p, 4, :], initial=0.0, op0=ALU.mult, op1=ALU.add),
                reads=["fmb", "m01"], writes=[clr])
        for p in range(2):
            b.cp("pool", clend[:, p, :], cl[:, p, SC - 1:SB:SC], [clr], ["clend"])
        e1, e1r = b.nx("e")
        b.tt("pool", e1[:], cl[:], lw, ALU.subtract, [clr, "fmb"], [e1r])
        b.act(e1[:], e1[:], AF.Exp, [e1r], [e1r])
        b.tt("dve", sc["KT"][:], fmb[:, :, 0, :], e1[:], ALU.mult, ["fmb", e1r], ["KT"])
        e2, e2r = b.nx("e")
        b.act(e2[:], cl[:], AF.Exp, [clr], [e2r], scale=-1.0)
        b.tt("pool", sc["NB"][:], fmb[:, :, 1, :], e2[:], ALU.mult, ["fmb", e2r], ["NB"])
        b.tt("dve", sc["KD"][:], fmb[:, :, 2, :], e2[:], ALU.mult, ["fmb", e2r], ["KD"])
        e3, e3r = b.nx("e")
        b.act(e3[:], cl[:], AF.Exp, [clr], [e3r])
        b.tt("pool", sc["RT"][:], fmb[:, :, 3, :], e3[:], ALU.mult, ["fmb", e3r], ["RT"])
        for p in range(2):
            b.cp("pool", gC[:, p, :], e3[:, p, SC - 1:SB:SC], [e3r], ["gC"])
        e4, e4r = b.nx("e")
        for p in range(2):
            for c in range(SB // SC):
                b.act(e4[:, p, c * SC:(c + 1) * SC], cl[:, p, c * SC:(c + 1) * SC], AF.Exp, [clr, "clend"], [e4r],
                      bias=clend[:, p, c:c + 1], scale=-1.0)
        b.tt("dve", sc["NB2"][:], fmb[:, :, 1, :], e4[:], ALU.mult, ["fmb", e4r], ["NB2"])
        b.tt("pool", sc["KD2"][:], fmb[:, :, 2, :], e4[:], ALU.mult, ["fmb", e4r], ["KD2"])

    def evac_act(dst, pi, M, N, wr, scale=None):
        if scale is None:
            b.cp("act", dst, b.ps[pi][0:M, 0:N], [("ps", pi)], wr)
        else:
            b.act(dst, b.ps[pi][0:M, 0:N], AF.Copy, [("ps", pi), "gC"], wr, scale=scale)

    def transpose(pi, in_ap, K, M, rd):
        out = b.ps[pi][0:M, 0:K]
        P.op("pe", lambda e: e.transpose(out, in_ap, I_[0:K, 0:K]), reads=rd + ["cst"], writes=[("ps", pi)])

    def stage1(blk):
        for i in range(NI):
            g, p = i // 2, i % 2
            ts = slice(g * SG, (g + 1) * SG)
            bf = it_buf[i]
            KT, NB, KD, RT = (sc[n][:, p, ts] for n in ("KT", "NB", "KD", "RT"))
            for (la, ln), (ra, rn), msk, dst in (((KT, "KT"), (NB, "NB"), Ml, "N0"), ((NB, "NB"), (KT, "KT"), Mu, "P0"),
                                                 ((KD, "KD"), (KT, "KT"), Mu, "AkT"), ((KD, "KD"), (RT, "RT"), MuI, "BkT"),
                                                 ((NB, "NB"), (RT, "RT"), MuI, "BnbT")):
                pi = b.nps()
                b.mm(pi, 128, 128, la, ra, True, True, [ln, rn])
                b.tt("dve", bf[dst][:], b.ps[pi][:, 0:128], msk, ALU.mult, [("ps", pi), "cst"], [r_(i, dst)])
            for src, sn, dst, dcols in ((sc["KT"], "KT", "X0", slice(0, 64)), (sc["NB"], "NB", "NBt", slice(0, 64)),
                                        (sc["KD"], "KD", "KDt", slice(0, 64)), (vb, "vb", "Vt", slice(0, 64)),
                                        (sc["NB2"], "NB2", "NB2t", slice(0, 64)), (sc["KD2"], "KD2", "KD2t", slice(0, 64))):
                pi = b.nps()
                transpose(pi, src[:, p, ts], 64, 128, [sn])
                evac_act(bf[dst][:, dcols], pi, 128, 64, [r_(i, dst)])
            pi = b.nps()
            b.mm(pi, 128, 64, bf["AkT"][:], bf["Vt"][:], True, True, [r_(i, "AkT"), r_(i, "Vt")])
            evac_act(bf["X0"][:, 64:128], pi, 128, 64, [r_(i, "X0")])

    def stage2(blk):
        for it in range(6):
            cur, nxt = it % 2, (it + 1) % 2
            for i in range(NI):
                bf = it_buf[i]
                Nc, Pc, Xc = bf[f"N{cur}"], bf[f"P{cur}"], bf[f"X{cur}"]
                Nn, Pn, Xn = bf[f"N{nxt}"], bf[f"P{nxt}"], bf[f"X{nxt}"]
                pi = b.nps()
                b.mm(pi, 128, 128, Pc[:], Xc[:], True, True, [r_(i, f"P{cur}"), r_(i, f"X{cur}")])
                b.tt("dve", Xn[:], Xc[:], b.ps[pi][:, 0:128], ALU.add, [("ps", pi), r_(i, f"X{cur}")], [r_(i, f"X{nxt}")])
                if it < 5:
                    pi = b.nps()
                    b.mm(pi, 128, 128, Nc[:], Pc[:], True, True, [r_(i, f"N{cur}"), r_(i, f"P{cur}")])
                    evac_act(Pn[:], pi, 128, 128, [r_(i, f"P{nxt}")])
                if it < 4:
                    pi = b.nps()
                    b.mm(pi, 128, 128, Pc[:], Nc[:], True, True, [r_(i, f"N{cur}"), r_(i, f"P{cur}")])
                    evac_act(Nn[:], pi, 128, 128, [r_(i, f"N{nxt}")])

    def stage3(blk):
        for i in range(NI):
            bf = it_buf[i]
            X = bf["X0"]
            pi = b.nps()
            transpose(pi, X[:, 0:64], 128, 64, [r_(i, "X0")])
            evac_act(bf["WT"][:], pi, 64, 128, [r_(i, "WT")])
            for c in range(2):
                cs = slice(c * SC, (c + 1) * SC)
                pi = b.nps()
                b.mm(pi, 64, 64, X[cs, 0:64], bf["NB2t"][cs, :], True, True, [r_(i, "X0"), r_(i, "NB2t")])
                pdiag, pdr = b.nx("dg")
                g, p = i // 2, i % 2
                cg = g * 2 + c
                b.ts("pool", pdiag[:], I_[0:64, 0:64], gC[:, p, cg:cg + 1], None, ALU.mult, None, ["cst", "gC"], [pdr])
                b.tt("dve", bf["G1"][:, c, :], b.ps[pi][0:64, 0:64], pdiag[:], ALU.add, [("ps", pi), pdr], [r_(i, "G1")])
                pi = b.nps()
                b.mm(pi, 64, 64, bf["NB2t"][cs, :], X[cs, 64:128], True, False, [r_(i, "X0"), r_(i, "NB2t")])
                b.mm(pi, 64, 64, bf["KD2t"][cs, :], bf["Vt"][cs, :], False, True, [r_(i, "KD2t"), r_(i, "Vt")])
                evac_act(bf["G2"][:, c, :], pi, 64, 64, [r_(i, "G2")])

    def stage4(blk):
        nchunk = SB // SC
        for cg in range(nchunk):
            for p in range(2):
                i = (cg // 2) * 2 + p
                c = cg % 2
                bf = it_buf[i]
                pi = b.nps()
                b.mm(pi, 64, 64, bf["G1"][:, c, :], Hs[:, p, cg, :], True, False, [r_(i, "G1"), ("Hs", p)])
                b.mm(pi, 64, 64, I_[0:64, 0:64], bf["G2"][:, c, :], False, True, ["cst", r_(i, "G2")])
                b.cp("act", Hs[:, p, cg + 1, :], b.ps[pi][0:64, 0:64], [("ps", pi)], [("Hs", p)])

    def stage5(blk):
        t0 = blk * SB
        for i in range(NI):
            g, p = i // 2, i % 2
            bf = it_buf[i]
            X = bf["X0"]
            for c in range(2):
                cs = slice(c * SC, (c + 1) * SC)
                cg = g * 2 + c
                pi = b.nps()
                b.mm(pi, 128, 64, bf["WT"][:], Hs[:, p, cg, :], True, True, [r_(i, "WT"), ("Hs", p)])
                b.tt("dve", bf["U"][cs, :], b.ps[pi][cs, 0:64], X[cs, 64:128], ALU.add, [("ps", pi), r_(i, "X0")], [r_(i, "U")])
            pi = b.nps()
            b.mm(pi, 64, 128, bf["Vt"][:], bf["BkT"][:], True, False, [r_(i, "Vt"), r_(i, "BkT")])
            b.mm(pi, 64, 128, bf["U"][:], bf["BnbT"][:], False, False, [r_(i, "U"), r_(i, "BnbT")])
            for c in range(2):
                cg = g * 2 + c
                out = b.ps[pi][0:64, c * SC:(c + 1) * SC]
                lhsT = Hs[:, p, cg, :]
                rhs = sc["RT"][:, p, g * SG + c * SC:g * SG + (c + 1) * SC]
                P.op("pe", lambda e, out=out, lhsT=lhsT, rhs=rhs, c=c: e.matmul(out, lhsT, rhs, start=False, stop=(c == 1)),
                     reads=[("Hs", p), "RT"], writes=[("ps", pi)])
            b.cp("act", yb[:, p, g * SG:(g + 1) * SG], b.ps[pi][0:64, 0:128], [("ps", pi)], ["yb"])
        b.store(y_d[:, :, t0:t0 + SB], yb[:], ["yb"])
        if blk + 1 < nblk:
            b.cp("pool", Hs[:, :, 0, :], Hs[:, :, SB // SC, :], [("Hs", 0), ("Hs", 1)], [("Hs", 0), ("Hs", 1)])

    b.ring("dg", 4, [64, 64])
    for blk in range(nblk):
        bulk(blk)
        stage1(blk)
        stage2(blk)
        stage3(blk)
        stage4(blk)
        stage5(blk)
    return b.finish()


def _scan2_consts():
    G, C = SG, SC
    Ml = np.zeros((G, G), np.float32)
    for t in range(G):
        for s in range(G):
            if t // C == s // C and s < t:
                Ml[t, s] = 1
    cst = np.stack([Ml, Ml.T, Ml.T + np.eye(G, dtype=np.float32), np.eye(G, dtype=np.float32)], 1)
    m01 = np.ones((64, 2 * SB), np.float32)
    m01[:, ::C] = 0
    return np.ascontiguousarray(cst), m01


def prep_scan2(o_rw, T=SEQ):
    cst, m01 = _scan2_consts()
    maps = []
    for h in range(NCORES):
        sl = slice(h * 64, (h + 1) * 64)
        fm = np.empty((64, 2, 5, T), np.float32)
        v = np.empty((64, 2, T), np.float32)
        for d in range(2):
            for k, a in enumerate((o_rw[2][sl], o_rw[3 + 3 * d][sl], o_rw[4 + 3 * d][sl], o_rw[0][sl], o_rw[5 + 3 * d][sl])):
                fm[:, d, k] = a if d == 0 else a[:, ::-1]
            v[:, d] = o_rw[1][sl] if d == 0 else o_rw[1][sl][:, ::-1]
        maps.append({"fm": fm, "v": v, "cst": cst, "m01": m01})
    return maps


def post_scan2(results):
    yf = np.concatenate([r["y"][:, 0] for r in results], 0)
    yb = np.concatenate([r["y"][:, 1][:, ::-1] for r in results], 0)
    return np.ascontiguousarray(yf), np.ascontiguousarray(yb)
```
